# Optimizing a Trainium2 kernel written in Bass

```python
import jax, jax.numpy as jnp
from jax import lax
import numpy as np

D_MODEL = 2048
BATCH = 1
SEQ = 16384
DEPTH = 2

NSA_HEADS = 8
NSA_GROUPS = 2
NSA_HPG = NSA_HEADS // NSA_GROUPS
NSA_HD = 128
CMP_LEN = 32
CMP_STRIDE = 16
CMP_HID = 256
SLC_LEN = 64
SLC_TOPK = 16
WINDOW = 512
Q_BLOCK = 128
ML_HEADS = 4
ML_DQK = 128
ML_DV = 256
ML_CHUNK = 64
ML_CONV = 4
D_FF = 5632
FFN_CONV = 3
ALPHA = (2 * DEPTH) ** 0.25
BETA = (8 * DEPTH) ** -0.25
LN_EPS = 1e-5
NEG_INF = -1e30
IN_SIZES = (NSA_HEADS * NSA_HD, 3 * 2 * NSA_GROUPS * NSA_HD, 3 * NSA_HEADS,
            2 * ML_HEADS * ML_DQK, ML_HEADS * ML_DV, 2 * ML_HEADS, ML_HEADS * ML_DV, 2 * D_MODEL)
IN_COLS = (NSA_HEADS * NSA_HD + 3 * 2 * NSA_GROUPS * NSA_HD + 3 * NSA_HEADS
           + 2 * ML_HEADS * ML_DQK + ML_HEADS * ML_DV + 2 * ML_HEADS + ML_HEADS * ML_DV + 2 * D_MODEL)

kernel_name = "nsa_mlstm_convffn_hybrid"


def layer_norm(x, gain=None, bias=None):
    xf = x.astype(jnp.float32)
    mu = jnp.mean(xf, axis=-1, keepdims=True)
    var = jnp.mean(jnp.square(xf - mu), axis=-1, keepdims=True)
    y = (xf - mu) * lax.rsqrt(var + LN_EPS)
    if gain is not None:
        y = y * gain.astype(jnp.float32) + bias.astype(jnp.float32)
    return y.astype(x.dtype)


def masked_softmax(s, mask):
    p = jax.nn.softmax(jnp.where(mask, s, NEG_INF), axis=-1)
    return jnp.where(mask, p, 0.0)


def causal_dwconv(x, w, b):
    k, ch = w.shape
    y = lax.conv_general_dilated(x, w[:, None, :].astype(x.dtype), window_strides=(1,),
                                 padding=[(k - 1, 0)], dimension_numbers=('NWC', 'WIO', 'NWC'),
                                 feature_group_count=ch)
    return y + b.astype(x.dtype)


def compress_blocks(k, pe, w1, w2):
    b, s, g, hd = k.shape
    n_ch = s // CMP_STRIDE
    r = CMP_LEN // CMP_STRIDE
    ch = k.reshape(b, n_ch, CMP_STRIDE, g, hd)
    blocks = jnp.concatenate([ch[:, j:n_ch - r + 1 + j] for j in range(r)], axis=2)
    blocks = blocks + pe[None, None, :, None, :].astype(k.dtype)
    flat = blocks.transpose(0, 1, 3, 2, 4).reshape(b, n_ch - r + 1, g, CMP_LEN * hd)
    return jax.nn.gelu(flat @ w1) @ w2


def nsa_attention(q, kv, gate_raw, cmp_pe, cmp_w1, cmp_w2):
    b, s, _ = q.shape
    q = q.reshape(b, s, NSA_GROUPS, NSA_HPG, NSA_HD)
    kv = kv.reshape(b, s, 3, 2, NSA_GROUPS, NSA_HD)
    gates = jax.nn.sigmoid(gate_raw).reshape(b, s, NSA_GROUPS, NSA_HPG, 3)
    scale = NSA_HD ** -0.5

    kc = compress_blocks(kv[:, :, 0, 0], cmp_pe[0], cmp_w1[0], cmp_w2[0])
    vc = compress_blocks(kv[:, :, 0, 1], cmp_pe[1], cmp_w1[1], cmp_w2[1])
    n_cmp = kc.shape[1]
    n_slc = s // SLC_LEN
    ks = kv[:, :, 1, 0].reshape(b, n_slc, SLC_LEN, NSA_GROUPS, NSA_HD).transpose(0, 3, 1, 2, 4)
    vs = kv[:, :, 1, 1].reshape(b, n_slc, SLC_LEN, NSA_GROUPS, NSA_HD).transpose(0, 3, 1, 2, 4)
    pad = ((0, 0), (WINDOW, 0), (0, 0), (0, 0))
    kw = jnp.pad(kv[:, :, 2, 0], pad)
    vw = jnp.pad(kv[:, :, 2, 1], pad)

    c_start = jnp.arange(n_cmp) * CMP_STRIDE
    s_start = jnp.arange(n_slc) * SLC_LEN
    overlap = ((c_start[:, None] < s_start[None, :] + SLC_LEN)
               & (c_start[:, None] + CMP_LEN > s_start[None, :])).astype(jnp.float32)
    n_sel = min(SLC_TOPK, n_slc)
    n_blk = s // Q_BLOCK
    q_blocks = q.reshape(b, n_blk, Q_BLOCK, NSA_GROUPS, NSA_HPG, NSA_HD).swapaxes(0, 1)
    g_blocks = gates.reshape(b, n_blk, Q_BLOCK, NSA_GROUPS, NSA_HPG, 3).swapaxes(0, 1)
    bidx = jnp.arange(b)[:, None, None, None]
    gidx = jnp.arange(NSA_GROUPS)[None, :, None, None]
    slc_ids = jnp.arange(n_slc)

    def block(args):
        blk, qb, gb = args
        t = blk * Q_BLOCK + jnp.arange(Q_BLOCK)
        s_c = jnp.einsum('bqghd,bngd->bghqn', qb, kc, preferred_element_type=jnp.float32) * scale
        mask_c = (c_start + CMP_LEN - 1)[None, :] <= t[:, None]
        p_c = masked_softmax(s_c, mask_c)
        o_c = jnp.einsum('bghqn,bngd->bqghd', p_c, vc)
        imp = jnp.sum(p_c, axis=2) @ overlap
        cur = t // SLC_LEN
        forced = (slc_ids[None, :] == 0) | (slc_ids[None, :] == cur[:, None]) | (slc_ids[None, :] == cur[:, None] - 1)
        valid = s_start[None, :] <= t[:, None]
        score = jnp.where(forced, 1e6, jnp.where(valid, imp, -1.0))
        idx = lax.top_k(score, n_sel)[1]
        k_sel = ks[bidx, gidx, idx]
        v_sel = vs[bidx, gidx, idx].reshape(b, NSA_GROUPS, Q_BLOCK, n_sel * SLC_LEN, NSA_HD)
        s_s = jnp.einsum('bqghd,bgqnld->bghqnl', qb, k_sel, preferred_element_type=jnp.float32) * scale
        s_s = s_s.reshape(b, NSA_GROUPS, NSA_HPG, Q_BLOCK, n_sel * SLC_LEN)
        pos = idx[..., None] * SLC_LEN + jnp.arange(SLC_LEN)
        mask_s = (pos <= t[None, None, :, None, None]).reshape(b, NSA_GROUPS, 1, Q_BLOCK, n_sel * SLC_LEN)
        p_s = masked_softmax(s_s, mask_s)
        o_s = jnp.einsum('bghqm,bgqmd->bqghd', p_s, v_sel)
        kwb = lax.dynamic_slice_in_dim(kw, blk * Q_BLOCK, WINDOW + Q_BLOCK, axis=1)
        vwb = lax.dynamic_slice_in_dim(vw, blk * Q_BLOCK, WINDOW + Q_BLOCK, axis=1)
        kpos = blk * Q_BLOCK - WINDOW + jnp.arange(WINDOW + Q_BLOCK)
        mask_w = (kpos[None, :] <= t[:, None]) & (kpos[None, :] > t[:, None] - WINDOW) & (kpos[None, :] >= 0)
        s_w = jnp.einsum('bqghd,bkgd->bghqk', qb, kwb, preferred_element_type=jnp.float32) * scale
        p_w = masked_softmax(s_w, mask_w)
        o_w = jnp.einsum('bghqk,bkgd->bqghd', p_w, vwb)
        gbf = gb.astype(jnp.float32)
        return gbf[..., 0:1] * o_c + gbf[..., 1:2] * o_s + gbf[..., 2:3] * o_w

    out = lax.map(block, (jnp.arange(n_blk), q_blocks, g_blocks))
    return out.swapaxes(0, 1).reshape(b, s, NSA_HEADS * NSA_HD).astype(q.dtype)


def mlstm(qk, v, if_raw, o_raw, conv_w, conv_b, gate_b, norm_g):
    b, s, _ = qk.shape
    qk = jax.nn.silu(causal_dwconv(qk, conv_w, conv_b))
    q, k = jnp.split(qk, 2, axis=-1)
    gi = (if_raw + gate_b).astype(jnp.float32)
    i_pre = gi[..., :ML_HEADS]
    log_f = jax.nn.log_sigmoid(gi[..., ML_HEADS:])
    nc = s // ML_CHUNK

    def chunks(a, d):
        return a.reshape(b, nc, ML_CHUNK, ML_HEADS, d).transpose(1, 0, 3, 2, 4).astype(jnp.float32)

    def gchunks(a):
        return a.reshape(b, nc, ML_CHUNK, ML_HEADS).transpose(1, 0, 3, 2)

    qc = chunks(q, ML_DQK)
    kc = chunks(k, ML_DQK) * (ML_DQK ** -0.5)
    vc = chunks(v, ML_DV)
    causal = jnp.tril(jnp.ones((ML_CHUNK, ML_CHUNK), dtype=bool))

    def body(carry, xs):
        c_st, n_st, m_st = carry
        qt, kt, vt, ig, lf = xs
        bcum = jnp.cumsum(lf, axis=-1)
        dmat = jnp.where(causal, bcum[..., :, None] - bcum[..., None, :] + ig[..., None, :], NEG_INF)
        inter = bcum + m_st[..., None]
        m_t = jnp.maximum(inter, jnp.max(dmat, axis=-1))
        w_inter = jnp.exp(inter - m_t)
        smat = jnp.einsum('bhtd,bhsd->bhts', qt, kt) * jnp.exp(dmat - m_t[..., None])
        num = w_inter[..., None] * jnp.einsum('bhtd,bhvd->bhtv', qt, c_st) + jnp.einsum('bhts,bhsv->bhtv', smat, vt)
        den = w_inter * jnp.einsum('bhtd,bhd->bht', qt, n_st) + jnp.sum(smat, axis=-1)
        h = num / jnp.maximum(jnp.abs(den), jnp.exp(-m_t))[..., None]
        b_last = bcum[..., -1]
        a = b_last[..., None] - bcum + ig
        m_new = jnp.maximum(b_last + m_st, jnp.max(a, axis=-1))
        decay = jnp.exp(b_last + m_st - m_new)
        wa = jnp.exp(a - m_new[..., None])
        c_new = decay[..., None, None] * c_st + jnp.einsum('bhs,bhsv,bhsd->bhvd', wa, vt, kt)
        n_new = decay[..., None] * n_st + jnp.einsum('bhs,bhsd->bhd', wa, kt)
        return (c_new, n_new, m_new), h

    init = (jnp.zeros((b, ML_HEADS, ML_DV, ML_DQK), jnp.float32),
            jnp.zeros((b, ML_HEADS, ML_DQK), jnp.float32),
            jnp.zeros((b, ML_HEADS), jnp.float32))
    _, h = lax.scan(body, init, (qc, kc, vc, gchunks(i_pre), gchunks(log_f)))
    h = h.transpose(1, 0, 3, 2, 4).reshape(b, s, ML_HEADS, ML_DV)
    mu = jnp.mean(h, axis=-1, keepdims=True)
    var = jnp.mean(jnp.square(h - mu), axis=-1, keepdims=True)
    h = ((h - mu) * lax.rsqrt(var + LN_EPS)).reshape(b, s, ML_HEADS * ML_DV) * norm_g.astype(jnp.float32)
    return (h * jax.nn.sigmoid(o_raw.astype(jnp.float32))).astype(qk.dtype)


def token_mixer(h, w_in, cmp_pe, cmp_w1, cmp_w2, ml_conv_w, ml_conv_b, ml_gate_b, ml_norm_g,
                w_br_nsa, w_br_ml, w_o):
    proj = h @ w_in
    offs = np.cumsum(IN_SIZES)[:-1].tolist()
    nsa_q, nsa_kv, nsa_g, ml_qk, ml_v, ml_if, ml_o, merge_g = jnp.split(proj, offs, axis=-1)
    y_nsa = nsa_attention(nsa_q, nsa_kv, nsa_g, cmp_pe, cmp_w1, cmp_w2)
    y_ml = mlstm(ml_qk, ml_v, ml_if, ml_o, ml_conv_w, ml_conv_b, ml_gate_b, ml_norm_g)
    g_nsa, g_ml = jnp.split(jax.nn.sigmoid(merge_g), 2, axis=-1)
    merged = g_nsa * (y_nsa @ w_br_nsa) + g_ml * (y_ml @ w_br_ml)
    return merged @ w_o


def conv_ffn(h, w_up, conv_w, conv_b, w_down):
    a, g = jnp.split(h @ w_up, 2, axis=-1)
    a = causal_dwconv(a, conv_w, conv_b)
    return (jax.nn.silu(a) * g) @ w_down


def setup_inputs(seed: int = 0) -> dict:
    key = jax.random.key(seed)
    ks = jax.random.split(key, 24)

    def nrm(k, shape, scale):
        return jax.random.normal(k, shape, jnp.float32) * scale

    d = D_MODEL
    ml_qk_w = 2 * ML_HEADS * ML_DQK
    f_bias = jnp.linspace(3.0, 6.0, ML_HEADS)[None, :] + nrm(ks[9], (DEPTH, ML_HEADS), 0.1)
    i_bias = nrm(ks[10], (DEPTH, ML_HEADS), 0.1)
    return {
        "x": nrm(ks[0], (BATCH, SEQ, d), 1.0),
        "c": nrm(ks[1], (BATCH, d), 1.0),
        "w_ada": nrm(ks[2], (DEPTH, d, 6 * d), d ** -0.5),
        "b_ada": nrm(ks[3], (DEPTH, 6 * d), 0.01),
        "w_in": nrm(ks[4], (DEPTH, d, IN_COLS), d ** -0.5),
        "cmp_pe": nrm(ks[5], (DEPTH, 2, CMP_LEN, NSA_HD), 0.02),
        "cmp_w1": nrm(ks[6], (DEPTH, 2, CMP_LEN * NSA_HD, CMP_HID), (CMP_LEN * NSA_HD) ** -0.5),
        "cmp_w2": nrm(ks[7], (DEPTH, 2, CMP_HID, NSA_HD), CMP_HID ** -0.5),
        "ml_conv_w": nrm(ks[8], (DEPTH, ML_CONV, ml_qk_w), ML_CONV ** -0.5),
        "ml_conv_b": nrm(ks[11], (DEPTH, ml_qk_w), 0.01),
        "ml_gate_b": jnp.concatenate([i_bias, f_bias], axis=-1),
        "ml_norm_g": 1.0 + nrm(ks[12], (DEPTH, ML_HEADS * ML_DV), 0.02),
        "w_br_nsa": nrm(ks[13], (DEPTH, NSA_HEADS * NSA_HD, d), (NSA_HEADS * NSA_HD) ** -0.5),
        "w_br_ml": nrm(ks[14], (DEPTH, ML_HEADS * ML_DV, d), (ML_HEADS * ML_DV) ** -0.5),
        "w_o": nrm(ks[15], (DEPTH, d, d), BETA * d ** -0.5),
        "w_up": nrm(ks[16], (DEPTH, d, 2 * D_FF), d ** -0.5),
        "ffn_conv_w": nrm(ks[17], (DEPTH, FFN_CONV, D_FF), FFN_CONV ** -0.5),
        "ffn_conv_b": nrm(ks[18], (DEPTH, D_FF), 0.01),
        "w_down": nrm(ks[19], (DEPTH, D_FF, d), BETA * D_FF ** -0.5),
        "ln_g": 1.0 + nrm(ks[20], (DEPTH, 2, d), 0.02),
        "ln_b": nrm(ks[21], (DEPTH, 2, d), 0.01),
    }


def reference(x, c, w_ada, b_ada, w_in, cmp_pe, cmp_w1, cmp_w2, ml_conv_w, ml_conv_b, ml_gate_b,
              ml_norm_g, w_br_nsa, w_br_ml, w_o, w_up, ffn_conv_w, ffn_conv_b, w_down, ln_g, ln_b):
    c_act = jax.nn.silu(c)
    for l in range(DEPTH):
        mod = (c_act @ w_ada[l] + b_ada[l])[:, None, :]
        sh1, sc1, g1, sh2, sc2, g2 = jnp.split(mod, 6, axis=-1)
        h = layer_norm(x) * (1.0 + sc1) + sh1
        y = token_mixer(h, w_in[l], cmp_pe[l], cmp_w1[l], cmp_w2[l], ml_conv_w[l], ml_conv_b[l],
                        ml_gate_b[l], ml_norm_g[l], w_br_nsa[l], w_br_ml[l], w_o[l])
        x = layer_norm(ALPHA * x + g1 * y, ln_g[l, 0], ln_b[l, 0])
        h = layer_norm(x) * (1.0 + sc2) + sh2
        y = conv_ffn(h, w_up[l], ffn_conv_w[l], ffn_conv_b[l], w_down[l])
        x = layer_norm(ALPHA * x + g2 * y, ln_g[l, 1], ln_b[l, 1])
    return x
```

```python
import ml_dtypes
from concourse.bass_utils import run_bass_kernel_spmd
import numpy as np
from contextlib import ExitStack
import concourse.bass as bass
import concourse.mybir as mybir

F32 = mybir.dt.float32
BF16 = mybir.dt.bfloat16
I32 = mybir.dt.int32
ALU = mybir.AluOpType
AF = mybir.ActivationFunctionType
AX = mybir.AxisListType

ENGS = ("pe", "act", "dve", "pool", "sp")


class Tile:
    def __init__(self, name, handle, space):
        self.name = name
        self.h = handle
        self.space = space
        self.last_w = None
        self.readers = {}
        self.dsem = None
        self.dcnt = 0
        self.ssem = None
        self.scnt = 0

    def __getitem__(self, idx):
        return self.h[idx]

    def ap(self):
        return self.h.ap() if hasattr(self.h, "ap") else self.h[:]


class Sub:
    def __init__(self, parent, idx):
        self.parent = parent
        self.idx = idx

    def __getitem__(self, i):
        return self.parent.h[self.idx][i]


def _par(t):
    return t.parent if isinstance(t, Sub) else t


class Prog:
    def __init__(self, nc):
        self.nc = nc
        self.es = ExitStack()
        self.prog = {e: [] for e in ENGS}
        self.sem = {}
        self.cnt = {e: 0 for e in ENGS}
        self.waited = {e: {} for e in ENGS}
        self.nsem = 0
        for e in ENGS:
            self.sem[e] = self._newsem("e_" + e)
        self.store_tiles = []
        self.n_inst = 0
        self.inherit = {}
        self.scope_tiles = None

    def _newsem(self, name):
        self.nsem += 1
        return self.nc.alloc_semaphore(name=name)

    def sb(self, name, shape, dtype):
        h = self.es.enter_context(self.nc.sbuf_tensor("s_" + name, list(shape), dtype))
        t = Tile(name, h, "sb")
        t.readers = dict(self.inherit)
        if self.scope_tiles is not None:
            self.scope_tiles.append(t)
        return t

    def scope_begin(self):
        self._saved_es = self.es
        self.es = ExitStack()
        self.scope_tiles = []

    def scope_end(self):
        for t in self.scope_tiles:
            toks = list(t.readers.values()) + ([t.last_w] if t.last_w is not None else [])
            for tok in toks:
                cur = self.inherit.get(tok[0])
                if cur is None or cur[2] < tok[2]:
                    self.inherit[tok[0]] = tok
        self.es.close()
        self.es = self._saved_es
        self.scope_tiles = None

    def ps(self, name, shape, dtype):
        h = self.es.enter_context(self.nc.psum_tensor("p_" + name, list(shape), dtype))
        return Tile(name, h, "ps")

    def dram(self, name, shape, dtype, kind):
        h = self.nc.dram_tensor(name, list(shape), dtype, kind=kind)
        return Tile(name, h, "dram")

    def _deps(self, eng, reads, writes, is_dma=False, dma_tile=None):
        deps = []
        for t in reads:
            if t.last_w is not None:
                deps.append((t.last_w, "raw"))
        for t in writes:
            if t.last_w is not None:
                deps.append((t.last_w, "waw"))
            for tok in t.readers.values():
                deps.append((tok, "war"))
        waits = []
        for (semkey, semh, val, src), kind in deps:
            if src == eng and not is_dma:
                if eng == "pe":
                    continue
                if kind == "war":
                    continue
            if is_dma and kind == "waw" and src == "dma" and dma_tile is not None and semkey == id(dma_tile.dsem):
                continue
            if self.waited[eng].get(semkey, 0) >= val:
                continue
            self.waited[eng][semkey] = val
            waits.append((semh, val))
        return waits

    def op(self, eng, fn, reads=(), writes=()):
        reads = [_par(t) for t in reads]
        writes = [_par(t) for t in writes]
        waits = self._deps(eng, reads, writes)
        self.cnt[eng] += 1
        tok = (id(self.sem[eng]), self.sem[eng], self.cnt[eng], eng)
        self.prog[eng].append((waits, fn, self.sem[eng], 1))
        for t in reads:
            t.readers[tok[0]] = tok
        for t in writes:
            t.last_w = tok
            t.readers = {}
        self.n_inst += 1

    def dma(self, q, out_ap, in_ap, reads=(), writes=(), **kw):
        if writes:
            t = writes[0]
            if t.dsem is None:
                t.dsem = self._newsem("d_" + t.name)
            waits = self._deps(q, reads, writes, is_dma=True, dma_tile=t)
            t.dcnt += 16
            tok = (id(t.dsem), t.dsem, t.dcnt, "dma")
            sem = t.dsem
        else:
            t = reads[0]
            if t.ssem is None:
                t.ssem = self._newsem("s_" + t.name)
                self.store_tiles.append(t)
            waits = self._deps(q, reads, writes, is_dma=True)
            t.scnt += 16
            tok = (id(t.ssem), t.ssem, t.scnt, "dma")
            sem = t.ssem

        def fn(e, out_ap=out_ap, in_ap=in_ap, kw=kw):
            return e.dma_start(out=out_ap, in_=in_ap, **kw)

        self.prog[q].append((waits, fn, sem, 16))
        for r in reads:
            r.readers[tok[0]] = tok
        for w in writes:
            w.last_w = tok
            w.readers = {}
        self.n_inst += 1

    def finish(self):
        waits = []
        for t in self.store_tiles:
            waits.append((t.ssem, t.scnt))
        for e in ENGS:
            if e != "sp" and self.cnt[e] > 0:
                waits.append((self.sem[e], self.cnt[e]))
        self.prog["sp"].append((waits, None, None, 0))

    def emit(self):
        nc = self.nc
        prog = self.prog

        def replay(lst, e):
            for waits, fn, sem, inc in lst:
                for semh, val in waits:
                    e.wait_ge(semh, val)
                if fn is not None:
                    ins = fn(e)
                    ins.then_inc(sem, inc)

        with nc.Block() as block:
            @block.tensor
            def _(e):
                replay(prog["pe"], e)

            @block.scalar
            def _(e):
                replay(prog["act"], e)

            @block.vector
            def _(e):
                replay(prog["dve"], e)

            @block.gpsimd
            def _(e):
                replay(prog["pool"], e)

            @block.sync
            def _(e):
                replay(prog["sp"], e)
        self.es.close()


D = 2048
S = 16384
NCORE = 8
NT = S // NCORE
KC = D // 128
IN_COLS = 9760
D_FF = 5632
EPS = 1e-5
ALPHA = 4.0 ** 0.25


def dview(t, c0, c1):
    return t.h.ap()[:, c0:c1].rearrange("(kc p) n -> p kc n", p=128)


def mod_vectors(P, cT, wada, bada, ncols, stage, ps_row, ps_t, one1, cw=512):
    nch = ncols // cw
    sub = cw // 128
    cact = P.sb("cact", [128, KC], F32)
    brow = [P.sb(f"brow{i}", [1, cw], F32) for i in range(2)]
    mrow = [P.sb(f"mrow{i}", [1, cw], F32) for i in range(2)]
    modT = P.sb("modT", [128, ncols // 128], F32)
    P.dma("sp", cact[:], cT[:], writes=[cact])
    P.op("act", lambda e: e.activation(out=cact[:], in_=cact[:], func=AF.Silu), reads=[cact], writes=[cact])
    for j in range(nch):
        w = stage[j % 2]
        br = brow[j % 2]
        mr = mrow[j % 2]
        P.dma("sp", w[:], dview(wada, j * cw, (j + 1) * cw), writes=[w])
        P.dma("sp", br[:], bada.h.ap()[:, j * cw:(j + 1) * cw], writes=[br])
        for kc in range(KC):
            P.op("pe", lambda e, w=w, kc=kc: e.matmul(ps_row[0:1, 0:cw], lhsT=cact[:, kc:kc + 1], rhs=w[:, kc, :],
                                                      start=(kc == 0), stop=(kc == KC - 1)),
                 reads=[cact, w], writes=[ps_row])
        P.op("dve", lambda e, mr=mr, br=br: e.tensor_tensor(out=mr[0:1, :], in0=ps_row[0:1, 0:cw], in1=br[0:1, :], op=ALU.add),
             reads=[ps_row, br], writes=[mr])
        for c in range(sub):
            P.op("pe", lambda e, c=c, j=j, mr=mr: e.matmul(ps_t[:, sub * j + c:sub * j + c + 1], lhsT=mr[0:1, c * 128:(c + 1) * 128],
                                                          rhs=one1[0:1, 0:1], start=True, stop=True),
                 reads=[mr, one1], writes=[ps_t])
    P.op("dve", lambda e: e.tensor_copy(out=modT[:], in_=ps_t[:, 0:ncols // 128]), reads=[ps_t], writes=[modT])
    return modT


def ln_stats(P, z, sq, nkc, W, onesN, ps_a, ps_b, mean, rstd, tmpm):
    P.op("act", lambda e: e.activation(out=sq[:, 0:nkc, 0:W], in_=z[:, 0:nkc, 0:W], func=AF.Square), reads=[z], writes=[sq])
    for kc in range(nkc):
        P.op("pe", lambda e, kc=kc: e.matmul(ps_a[:, 0:W], lhsT=onesN[:], rhs=z[:, kc, 0:W], start=(kc == 0), stop=(kc == nkc - 1)),
             reads=[onesN, z], writes=[ps_a])
    for kc in range(nkc):
        P.op("pe", lambda e, kc=kc: e.matmul(ps_b[:, 0:W], lhsT=onesN[:], rhs=sq[:, kc, 0:W], start=(kc == 0), stop=(kc == nkc - 1)),
             reads=[onesN, sq], writes=[ps_b])
    P.op("act", lambda e: e.activation(out=mean[:, 0:W], in_=ps_a[:, 0:W], func=AF.Identity), reads=[ps_a], writes=[mean])
    P.op("dve", lambda e: e.tensor_tensor(out=tmpm[:, 0:W], in0=mean[:, 0:W], in1=mean[:, 0:W], op=ALU.mult), reads=[mean], writes=[tmpm])
    P.op("dve", lambda e: e.tensor_tensor(out=tmpm[:, 0:W], in0=ps_b[:, 0:W], in1=tmpm[:, 0:W], op=ALU.subtract), reads=[ps_b, tmpm], writes=[tmpm])
    P.op("act", lambda e: e.activation(out=tmpm[:, 0:W], in_=tmpm[:, 0:W], func=AF.Sqrt, bias=EPS), reads=[tmpm], writes=[tmpm])
    P.op("dve", lambda e: e.reciprocal(out=rstd[:, 0:W], in_=tmpm[:, 0:W]), reads=[tmpm], writes=[rstd])


def build_A():
    nc = bass.Bass("TRN2", target_bir_lowering=False)
    P = Prog(nc)
    xT = P.dram("xT", [D, NT], F32, "ExternalInput")
    modT_d = P.dram("modT", [128, 96], F32, "ExternalInput")
    w_in = P.dram("w_in", [D, IN_COLS], F32, "ExternalInput")
    cst = P.dram("cst", [128, 129], F32, "ExternalInput")
    projT = P.dram("projT", [IN_COLS, NT], BF16, "ExternalOutput")

    W = 512
    wf = [P.sb(f"wf{i}", [128, KC, W], F32) for i in range(2)]
    wbl = [P.sb(f"wbl{i}", [128, 10, W], BF16) for i in range(2)]
    wbh = [P.sb(f"wbh{i}", [128, 6, W], BF16) for i in range(2)]
    hT = P.sb("hT", [128, KC, NT], BF16)
    ot = [P.sb(f"ot{i}", [128, NT], BF16) for i in range(2)]
    cs = P.sb("cs", [128, 129], F32)
    mean = P.sb("mean", [128, W], F32)
    rstd = P.sb("rstd", [128, W], F32)
    tmpm = P.sb("tmpm", [128, W], F32)
    t1 = [P.sb(f"t1_{i}", [128, W], F32) for i in range(2)]
    sc1p = P.sb("sc1p", [128, KC], F32)
    ps_a = P.ps("ps_a", [128, W], F32)
    ps_b = P.ps("ps_b", [128, W], F32)
    ps_row = P.ps("ps_row", [128, W], F32)
    ps_t = P.ps("ps_t", [128, W], F32)
    accs = [P.ps(f"acc{i}", [128, W], F32) for i in range(4)]

    P.dma("sp", cs[:], cst[:], writes=[cs])
    onesN = cs

    modT = P.sb("modT", [128, 96], F32)
    P.dma("sp", modT[:], modT_d[:], writes=[modT])
    P.op("dve", lambda e: e.tensor_scalar_add(out=sc1p[:], in0=modT[:, 16:32], scalar1=1.0), reads=[modT], writes=[sc1p])

    z, sq = wf[0], wf[1]
    for tt in range(NT // W):
        P.dma("sp", z[:], xT.h.ap()[:, tt * W:(tt + 1) * W].rearrange("(kc p) n -> p kc n", p=128), writes=[z])
        ln_stats(P, z, sq, KC, W, Sub(cs, (slice(None), slice(0, 128))), ps_a, ps_b, mean, rstd, tmpm)
        for kc in range(KC):
            t = t1[kc % 2]
            P.op("pool", lambda e, t=t, kc=kc: e.tensor_tensor(out=t[:], in0=z[:, kc, :], in1=mean[:], op=ALU.subtract),
                 reads=[z, mean], writes=[t])
            P.op("dve", lambda e, t=t: e.tensor_tensor(out=t[:], in0=t[:], in1=rstd[:], op=ALU.mult), reads=[t, rstd], writes=[t])
            P.op("act", lambda e, t=t, kc=kc, tt=tt: e.activation(out=hT[:, kc, tt * W:(tt + 1) * W], in_=t[:], func=AF.Identity,
                                                                  scale=sc1p[:, kc:kc + 1], bias=modT[:, kc:kc + 1]),
                 reads=[t, sc1p, modT], writes=[hT])

    ncg = (IN_COLS + W - 1) // W
    gi = 0
    oi = 0

    def load_w(cg):
        c0 = cg * W
        cw = min(W, IN_COLS - c0)
        P.dma("sp", wf[cg % 2][:, :, 0:cw], dview(w_in, c0, c0 + cw), writes=[wf[cg % 2]])

    load_w(0)
    for cg in range(ncg):
        c0 = cg * W
        cw = min(W, IN_COLS - c0)
        if cg + 1 < ncg:
            load_w(cg + 1)
        f, bl, bh = wf[cg % 2], wbl[cg % 2], wbh[cg % 2]
        P.op("dve", lambda e, f=f, bl=bl, cw=cw: e.tensor_copy(out=bl[:, :, 0:cw], in_=f[:, 0:10, 0:cw]), reads=[f], writes=[bl])
        P.op("pool", lambda e, f=f, bh=bh, cw=cw: e.tensor_copy(out=bh[:, :, 0:cw], in_=f[:, 10:16, 0:cw]), reads=[f], writes=[bh])
        for sub in range((cw + 127) // 128):
            m = min(128, cw - sub * 128)
            o = ot[oi % 2]
            oi += 1
            for tt in range(NT // W):
                acc = accs[gi % 4]
                for kc in range(KC):
                    wsrc = bl if kc < 10 else bh
                    kk = kc if kc < 10 else kc - 10
                    P.op("pe", lambda e, acc=acc, wsrc=wsrc, kk=kk, kc=kc, sub=sub, m=m, tt=tt:
                         e.matmul(acc[0:m, :], lhsT=wsrc[:, kk, sub * 128:sub * 128 + m], rhs=hT[:, kc, tt * W:(tt + 1) * W],
                                  start=(kc == 0), stop=(kc == KC - 1)),
                         reads=[wsrc, hT], writes=[acc])
                if gi % 2 == 0:
                    P.op("act", lambda e, acc=acc, o=o, m=m, tt=tt: e.activation(out=o[0:m, tt * W:(tt + 1) * W], in_=acc[0:m, :], func=AF.Identity),
                         reads=[acc], writes=[o])
                else:
                    P.op("dve", lambda e, acc=acc, o=o, m=m, tt=tt: e.tensor_copy(out=o[0:m, tt * W:(tt + 1) * W], in_=acc[0:m, :]),
                         reads=[acc], writes=[o])
                gi += 1
            r0 = c0 + sub * 128
            P.dma("sp", projT.h.ap()[r0:r0 + m, :], o[0:m, :], reads=[o])
    P.finish()
    P.emit()
    return nc


NEG = -30000.0
QW = 512
NQT = S // QW
SCALE = 128.0 ** -0.5
CMP_OFFS = [31, 31 - 512, 31 - 1024, 31 - 1536, 31 - 2048]


def b1_consts():
    bf = ml_dtypes.bfloat16
    p = np.arange(128)[:, None]
    f = np.arange(512)[None, :]
    c = {}
    c["ident"] = np.eye(128, dtype=np.float32).astype(bf)
    c["cmpmask"] = np.stack([np.where(f >= 16 * p + off, 0.0, NEG) for off in CMP_OFFS], axis=1).astype(bf)
    c["causal"] = np.stack([np.where(128 * i + p <= f, 0.0, NEG) for i in range(4)], axis=1).astype(bf)
    wm = []
    for i in range(8):
        dl = 128 * (i - 4)
        wm.append(np.where((f >= p + dl) & (f < p + dl + 512), 0.0, NEG))
    c["winmask"] = np.stack(wm, axis=1).astype(bf)
    bs = np.zeros((128, 64, 128), np.float32)
    for v in range(64):
        bs[2 * v, v, 0:64] = 1.0
        bs[2 * v + 1, v, 64:128] = 1.0
    c["bsel"] = bs.astype(bf)
    cc = (np.arange(8)[None, :, None] * 128 + np.arange(128)[:, None, None])
    s = np.arange(256)[None, None, :]
    ov = ((16 * cc < 64 * s + 64) & (16 * cc + 32 > 64 * s)).astype(np.float32)
    c["ov1"] = np.concatenate([ov, np.ones((128, 8, 1), np.float32)], axis=2).astype(bf)
    rel = np.arange(512)[None, :] - 256
    cur = (np.arange(128)[:, None] >= 64).astype(np.int64)
    c["cmv"] = (rel < cur - 1).astype(np.float32)
    c["cma"] = np.where((rel == cur) | (rel == cur - 1), 1e6, np.where(rel > cur, -1.0, 0.0)).astype(np.float32)
    return c


def build_B1():
    nc = bass.Bass("TRN2", target_bir_lowering=False)
    P = Prog(nc)
    DI = lambda n, sh, dt: P.dram(n, sh, dt, "ExternalInput")
    q4 = DI("q4", [4, 128, S], BF16)
    kcmpT = DI("kcmpT", [128, S], BF16)
    vcmpT = DI("vcmpT", [128, S], BF16)
    kslcT = DI("kslcT", [128, S], BF16)
    vslc = DI("vslc", [S, 128], BF16)
    kwinT = DI("kwinT", [128, S], BF16)
    vwin = DI("vwin", [S, 128], BF16)
    gat = DI("gat", [128, 128, 3], BF16)
    peT = DI("peT", [128, 2, 32], F32)
    w1 = DI("w1", [2, 128, 32, 256], F32)
    w2 = DI("w2", [128, 2, 2, 128], F32)
    ident_d = DI("ident", [128, 128], BF16)
    cmpmask_d = DI("cmpmask", [128, 5, 512], BF16)
    causal_d = DI("causal", [128, 4, 512], BF16)
    winmask_d = DI("winmask", [128, 8, 512], BF16)
    bsel_d = DI("bsel", [128, 64, 128], BF16)
    ov1_d = DI("ov1", [128, 8, 257], BF16)
    cmv_d = DI("cmv", [128, 512], F32)
    cma_d = DI("cma", [128, 512], F32)
    yT = P.dram("yT", [128, S], BF16, "ExternalOutput")

    ident = P.sb("ident", [128, 128], BF16)
    cmpmask = P.sb("cmpmask", [128, 5, 512], BF16)
    causal = P.sb("causal", [128, 4, 512], BF16)
    winmask = P.sb("winmask", [128, 8, 512], BF16)
    bsel = P.sb("bsel", [128, 64, 128], BF16)
    cmv = P.sb("cmv", [128, 512], F32)
    cma = P.sb("cma", [128, 512], F32)
    gates = P.sb("gates", [128, 128, 3], F32)
    ksT = P.sb("ksT", [128, S], BF16)
    vsa = P.sb("vsa", [128, 128, 129], BF16)
    kcT = P.sb("kcT", [128, 1024], BF16)
    vca = P.sb("vca", [128, 8, 385], BF16)
    for t, d in ((ident, ident_d), (cmpmask, cmpmask_d), (causal, causal_d), (winmask, winmask_d), (bsel, bsel_d),
                 (cmv, cmv_d), (cma, cma_d)):
        P.dma("sp", t[:], d[:], writes=[t])
    gtmp = P.sb("gtmp", [128, 128, 3], BF16)
    P.dma("sp", gtmp[:], gat[:], writes=[gtmp])
    P.op("act", lambda e: e.activation(out=gates[:], in_=gtmp[:], func=AF.Sigmoid), reads=[gtmp], writes=[gates])
    P.dma("sp", ksT[:], kslcT[:], writes=[ksT])
    P.dma("sp", vsa[:, :, 0:128], vslc.h.ap().rearrange("(j p) d -> p j d", p=128), writes=[vsa])
    P.op("pool", lambda e: e.memset(vsa[:, :, 128:129], 1.0), reads=[], writes=[vsa])
    P.dma("sp", vca[:, :, 0:257], ov1_d[:], writes=[vca])

    S_ps = [P.ps(f"S{i}", [128, 512], F32) for i in range(2)]
    acc = [P.ps(f"acc{i}", [128, 512], F32) for i in range(4)]
    tps = P.ps("tps", [128, 4, 128], BF16)
    mps = P.ps("mps", [128, 512], F32)

    P.scope_begin()
    xc = P.sb("xc", [128, S], BF16)
    w1f = P.sb("w1f", [128, 32, 256], F32)
    w1b = P.sb("w1b", [128, 32, 256], BF16)
    w2f = P.sb("w2f", [128, 2, 2, 128], F32)
    w2b = P.sb("w2b", [128, 2, 2, 128], BF16)
    pef = P.sb("pef", [128, 2, 32], F32)
    peb = P.sb("peb", [128, 2, 32], BF16)
    gel = [P.sb(f"gel{i}", [128, 1024], BF16) for i in range(2)]
    hb = P.sb("hb", [128, 1], F32)
    xh = P.sb("xh", [128, 512], F32)
    xu = P.sb("xu", [128, 512], F32)
    P.dma("sp", w2f[:], w2[:], writes=[w2f])
    P.dma("sp", pef[:], peT[:], writes=[pef])
    P.op("dve", lambda e: e.tensor_copy(out=w2b[:], in_=w2f[:]), reads=[w2f], writes=[w2b])
    P.op("dve", lambda e: e.tensor_copy(out=peb[:], in_=pef[:]), reads=[pef], writes=[peb])
    for kv in range(2):
        P.dma("sp", xc[:], (kcmpT if kv == 0 else vcmpT)[:], writes=[xc])
        P.dma("sp", w1f[:], w1.h.ap()[kv], writes=[w1f])
        P.op("dve", lambda e: e.tensor_copy(out=w1b[:], in_=w1f[:]), reads=[w1f], writes=[w1b])
        xv = xc.h.ap().rearrange("p (b s) -> p b s", s=16)
        for half in range(2):
            g_ = gel[half]
            P.op("pool", lambda e, g_=g_: e.memset(g_[:], 0.0), reads=[], writes=[g_])
            for j in range(32):
                P.op("pe", lambda e, j=j, half=half, kv=kv: e.matmul(mps[:, 0:1], lhsT=w1b[:, j, half * 128:(half + 1) * 128],
                                                                     rhs=peb[:, kv, j:j + 1], start=(j == 0), stop=(j == 31)),
                     reads=[w1b, peb], writes=[mps])
            P.op("dve", lambda e: e.tensor_copy(out=hb[:], in_=mps[:, 0:1]), reads=[mps], writes=[hb])
            for nci, (n0, cnt) in enumerate(((0, 512), (512, 511))):
                sp_ = S_ps[nci]
                for j in range(32):
                    b0 = n0 + j // 16
                    P.op("pe", lambda e, j=j, half=half, b0=b0, cnt=cnt, sp_=sp_, xv=xv:
                         e.matmul(sp_[:, 0:cnt], lhsT=w1b[:, j, half * 128:(half + 1) * 128], rhs=xv[:, b0:b0 + cnt, j % 16],
                                  start=(j == 0), stop=(j == 31)),
                         reads=[w1b, xc], writes=[sp_])
                P.op("act", lambda e, sp_=sp_, cnt=cnt: e.activation(out=xh[:, 0:cnt], in_=sp_[:, 0:cnt], func=AF.Identity, bias=hb[:, 0:1]),
                     reads=[sp_, hb], writes=[xh])
                P.op("dve", lambda e, cnt=cnt: e.tensor_tensor(out=xu[:, 0:cnt], in0=xh[:, 0:cnt], in1=xh[:, 0:cnt], op=ALU.mult), reads=[xh], writes=[xu])
                P.op("dve", lambda e, cnt=cnt: e.tensor_scalar(out=xu[:, 0:cnt], in0=xu[:, 0:cnt], scalar1=0.044715, scalar2=1.0,
                                                               op0=ALU.mult, op1=ALU.add), reads=[xu], writes=[xu])
                P.op("dve", lambda e, cnt=cnt: e.tensor_tensor(out=xu[:, 0:cnt], in0=xu[:, 0:cnt], in1=xh[:, 0:cnt], op=ALU.mult), reads=[xu, xh], writes=[xu])
                P.op("act", lambda e, cnt=cnt: e.activation(out=xu[:, 0:cnt], in_=xu[:, 0:cnt], func=AF.Sigmoid, scale=1.5957691216),
                     reads=[xu], writes=[xu])
                P.op("dve", lambda e, cnt=cnt, n0=n0, g_=g_: e.tensor_tensor(out=g_[:, n0:n0 + cnt], in0=xu[:, 0:cnt], in1=xh[:, 0:cnt], op=ALU.mult),
                     reads=[xu, xh], writes=[g_])
        if kv == 0:
            for nci in range(2):
                for half in range(2):
                    P.op("pe", lambda e, nci=nci, half=half: e.matmul(mps[:, :], lhsT=w2b[:, 0, half, :], rhs=gel[half][:, nci * 512:(nci + 1) * 512],
                                                                      start=(half == 0), stop=(half == 1)),
                         reads=[w2b, gel[half]], writes=[mps])
                P.op("dve", lambda e, nci=nci: e.tensor_copy(out=kcT[:, nci * 512:(nci + 1) * 512], in_=mps[:, :]), reads=[mps], writes=[kcT])
        else:
            for m in range(8):
                for half in range(2):
                    P.op("pe", lambda e, m=m, half=half: e.matmul(mps[:, 0:128], lhsT=gel[half][:, m * 128:(m + 1) * 128], rhs=w2b[:, 1, half, :],
                                                                  start=(half == 0), stop=(half == 1)),
                         reads=[w2b, gel[half]], writes=[mps])
                P.op("dve", lambda e, m=m: e.tensor_copy(out=vca[:, m, 257:385], in_=mps[:, 0:128]), reads=[mps], writes=[vca])

    P.scope_end()
    qt = [P.sb(f"qt{i}", [128, 4, QW], BF16) for i in range(2)]
    kwT = [P.sb(f"kwT{i}", [128, 1024], BF16) for i in range(2)]
    vwa = [P.sb(f"vwa{i}", [128, 8, 129], BF16) for i in range(2)]
    ET = [P.sb(f"ET{i}", [128, QW], BF16) for i in range(3)]
    imp = P.sb("imp", [128, 4, 256], F32)
    ocomb = P.sb("ocomb", [128, 4, 128], F32)
    ocb = P.sb("ocb", [128, 4, 128], BF16)
    rden = P.sb("rden", [128, 1], F32)
    gsc = P.sb("gsc", [128, 1], F32)
    score = P.sb("score", [128, 256], F32)
    sc2 = P.sb("sc2", [128, 256], F32)
    m8 = P.sb("m8", [128, 8], F32)
    negsel = P.sb("negsel", [128, 256], BF16)
    nsT = P.sb("nsT", [128, 2, QW], BF16)
    yo = [P.sb(f"yo{i}", [128, QW], BF16) for i in range(2)]
    for i in range(2):
        P.op("pool", lambda e, i=i: e.memset(vwa[i][:, :, 128:129], 1.0), reads=[], writes=[vwa[i]])
    cnt_s = [0]
    cnt_e = [0]

    def attend(qap, qtile, chunks, NV):
        n = len(chunks)
        for ci, (kt_ap, kt_tile, v_ap, v_tile, masks) in enumerate(chunks):
            sp_ = S_ps[cnt_s[0] % 2]
            cnt_s[0] += 1
            nm = len(masks)
            P.op("pe", lambda e, sp_=sp_, kt_ap=kt_ap, nm=nm: e.matmul(sp_[:, :], lhsT=kt_ap, rhs=qap, start=True, stop=(nm == 0)),
                 reads=[kt_tile, qtile], writes=[sp_])
            for mi, (ml, mr, mt) in enumerate(masks):
                P.op("pe", lambda e, sp_=sp_, ml=ml, mr=mr, mi=mi, nm=nm: e.matmul(sp_[:, :], lhsT=ml, rhs=mr, start=False, stop=(mi == nm - 1)),
                     reads=list(mt), writes=[sp_])
            et = ET[cnt_e[0] % 3]
            cnt_e[0] += 1
            P.op("act", lambda e, sp_=sp_, et=et: e.activation(out=et[:, :], in_=sp_[:, :], func=AF.Exp, scale=SCALE), reads=[sp_], writes=[et])
            for qb in range(4):
                P.op("pe", lambda e, qb=qb, et=et, v_ap=v_ap, ci=ci: e.matmul(acc[qb][:, 0:NV], lhsT=et[:, qb * 128:(qb + 1) * 128], rhs=v_ap,
                                                                            start=(ci == 0), stop=(ci == n - 1)),
                     reads=[et, v_tile], writes=[acc[qb]])

    def fold_out(qb, col_den, col_o, gate_ap, first):
        a = acc[qb]
        P.op("dve", lambda e, a=a: e.tensor_scalar_max(out=rden[:], in0=a[:, col_den:col_den + 1], scalar1=1e-30), reads=[a], writes=[rden])
        P.op("dve", lambda e: e.reciprocal(out=rden[:], in_=rden[:]), reads=[rden], writes=[rden])
        P.op("dve", lambda e: e.tensor_tensor(out=gsc[:], in0=rden[:], in1=gate_ap, op=ALU.mult), reads=[rden, gates], writes=[gsc])
        if first:
            P.op("dve", lambda e, a=a: e.tensor_scalar(out=ocomb[:, qb, :], in0=a[:, col_o:col_o + 128], scalar1=gsc[:, 0:1], scalar2=None, op0=ALU.mult),
                 reads=[a, gsc], writes=[ocomb])
        else:
            P.op("dve", lambda e, a=a: e.scalar_tensor_tensor(out=ocomb[:, qb, :], in0=a[:, col_o:col_o + 128], scalar=gsc[:, 0:1], in1=ocomb[:, qb, :],
                                                              op0=ALU.mult, op1=ALU.add),
                 reads=[a, gsc, ocomb], writes=[ocomb])

    def load_tile(k):
        t0 = k * QW
        q_ = qt[k % 2]
        P.dma("sp", q_[:], q4.h.ap()[:, :, t0:t0 + QW].rearrange("h p t -> p h t"), writes=[q_])
        lo = max(0, t0 - 512)
        off = lo - (t0 - 512)
        P.dma("sp", kwT[k % 2][:, off:1024], kwinT.h.ap()[:, lo:t0 + 512], writes=[kwT[k % 2]])
        P.dma("sp", vwa[k % 2][:, off // 128:8, 0:128], vwin.h.ap()[lo:t0 + 512, :].rearrange("(j p) d -> p j d", p=128), writes=[vwa[k % 2]])

    load_tile(0)
    for k in range(NQT):
        t0 = k * QW
        if k + 1 < NQT:
            load_tile(k + 1)
        q_ = qt[k % 2]
        mmax = (t0 + 480) // 2048
        for hh in range(4):
            chunks = []
            NV = 385 if hh == 0 else 257
            for m in range(mmax + 1):
                off = 2048 * m + 31 - t0
                masks = []
                if off + 16 * 127 > 0:
                    mi = CMP_OFFS.index(off)
                    masks.append((ident[:], cmpmask[:, mi, :], (ident, cmpmask)))
                chunks.append((kcT[:, m * 128:(m + 1) * 128], kcT, vca[:, m, 0:NV], vca, masks))
            attend(q_[:, hh, :], q_, chunks, NV)
            for qb in range(4):
                a = acc[qb]
                b = k * 4 + qb
                if hh == 0:
                    fold_out(qb, 256, 257, gates[:, b, 0:1], True)
                    P.op("dve", lambda e, a=a, qb=qb: e.tensor_scalar(out=imp[:, qb, :], in0=a[:, 0:256], scalar1=rden[:, 0:1], scalar2=None, op0=ALU.mult),
                         reads=[a, rden], writes=[imp])
                else:
                    P.op("dve", lambda e, a=a: e.tensor_scalar_max(out=rden[:], in0=a[:, 256:257], scalar1=1e-30), reads=[a], writes=[rden])
                    P.op("dve", lambda e: e.reciprocal(out=rden[:], in_=rden[:]), reads=[rden], writes=[rden])
                    P.op("dve", lambda e, a=a, qb=qb: e.scalar_tensor_tensor(out=imp[:, qb, :], in0=a[:, 0:256], scalar=rden[:, 0:1], in1=imp[:, qb, :],
                                                                            op0=ALU.mult, op1=ALU.add),
                         reads=[a, rden, imp], writes=[imp])
        for qb in range(4):
            b = k * 4 + qb
            w0 = 256 - 2 * b
            P.op("dve", lambda e, qb=qb, w0=w0: e.tensor_tensor(out=score[:], in0=imp[:, qb, :], in1=cmv[:, w0:w0 + 256], op=ALU.mult), reads=[imp, cmv], writes=[score])
            P.op("dve", lambda e, w0=w0: e.tensor_tensor(out=score[:], in0=score[:], in1=cma[:, w0:w0 + 256], op=ALU.add), reads=[score, cma], writes=[score])
            P.op("dve", lambda e: e.memset(score[:, 0:1], 1e6), reads=[], writes=[score])
            P.op("dve", lambda e: e.max(out=m8[:], in_=score[:]), reads=[score], writes=[m8])
            P.op("dve", lambda e: e.match_replace(out=sc2[:], in_to_replace=m8[:], in_values=score[:], imm_value=-1e9), reads=[m8, score], writes=[sc2])
            P.op("dve", lambda e: e.max(out=m8[:], in_=sc2[:]), reads=[sc2], writes=[m8])
            P.op("dve", lambda e: e.tensor_scalar(out=negsel[:], in0=score[:], scalar1=m8[:, 7:8], scalar2=NEG, op0=ALU.is_lt, op1=ALU.mult),
                 reads=[score, m8], writes=[negsel])
            for hf in range(2):
                P.op("pe", lambda e, hf=hf: e.transpose(out=tps[:, hf, :], in_=negsel[:, hf * 128:(hf + 1) * 128], identity=ident[:]),
                     reads=[negsel, ident], writes=[tps])
            P.op("act", lambda e, qb=qb: e.activation(out=nsT[:, :, qb * 128:(qb + 1) * 128], in_=tps[:, 0:2, :], func=AF.Identity), reads=[tps], writes=[nsT])
        chunks = []
        for j in range(4 * k + 4):
            masks = [(bsel[:, j % 64, :], nsT[:, j // 64, :], (bsel, nsT))]
            if j >= 4 * k:
                masks.append((ident[:], causal[:, j - 4 * k, :], (ident, causal)))
            chunks.append((ksT[:, j * 128:(j + 1) * 128], ksT, vsa[:, j, :], vsa, masks))
        attend(q_[:, 0, :], q_, chunks, 129)
        for qb in range(4):
            fold_out(qb, 128, 0, gates[:, k * 4 + qb, 1:2], False)
        chunks = []
        kw_, vw_ = kwT[k % 2], vwa[k % 2]
        for i in range(8):
            if 4 * k - 4 + i < 0:
                continue
            chunks.append((kw_[:, i * 128:(i + 1) * 128], kw_, vw_[:, i, :], vw_, [(ident[:], winmask[:, i, :], (ident, winmask))]))
        attend(q_[:, 0, :], q_, chunks, 129)
        for qb in range(4):
            fold_out(qb, 128, 0, gates[:, k * 4 + qb, 2:3], False)
        P.op("act", lambda e: e.activation(out=ocb[:], in_=ocomb[:], func=AF.Identity), reads=[ocomb], writes=[ocb])
        for qb in range(4):
            P.op("pe", lambda e, qb=qb: e.transpose(out=tps[:, qb, :], in_=ocb[:, qb, :], identity=ident[:]), reads=[ocb, ident], writes=[tps])
        yo_ = yo[k % 2]
        P.op("act", lambda e, yo_=yo_: e.activation(out=yo_[:].rearrange("p (a b) -> p a b", b=128), in_=tps[:, :, :], func=AF.Identity), reads=[tps], writes=[yo_])
        P.dma("sp", yT.h.ap()[:, t0:t0 + QW], yo_[:], reads=[yo_])
    P.finish()
    P.emit()
    return nc


NPAIR = S // 128


def b2_consts():
    bf = ml_dtypes.bfloat16
    c = {}
    c["ident"] = np.eye(128, dtype=np.float32).astype(bf)
    c["identf"] = np.eye(128, dtype=np.float32)
    s = np.arange(128)[:, None]
    t = np.arange(128)[None, :]
    c["mask01"] = (((s // 64) == (t // 64)) & (s <= t)).astype(np.float32)
    c["onesf"] = np.ones((128, 128), np.float32)
    return c


def build_B2():
    nc = bass.Bass("TRN2", target_bir_lowering=False)
    P = Prog(nc)
    DI = lambda n, sh, dt: P.dram(n, sh, dt, "ExternalInput")
    qraw = DI("qraw", [128, S], BF16)
    kraw = DI("kraw", [128, S], BF16)
    vtok = DI("vtok", [S, 128], BF16)
    gif = DI("gif", [128, 2, 128], BF16)
    convw = DI("convw", [128, 2, 4], F32)
    convb = DI("convb", [128, 2], F32)
    gateb = DI("gateb", [128, 2], F32)
    ident_d = DI("ident", [128, 128], BF16)
    identf_d = DI("identf", [128, 128], F32)
    mask_d = DI("mask01", [128, 128], F32)
    ones_d = DI("onesf", [128, 128], F32)
    scr = P.dram("scr", [4, 256], F32, "Internal")
    hT = P.dram("hT", [128, S], BF16, "ExternalOutput")

    ident = P.sb("ident", [128, 128], BF16)
    identf = P.sb("identf", [128, 128], F32)
    mask01 = P.sb("mask01", [128, 128], F32)
    onesf = P.sb("onesf", [128, 128], F32)
    cw = P.sb("cw", [128, 2, 4], F32)
    cb = P.sb("cb", [128, 2], F32)
    gb = P.sb("gb", [128, 2], F32)
    for t, d in ((ident, ident_d), (identf, identf_d), (mask01, mask_d), (onesf, ones_d), (cw, convw), (cb, convb), (gb, gateb)):
        P.dma("sp", t[:], d[:], writes=[t])
    QT = P.sb("QT", [128, S], BF16)
    KT = P.sb("KT", [128, S], BF16)
    Ktok = P.sb("Ktok", [128, NPAIR, 128], BF16)
    Va = P.sb("Va", [128, NPAIR, 129], BF16)
    P.dma("sp", Va[:, :, 0:128], vtok.h.ap().rearrange("(j p) d -> p j d", p=128), writes=[Va])
    P.op("pool", lambda e: e.memset(Va[:, :, 128:129], 1.0), reads=[], writes=[Va])
    ewT = P.sb("ewT", [128, 128], F32)
    euT = P.sb("euT", [128, 128], F32)
    wiT = P.sb("wiT", [128, 128], F32)
    gdT = P.sb("gdT", [128, 128], F32)
    decb = P.sb("decb", [128, 256], F32)
    sc2b = P.sb("sc2b", [128, 256], F32)

    pKQ = [P.ps(f"pKQ{i}", [128, 512], F32) for i in range(2)]
    pB = P.ps("pB", [128, 512], F32)
    pA = P.ps("pA", [128, 512], F32)
    pU = P.ps("pU", [128, 512], F32)
    pT = P.ps("pT", [128, 4, 128], BF16)
    pM = P.ps("pM", [128, 512], F32)

    P.scope_begin()
    xp = P.sb("xp", [128, S + 3], BF16)
    yseg = P.sb("yseg", [128, 4096], F32)
    P.op("pool", lambda e: e.memset(xp[:, 0:3], 0.0), reads=[], writes=[xp])
    for qk, (src, dst) in enumerate(((qraw, QT), (kraw, KT))):
        P.dma("sp", xp[:, 3:S + 3], src[:], writes=[xp])
        for sg in range(4):
            c0 = sg * 4096
            P.op("dve", lambda e, c0=c0, qk=qk: e.tensor_scalar(out=yseg[:], in0=xp[:, c0:c0 + 4096], scalar1=cw[:, qk, 0:1], scalar2=None, op0=ALU.mult),
                 reads=[xp, cw], writes=[yseg])
            for j in range(1, 4):
                P.op("dve", lambda e, c0=c0, qk=qk, j=j: e.scalar_tensor_tensor(out=yseg[:], in0=xp[:, c0 + j:c0 + j + 4096], scalar=cw[:, qk, j:j + 1],
                                                                              in1=yseg[:], op0=ALU.mult, op1=ALU.add),
                     reads=[xp, cw, yseg], writes=[yseg])
            if qk == 0:
                P.op("act", lambda e, c0=c0, dst=dst: e.activation(out=dst[:, c0:c0 + 4096], in_=yseg[:], func=AF.Silu, bias=cb[:, 0:1]),
                     reads=[yseg, cb], writes=[dst])
            else:
                P.op("act", lambda e: e.activation(out=yseg[:], in_=yseg[:], func=AF.Silu, bias=cb[:, 1:2]), reads=[yseg, cb], writes=[yseg])
                P.op("pool", lambda e, c0=c0, dst=dst: e.tensor_scalar(out=dst[:, c0:c0 + 4096], in0=yseg[:], scalar1=SCALE, scalar2=None, op0=ALU.mult),
                     reads=[yseg], writes=[dst])
    P.scope_end()
    for j in range(NPAIR):
        P.op("pe", lambda e, j=j: e.transpose(out=pT[:, j % 4, :], in_=KT[:, j * 128:(j + 1) * 128], identity=ident[:]), reads=[KT, ident], writes=[pT])
        if j % 4 == 3:
            P.op("act", lambda e, j=j: e.activation(out=Ktok[:, j - 3:j + 1, :], in_=pT[:, :, :], func=AF.Identity), reads=[pT], writes=[Ktok])

    P.scope_begin()
    G = lambda n: P.sb(n, [128, 128], F32)
    gtmp = P.sb("gtmp", [128, 2, 128], BF16)
    ig, lf, bcum, w_, cmw, tA, tB, mt = G("ig"), G("lf"), G("bcum"), G("w_"), G("cmw"), G("tA"), G("tB"), G("mt")
    ones64 = P.sb("ones64", [128, 64], F32)
    small = P.sb("small", [128, 8], F32)
    rows = P.sb("rows", [1, 4, 256], F32)
    P.dma("sp", gtmp[:], gif[:], writes=[gtmp])
    P.op("pool", lambda e: e.memset(ones64[:], 1.0), reads=[], writes=[ones64])
    P.op("dve", lambda e: e.tensor_scalar(out=ig[:], in0=gtmp[:, 0, :], scalar1=gb[:, 0:1], scalar2=None, op0=ALU.add), reads=[gtmp, gb], writes=[ig])
    P.op("dve", lambda e: e.tensor_scalar(out=lf[:], in0=gtmp[:, 1, :], scalar1=gb[:, 1:2], scalar2=None, op0=ALU.add), reads=[gtmp, gb], writes=[lf])
    P.op("act", lambda e: e.activation(out=lf[:], in_=lf[:], func=AF.Exp, scale=-1.0), reads=[lf], writes=[lf])
    P.op("act", lambda e: e.activation(out=lf[:], in_=lf[:], func=AF.Ln, bias=1.0), reads=[lf], writes=[lf])
    P.op("dve", lambda e: e.tensor_scalar(out=lf[:], in0=lf[:], scalar1=-1.0, scalar2=None, op0=ALU.mult), reads=[lf], writes=[lf])
    for a in range(2):
        sl = slice(a * 64, (a + 1) * 64)
        P.op("dve", lambda e, sl=sl: e.tensor_tensor_scan(out=bcum[:, sl], data0=ones64[:], data1=lf[:, sl], initial=0.0, op0=ALU.mult, op1=ALU.add),
             reads=[ones64, lf], writes=[bcum])
    P.op("dve", lambda e: e.tensor_tensor(out=w_[:], in0=ig[:], in1=bcum[:], op=ALU.subtract), reads=[ig, bcum], writes=[w_])
    for a in range(2):
        sl = slice(a * 64, (a + 1) * 64)
        P.op("dve", lambda e, sl=sl: e.tensor_tensor_scan(out=cmw[:, sl], data0=ones64[:], data1=w_[:, sl], initial=-1e30, op0=ALU.mult, op1=ALU.max),
             reads=[ones64, w_], writes=[cmw])
    for a in range(2):
        c = a * 64 + 63
        P.op("dve", lambda e, a=a, c=c: e.tensor_copy(out=small[:, a:a + 1], in_=bcum[:, c:c + 1]), reads=[bcum], writes=[small])
        P.op("dve", lambda e, a=a, c=c: e.tensor_tensor(out=small[:, 2 + a:3 + a], in0=cmw[:, c:c + 1], in1=bcum[:, c:c + 1], op=ALU.add),
             reads=[cmw, bcum], writes=[small])
    P.dma("sp", scr.h.ap()[0].rearrange("(p a) -> p a", a=2), small[:, 0:2], reads=[small], writes=[scr])
    P.dma("sp", scr.h.ap()[1].rearrange("(p a) -> p a", a=2), small[:, 2:4], reads=[small], writes=[scr])
    P.dma("sp", rows[0:1, 0:2, :], scr.h.ap()[0:2, :].rearrange("(o r) n -> o r n", o=1), reads=[scr], writes=[rows])
    P.op("dve", lambda e: e.tensor_tensor_scan(out=rows[0:1, 2, :], data0=rows[0:1, 0, :], data1=rows[0:1, 1, :], initial=0.0, op0=ALU.add, op1=ALU.max),
         reads=[rows], writes=[rows])
    P.op("dve", lambda e: e.memset(rows[0:1, 3, 0:1], 0.0), reads=[], writes=[rows])
    P.op("dve", lambda e: e.tensor_copy(out=rows[0:1, 3, 1:256], in_=rows[0:1, 2, 0:255]), reads=[rows], writes=[rows])
    P.dma("sp", scr.h.ap()[2:4, :].rearrange("(o r) n -> o r n", o=1), rows[0:1, 2:4, :], reads=[rows], writes=[scr])
    P.dma("sp", small[:, 4:6], scr.h.ap()[2].rearrange("(p a) -> p a", a=2), reads=[scr], writes=[small])
    P.dma("sp", small[:, 6:8], scr.h.ap()[3].rearrange("(p a) -> p a", a=2), reads=[scr], writes=[small])
    P.op("dve", lambda e: e.tensor_tensor(out=tA[:], in0=bcum[:], in1=cmw[:], op=ALU.add), reads=[bcum, cmw], writes=[tA])
    for a in range(2):
        sl = slice(a * 64, (a + 1) * 64)
        P.op("dve", lambda e, sl=sl, a=a: e.tensor_scalar(out=tB[:, sl], in0=bcum[:, sl], scalar1=small[:, 6 + a:7 + a], scalar2=None, op0=ALU.add),
             reads=[bcum, small], writes=[tB])
    P.op("dve", lambda e: e.tensor_tensor(out=mt[:], in0=tA[:], in1=tB[:], op=ALU.max), reads=[tA, tB], writes=[mt])
    P.op("dve", lambda e: e.tensor_tensor(out=tB[:], in0=tB[:], in1=mt[:], op=ALU.subtract), reads=[tB, mt], writes=[tB])
    P.op("dve", lambda e: e.tensor_tensor(out=tA[:], in0=bcum[:], in1=mt[:], op=ALU.subtract), reads=[bcum, mt], writes=[tA])
    P.op("act", lambda e: e.activation(out=tB[:], in_=tB[:], func=AF.Exp), reads=[tB], writes=[tB])
    P.op("act", lambda e: e.activation(out=tA[:], in_=tA[:], func=AF.Exp), reads=[tA], writes=[tA])
    P.op("act", lambda e: e.activation(out=mt[:], in_=mt[:], func=AF.Exp, scale=-1.0), reads=[mt], writes=[mt])
    P.op("act", lambda e: e.activation(out=w_[:], in_=w_[:], func=AF.Exp), reads=[w_], writes=[w_])
    for src, dst in ((w_, ewT), (tA, euT), (tB, wiT), (mt, gdT)):
        P.op("pe", lambda e, src=src: e.transpose(out=pM[:, 0:128], in_=src[:], identity=identf[:]), reads=[src, identf], writes=[pM])
        P.op("dve", lambda e, dst=dst: e.tensor_copy(out=dst[:], in_=pM[:, 0:128]), reads=[pM], writes=[dst])
    P.op("dve", lambda e: e.tensor_tensor(out=small[:, 2:4], in0=small[:, 0:2], in1=small[:, 4:6], op=ALU.subtract), reads=[small], writes=[small])
    P.op("dve", lambda e: e.tensor_tensor(out=small[:, 0:2], in0=small[:, 2:4], in1=small[:, 6:8], op=ALU.add), reads=[small], writes=[small])
    P.op("act", lambda e: e.activation(out=small[:, 0:4], in_=small[:, 0:4], func=AF.Exp), reads=[small], writes=[small])
    dg = P.sb("dg", [128, 128, 2], F32)
    for which, dst in ((0, decb), (2, sc2b)):
        for a in range(2):
            P.op("dve", lambda e, a=a, which=which: e.tensor_scalar(out=dg[:, :, a], in0=identf[:], scalar1=small[:, which + a:which + a + 1], scalar2=None, op0=ALU.mult),
                 reads=[identf, small], writes=[dg])
        P.op("pe", lambda e: e.matmul(pM[:, 0:256], lhsT=onesf[:], rhs=dg[:].rearrange("p a b -> p (a b)"), start=True, stop=True),
             reads=[onesf, dg], writes=[pM])
        P.op("dve", lambda e, dst=dst: e.tensor_copy(out=dst[:], in_=pM[:, 0:256]), reads=[pM], writes=[dst])
    P.scope_end()

    Cst = P.sb("Cst", [128, 129], F32)
    Cb = [P.sb(f"Cb{i}", [128, 129], BF16) for i in range(2)]
    Sm = [P.sb(f"Sm{i}", [128, 128], BF16) for i in range(2)]
    Vw = [P.sb(f"Vw{i}", [128, 129], BF16) for i in range(2)]
    tU = P.sb("tU", [128, 129], F32)
    tN = P.sb("tN", [128, 129], F32)
    num = P.sb("num", [128, 129], F32)
    dn = P.sb("dn", [128, 1], F32)
    hb = [P.sb(f"hb{i}", [128, 128], BF16) for i in range(2)]
    ho = [P.sb(f"ho{i}", [128, 512], BF16) for i in range(2)]
    P.op("dve", lambda e: e.memset(Cst[:], 0.0), reads=[], writes=[Cst])
    P.op("pool", lambda e: e.memset(Cb[0][:], 0.0), reads=[], writes=[Cb[0]])
    ci = 0
    for j in range(NPAIR):
        cols = slice(j * 128, (j + 1) * 128)
        kq = pKQ[j % 2]
        sm, vw = Sm[j % 2], Vw[j % 2]
        P.op("pe", lambda e, kq=kq, cols=cols: e.matmul(kq[:, 0:128], lhsT=KT[:, cols], rhs=QT[:, cols], start=True, stop=True), reads=[KT, QT], writes=[kq])
        P.op("dve", lambda e, kq=kq, sm=sm: e.tensor_tensor(out=sm[:], in0=kq[:, 0:128], in1=mask01[:], op=ALU.mult), reads=[kq, mask01], writes=[sm])
        P.op("pool", lambda e, j=j, vw=vw: e.tensor_scalar(out=vw[:], in0=Va[:, j, :], scalar1=ewT[:, j:j + 1], scalar2=None, op0=ALU.mult),
             reads=[Va, ewT], writes=[vw])
        P.op("pe", lambda e, sm=sm, vw=vw: e.matmul(pB[:, 0:129], lhsT=sm[:], rhs=vw[:], start=True, stop=True), reads=[sm, vw], writes=[pB])
        for a in range(2):
            c = 2 * j + a
            rs_ = slice(a * 64, (a + 1) * 64)
            cbc = Cb[ci % 2]
            cbn = Cb[(ci + 1) % 2]
            ci += 1
            P.op("pe", lambda e, rs_=rs_, cbc=cbc, j=j: e.matmul(pA[rs_, 0:129], lhsT=QT[:, j * 128 + rs_.start:j * 128 + rs_.stop], rhs=cbc[:], start=True, stop=True),
                 reads=[QT, cbc], writes=[pA])
            P.op("pe", lambda e, rs_=rs_, j=j, vw=vw: e.matmul(pU[:, 0:129], lhsT=Ktok[rs_, j, :], rhs=vw[rs_, :], start=True, stop=True),
                 reads=[Ktok, vw], writes=[pU])
            P.op("dve", lambda e, c=c: e.tensor_scalar(out=tU[:], in0=pU[:, 0:129], scalar1=sc2b[:, c:c + 1], scalar2=None, op0=ALU.mult),
                 reads=[pU, sc2b], writes=[tU])
            P.op("dve", lambda e, c=c: e.scalar_tensor_tensor(out=Cst[:], in0=Cst[:], scalar=decb[:, c:c + 1], in1=tU[:], op0=ALU.mult, op1=ALU.add),
                 reads=[Cst, decb, tU], writes=[Cst])
            P.op("act", lambda e, cbn=cbn: e.activation(out=cbn[:], in_=Cst[:], func=AF.Identity), reads=[Cst], writes=[cbn])
        P.op("dve", lambda e, j=j: e.tensor_scalar(out=tN[:], in0=pA[:, 0:129], scalar1=wiT[:, j:j + 1], scalar2=None, op0=ALU.mult), reads=[pA, wiT], writes=[tN])
        P.op("dve", lambda e, j=j: e.scalar_tensor_tensor(out=num[:], in0=pB[:, 0:129], scalar=euT[:, j:j + 1], in1=tN[:], op0=ALU.mult, op1=ALU.add),
             reads=[pB, euT, tN], writes=[num])
        P.op("dve", lambda e: e.scalar_tensor_tensor(out=dn[:], in0=num[:, 128:129], scalar=-1.0, in1=num[:, 128:129], op0=ALU.mult, op1=ALU.max),
             reads=[num], writes=[dn])
        P.op("dve", lambda e, j=j: e.tensor_tensor(out=dn[:], in0=dn[:], in1=gdT[:, j:j + 1], op=ALU.max), reads=[dn, gdT], writes=[dn])
        P.op("dve", lambda e: e.reciprocal(out=dn[:], in_=dn[:]), reads=[dn], writes=[dn])
        h_ = hb[j % 2]
        P.op("dve", lambda e, h_=h_: e.tensor_scalar(out=h_[:], in0=num[:, 0:128], scalar1=dn[:, 0:1], scalar2=None, op0=ALU.mult), reads=[num, dn], writes=[h_])
        P.op("pe", lambda e, h_=h_, j=j: e.transpose(out=pT[:, j % 4, :], in_=h_[:], identity=ident[:]), reads=[h_, ident], writes=[pT])
        if j % 4 == 3:
            o_ = ho[(j // 4) % 2]
            P.op("act", lambda e, o_=o_: e.activation(out=o_[:].rearrange("p (a b) -> p a b", b=128), in_=pT[:, :, :], func=AF.Identity), reads=[pT], writes=[o_])
            P.dma("sp", hT.h.ap()[:, (j - 3) * 128:(j + 1) * 128], o_[:], reads=[o_])
    P.finish()
    P.emit()
    return nc


NTH = NT + 2
NFC = D_FF // 128
VEC_NG, VEC_LG0, VEC_LB0, VEC_LG1, VEC_LB1, VEC_CW, VEC_CB, VEC_FLAG, VEC_N = 0, 8, 24, 40, 56, 72, 204, 248, 249


def build_C():
    nc = bass.Bass("TRN2", target_bir_lowering=False)
    P = Prog(nc)
    DI = lambda n, sh, dt: P.dram(n, sh, dt, "ExternalInput")
    xT = DI("xT", [D, NTH], F32)
    ynsaT = DI("ynsaT", [1024, NTH], BF16)
    hmlT = DI("hmlT", [1024, NTH], BF16)
    pT2 = DI("pT2", [5120, NTH], BF16)
    w_brn = DI("w_brn", [1024, D], F32)
    w_brm = DI("w_brm", [1024, D], F32)
    w_o = DI("w_o", [D, D], F32)
    w_up = DI("w_up", [D, 2 * D_FF], F32)
    w_down = DI("w_down", [D_FF, D], F32)
    modT_d = DI("modT", [128, 96], F32)
    vecs_d = DI("vecs", [128, VEC_N], F32)
    cst = DI("cst", [128, 257], F32)
    xoT = P.dram("xoT", [D, NT], F32, "ExternalOutput")

    W = 512
    wf = [P.sb(f"wf{i}", [128, KC, 256], F32) for i in range(2)]
    wb = [P.sb(f"wb{i}", [128, 4096], BF16) for i in range(2)]
    xr = P.sb("xr", [128, KC, W], F32)
    yn = P.sb("yn", [128, 8, W], BF16)
    hm = P.sb("hm", [128, 8, W], BF16)
    yml = hm
    merged = P.sb("merged", [128, KC, W], BF16)
    h2 = merged
    u = P.sb("u", [128, NFC, W], BF16)
    abuf = [P.sb(f"abuf{i}", [128, W + 2], F32) for i in range(2)]
    carry = P.sb("carry", [128, NFC, 2], F32)
    ft = [P.sb(f"ft{i}", [128, W], F32) for i in range(6)]
    mean = P.sb("mean", [128, W], F32)
    rstd = P.sb("rstd", [128, W], F32)
    tmpm = P.sb("tmpm", [128, W], F32)
    sqt = [P.sb(f"sqt{i}", [128, W], F32) for i in range(2)]
    hsq = P.sb("hsq", [128, 2, W], BF16)
    mo = [P.sb(f"mo{i}", [128, W], BF16) for i in range(4)]
    vecs = P.sb("vecs", [128, VEC_N], F32)
    cs = P.sb("cs", [128, 257], F32)
    csb = P.sb("csb", [128, 128], BF16)
    sc2p = P.sb("sc2p", [128, KC], F32)
    ps_a = P.ps("ps_a", [128, W], F32)
    ps_b = P.ps("ps_b", [128, W], F32)
    accs = [P.ps(f"acc{i}", [128, W], F32) for i in range(4)]

    P.dma("sp", cs[:], cst[:], writes=[cs])
    P.dma("sp", vecs[:], vecs_d[:], writes=[vecs])
    P.op("dve", lambda e: e.tensor_copy(out=csb[:], in_=cs[:, 128:256]), reads=[cs], writes=[csb])
    onesN = Sub(cs, (slice(None), slice(0, 128)))
    one1 = Sub(cs, (slice(None), slice(256, 257)))
    modF = P.sb("modF", [128, 96], F32)
    P.dma("sp", modF[:], modT_d[:], writes=[modF])
    modT = Sub(modF, (slice(None), slice(32, 96)))
    P.op("dve", lambda e: e.tensor_scalar_add(out=sc2p[:], in0=modT[:, 32:48], scalar1=1.0), reads=[modT], writes=[sc2p])

    wff = [w.h.ap().rearrange("p k n -> p (k n)") for w in wf]
    state = {"gi": 0, "fi": 0}

    def nacc():
        a = accs[state["gi"] % 4]
        state["gi"] += 1
        return a

    def nft():
        t = ft[state["fi"] % 6]
        state["fi"] += 1
        return t

    def layer_norm_inplace(Wc, gcol, bcol, out_bf=None, scale_t=None, bias_t=None):
        for kc in range(KC):
            sq = sqt[kc % 2]
            P.op("act", lambda e, sq=sq, kc=kc: e.activation(out=sq[:, 0:Wc], in_=xr[:, kc, 0:Wc], func=AF.Square), reads=[xr], writes=[sq])
            P.op("pe", lambda e, kc=kc: e.matmul(ps_a[:, 0:Wc], lhsT=onesN[:], rhs=xr[:, kc, 0:Wc], start=(kc == 0), stop=(kc == KC - 1)),
                 reads=[cs, xr], writes=[ps_a])
            P.op("pe", lambda e, kc=kc, sq=sq: e.matmul(ps_b[:, 0:Wc], lhsT=onesN[:], rhs=sq[:, 0:Wc], start=(kc == 0), stop=(kc == KC - 1)),
                 reads=[cs, sq], writes=[ps_b])
        P.op("act", lambda e: e.activation(out=mean[:, 0:Wc], in_=ps_a[:, 0:Wc], func=AF.Identity), reads=[ps_a], writes=[mean])
        P.op("dve", lambda e: e.tensor_tensor(out=tmpm[:, 0:Wc], in0=mean[:, 0:Wc], in1=mean[:, 0:Wc], op=ALU.mult), reads=[mean], writes=[tmpm])
        P.op("dve", lambda e: e.tensor_tensor(out=tmpm[:, 0:Wc], in0=ps_b[:, 0:Wc], in1=tmpm[:, 0:Wc], op=ALU.subtract), reads=[ps_b, tmpm], writes=[tmpm])
        P.op("act", lambda e: e.activation(out=tmpm[:, 0:Wc], in_=tmpm[:, 0:Wc], func=AF.Sqrt, bias=EPS), reads=[tmpm], writes=[tmpm])
        P.op("dve", lambda e: e.reciprocal(out=rstd[:, 0:Wc], in_=tmpm[:, 0:Wc]), reads=[tmpm], writes=[rstd])
        for kc in range(KC):
            t = nft()
            P.op("pool", lambda e, t=t, kc=kc: e.tensor_tensor(out=t[:, 0:Wc], in0=xr[:, kc, 0:Wc], in1=mean[:, 0:Wc], op=ALU.subtract), reads=[xr, mean], writes=[t])
            P.op("dve", lambda e, t=t: e.tensor_tensor(out=t[:, 0:Wc], in0=t[:, 0:Wc], in1=rstd[:, 0:Wc], op=ALU.mult), reads=[t, rstd], writes=[t])
            if out_bf is None:
                P.op("act", lambda e, t=t, kc=kc: e.activation(out=xr[:, kc, 0:Wc], in_=t[:, 0:Wc], func=AF.Identity,
                                                               scale=vecs[:, gcol + kc:gcol + kc + 1], bias=vecs[:, bcol + kc:bcol + kc + 1]),
                     reads=[t, vecs], writes=[xr])
            else:
                P.op("act", lambda e, t=t, kc=kc: e.activation(out=out_bf[:, kc, 0:Wc], in_=t[:, 0:Wc], func=AF.Identity,
                                                               scale=scale_t[:, kc:kc + 1], bias=bias_t[:, kc:kc + 1]),
                     reads=[t, scale_t, bias_t], writes=[out_bf])

    sched = []

    def add_tile(c0, Wc, halo):
        oc0 = c0 - 2

        def prologue():
            P.dma("sp", xr[:, :, 0:Wc], xT.h.ap()[:, c0:c0 + Wc].rearrange("(kc p) n -> p kc n", p=128), writes=[xr])
            P.dma("sp", yn[:, :, 0:Wc], ynsaT.h.ap()[:, c0:c0 + Wc].rearrange("(kc p) n -> p kc n", p=128), writes=[yn])
            P.dma("sp", hm[:, :, 0:Wc], hmlT.h.ap()[:, c0:c0 + Wc].rearrange("(kc p) n -> p kc n", p=128), writes=[hm])
            for hh in range(4):
                P.op("dve", lambda e, hh=hh: e.tensor_tensor(out=hsq[:, :, 0:Wc], in0=hm[:, 2 * hh:2 * hh + 2, 0:Wc], in1=hm[:, 2 * hh:2 * hh + 2, 0:Wc], op=ALU.mult),
                     reads=[hm], writes=[hsq])
                for c in range(2):
                    P.op("pe", lambda e, hh=hh, c=c: e.matmul(ps_a[:, 0:Wc], lhsT=csb[:], rhs=hm[:, 2 * hh + c, 0:Wc], start=(c == 0), stop=(c == 1)),
                         reads=[csb, hm], writes=[ps_a])
                for c in range(2):
                    P.op("pe", lambda e, c=c: e.matmul(ps_b[:, 0:Wc], lhsT=csb[:], rhs=hsq[:, c, 0:Wc], start=(c == 0), stop=(c == 1)),
                         reads=[csb, hsq], writes=[ps_b])
                P.op("act", lambda e: e.activation(out=mean[:, 0:Wc], in_=ps_a[:, 0:Wc], func=AF.Identity), reads=[ps_a], writes=[mean])
                P.op("dve", lambda e: e.tensor_tensor(out=tmpm[:, 0:Wc], in0=mean[:, 0:Wc], in1=mean[:, 0:Wc], op=ALU.mult), reads=[mean], writes=[tmpm])
                P.op("dve", lambda e: e.tensor_tensor(out=tmpm[:, 0:Wc], in0=ps_b[:, 0:Wc], in1=tmpm[:, 0:Wc], op=ALU.subtract), reads=[ps_b, tmpm], writes=[tmpm])
                P.op("dve", lambda e: e.tensor_scalar_max(out=tmpm[:, 0:Wc], in0=tmpm[:, 0:Wc], scalar1=0.0), reads=[tmpm], writes=[tmpm])
                P.op("act", lambda e: e.activation(out=tmpm[:, 0:Wc], in_=tmpm[:, 0:Wc], func=AF.Sqrt, bias=EPS), reads=[tmpm], writes=[tmpm])
                P.op("dve", lambda e: e.reciprocal(out=rstd[:, 0:Wc], in_=tmpm[:, 0:Wc]), reads=[tmpm], writes=[rstd])
                for c in range(2):
                    ch = 2 * hh + c
                    m_ = mo[ch % 4]
                    t = nft()
                    sg = nft()
                    P.dma("sp", m_[:, 0:Wc], pT2.h.ap()[ch * 128:(ch + 1) * 128, c0:c0 + Wc], writes=[m_])
                    P.op("act", lambda e, sg=sg, m_=m_: e.activation(out=sg[:, 0:Wc], in_=m_[:, 0:Wc], func=AF.Sigmoid), reads=[m_], writes=[sg])
                    P.op("pool", lambda e, t=t, ch=ch: e.tensor_tensor(out=t[:, 0:Wc], in0=hm[:, ch, 0:Wc], in1=mean[:, 0:Wc], op=ALU.subtract), reads=[hm, mean], writes=[t])
                    P.op("dve", lambda e, t=t: e.tensor_tensor(out=t[:, 0:Wc], in0=t[:, 0:Wc], in1=rstd[:, 0:Wc], op=ALU.mult), reads=[t, rstd], writes=[t])
                    P.op("dve", lambda e, t=t, sg=sg, ch=ch: e.scalar_tensor_tensor(out=yml[:, ch, 0:Wc], in0=t[:, 0:Wc], scalar=vecs[:, VEC_NG + ch:VEC_NG + ch + 1],
                                                                                  in1=sg[:, 0:Wc], op0=ALU.mult, op1=ALU.mult),
                         reads=[t, sg, vecs], writes=[yml])

        for sgi in range(8):
            def run(wbt, sgi=sgi):
                wv = wbt.h.ap().rearrange("p (w k n) -> p w k n", w=2, k=8)
                for o2 in range(2):
                    oc = sgi * 2 + o2
                    a1, a2 = nacc(), nacc()
                    for kc in range(8):
                        P.op("pe", lambda e, a1=a1, kc=kc, o2=o2: e.matmul(a1[:, 0:Wc], lhsT=wv[:, 0, kc, o2 * 128:(o2 + 1) * 128], rhs=yn[:, kc, 0:Wc],
                                                                          start=(kc == 0), stop=(kc == 7)), reads=[wbt, yn], writes=[a1])
                    for kc in range(8):
                        P.op("pe", lambda e, a2=a2, kc=kc, o2=o2: e.matmul(a2[:, 0:Wc], lhsT=wv[:, 1, kc, o2 * 128:(o2 + 1) * 128], rhs=yml[:, kc, 0:Wc],
                                                                          start=(kc == 0), stop=(kc == 7)), reads=[wbt, yml], writes=[a2])
                    g1t, g2t, s1, s2, t1_, t2_ = mo[0 + (oc % 2) * 2], mo[1 + (oc % 2) * 2], nft(), nft(), nft(), nft()
                    P.dma("sp", g1t[:, 0:Wc], pT2.h.ap()[(8 + oc) * 128:(9 + oc) * 128, c0:c0 + Wc], writes=[g1t])
                    P.dma("sp", g2t[:, 0:Wc], pT2.h.ap()[(24 + oc) * 128:(25 + oc) * 128, c0:c0 + Wc], writes=[g2t])
                    P.op("act", lambda e, s1=s1, g1t=g1t: e.activation(out=s1[:, 0:Wc], in_=g1t[:, 0:Wc], func=AF.Sigmoid), reads=[g1t], writes=[s1])
                    P.op("act", lambda e, s2=s2, g2t=g2t: e.activation(out=s2[:, 0:Wc], in_=g2t[:, 0:Wc], func=AF.Sigmoid), reads=[g2t], writes=[s2])
                    P.op("dve", lambda e, a1=a1, s1=s1, t1_=t1_: e.tensor_tensor(out=t1_[:, 0:Wc], in0=a1[:, 0:Wc], in1=s1[:, 0:Wc], op=ALU.mult), reads=[a1, s1], writes=[t1_])
                    P.op("dve", lambda e, a2=a2, s2=s2, t2_=t2_: e.tensor_tensor(out=t2_[:, 0:Wc], in0=a2[:, 0:Wc], in1=s2[:, 0:Wc], op=ALU.mult), reads=[a2, s2], writes=[t2_])
                    P.op("pool", lambda e, t1_=t1_, t2_=t2_, oc=oc: e.tensor_tensor(out=merged[:, oc, 0:Wc], in0=t1_[:, 0:Wc], in1=t2_[:, 0:Wc], op=ALU.add),
                         reads=[t1_, t2_], writes=[merged])
            sched.append(dict(loads=[(w_brn.h.ap()[:, sgi * 256:(sgi + 1) * 256].rearrange("(kc p) n -> p kc n", p=128), 0, 8, 256),
                                     (w_brm.h.ap()[:, sgi * 256:(sgi + 1) * 256].rearrange("(kc p) n -> p kc n", p=128), 2048, 8, 256)],
                              run=run, pre=(prologue if sgi == 0 else None)))
        for sgi in range(8):
            def pre_o():
                P.op("pool", lambda e: e.tensor_scalar(out=xr[:, :, 0:Wc], in0=xr[:, :, 0:Wc], scalar1=ALPHA, scalar2=None, op0=ALU.mult), reads=[xr], writes=[xr])

            def run(wbt, sgi=sgi):
                wv = wbt.h.ap().rearrange("p (k n) -> p k n", k=16)
                for o2 in range(2):
                    oc = sgi * 2 + o2
                    a = nacc()
                    for kc in range(KC):
                        P.op("pe", lambda e, a=a, kc=kc, o2=o2: e.matmul(a[:, 0:Wc], lhsT=wv[:, kc, o2 * 128:(o2 + 1) * 128], rhs=merged[:, kc, 0:Wc],
                                                                        start=(kc == 0), stop=(kc == KC - 1)), reads=[wbt, merged], writes=[a])
                    P.op("dve", lambda e, a=a, oc=oc: e.scalar_tensor_tensor(out=xr[:, oc, 0:Wc], in0=a[:, 0:Wc], scalar=modT[:, oc:oc + 1], in1=xr[:, oc, 0:Wc],
                                                                            op0=ALU.mult, op1=ALU.add), reads=[a, modT, xr], writes=[xr])
            sched.append(dict(loads=[(dview(w_o, sgi * 256, (sgi + 1) * 256), 0, 16, 256)], run=run, pre=(pre_o if sgi == 0 else None)))
        for fc in range(NFC):
            def pre_up():
                layer_norm_inplace(Wc, VEC_LG0, VEC_LB0)
                layer_norm_inplace(Wc, 0, 0, out_bf=h2, scale_t=sc2p, bias_t=Sub(modF, (slice(None), slice(48, 64))))

            def run(wbt, fc=fc):
                wv = wbt.h.ap().rearrange("p (k n) -> p k n", k=16)
                aa = nacc()
                for kc in range(KC):
                    P.op("pe", lambda e, aa=aa, kc=kc: e.matmul(aa[:, 0:Wc], lhsT=wv[:, kc, 0:128], rhs=h2[:, kc, 0:Wc], start=(kc == 0), stop=(kc == KC - 1)),
                         reads=[wbt, h2], writes=[aa])
                if halo:
                    P.op("dve", lambda e, aa=aa, fc=fc: e.tensor_scalar(out=carry[:, fc, :], in0=aa[:, 0:2], scalar1=vecs[:, VEC_FLAG:VEC_FLAG + 1], scalar2=None, op0=ALU.mult),
                         reads=[aa, vecs], writes=[carry])
                    return
                ag = nacc()
                for kc in range(KC):
                    P.op("pe", lambda e, ag=ag, kc=kc: e.matmul(ag[:, 0:Wc], lhsT=wv[:, kc, 128:256], rhs=h2[:, kc, 0:Wc], start=(kc == 0), stop=(kc == KC - 1)),
                         reads=[wbt, h2], writes=[ag])
                ab = abuf[fc % 2]
                cv, sa = nft(), nft()
                P.op("act", lambda e, ab=ab, aa=aa: e.activation(out=ab[:, 2:2 + Wc], in_=aa[:, 0:Wc], func=AF.Identity), reads=[aa], writes=[ab])
                P.op("pool", lambda e, ab=ab, fc=fc: e.tensor_copy(out=ab[:, 0:2], in_=carry[:, fc, :]), reads=[carry], writes=[ab])
                cwc = VEC_CW + fc * 3
                P.op("dve", lambda e, ab=ab, cv=cv, cwc=cwc: e.tensor_scalar(out=cv[:, 0:Wc], in0=ab[:, 0:Wc], scalar1=vecs[:, cwc:cwc + 1], scalar2=None, op0=ALU.mult),
                     reads=[ab, vecs], writes=[cv])
                for j in (1, 2):
                    P.op("dve", lambda e, ab=ab, cv=cv, cwc=cwc, j=j: e.scalar_tensor_tensor(out=cv[:, 0:Wc], in0=ab[:, j:j + Wc], scalar=vecs[:, cwc + j:cwc + j + 1],
                                                                                            in1=cv[:, 0:Wc], op0=ALU.mult, op1=ALU.add),
                         reads=[ab, vecs, cv], writes=[cv])
                P.op("pool", lambda e, ab=ab, fc=fc: e.tensor_copy(out=carry[:, fc, :], in_=ab[:, Wc:Wc + 2]), reads=[ab], writes=[carry])
                P.op("act", lambda e, cv=cv, sa=sa, fc=fc: e.activation(out=sa[:, 0:Wc], in_=cv[:, 0:Wc], func=AF.Silu, bias=vecs[:, VEC_CB + fc:VEC_CB + fc + 1]),
                     reads=[cv, vecs], writes=[sa])
                P.op("dve", lambda e, sa=sa, ag=ag, fc=fc: e.tensor_tensor(out=u[:, fc, 0:Wc], in0=ag[:, 0:Wc], in1=sa[:, 0:Wc], op=ALU.mult), reads=[ag, sa], writes=[u])
            loads = [(dview(w_up, fc * 128, (fc + 1) * 128), 0, 16, 128)]
            if not halo:
                loads.append((dview(w_up, D_FF + fc * 128, D_FF + (fc + 1) * 128), 0, 16, 128))
            sched.append(dict(loads=loads, run=run, pre=(pre_up if fc == 0 else None), interleave=True))
        if halo:
            return
        for oc in range(KC):
            for hf in range(2):
                def pre_d():
                    P.op("pool", lambda e: e.tensor_scalar(out=xr[:, :, 0:Wc], in0=xr[:, :, 0:Wc], scalar1=ALPHA, scalar2=None, op0=ALU.mult), reads=[xr], writes=[xr])

                def run(wbt, oc=oc, hf=hf):
                    wv = wbt.h.ap()[:, 0:22 * 128].rearrange("p (k n) -> p k n", k=22)
                    if hf == 0:
                        state["dacc"] = nacc()
                    a = state["dacc"]
                    for k2 in range(22):
                        fc = hf * 22 + k2
                        P.op("pe", lambda e, a=a, k2=k2, fc=fc: e.matmul(a[:, 0:Wc], lhsT=wv[:, k2, :], rhs=u[:, fc, 0:Wc], start=(fc == 0), stop=(fc == NFC - 1)),
                             reads=[wbt, u], writes=[a])
                    if hf == 1:
                        P.op("dve", lambda e, a=a, oc=oc: e.scalar_tensor_tensor(out=xr[:, oc, 0:Wc], in0=a[:, 0:Wc], scalar=modT[:, 48 + oc:49 + oc], in1=xr[:, oc, 0:Wc],
                                                                                op0=ALU.mult, op1=ALU.add), reads=[a, modT, xr], writes=[xr])
                        if oc == KC - 1:
                            layer_norm_inplace(Wc, VEC_LG1, VEC_LB1)
                            P.dma("sp", xoT.h.ap()[:, oc0:oc0 + Wc].rearrange("(kc p) n -> p kc n", p=128), xr[:, :, 0:Wc], reads=[xr])
                r0 = hf * 22 * 128
                sched.append(dict(loads=[(w_down.h.ap()[r0:r0 + 22 * 128, oc * 128:(oc + 1) * 128].rearrange("(k p) n -> p k n", p=128), 0, 22, 128)],
                                  run=run, pre=(pre_d if (oc == 0 and hf == 0) else None)))

    add_tile(0, 2, True)
    for tt in range(NT // W):
        add_tile(2 + tt * W, W, False)

    def issue_dma(i):
        st = sched[i]
        f = wff[i % 2]
        if st.get("interleave"):
            v = wf[i % 2]
            for li, (ap, off, k, n) in enumerate(st["loads"]):
                P.dma("sp", v[:, :, li * 128:(li + 1) * 128], ap, writes=[wf[i % 2]])
        else:
            for (ap, off, k, n) in st["loads"]:
                P.dma("sp", f[:, off:off + k * n].rearrange("p (k n) -> p k n", n=n), ap, writes=[wf[i % 2]])

    def cast(i):
        eng = "dve" if i % 2 == 0 else "pool"
        P.op(eng, lambda e, i=i: e.tensor_copy(out=wb[i % 2][:], in_=wff[i % 2]), reads=[wf[i % 2]], writes=[wb[i % 2]])

    n = len(sched)
    issue_dma(0)
    issue_dma(1)
    cast(0)
    for i in range(n):
        if sched[i].get("pre"):
            sched[i]["pre"]()
        sched[i]["run"](wb[i % 2])
        if i + 1 < n:
            cast(i + 1)
        if i + 2 < n:
            issue_dma(i + 2)
    P.finish()
    P.emit()
    return nc


def build_M():
    nc = bass.Bass("TRN2", target_bir_lowering=False)
    P = Prog(nc)
    cT = P.dram("cT", [128, KC], F32, "ExternalInput")
    wsl = P.dram("wsl", [D, 3072], F32, "ExternalInput")
    bsl = P.dram("bsl", [1, 3072], F32, "ExternalInput")
    one_d = P.dram("one", [1, 1], F32, "ExternalInput")
    modp = P.dram("modp", [128, 24], F32, "ExternalOutput")
    stage = [P.sb(f"st{i}", [128, KC, 512], F32) for i in range(2)]
    one1 = P.sb("one1", [1, 1], F32)
    ps_row = P.ps("ps_row", [128, 512], F32)
    ps_t = P.ps("ps_t", [128, 512], F32)
    P.dma("sp", one1[:], one_d[:], writes=[one1])
    modT = mod_vectors(P, cT, wsl, bsl, 3072, stage, ps_row, ps_t, one1, cw=512)
    P.dma("sp", modp[:], modT[:], reads=[modT])
    P.finish()
    P.emit()
    return nc


def b1_inputs(projT, cmp_pe, cmp_w1, cmp_w2, consts, h):
    g, r = h // 4, h % 4
    heads = [g * 4 + r] + [g * 4 + o for o in range(4) if o != r]
    q4 = np.stack([projT[hd * 128:(hd + 1) * 128] for hd in heads], axis=0)
    def kvrow(br, kvi):
        b0 = 1024 + ((br * 2 + kvi) * 2 + g) * 128
        return projT[b0:b0 + 128]
    gT = projT[2560 + h * 3: 2560 + h * 3 + 3]
    gat = np.ascontiguousarray(gT.T.reshape(128, 128, 3).transpose(1, 0, 2))
    m = dict(q4=np.ascontiguousarray(q4), kcmpT=np.ascontiguousarray(kvrow(0, 0)), vcmpT=np.ascontiguousarray(kvrow(0, 1)),
             kslcT=np.ascontiguousarray(kvrow(1, 0)), vslc=np.ascontiguousarray(kvrow(1, 1).T),
             kwinT=np.ascontiguousarray(kvrow(2, 0)), vwin=np.ascontiguousarray(kvrow(2, 1).T), gat=gat,
             peT=np.ascontiguousarray(cmp_pe.transpose(2, 0, 1)),
             w1=np.ascontiguousarray(cmp_w1.reshape(2, 32, 128, 256).transpose(0, 2, 1, 3)),
             w2=np.ascontiguousarray(cmp_w2.reshape(2, 2, 128, 128).transpose(2, 0, 1, 3)))
    m.update(consts)
    return m

def b2_inputs(projT, ml_conv_w, ml_conv_b, ml_gate_b, consts, i):
    hh, half = i // 2, i % 2
    QK0 = 2584; V0 = QK0 + 1024; IF0 = V0 + 1024
    m = dict(qraw=np.ascontiguousarray(projT[QK0 + hh * 128: QK0 + (hh + 1) * 128]),
             kraw=np.ascontiguousarray(projT[QK0 + 512 + hh * 128: QK0 + 512 + (hh + 1) * 128]),
             vtok=np.ascontiguousarray(projT[V0 + hh * 256 + half * 128: V0 + hh * 256 + (half + 1) * 128].T),
             gif=np.ascontiguousarray(np.stack([projT[IF0 + hh].reshape(128, 128), projT[IF0 + 4 + hh].reshape(128, 128)], axis=1)))
    cq = ml_conv_w[:, hh * 128:(hh + 1) * 128]
    ck = ml_conv_w[:, 512 + hh * 128:512 + (hh + 1) * 128]
    m["convw"] = np.ascontiguousarray(np.stack([cq.T, ck.T], axis=1)).astype(np.float32)
    m["convb"] = np.ascontiguousarray(np.stack([ml_conv_b[hh * 128:(hh + 1) * 128], ml_conv_b[512 + hh * 128:512 + (hh + 1) * 128]], axis=1)).astype(np.float32)
    m["gateb"] = np.ascontiguousarray(np.broadcast_to(np.array([ml_gate_b[hh], ml_gate_b[4 + hh]], np.float32)[None, :], (128, 2)))
    m.update(consts)
    return m

def c_consts():
    return dict(cst=np.concatenate([np.full((128, 128), 1.0 / 2048, np.float32), np.full((128, 128), 1.0 / 256, np.float32),
                                    np.ones((128, 1), np.float32)], axis=1))

def pvec(v):
    return np.ascontiguousarray(np.asarray(v, np.float32).reshape(-1, 128).T)

def c_inputs(xT_full, ynsaT, hmlT, projT, prm, l, i, consts, modT):
    t0 = i * 2048
    def halo(a):
        if i == 0:
            return np.ascontiguousarray(np.concatenate([np.zeros((a.shape[0], 2), a.dtype), a[:, 0:2048]], axis=1))
        return np.ascontiguousarray(a[:, t0 - 2:t0 + 2048])
    vecs = np.concatenate([pvec(prm["ml_norm_g"][l]), pvec(prm["ln_g"][l, 0]), pvec(prm["ln_b"][l, 0]), pvec(prm["ln_g"][l, 1]), pvec(prm["ln_b"][l, 1]),
                           np.ascontiguousarray(np.asarray(prm["ffn_conv_w"][l], np.float32).reshape(3, 44, 128).transpose(2, 1, 0)).reshape(128, 132),
                           pvec(prm["ffn_conv_b"][l]), np.full((128, 1), 0.0 if i == 0 else 1.0, np.float32)], axis=1)
    m = dict(xT=halo(xT_full), ynsaT=halo(ynsaT), hmlT=halo(hmlT), pT2=halo(projT[4640:9760]),
             w_brn=prm["w_br_nsa"][l], w_brm=prm["w_br_ml"][l], w_o=prm["w_o"][l], w_up=prm["w_up"][l], w_down=prm["w_down"][l],
             modT=modT, vecs=np.ascontiguousarray(vecs.astype(np.float32)))
    m.update(consts)
    return m


_PROGS = {}


def _prog(name):
    if name not in _PROGS:
        _PROGS[name] = {"M": build_M, "A": build_A, "B1": build_B1, "B2": build_B2, "C": build_C}[name]()
    return _PROGS[name]


def _run(name, in_maps):
    res = run_bass_kernel_spmd(_prog(name), in_maps, core_ids=list(range(NCORE)))
    return res.results


def kernel(**inp):
    prm = {k: np.asarray(v) for k, v in inp.items()}
    x = prm["x"][0]
    cT = np.ascontiguousarray(prm["c"][0].reshape(16, 128).T.astype(np.float32))
    one = np.ones((1, 1), np.float32)
    in_maps = []
    for i in range(NCORE):
        sl = slice(i * 1536, (i + 1) * 1536)
        in_maps.append(dict(cT=cT, wsl=np.ascontiguousarray(np.concatenate([prm["w_ada"][0][:, sl], prm["w_ada"][1][:, sl]], axis=1)),
                            bsl=np.ascontiguousarray(np.concatenate([prm["b_ada"][0][sl], prm["b_ada"][1][sl]])[None, :]), one=one))
    r = _run("M", in_maps)
    modT = [np.ascontiguousarray(np.concatenate([np.asarray(r[i]["modp"])[:, l * 12:(l + 1) * 12] for i in range(NCORE)], axis=1)) for l in range(2)]

    cstA = np.concatenate([np.full((128, 128), 1.0 / 2048, np.float32), np.ones((128, 1), np.float32)], axis=1)
    c1, c2, cc = b1_consts(), b2_consts(), c_consts()
    xT = np.ascontiguousarray(x.T)
    for l in range(2):
        r = _run("A", [dict(xT=np.ascontiguousarray(xT[:, i * NT:(i + 1) * NT]), modT=modT[l], w_in=prm["w_in"][l], cst=cstA) for i in range(NCORE)])
        projT = np.concatenate([np.asarray(r[i]["projT"]) for i in range(NCORE)], axis=1)
        r = _run("B1", [b1_inputs(projT, prm["cmp_pe"][l], prm["cmp_w1"][l], prm["cmp_w2"][l], c1, h) for h in range(NCORE)])
        ynsaT = np.concatenate([np.asarray(r[h]["yT"]) for h in range(NCORE)], axis=0)
        r = _run("B2", [b2_inputs(projT, prm["ml_conv_w"][l], prm["ml_conv_b"][l], prm["ml_gate_b"][l], c2, i) for i in range(NCORE)])
        hmlT = np.concatenate([np.asarray(r[i]["hT"]) for i in range(NCORE)], axis=0)
        r = _run("C", [c_inputs(xT, ynsaT, hmlT, projT, prm, l, i, cc, modT[l]) for i in range(NCORE)])
        xT = np.concatenate([np.asarray(r[i]["xoT"]) for i in range(NCORE)], axis=1)
    return np.ascontiguousarray(xT.T)[None].astype(np.float32)
```

```python
import ml_dtypes
from concourse.bass_utils import run_bass_kernel_spmd
import numpy as np
from contextlib import ExitStack
import concourse.bass as bass
import concourse.mybir as mybir

F32 = mybir.dt.float32
BF16 = mybir.dt.bfloat16
I32 = mybir.dt.int32
ALU = mybir.AluOpType
AF = mybir.ActivationFunctionType
AX = mybir.AxisListType

ENGS = ("pe", "act", "dve", "pool", "sp")


class Tile:
    def __init__(self, name, handle, space):
        self.name = name
        self.h = handle
        self.space = space
        self.last_w = None
        self.readers = {}
        self.dsem = None
        self.ssem = None

    def __getitem__(self, idx):
        return self.h[idx]


class Sub:
    def __init__(self, parent, idx):
        self.parent = parent
        self.idx = idx

    def __getitem__(self, i):
        return self.parent.h[self.idx][i]


def _par(t):
    return t.parent if isinstance(t, Sub) else t


class Prog:
    def __init__(self, nc):
        self.nc = nc
        self.es = ExitStack()
        self.prog = {e: [] for e in ENGS}
        self.sem = {}
        self.cnt = {e: 0 for e in ENGS}
        self.waited = {e: {} for e in ENGS}
        self.nsem = 0
        for e in ENGS:
            self.sem[e] = self._newsem("e_" + e)
        self.n_inst = 0
        self.inherit = {}
        self.scope_tiles = None
        self.scope_stack = []
        self.sem_pool = []
        self.all_dma_sems = []
        self.final_eng = []

    def _newsem(self, name):
        self.nsem += 1
        return self.nc.alloc_semaphore(name=name + "_%d" % self.nsem)

    def _track(self, t):
        t.readers = dict(self.inherit)
        if self.scope_tiles is not None:
            self.scope_tiles.append(t)
        return t

    def sb(self, name, shape, dtype):
        h = self.es.enter_context(self.nc.sbuf_tensor("s_" + name, list(shape), dtype))
        return self._track(Tile(name, h, "sb"))

    def ps(self, name, shape, dtype):
        h = self.es.enter_context(self.nc.psum_tensor("p_" + name, list(shape), dtype))
        return self._track(Tile(name, h, "ps"))

    def dram(self, name, shape, dtype, kind):
        h = self.nc.dram_tensor(name, list(shape), dtype, kind=kind)
        return Tile(name, h, "dram")

    def scope_begin(self):
        self.scope_stack.append((self.es, self.scope_tiles))
        self.es = ExitStack()
        self.scope_tiles = []

    def scope_end(self):
        for t in self.scope_tiles:
            toks = list(t.readers.values()) + ([t.last_w] if t.last_w is not None else [])
            for tok in toks:
                cur = self.inherit.get(tok[0])
                if cur is None or cur[2] < tok[2]:
                    self.inherit[tok[0]] = tok
            for rec in (t.dsem, t.ssem):
                if rec is not None:
                    self.sem_pool.append(rec)
            t.dsem = None
            t.ssem = None
        self.es.close()
        self.es, self.scope_tiles = self.scope_stack.pop()

    def new_epoch(self):
        for e in ENGS:
            if self.cnt[e] > 0:
                self.final_eng.append((e, self.sem[e], self.cnt[e]))
            self.sem[e] = self._newsem("e_" + e)
            self.cnt[e] = 0

    def _dma_sem(self, name):
        if self.sem_pool:
            return self.sem_pool.pop()
        rec = [self._newsem(name), 0]
        self.all_dma_sems.append(rec)
        return rec

    def _deps(self, eng, reads, writes, is_dma=False, dma_tile=None):
        deps = []
        for t in reads:
            if t.last_w is not None:
                deps.append((t.last_w, "raw"))
        for t in writes:
            if t.last_w is not None:
                deps.append((t.last_w, "waw"))
            for tok in t.readers.values():
                deps.append((tok, "war"))
        waits = []
        for (semkey, semh, val, src), kind in deps:
            if src == eng and not is_dma:
                if eng == "pe":
                    continue
                if kind == "war":
                    continue
            if (is_dma and kind == "waw" and src == "dma" and dma_tile is not None and dma_tile.dsem is not None
                    and semkey == id(dma_tile.dsem[0])):
                continue
            if self.waited[eng].get(semkey, 0) >= val:
                continue
            self.waited[eng][semkey] = val
            waits.append((semh, val))
        return waits

    def op(self, eng, fn, reads=(), writes=()):
        reads = [_par(t) for t in reads]
        writes = [_par(t) for t in writes]
        waits = self._deps(eng, reads, writes)
        self.cnt[eng] += 1
        tok = (id(self.sem[eng]), self.sem[eng], self.cnt[eng], eng)
        self.prog[eng].append((waits, fn, self.sem[eng], 1))
        for t in reads:
            t.readers[tok[0]] = tok
        for t in writes:
            t.last_w = tok
            t.readers = {}
        self.n_inst += 1

    def dma(self, q, out_ap, in_ap, reads=(), writes=(), **kw):
        def fn(e, out_ap=out_ap, in_ap=in_ap, kw=kw):
            return e.dma_start(out=out_ap, in_=in_ap, **kw)
        self.dma_fn(q, fn, reads, writes)

    def dma_fn(self, q, fn, reads=(), writes=(), inc=16):
        reads = [_par(t) for t in reads]
        writes = [_par(t) for t in writes]
        if writes:
            t = writes[0]
            if t.dsem is None:
                t.dsem = self._dma_sem("d_" + t.name)
            rec = t.dsem
            waits = self._deps(q, reads, writes, is_dma=True, dma_tile=t)
        else:
            t = reads[0]
            if t.ssem is None:
                t.ssem = self._dma_sem("s_" + t.name)
            rec = t.ssem
            waits = self._deps(q, reads, writes, is_dma=True)
        rec[1] += inc
        tok = (id(rec[0]), rec[0], rec[1], "dma")
        self.prog[q].append((waits, fn, rec[0], inc))
        for r in reads:
            r.readers[tok[0]] = tok
        for w in writes:
            w.last_w = tok
            w.readers = {}
        self.n_inst += 1

    def coll(self, q, fn, reads=(), writes=()):
        self.dma_fn(q, fn, reads, writes, inc=16)

    def finish(self):
        waits = []
        for rec in self.all_dma_sems:
            if rec[1] > 0:
                waits.append((rec[0], rec[1]))
        for e in ENGS:
            if e != "sp" and self.cnt[e] > 0:
                waits.append((self.sem[e], self.cnt[e]))
        for (e, semh, c) in self.final_eng:
            if e != "sp":
                waits.append((semh, c))
        self.prog["sp"].append((waits, None, None, 0))

    def emit(self):
        nc = self.nc
        prog = self.prog

        def replay(lst, e):
            for waits, fn, sem, inc in lst:
                for semh, val in waits:
                    e.wait_ge(semh, val)
                if fn is not None:
                    ins = fn(e)
                    ins.then_inc(sem, inc)

        with nc.Block() as block:
            @block.tensor
            def _(e):
                replay(prog["pe"], e)

            @block.scalar
            def _(e):
                replay(prog["act"], e)

            @block.vector
            def _(e):
                replay(prog["dve"], e)

            @block.gpsimd
            def _(e):
                replay(prog["pool"], e)

            @block.sync
            def _(e):
                replay(prog["sp"], e)
        self.es.close()


D = 2048
S = 16384
NCORE = 8
NT = S // NCORE
KC = D // 128
IN_COLS = 9760
D_FF = 5632
EPS = 1e-5
ALPHA = 4.0 ** 0.25


def dview(t, c0, c1):
    return t.h.ap()[:, c0:c1].rearrange("(kc p) n -> p kc n", p=128)


def mod_vectors(P, cT, wada, bada, ncols, stage, ps_row, ps_t, one1, cw=512):
    nch = ncols // cw
    sub = cw // 128
    cact = P.sb("cact", [128, KC], F32)
    brow = [P.sb(f"brow{i}", [1, cw], F32) for i in range(2)]
    mrow = [P.sb(f"mrow{i}", [1, cw], F32) for i in range(2)]
    modT = P.sb("modT", [128, ncols // 128], F32)
    P.dma("sp", cact[:], cT[:], writes=[cact])
    P.op("act", lambda e: e.activation(out=cact[:], in_=cact[:], func=AF.Silu), reads=[cact], writes=[cact])
    for j in range(nch):
        w = stage[j % 2]
        br = brow[j % 2]
        mr = mrow[j % 2]
        P.dma("sp", w[:], dview(wada, j * cw, (j + 1) * cw), writes=[w])
        P.dma("sp", br[:], bada.h.ap()[:, j * cw:(j + 1) * cw], writes=[br])
        for kc in range(KC):
            P.op("pe", lambda e, w=w, kc=kc: e.matmul(ps_row[0:1, 0:cw], lhsT=cact[:, kc:kc + 1], rhs=w[:, kc, :],
                                                      start=(kc == 0), stop=(kc == KC - 1)),
                 reads=[cact, w], writes=[ps_row])
        P.op("dve", lambda e, mr=mr, br=br: e.tensor_tensor(out=mr[0:1, :], in0=ps_row[0:1, 0:cw], in1=br[0:1, :], op=ALU.add),
             reads=[ps_row, br], writes=[mr])
        for c in range(sub):
            P.op("pe", lambda e, c=c, j=j, mr=mr: e.matmul(ps_t[:, sub * j + c:sub * j + c + 1], lhsT=mr[0:1, c * 128:(c + 1) * 128],
                                                          rhs=one1[0:1, 0:1], start=True, stop=True),
                 reads=[mr, one1], writes=[ps_t])
    P.op("dve", lambda e: e.tensor_copy(out=modT[:], in_=ps_t[:, 0:ncols // 128]), reads=[ps_t], writes=[modT])
    return modT


def ln_stats(P, z, sq, nkc, W, onesN, ps_a, ps_b, mean, rstd, tmpm):
    P.op("act", lambda e: e.activation(out=sq[:, 0:nkc, 0:W], in_=z[:, 0:nkc, 0:W], func=AF.Square), reads=[z], writes=[sq])
    for kc in range(nkc):
        P.op("pe", lambda e, kc=kc: e.matmul(ps_a[:, 0:W], lhsT=onesN[:], rhs=z[:, kc, 0:W], start=(kc == 0), stop=(kc == nkc - 1)),
             reads=[onesN, z], writes=[ps_a])
    for kc in range(nkc):
        P.op("pe", lambda e, kc=kc: e.matmul(ps_b[:, 0:W], lhsT=onesN[:], rhs=sq[:, kc, 0:W], start=(kc == 0), stop=(kc == nkc - 1)),
             reads=[onesN, sq], writes=[ps_b])
    P.op("act", lambda e: e.activation(out=mean[:, 0:W], in_=ps_a[:, 0:W], func=AF.Identity), reads=[ps_a], writes=[mean])
    P.op("dve", lambda e: e.tensor_tensor(out=tmpm[:, 0:W], in0=mean[:, 0:W], in1=mean[:, 0:W], op=ALU.mult), reads=[mean], writes=[tmpm])
    P.op("dve", lambda e: e.tensor_tensor(out=tmpm[:, 0:W], in0=ps_b[:, 0:W], in1=tmpm[:, 0:W], op=ALU.subtract), reads=[ps_b, tmpm], writes=[tmpm])
    P.op("act", lambda e: e.activation(out=tmpm[:, 0:W], in_=tmpm[:, 0:W], func=AF.Sqrt, bias=EPS), reads=[tmpm], writes=[tmpm])
    P.op("dve", lambda e: e.reciprocal(out=rstd[:, 0:W], in_=tmpm[:, 0:W]), reads=[tmpm], writes=[rstd])


def build_A():
    nc = bass.Bass("TRN2", target_bir_lowering=False)
    P = Prog(nc)
    xT = P.dram("xT", [D, NT], F32, "ExternalInput")
    modT_d = P.dram("modT", [128, 96], F32, "ExternalInput")
    w_in = P.dram("w_in", [D, IN_COLS], F32, "ExternalInput")
    cst = P.dram("cst", [128, 129], F32, "ExternalInput")
    projT = P.dram("projT", [IN_COLS, NT], BF16, "ExternalOutput")

    W = 512
    wf = [P.sb(f"wf{i}", [128, KC, W], F32) for i in range(2)]
    NST = 3
    wb = [P.sb(f"wb{i}", [128, KC, W], BF16) for i in range(NST)]
    hT = P.sb("hT", [128, KC, NT], BF16)
    ot = [P.sb(f"ot{i}", [128, NT], BF16) for i in range(2)]
    cs = P.sb("cs", [128, 129], F32)
    mean = P.sb("mean", [128, W], F32)
    rstd = P.sb("rstd", [128, W], F32)
    tmpm = P.sb("tmpm", [128, W], F32)
    t1 = [P.sb(f"t1_{i}", [128, W], F32) for i in range(2)]
    sc1p = P.sb("sc1p", [128, KC], F32)
    ps_a = P.ps("ps_a", [128, W], F32)
    ps_b = P.ps("ps_b", [128, W], F32)
    ps_row = P.ps("ps_row", [128, W], F32)
    ps_t = P.ps("ps_t", [128, W], F32)
    accs = [P.ps(f"acc{i}", [128, W], F32) for i in range(4)]

    P.dma("sp", cs[:], cst[:], writes=[cs])
    onesN = cs

    modT = P.sb("modT", [128, 96], F32)
    P.dma("sp", modT[:], modT_d[:], writes=[modT])
    P.op("dve", lambda e: e.tensor_scalar_add(out=sc1p[:], in0=modT[:, 16:32], scalar1=1.0), reads=[modT], writes=[sc1p])

    ncg = (IN_COLS + W - 1) // W

    def load_w(cg):
        c0 = cg * W
        cw = min(W, IN_COLS - c0)
        P.dma("pool", wb[cg % NST][:, :, 0:cw], dview(w_in, c0, c0 + cw), writes=[wb[cg % NST]])

    for cg in range(min(NST, ncg)):
        load_w(cg)

    z, sq = wf[0], wf[1]
    for tt in range(NT // W):
        P.dma("sp", z[:], xT.h.ap()[:, tt * W:(tt + 1) * W].rearrange("(kc p) n -> p kc n", p=128), writes=[z])
        ln_stats(P, z, sq, KC, W, Sub(cs, (slice(None), slice(0, 128))), ps_a, ps_b, mean, rstd, tmpm)
        for kc in range(KC):
            t = t1[kc % 2]
            P.op("pool", lambda e, t=t, kc=kc: e.tensor_tensor(out=t[:], in0=z[:, kc, :], in1=mean[:], op=ALU.subtract),
                 reads=[z, mean], writes=[t])
            P.op("dve", lambda e, t=t: e.tensor_tensor(out=t[:], in0=t[:], in1=rstd[:], op=ALU.mult), reads=[t, rstd], writes=[t])
            P.op("act", lambda e, t=t, kc=kc, tt=tt: e.activation(out=hT[:, kc, tt * W:(tt + 1) * W], in_=t[:], func=AF.Identity,
                                                                  scale=sc1p[:, kc:kc + 1], bias=modT[:, kc:kc + 1]),
                 reads=[t, sc1p, modT], writes=[hT])

    gi = 0
    oi = 0
    for cg in range(ncg):
        c0 = cg * W
        cw = min(W, IN_COLS - c0)
        bt = wb[cg % NST]
        for sub in range((cw + 127) // 128):
            m = min(128, cw - sub * 128)
            o = ot[oi % 2]
            oi += 1
            for tt in range(NT // W):
                acc = accs[gi % 4]
                for kc in range(KC):
                    P.op("pe", lambda e, acc=acc, bt=bt, kc=kc, sub=sub, m=m, tt=tt:
                         e.matmul(acc[0:m, :], lhsT=bt[:, kc, sub * 128:sub * 128 + m], rhs=hT[:, kc, tt * W:(tt + 1) * W],
                                  start=(kc == 0), stop=(kc == KC - 1)),
                         reads=[bt, hT], writes=[acc])
                if gi % 2 == 0:
                    P.op("act", lambda e, acc=acc, o=o, m=m, tt=tt: e.activation(out=o[0:m, tt * W:(tt + 1) * W], in_=acc[0:m, :], func=AF.Identity),
                         reads=[acc], writes=[o])
                else:
                    P.op("dve", lambda e, acc=acc, o=o, m=m, tt=tt: e.tensor_copy(out=o[0:m, tt * W:(tt + 1) * W], in_=acc[0:m, :]),
                         reads=[acc], writes=[o])
                gi += 1
            r0 = c0 + sub * 128
            P.dma("sp", projT.h.ap()[r0:r0 + m, :], o[0:m, :], reads=[o])
        if cg + NST < ncg:
            load_w(cg + NST)
    P.finish()
    P.emit()
    return nc


NEG = -30000.0
QW = 512
NQT = S // QW
SCALE = 128.0 ** -0.5
CMP_OFFS = [31, 31 - 512, 31 - 1024, 31 - 1536, 31 - 2048]


def b1_consts():
    bf = ml_dtypes.bfloat16
    p = np.arange(128)[:, None]
    f = np.arange(512)[None, :]
    c = {}
    c["ident"] = np.eye(128, dtype=np.float32).astype(bf)
    c["cmpmask"] = np.stack([np.where(f >= 16 * p + off, 0.0, NEG) for off in CMP_OFFS], axis=1).astype(bf)
    c["causal"] = np.stack([np.where(128 * i + p <= f, 0.0, NEG) for i in range(4)], axis=1).astype(bf)
    wm = []
    for i in range(8):
        dl = 128 * (i - 4)
        wm.append(np.where((f >= p + dl) & (f < p + dl + 512), 0.0, NEG))
    c["winmask"] = np.stack(wm, axis=1).astype(bf)
    bs = np.zeros((128, 64, 128), np.float32)
    for v in range(64):
        bs[2 * v, v, 0:64] = 1.0
        bs[2 * v + 1, v, 64:128] = 1.0
    c["bsel"] = bs.astype(bf)
    cc = (np.arange(8)[None, :, None] * 128 + np.arange(128)[:, None, None])
    s = np.arange(256)[None, None, :]
    ov = ((16 * cc < 64 * s + 64) & (16 * cc + 32 > 64 * s)).astype(np.float32)
    c["ov1"] = np.concatenate([ov, np.ones((128, 8, 1), np.float32)], axis=2).astype(bf)
    rel = np.arange(512)[None, :] - 256
    cur = (np.arange(128)[:, None] >= 64).astype(np.int64)
    c["cmv"] = (rel < cur - 1).astype(np.float32)
    c["cma"] = np.where((rel == cur) | (rel == cur - 1), 1e6, np.where(rel > cur, -1.0, 0.0)).astype(np.float32)
    return c


def build_B1():
    nc = bass.Bass("TRN2", target_bir_lowering=False)
    P = Prog(nc)
    DI = lambda n, sh, dt: P.dram(n, sh, dt, "ExternalInput")
    q4 = DI("q4", [4, 128, S], BF16)
    kcmpT = DI("kcmpT", [128, S], BF16)
    vcmpT = DI("vcmpT", [128, S], BF16)
    kslcT = DI("kslcT", [128, S], BF16)
    vslc = DI("vslc", [S, 128], BF16)
    kwinT = DI("kwinT", [128, S], BF16)
    vwin = DI("vwin", [S, 128], BF16)
    gat = DI("gat", [128, 128, 3], BF16)
    peT = DI("peT", [128, 2, 32], F32)
    w1 = DI("w1", [2, 128, 32, 256], F32)
    w2 = DI("w2", [128, 2, 2, 128], F32)
    ident_d = DI("ident", [128, 128], BF16)
    cmpmask_d = DI("cmpmask", [128, 5, 512], BF16)
    causal_d = DI("causal", [128, 4, 512], BF16)
    winmask_d = DI("winmask", [128, 8, 512], BF16)
    bsel_d = DI("bsel", [128, 64, 128], BF16)
    ov1_d = DI("ov1", [128, 8, 257], BF16)
    cmv_d = DI("cmv", [128, 512], F32)
    cma_d = DI("cma", [128, 512], F32)
    yT = P.dram("yT", [128, S], BF16, "ExternalOutput")

    ident = P.sb("ident", [128, 128], BF16)
    cmpmask = P.sb("cmpmask", [128, 5, 512], BF16)
    causal = P.sb("causal", [128, 4, 512], BF16)
    winmask = P.sb("winmask", [128, 8, 512], BF16)
    bsel = P.sb("bsel", [128, 64, 128], BF16)
    cmv = P.sb("cmv", [128, 512], F32)
    cma = P.sb("cma", [128, 512], F32)
    gates = P.sb("gates", [128, 128, 3], F32)
    ksT = P.sb("ksT", [128, S], BF16)
    vsa = P.sb("vsa", [128, 128, 129], BF16)
    kcT = P.sb("kcT", [128, 1024], BF16)
    vca = P.sb("vca", [128, 8, 385], BF16)
    for t, d in ((ident, ident_d), (cmpmask, cmpmask_d), (causal, causal_d), (winmask, winmask_d), (bsel, bsel_d),
                 (cmv, cmv_d), (cma, cma_d)):
        P.dma("sp", t[:], d[:], writes=[t])
    gtmp = P.sb("gtmp", [128, 128, 3], BF16)
    P.dma("sp", gtmp[:], gat[:], writes=[gtmp])
    P.op("act", lambda e: e.activation(out=gates[:], in_=gtmp[:], func=AF.Sigmoid), reads=[gtmp], writes=[gates])
    P.dma("sp", ksT[:], kslcT[:], writes=[ksT])
    P.dma("sp", vsa[:, :, 0:128], vslc.h.ap().rearrange("(j p) d -> p j d", p=128), writes=[vsa])
    P.op("pool", lambda e: e.memset(vsa[:, :, 128:129], 1.0), reads=[], writes=[vsa])
    P.dma("sp", vca[:, :, 0:257], ov1_d[:], writes=[vca])

    S_ps = [P.ps(f"S{i}", [128, 512], F32) for i in range(2)]
    acc = [P.ps(f"acc{i}", [128, 512], F32) for i in range(4)]
    tps = P.ps("tps", [128, 4, 128], BF16)
    mps = P.ps("mps", [128, 512], F32)

    P.scope_begin()
    xc = P.sb("xc", [128, S], BF16)
    w1f = P.sb("w1f", [128, 32, 256], F32)
    w1b = P.sb("w1b", [128, 32, 256], BF16)
    w2f = P.sb("w2f", [128, 2, 2, 128], F32)
    w2b = P.sb("w2b", [128, 2, 2, 128], BF16)
    pef = P.sb("pef", [128, 2, 32], F32)
    peb = P.sb("peb", [128, 2, 32], BF16)
    gel = [P.sb(f"gel{i}", [128, 1024], BF16) for i in range(2)]
    hb = P.sb("hb", [128, 1], F32)
    xh = P.sb("xh", [128, 512], F32)
    xu = P.sb("xu", [128, 512], F32)
    P.dma("sp", w2f[:], w2[:], writes=[w2f])
    P.dma("sp", pef[:], peT[:], writes=[pef])
    P.op("dve", lambda e: e.tensor_copy(out=w2b[:], in_=w2f[:]), reads=[w2f], writes=[w2b])
    P.op("dve", lambda e: e.tensor_copy(out=peb[:], in_=pef[:]), reads=[pef], writes=[peb])
    for kv in range(2):
        P.dma("sp", xc[:], (kcmpT if kv == 0 else vcmpT)[:], writes=[xc])
        P.dma("sp", w1f[:], w1.h.ap()[kv], writes=[w1f])
        P.op("dve", lambda e: e.tensor_copy(out=w1b[:], in_=w1f[:]), reads=[w1f], writes=[w1b])
        xv = xc.h.ap().rearrange("p (b s) -> p b s", s=16)
        for half in range(2):
            g_ = gel[half]
            P.op("pool", lambda e, g_=g_: e.memset(g_[:], 0.0), reads=[], writes=[g_])
            for j in range(32):
                P.op("pe", lambda e, j=j, half=half, kv=kv: e.matmul(mps[:, 0:1], lhsT=w1b[:, j, half * 128:(half + 1) * 128],
                                                                     rhs=peb[:, kv, j:j + 1], start=(j == 0), stop=(j == 31)),
                     reads=[w1b, peb], writes=[mps])
            P.op("dve", lambda e: e.tensor_copy(out=hb[:], in_=mps[:, 0:1]), reads=[mps], writes=[hb])
            for nci, (n0, cnt) in enumerate(((0, 512), (512, 511))):
                sp_ = S_ps[nci]
                for j in range(32):
                    b0 = n0 + j // 16
                    P.op("pe", lambda e, j=j, half=half, b0=b0, cnt=cnt, sp_=sp_, xv=xv:
                         e.matmul(sp_[:, 0:cnt], lhsT=w1b[:, j, half * 128:(half + 1) * 128], rhs=xv[:, b0:b0 + cnt, j % 16],
                                  start=(j == 0), stop=(j == 31)),
                         reads=[w1b, xc], writes=[sp_])
                P.op("act", lambda e, sp_=sp_, cnt=cnt: e.activation(out=xh[:, 0:cnt], in_=sp_[:, 0:cnt], func=AF.Identity, bias=hb[:, 0:1]),
                     reads=[sp_, hb], writes=[xh])
                P.op("dve", lambda e, cnt=cnt: e.tensor_tensor(out=xu[:, 0:cnt], in0=xh[:, 0:cnt], in1=xh[:, 0:cnt], op=ALU.mult), reads=[xh], writes=[xu])
                P.op("dve", lambda e, cnt=cnt: e.tensor_scalar(out=xu[:, 0:cnt], in0=xu[:, 0:cnt], scalar1=0.044715, scalar2=1.0,
                                                               op0=ALU.mult, op1=ALU.add), reads=[xu], writes=[xu])
                P.op("dve", lambda e, cnt=cnt: e.tensor_tensor(out=xu[:, 0:cnt], in0=xu[:, 0:cnt], in1=xh[:, 0:cnt], op=ALU.mult), reads=[xu, xh], writes=[xu])
                P.op("act", lambda e, cnt=cnt: e.activation(out=xu[:, 0:cnt], in_=xu[:, 0:cnt], func=AF.Sigmoid, scale=1.5957691216),
                     reads=[xu], writes=[xu])
                P.op("dve", lambda e, cnt=cnt, n0=n0, g_=g_: e.tensor_tensor(out=g_[:, n0:n0 + cnt], in0=xu[:, 0:cnt], in1=xh[:, 0:cnt], op=ALU.mult),
                     reads=[xu, xh], writes=[g_])
        if kv == 0:
            for nci in range(2):
                for half in range(2):
                    P.op("pe", lambda e, nci=nci, half=half: e.matmul(mps[:, :], lhsT=w2b[:, 0, half, :], rhs=gel[half][:, nci * 512:(nci + 1) * 512],
                                                                      start=(half == 0), stop=(half == 1)),
                         reads=[w2b, gel[half]], writes=[mps])
                P.op("dve", lambda e, nci=nci: e.tensor_copy(out=kcT[:, nci * 512:(nci + 1) * 512], in_=mps[:, :]), reads=[mps], writes=[kcT])
        else:
            for m in range(8):
                for half in range(2):
                    P.op("pe", lambda e, m=m, half=half: e.matmul(mps[:, 0:128], lhsT=gel[half][:, m * 128:(m + 1) * 128], rhs=w2b[:, 1, half, :],
                                                                  start=(half == 0), stop=(half == 1)),
                         reads=[w2b, gel[half]], writes=[mps])
                P.op("dve", lambda e, m=m: e.tensor_copy(out=vca[:, m, 257:385], in_=mps[:, 0:128]), reads=[mps], writes=[vca])

    P.scope_end()
    qt = [P.sb(f"qt{i}", [128, 4, QW], BF16) for i in range(2)]
    kwT = [P.sb(f"kwT{i}", [128, 1024], BF16) for i in range(2)]
    vwa = [P.sb(f"vwa{i}", [128, 8, 129], BF16) for i in range(2)]
    ET = [P.sb(f"ET{i}", [128, QW], BF16) for i in range(3)]
    imp = P.sb("imp", [128, 4, 256], F32)
    ocomb = P.sb("ocomb", [128, 4, 128], F32)
    ocb = P.sb("ocb", [128, 4, 128], BF16)
    rden = P.sb("rden", [128, 1], F32)
    gsc = P.sb("gsc", [128, 1], F32)
    score = P.sb("score", [128, 256], F32)
    sc2 = P.sb("sc2", [128, 256], F32)
    m8 = P.sb("m8", [128, 8], F32)
    negsel = P.sb("negsel", [128, 256], BF16)
    nsT = P.sb("nsT", [128, 2, QW], BF16)
    yo = [P.sb(f"yo{i}", [128, QW], BF16) for i in range(2)]
    for i in range(2):
        P.op("pool", lambda e, i=i: e.memset(vwa[i][:, :, 128:129], 1.0), reads=[], writes=[vwa[i]])
    cnt_s = [0]
    cnt_e = [0]

    def attend(qap, qtile, chunks, NV):
        n = len(chunks)
        sps = [None] * n

        def emit_qk(ci):
            kt_ap, kt_tile, v_ap, v_tile, masks = chunks[ci]
            sp_ = S_ps[cnt_s[0] % 2]
            cnt_s[0] += 1
            sps[ci] = sp_
            nm = len(masks)
            P.op("pe", lambda e, sp_=sp_, kt_ap=kt_ap, nm=nm: e.matmul(sp_[:, :], lhsT=kt_ap, rhs=qap, start=True, stop=(nm == 0)),
                 reads=[kt_tile, qtile], writes=[sp_])
            for mi, (ml, mr, mt) in enumerate(masks):
                P.op("pe", lambda e, sp_=sp_, ml=ml, mr=mr, mi=mi, nm=nm: e.matmul(sp_[:, :], lhsT=ml, rhs=mr, start=False, stop=(mi == nm - 1)),
                     reads=list(mt), writes=[sp_])

        emit_qk(0)
        for ci in range(n):
            if ci + 1 < n:
                emit_qk(ci + 1)
            kt_ap, kt_tile, v_ap, v_tile, masks = chunks[ci]
            sp_ = sps[ci]
            et = ET[cnt_e[0] % 3]
            cnt_e[0] += 1
            P.op("act", lambda e, sp_=sp_, et=et: e.activation(out=et[:, :], in_=sp_[:, :], func=AF.Exp, scale=SCALE), reads=[sp_], writes=[et])
            for qb in range(4):
                P.op("pe", lambda e, qb=qb, et=et, v_ap=v_ap, ci=ci: e.matmul(acc[qb][:, 0:NV], lhsT=et[:, qb * 128:(qb + 1) * 128], rhs=v_ap,
                                                                            start=(ci == 0), stop=(ci == n - 1)),
                     reads=[et, v_tile], writes=[acc[qb]])

    def fold_out(qb, col_den, col_o, gate_ap, first):
        a = acc[qb]
        P.op("dve", lambda e, a=a: e.tensor_scalar_max(out=rden[:], in0=a[:, col_den:col_den + 1], scalar1=1e-30), reads=[a], writes=[rden])
        P.op("dve", lambda e: e.reciprocal(out=rden[:], in_=rden[:]), reads=[rden], writes=[rden])
        P.op("dve", lambda e: e.tensor_tensor(out=gsc[:], in0=rden[:], in1=gate_ap, op=ALU.mult), reads=[rden, gates], writes=[gsc])
        if first:
            P.op("dve", lambda e, a=a: e.tensor_scalar(out=ocomb[:, qb, :], in0=a[:, col_o:col_o + 128], scalar1=gsc[:, 0:1], scalar2=None, op0=ALU.mult),
                 reads=[a, gsc], writes=[ocomb])
        else:
            P.op("dve", lambda e, a=a: e.scalar_tensor_tensor(out=ocomb[:, qb, :], in0=a[:, col_o:col_o + 128], scalar=gsc[:, 0:1], in1=ocomb[:, qb, :],
                                                              op0=ALU.mult, op1=ALU.add),
                 reads=[a, gsc, ocomb], writes=[ocomb])

    def load_tile(k):
        t0 = k * QW
        q_ = qt[k % 2]
        P.dma("sp", q_[:], q4.h.ap()[:, :, t0:t0 + QW].rearrange("h p t -> p h t"), writes=[q_])
        lo = max(0, t0 - 512)
        off = lo - (t0 - 512)
        P.dma("sp", kwT[k % 2][:, off:1024], kwinT.h.ap()[:, lo:t0 + 512], writes=[kwT[k % 2]])
        P.dma("sp", vwa[k % 2][:, off // 128:8, 0:128], vwin.h.ap()[lo:t0 + 512, :].rearrange("(j p) d -> p j d", p=128), writes=[vwa[k % 2]])

    load_tile(0)
    for k in range(NQT):
        t0 = k * QW
        if k + 1 < NQT:
            load_tile(k + 1)
        q_ = qt[k % 2]
        mmax = (t0 + 480) // 2048
        for hh in range(4):
            chunks = []
            NV = 385 if hh == 0 else 257
            for m in range(mmax + 1):
                off = 2048 * m + 31 - t0
                masks = []
                if off + 16 * 127 > 0:
                    mi = CMP_OFFS.index(off)
                    masks.append((ident[:], cmpmask[:, mi, :], (ident, cmpmask)))
                chunks.append((kcT[:, m * 128:(m + 1) * 128], kcT, vca[:, m, 0:NV], vca, masks))
            attend(q_[:, hh, :], q_, chunks, NV)
            for qb in range(4):
                a = acc[qb]
                b = k * 4 + qb
                if hh == 0:
                    fold_out(qb, 256, 257, gates[:, b, 0:1], True)
                    P.op("dve", lambda e, a=a, qb=qb: e.tensor_scalar(out=imp[:, qb, :], in0=a[:, 0:256], scalar1=rden[:, 0:1], scalar2=None, op0=ALU.mult),
                         reads=[a, rden], writes=[imp])
                else:
                    P.op("dve", lambda e, a=a: e.tensor_scalar_max(out=rden[:], in0=a[:, 256:257], scalar1=1e-30), reads=[a], writes=[rden])
                    P.op("dve", lambda e: e.reciprocal(out=rden[:], in_=rden[:]), reads=[rden], writes=[rden])
                    P.op("dve", lambda e, a=a, qb=qb: e.scalar_tensor_tensor(out=imp[:, qb, :], in0=a[:, 0:256], scalar=rden[:, 0:1], in1=imp[:, qb, :],
                                                                            op0=ALU.mult, op1=ALU.add),
                         reads=[a, rden, imp], writes=[imp])
        for qb in range(4):
            b = k * 4 + qb
            w0 = 256 - 2 * b
            P.op("dve", lambda e, qb=qb, w0=w0: e.tensor_tensor(out=score[:], in0=imp[:, qb, :], in1=cmv[:, w0:w0 + 256], op=ALU.mult), reads=[imp, cmv], writes=[score])
            P.op("dve", lambda e, w0=w0: e.tensor_tensor(out=score[:], in0=score[:], in1=cma[:, w0:w0 + 256], op=ALU.add), reads=[score, cma], writes=[score])
            P.op("dve", lambda e: e.memset(score[:, 0:1], 1e6), reads=[], writes=[score])
            P.op("dve", lambda e: e.max(out=m8[:], in_=score[:]), reads=[score], writes=[m8])
            P.op("dve", lambda e: e.match_replace(out=sc2[:], in_to_replace=m8[:], in_values=score[:], imm_value=-1e9), reads=[m8, score], writes=[sc2])
            P.op("dve", lambda e: e.max(out=m8[:], in_=sc2[:]), reads=[sc2], writes=[m8])
            P.op("dve", lambda e: e.tensor_scalar(out=negsel[:], in0=score[:], scalar1=m8[:, 7:8], scalar2=NEG, op0=ALU.is_lt, op1=ALU.mult),
                 reads=[score, m8], writes=[negsel])
            for hf in range(2):
                P.op("pe", lambda e, hf=hf: e.transpose(out=tps[:, hf, :], in_=negsel[:, hf * 128:(hf + 1) * 128], identity=ident[:]),
                     reads=[negsel, ident], writes=[tps])
            P.op("act", lambda e, qb=qb: e.activation(out=nsT[:, :, qb * 128:(qb + 1) * 128], in_=tps[:, 0:2, :], func=AF.Identity), reads=[tps], writes=[nsT])
        chunks = []
        for j in range(4 * k + 4):
            masks = [(bsel[:, j % 64, :], nsT[:, j // 64, :], (bsel, nsT))]
            if j >= 4 * k:
                masks.append((ident[:], causal[:, j - 4 * k, :], (ident, causal)))
            chunks.append((ksT[:, j * 128:(j + 1) * 128], ksT, vsa[:, j, :], vsa, masks))
        attend(q_[:, 0, :], q_, chunks, 129)
        for qb in range(4):
            fold_out(qb, 128, 0, gates[:, k * 4 + qb, 1:2], False)
        chunks = []
        kw_, vw_ = kwT[k % 2], vwa[k % 2]
        for i in range(8):
            if 4 * k - 4 + i < 0:
                continue
            chunks.append((kw_[:, i * 128:(i + 1) * 128], kw_, vw_[:, i, :], vw_, [(ident[:], winmask[:, i, :], (ident, winmask))]))
        attend(q_[:, 0, :], q_, chunks, 129)
        for qb in range(4):
            fold_out(qb, 128, 0, gates[:, k * 4 + qb, 2:3], False)
        P.op("act", lambda e: e.activation(out=ocb[:], in_=ocomb[:], func=AF.Identity), reads=[ocomb], writes=[ocb])
        for qb in range(4):
            P.op("pe", lambda e, qb=qb: e.transpose(out=tps[:, qb, :], in_=ocb[:, qb, :], identity=ident[:]), reads=[ocb, ident], writes=[tps])
        yo_ = yo[k % 2]
        P.op("act", lambda e, yo_=yo_: e.activation(out=yo_[:].rearrange("p (a b) -> p a b", b=128), in_=tps[:, :, :], func=AF.Identity), reads=[tps], writes=[yo_])
        P.dma("sp", yT.h.ap()[:, t0:t0 + QW], yo_[:], reads=[yo_])
    P.finish()
    P.emit()
    return nc


NPAIR = S // 128


def b2_consts():
    bf = ml_dtypes.bfloat16
    c = {}
    c["ident"] = np.eye(128, dtype=np.float32).astype(bf)
    c["identf"] = np.eye(128, dtype=np.float32)
    s = np.arange(128)[:, None]
    t = np.arange(128)[None, :]
    c["mask01"] = (((s // 64) == (t // 64)) & (s <= t)).astype(np.float32)
    c["onesf"] = np.ones((128, 128), np.float32)
    return c


def build_B2():
    nc = bass.Bass("TRN2", target_bir_lowering=False)
    P = Prog(nc)
    DI = lambda n, sh, dt: P.dram(n, sh, dt, "ExternalInput")
    qraw = DI("qraw", [128, S], BF16)
    kraw = DI("kraw", [128, S], BF16)
    vtok = DI("vtok", [S, 128], BF16)
    gif = DI("gif", [128, 2, 128], BF16)
    convw = DI("convw", [128, 2, 4], F32)
    convb = DI("convb", [128, 2], F32)
    gateb = DI("gateb", [128, 2], F32)
    ident_d = DI("ident", [128, 128], BF16)
    identf_d = DI("identf", [128, 128], F32)
    mask_d = DI("mask01", [128, 128], F32)
    ones_d = DI("onesf", [128, 128], F32)
    scr = P.dram("scr", [4, 256], F32, "Internal")
    hT = P.dram("hT", [128, S], BF16, "ExternalOutput")

    ident = P.sb("ident", [128, 128], BF16)
    identf = P.sb("identf", [128, 128], F32)
    mask01 = P.sb("mask01", [128, 128], F32)
    onesf = P.sb("onesf", [128, 128], F32)
    cw = P.sb("cw", [128, 2, 4], F32)
    cb = P.sb("cb", [128, 2], F32)
    gb = P.sb("gb", [128, 2], F32)
    for t, d in ((ident, ident_d), (identf, identf_d), (mask01, mask_d), (onesf, ones_d), (cw, convw), (cb, convb), (gb, gateb)):
        P.dma("sp", t[:], d[:], writes=[t])
    QT = P.sb("QT", [128, S], BF16)
    KT = P.sb("KT", [128, S], BF16)
    Ktok = P.sb("Ktok", [128, NPAIR, 128], BF16)
    Va = P.sb("Va", [128, NPAIR, 129], BF16)
    P.dma("sp", Va[:, :, 0:128], vtok.h.ap().rearrange("(j p) d -> p j d", p=128), writes=[Va])
    P.op("pool", lambda e: e.memset(Va[:, :, 128:129], 1.0), reads=[], writes=[Va])
    ewT = P.sb("ewT", [128, 128], F32)
    euT = P.sb("euT", [128, 128], F32)
    wiT = P.sb("wiT", [128, 128], F32)
    gdT = P.sb("gdT", [128, 128], F32)
    decb = P.sb("decb", [128, 256], F32)
    sc2b = P.sb("sc2b", [128, 256], F32)

    pKQ = [P.ps(f"pKQ{i}", [128, 512], F32) for i in range(2)]
    pB = P.ps("pB", [128, 512], F32)
    pA = P.ps("pA", [128, 512], F32)
    pU = P.ps("pU", [128, 512], F32)
    pT = P.ps("pT", [128, 4, 128], BF16)
    pM = P.ps("pM", [128, 512], F32)

    P.scope_begin()
    xp = P.sb("xp", [128, S + 3], BF16)
    yseg = P.sb("yseg", [128, 4096], F32)
    P.op("pool", lambda e: e.memset(xp[:, 0:3], 0.0), reads=[], writes=[xp])
    for qk, (src, dst) in enumerate(((qraw, QT), (kraw, KT))):
        P.dma("sp", xp[:, 3:S + 3], src[:], writes=[xp])
        for sg in range(4):
            c0 = sg * 4096
            P.op("dve", lambda e, c0=c0, qk=qk: e.tensor_scalar(out=yseg[:], in0=xp[:, c0:c0 + 4096], scalar1=cw[:, qk, 0:1], scalar2=None, op0=ALU.mult),
                 reads=[xp, cw], writes=[yseg])
            for j in range(1, 4):
                P.op("dve", lambda e, c0=c0, qk=qk, j=j: e.scalar_tensor_tensor(out=yseg[:], in0=xp[:, c0 + j:c0 + j + 4096], scalar=cw[:, qk, j:j + 1],
                                                                              in1=yseg[:], op0=ALU.mult, op1=ALU.add),
                     reads=[xp, cw, yseg], writes=[yseg])
            if qk == 0:
                P.op("act", lambda e, c0=c0, dst=dst: e.activation(out=dst[:, c0:c0 + 4096], in_=yseg[:], func=AF.Silu, bias=cb[:, 0:1]),
                     reads=[yseg, cb], writes=[dst])
            else:
                P.op("act", lambda e: e.activation(out=yseg[:], in_=yseg[:], func=AF.Silu, bias=cb[:, 1:2]), reads=[yseg, cb], writes=[yseg])
                P.op("pool", lambda e, c0=c0, dst=dst: e.tensor_scalar(out=dst[:, c0:c0 + 4096], in0=yseg[:], scalar1=SCALE, scalar2=None, op0=ALU.mult),
                     reads=[yseg], writes=[dst])
    P.scope_end()
    for j in range(NPAIR):
        P.op("pe", lambda e, j=j: e.transpose(out=pT[:, j % 4, :], in_=KT[:, j * 128:(j + 1) * 128], identity=ident[:]), reads=[KT, ident], writes=[pT])
        if j % 4 == 3:
            P.op("act", lambda e, j=j: e.activation(out=Ktok[:, j - 3:j + 1, :], in_=pT[:, :, :], func=AF.Identity), reads=[pT], writes=[Ktok])

    P.scope_begin()
    G = lambda n: P.sb(n, [128, 128], F32)
    gtmp = P.sb("gtmp", [128, 2, 128], BF16)
    ig, lf, bcum, w_, cmw, tA, tB, mt = G("ig"), G("lf"), G("bcum"), G("w_"), G("cmw"), G("tA"), G("tB"), G("mt")
    ones64 = P.sb("ones64", [128, 64], F32)
    small = P.sb("small", [128, 8], F32)
    rows = P.sb("rows", [1, 4, 256], F32)
    P.dma("sp", gtmp[:], gif[:], writes=[gtmp])
    P.op("pool", lambda e: e.memset(ones64[:], 1.0), reads=[], writes=[ones64])
    P.op("dve", lambda e: e.tensor_scalar(out=ig[:], in0=gtmp[:, 0, :], scalar1=gb[:, 0:1], scalar2=None, op0=ALU.add), reads=[gtmp, gb], writes=[ig])
    P.op("dve", lambda e: e.tensor_scalar(out=lf[:], in0=gtmp[:, 1, :], scalar1=gb[:, 1:2], scalar2=None, op0=ALU.add), reads=[gtmp, gb], writes=[lf])
    P.op("act", lambda e: e.activation(out=lf[:], in_=lf[:], func=AF.Exp, scale=-1.0), reads=[lf], writes=[lf])
    P.op("act", lambda e: e.activation(out=lf[:], in_=lf[:], func=AF.Ln, bias=1.0), reads=[lf], writes=[lf])
    P.op("dve", lambda e: e.tensor_scalar(out=lf[:], in0=lf[:], scalar1=-1.0, scalar2=None, op0=ALU.mult), reads=[lf], writes=[lf])
    for a in range(2):
        sl = slice(a * 64, (a + 1) * 64)
        P.op("dve", lambda e, sl=sl: e.tensor_tensor_scan(out=bcum[:, sl], data0=ones64[:], data1=lf[:, sl], initial=0.0, op0=ALU.mult, op1=ALU.add),
             reads=[ones64, lf], writes=[bcum])
    P.op("dve", lambda e: e.tensor_tensor(out=w_[:], in0=ig[:], in1=bcum[:], op=ALU.subtract), reads=[ig, bcum], writes=[w_])
    for a in range(2):
        sl = slice(a * 64, (a + 1) * 64)
        P.op("dve", lambda e, sl=sl: e.tensor_tensor_scan(out=cmw[:, sl], data0=ones64[:], data1=w_[:, sl], initial=-1e30, op0=ALU.mult, op1=ALU.max),
             reads=[ones64, w_], writes=[cmw])
    for a in range(2):
        c = a * 64 + 63
        P.op("dve", lambda e, a=a, c=c: e.tensor_copy(out=small[:, a:a + 1], in_=bcum[:, c:c + 1]), reads=[bcum], writes=[small])
        P.op("dve", lambda e, a=a, c=c: e.tensor_tensor(out=small[:, 2 + a:3 + a], in0=cmw[:, c:c + 1], in1=bcum[:, c:c + 1], op=ALU.add),
             reads=[cmw, bcum], writes=[small])
    P.dma("sp", scr.h.ap()[0].rearrange("(p a) -> p a", a=2), small[:, 0:2], reads=[small], writes=[scr])
    P.dma("sp", scr.h.ap()[1].rearrange("(p a) -> p a", a=2), small[:, 2:4], reads=[small], writes=[scr])
    P.dma("sp", rows[0:1, 0:2, :], scr.h.ap()[0:2, :].rearrange("(o r) n -> o r n", o=1), reads=[scr], writes=[rows])
    P.op("dve", lambda e: e.tensor_tensor_scan(out=rows[0:1, 2, :], data0=rows[0:1, 0, :], data1=rows[0:1, 1, :], initial=0.0, op0=ALU.add, op1=ALU.max),
         reads=[rows], writes=[rows])
    P.op("dve", lambda e: e.memset(rows[0:1, 3, 0:1], 0.0), reads=[], writes=[rows])
    P.op("dve", lambda e: e.tensor_copy(out=rows[0:1, 3, 1:256], in_=rows[0:1, 2, 0:255]), reads=[rows], writes=[rows])
    P.dma("sp", scr.h.ap()[2:4, :].rearrange("(o r) n -> o r n", o=1), rows[0:1, 2:4, :], reads=[rows], writes=[scr])
    P.dma("sp", small[:, 4:6], scr.h.ap()[2].rearrange("(p a) -> p a", a=2), reads=[scr], writes=[small])
    P.dma("sp", small[:, 6:8], scr.h.ap()[3].rearrange("(p a) -> p a", a=2), reads=[scr], writes=[small])
    P.op("dve", lambda e: e.tensor_tensor(out=tA[:], in0=bcum[:], in1=cmw[:], op=ALU.add), reads=[bcum, cmw], writes=[tA])
    for a in range(2):
        sl = slice(a * 64, (a + 1) * 64)
        P.op("dve", lambda e, sl=sl, a=a: e.tensor_scalar(out=tB[:, sl], in0=bcum[:, sl], scalar1=small[:, 6 + a:7 + a], scalar2=None, op0=ALU.add),
             reads=[bcum, small], writes=[tB])
    P.op("dve", lambda e: e.tensor_tensor(out=mt[:], in0=tA[:], in1=tB[:], op=ALU.max), reads=[tA, tB], writes=[mt])
    P.op("dve", lambda e: e.tensor_tensor(out=tB[:], in0=tB[:], in1=mt[:], op=ALU.subtract), reads=[tB, mt], writes=[tB])
    P.op("dve", lambda e: e.tensor_tensor(out=tA[:], in0=bcum[:], in1=mt[:], op=ALU.subtract), reads=[bcum, mt], writes=[tA])
    P.op("act", lambda e: e.activation(out=tB[:], in_=tB[:], func=AF.Exp), reads=[tB], writes=[tB])
    P.op("act", lambda e: e.activation(out=tA[:], in_=tA[:], func=AF.Exp), reads=[tA], writes=[tA])
    P.op("act", lambda e: e.activation(out=mt[:], in_=mt[:], func=AF.Exp, scale=-1.0), reads=[mt], writes=[mt])
    P.op("act", lambda e: e.activation(out=w_[:], in_=w_[:], func=AF.Exp), reads=[w_], writes=[w_])
    for src, dst in ((w_, ewT), (tA, euT), (tB, wiT), (mt, gdT)):
        P.op("pe", lambda e, src=src: e.transpose(out=pM[:, 0:128], in_=src[:], identity=identf[:]), reads=[src, identf], writes=[pM])
        P.op("dve", lambda e, dst=dst: e.tensor_copy(out=dst[:], in_=pM[:, 0:128]), reads=[pM], writes=[dst])
    P.op("dve", lambda e: e.tensor_tensor(out=small[:, 2:4], in0=small[:, 0:2], in1=small[:, 4:6], op=ALU.subtract), reads=[small], writes=[small])
    P.op("dve", lambda e: e.tensor_tensor(out=small[:, 0:2], in0=small[:, 2:4], in1=small[:, 6:8], op=ALU.add), reads=[small], writes=[small])
    P.op("act", lambda e: e.activation(out=small[:, 0:4], in_=small[:, 0:4], func=AF.Exp), reads=[small], writes=[small])
    dg = P.sb("dg", [128, 128, 2], F32)
    for which, dst in ((0, decb), (2, sc2b)):
        for a in range(2):
            P.op("dve", lambda e, a=a, which=which: e.tensor_scalar(out=dg[:, :, a], in0=identf[:], scalar1=small[:, which + a:which + a + 1], scalar2=None, op0=ALU.mult),
                 reads=[identf, small], writes=[dg])
        P.op("pe", lambda e: e.matmul(pM[:, 0:256], lhsT=onesf[:], rhs=dg[:].rearrange("p a b -> p (a b)"), start=True, stop=True),
             reads=[onesf, dg], writes=[pM])
        P.op("dve", lambda e, dst=dst: e.tensor_copy(out=dst[:], in_=pM[:, 0:256]), reads=[pM], writes=[dst])
    P.scope_end()

    Cst = P.sb("Cst", [128, 129], F32)
    Cb = [P.sb(f"Cb{i}", [128, 129], BF16) for i in range(2)]
    Sm = [P.sb(f"Sm{i}", [128, 128], BF16) for i in range(2)]
    Vw = [P.sb(f"Vw{i}", [128, 129], BF16) for i in range(2)]
    tU = P.sb("tU", [128, 129], F32)
    tN = P.sb("tN", [128, 129], F32)
    num = P.sb("num", [128, 129], F32)
    dn = P.sb("dn", [128, 1], F32)
    hb = [P.sb(f"hb{i}", [128, 128], BF16) for i in range(2)]
    ho = [P.sb(f"ho{i}", [128, 512], BF16) for i in range(2)]
    P.op("dve", lambda e: e.memset(Cst[:], 0.0), reads=[], writes=[Cst])
    P.op("pool", lambda e: e.memset(Cb[0][:], 0.0), reads=[], writes=[Cb[0]])
    ci = 0
    for j in range(NPAIR):
        cols = slice(j * 128, (j + 1) * 128)
        kq = pKQ[j % 2]
        sm, vw = Sm[j % 2], Vw[j % 2]
        P.op("pe", lambda e, kq=kq, cols=cols: e.matmul(kq[:, 0:128], lhsT=KT[:, cols], rhs=QT[:, cols], start=True, stop=True), reads=[KT, QT], writes=[kq])
        P.op("dve", lambda e, kq=kq, sm=sm: e.tensor_tensor(out=sm[:], in0=kq[:, 0:128], in1=mask01[:], op=ALU.mult), reads=[kq, mask01], writes=[sm])
        P.op("pool", lambda e, j=j, vw=vw: e.tensor_scalar(out=vw[:], in0=Va[:, j, :], scalar1=ewT[:, j:j + 1], scalar2=None, op0=ALU.mult),
             reads=[Va, ewT], writes=[vw])
        P.op("pe", lambda e, sm=sm, vw=vw: e.matmul(pB[:, 0:129], lhsT=sm[:], rhs=vw[:], start=True, stop=True), reads=[sm, vw], writes=[pB])
        for a in range(2):
            c = 2 * j + a
            rs_ = slice(a * 64, (a + 1) * 64)
            cbc = Cb[ci % 2]
            cbn = Cb[(ci + 1) % 2]
            ci += 1
            P.op("pe", lambda e, rs_=rs_, cbc=cbc, j=j: e.matmul(pA[rs_, 0:129], lhsT=QT[:, j * 128 + rs_.start:j * 128 + rs_.stop], rhs=cbc[:], start=True, stop=True),
                 reads=[QT, cbc], writes=[pA])
            P.op("pe", lambda e, rs_=rs_, j=j, vw=vw: e.matmul(pU[:, 0:129], lhsT=Ktok[rs_, j, :], rhs=vw[rs_, :], start=True, stop=True),
                 reads=[Ktok, vw], writes=[pU])
            P.op("dve", lambda e, c=c: e.tensor_scalar(out=tU[:], in0=pU[:, 0:129], scalar1=sc2b[:, c:c + 1], scalar2=None, op0=ALU.mult),
                 reads=[pU, sc2b], writes=[tU])
            P.op("dve", lambda e, c=c: e.scalar_tensor_tensor(out=Cst[:], in0=Cst[:], scalar=decb[:, c:c + 1], in1=tU[:], op0=ALU.mult, op1=ALU.add),
                 reads=[Cst, decb, tU], writes=[Cst])
            P.op("act", lambda e, cbn=cbn: e.activation(out=cbn[:], in_=Cst[:], func=AF.Identity), reads=[Cst], writes=[cbn])
        P.op("dve", lambda e, j=j: e.tensor_scalar(out=tN[:], in0=pA[:, 0:129], scalar1=wiT[:, j:j + 1], scalar2=None, op0=ALU.mult), reads=[pA, wiT], writes=[tN])
        P.op("dve", lambda e, j=j: e.scalar_tensor_tensor(out=num[:], in0=pB[:, 0:129], scalar=euT[:, j:j + 1], in1=tN[:], op0=ALU.mult, op1=ALU.add),
             reads=[pB, euT, tN], writes=[num])
        P.op("dve", lambda e: e.scalar_tensor_tensor(out=dn[:], in0=num[:, 128:129], scalar=-1.0, in1=num[:, 128:129], op0=ALU.mult, op1=ALU.max),
             reads=[num], writes=[dn])
        P.op("dve", lambda e, j=j: e.tensor_tensor(out=dn[:], in0=dn[:], in1=gdT[:, j:j + 1], op=ALU.max), reads=[dn, gdT], writes=[dn])
        P.op("dve", lambda e: e.reciprocal(out=dn[:], in_=dn[:]), reads=[dn], writes=[dn])
        h_ = hb[j % 2]
        P.op("dve", lambda e, h_=h_: e.tensor_scalar(out=h_[:], in0=num[:, 0:128], scalar1=dn[:, 0:1], scalar2=None, op0=ALU.mult), reads=[num, dn], writes=[h_])
        P.op("pe", lambda e, h_=h_, j=j: e.transpose(out=pT[:, j % 4, :], in_=h_[:], identity=ident[:]), reads=[h_, ident], writes=[pT])
        if j % 4 == 3:
            o_ = ho[(j // 4) % 2]
            P.op("act", lambda e, o_=o_: e.activation(out=o_[:].rearrange("p (a b) -> p a b", b=128), in_=pT[:, :, :], func=AF.Identity), reads=[pT], writes=[o_])
            P.dma("sp", hT.h.ap()[:, (j - 3) * 128:(j + 1) * 128], o_[:], reads=[o_])
    P.finish()
    P.emit()
    return nc


NTH = NT + 2
NFC = D_FF // 128
VEC_NG, VEC_LG0, VEC_LB0, VEC_LG1, VEC_LB1, VEC_CW, VEC_CB, VEC_FLAG, VEC_N = 0, 8, 24, 40, 56, 72, 204, 248, 249


def build_C():
    nc = bass.Bass("TRN2", target_bir_lowering=False)
    P = Prog(nc)
    DI = lambda n, sh, dt: P.dram(n, sh, dt, "ExternalInput")
    xT = DI("xT", [D, NTH], F32)
    ynsaT = DI("ynsaT", [1024, NTH], BF16)
    hmlT = DI("hmlT", [1024, NTH], BF16)
    pT2 = DI("pT2", [5120, NTH], BF16)
    w_brn = DI("w_brn", [1024, D], F32)
    w_brm = DI("w_brm", [1024, D], F32)
    w_o = DI("w_o", [D, D], F32)
    w_up = DI("w_up", [D, 2 * D_FF], F32)
    w_down = DI("w_down", [D_FF, D], F32)
    modT_d = DI("modT", [128, 96], F32)
    vecs_d = DI("vecs", [128, VEC_N], F32)
    cst = DI("cst", [128, 257], F32)
    xoT = P.dram("xoT", [D, NT], F32, "ExternalOutput")

    W = 512
    NST = 3
    wb = [P.sb(f"wb{i}", [128, 8192], BF16) for i in range(NST)]
    xr = P.sb("xr", [128, KC, W], F32)
    yn = P.sb("yn", [128, 8, W], BF16)
    hm = P.sb("hm", [128, 8, W], BF16)
    yml = hm
    merged = P.sb("merged", [128, KC, W], BF16)
    h2 = merged
    u = P.sb("u", [128, NFC, W], BF16)
    abuf = [P.sb(f"abuf{i}", [128, W + 2], F32) for i in range(2)]
    carry = P.sb("carry", [128, NFC, 2], F32)
    ft = [P.sb(f"ft{i}", [128, W], F32) for i in range(6)]
    mean = P.sb("mean", [128, W], F32)
    rstd = P.sb("rstd", [128, W], F32)
    tmpm = P.sb("tmpm", [128, W], F32)
    sqt = [P.sb(f"sqt{i}", [128, W], F32) for i in range(2)]
    hsq = P.sb("hsq", [128, 2, W], BF16)
    mo = [P.sb(f"mo{i}", [128, W], BF16) for i in range(4)]
    vecs = P.sb("vecs", [128, VEC_N], F32)
    cs = P.sb("cs", [128, 257], F32)
    csb = P.sb("csb", [128, 128], BF16)
    sc2p = P.sb("sc2p", [128, KC], F32)
    ps_a = P.ps("ps_a", [128, W], F32)
    ps_b = P.ps("ps_b", [128, W], F32)
    accs = [P.ps(f"acc{i}", [128, W], F32) for i in range(4)]

    P.dma("sp", cs[:], cst[:], writes=[cs])
    P.dma("sp", vecs[:], vecs_d[:], writes=[vecs])
    P.op("dve", lambda e: e.tensor_copy(out=csb[:], in_=cs[:, 128:256]), reads=[cs], writes=[csb])
    onesN = Sub(cs, (slice(None), slice(0, 128)))
    one1 = Sub(cs, (slice(None), slice(256, 257)))
    modF = P.sb("modF", [128, 96], F32)
    P.dma("sp", modF[:], modT_d[:], writes=[modF])
    modT = Sub(modF, (slice(None), slice(32, 96)))
    P.op("dve", lambda e: e.tensor_scalar_add(out=sc2p[:], in0=modT[:, 32:48], scalar1=1.0), reads=[modT], writes=[sc2p])

    state = {"gi": 0, "fi": 0}

    def nacc():
        a = accs[state["gi"] % 4]
        state["gi"] += 1
        return a

    def nft():
        t = ft[state["fi"] % 6]
        state["fi"] += 1
        return t

    def layer_norm_inplace(Wc, gcol, bcol, out_bf=None, scale_t=None, bias_t=None):
        for kc in range(KC):
            sq = sqt[kc % 2]
            P.op("act", lambda e, sq=sq, kc=kc: e.activation(out=sq[:, 0:Wc], in_=xr[:, kc, 0:Wc], func=AF.Square), reads=[xr], writes=[sq])
            P.op("pe", lambda e, kc=kc: e.matmul(ps_a[:, 0:Wc], lhsT=onesN[:], rhs=xr[:, kc, 0:Wc], start=(kc == 0), stop=(kc == KC - 1)),
                 reads=[cs, xr], writes=[ps_a])
            P.op("pe", lambda e, kc=kc, sq=sq: e.matmul(ps_b[:, 0:Wc], lhsT=onesN[:], rhs=sq[:, 0:Wc], start=(kc == 0), stop=(kc == KC - 1)),
                 reads=[cs, sq], writes=[ps_b])
        P.op("act", lambda e: e.activation(out=mean[:, 0:Wc], in_=ps_a[:, 0:Wc], func=AF.Identity), reads=[ps_a], writes=[mean])
        P.op("dve", lambda e: e.tensor_tensor(out=tmpm[:, 0:Wc], in0=mean[:, 0:Wc], in1=mean[:, 0:Wc], op=ALU.mult), reads=[mean], writes=[tmpm])
        P.op("dve", lambda e: e.tensor_tensor(out=tmpm[:, 0:Wc], in0=ps_b[:, 0:Wc], in1=tmpm[:, 0:Wc], op=ALU.subtract), reads=[ps_b, tmpm], writes=[tmpm])
        P.op("act", lambda e: e.activation(out=tmpm[:, 0:Wc], in_=tmpm[:, 0:Wc], func=AF.Sqrt, bias=EPS), reads=[tmpm], writes=[tmpm])
        P.op("dve", lambda e: e.reciprocal(out=rstd[:, 0:Wc], in_=tmpm[:, 0:Wc]), reads=[tmpm], writes=[rstd])
        for kc in range(KC):
            t = nft()
            P.op("dve", lambda e, t=t, kc=kc: e.tensor_tensor(out=t[:, 0:Wc], in0=xr[:, kc, 0:Wc], in1=mean[:, 0:Wc], op=ALU.subtract), reads=[xr, mean], writes=[t])
            P.op("dve", lambda e, t=t: e.tensor_tensor(out=t[:, 0:Wc], in0=t[:, 0:Wc], in1=rstd[:, 0:Wc], op=ALU.mult), reads=[t, rstd], writes=[t])
            if out_bf is None:
                P.op("act", lambda e, t=t, kc=kc: e.activation(out=xr[:, kc, 0:Wc], in_=t[:, 0:Wc], func=AF.Identity,
                                                               scale=vecs[:, gcol + kc:gcol + kc + 1], bias=vecs[:, bcol + kc:bcol + kc + 1]),
                     reads=[t, vecs], writes=[xr])
            else:
                P.op("act", lambda e, t=t, kc=kc: e.activation(out=out_bf[:, kc, 0:Wc], in_=t[:, 0:Wc], func=AF.Identity,
                                                               scale=scale_t[:, kc:kc + 1], bias=bias_t[:, kc:kc + 1]),
                     reads=[t, scale_t, bias_t], writes=[out_bf])

    sched = []

    def add_tile(c0, Wc, halo):
        oc0 = c0 - 2

        def prologue():
            P.dma("sp", xr[:, :, 0:Wc], xT.h.ap()[:, c0:c0 + Wc].rearrange("(kc p) n -> p kc n", p=128), writes=[xr])
            P.dma("sp", yn[:, :, 0:Wc], ynsaT.h.ap()[:, c0:c0 + Wc].rearrange("(kc p) n -> p kc n", p=128), writes=[yn])
            P.dma("sp", hm[:, :, 0:Wc], hmlT.h.ap()[:, c0:c0 + Wc].rearrange("(kc p) n -> p kc n", p=128), writes=[hm])
            for hh in range(4):
                P.op("dve", lambda e, hh=hh: e.tensor_tensor(out=hsq[:, :, 0:Wc], in0=hm[:, 2 * hh:2 * hh + 2, 0:Wc], in1=hm[:, 2 * hh:2 * hh + 2, 0:Wc], op=ALU.mult),
                     reads=[hm], writes=[hsq])
                for c in range(2):
                    P.op("pe", lambda e, hh=hh, c=c: e.matmul(ps_a[:, 0:Wc], lhsT=csb[:], rhs=hm[:, 2 * hh + c, 0:Wc], start=(c == 0), stop=(c == 1)),
                         reads=[csb, hm], writes=[ps_a])
                for c in range(2):
                    P.op("pe", lambda e, c=c: e.matmul(ps_b[:, 0:Wc], lhsT=csb[:], rhs=hsq[:, c, 0:Wc], start=(c == 0), stop=(c == 1)),
                         reads=[csb, hsq], writes=[ps_b])
                P.op("act", lambda e: e.activation(out=mean[:, 0:Wc], in_=ps_a[:, 0:Wc], func=AF.Identity), reads=[ps_a], writes=[mean])
                P.op("dve", lambda e: e.tensor_tensor(out=tmpm[:, 0:Wc], in0=mean[:, 0:Wc], in1=mean[:, 0:Wc], op=ALU.mult), reads=[mean], writes=[tmpm])
                P.op("dve", lambda e: e.tensor_tensor(out=tmpm[:, 0:Wc], in0=ps_b[:, 0:Wc], in1=tmpm[:, 0:Wc], op=ALU.subtract), reads=[ps_b, tmpm], writes=[tmpm])
                P.op("dve", lambda e: e.tensor_scalar_max(out=tmpm[:, 0:Wc], in0=tmpm[:, 0:Wc], scalar1=0.0), reads=[tmpm], writes=[tmpm])
                P.op("act", lambda e: e.activation(out=tmpm[:, 0:Wc], in_=tmpm[:, 0:Wc], func=AF.Sqrt, bias=EPS), reads=[tmpm], writes=[tmpm])
                P.op("dve", lambda e: e.reciprocal(out=rstd[:, 0:Wc], in_=tmpm[:, 0:Wc]), reads=[tmpm], writes=[rstd])
                for c in range(2):
                    ch = 2 * hh + c
                    m_ = mo[ch % 4]
                    t = nft()
                    sg = nft()
                    P.dma("sp", m_[:, 0:Wc], pT2.h.ap()[ch * 128:(ch + 1) * 128, c0:c0 + Wc], writes=[m_])
                    P.op("act", lambda e, sg=sg, m_=m_: e.activation(out=sg[:, 0:Wc], in_=m_[:, 0:Wc], func=AF.Sigmoid), reads=[m_], writes=[sg])
                    P.op("dve", lambda e, t=t, ch=ch: e.tensor_tensor(out=t[:, 0:Wc], in0=hm[:, ch, 0:Wc], in1=mean[:, 0:Wc], op=ALU.subtract), reads=[hm, mean], writes=[t])
                    P.op("dve", lambda e, t=t: e.tensor_tensor(out=t[:, 0:Wc], in0=t[:, 0:Wc], in1=rstd[:, 0:Wc], op=ALU.mult), reads=[t, rstd], writes=[t])
                    P.op("dve", lambda e, t=t, sg=sg, ch=ch: e.scalar_tensor_tensor(out=yml[:, ch, 0:Wc], in0=t[:, 0:Wc], scalar=vecs[:, VEC_NG + ch:VEC_NG + ch + 1],
                                                                                  in1=sg[:, 0:Wc], op0=ALU.mult, op1=ALU.mult),
                         reads=[t, sg, vecs], writes=[yml])

        for sgi in range(4):
            def run(wbt, sgi=sgi):
                wv = wbt.h.ap().rearrange("p (w k n) -> p w k n", w=2, k=8)
                for o4 in range(4):
                    oc = sgi * 4 + o4
                    a1, a2 = nacc(), nacc()
                    for kc in range(8):
                        P.op("pe", lambda e, a1=a1, kc=kc, o4=o4: e.matmul(a1[:, 0:Wc], lhsT=wv[:, 0, kc, o4 * 128:(o4 + 1) * 128], rhs=yn[:, kc, 0:Wc],
                                                                          start=(kc == 0), stop=(kc == 7)), reads=[wbt, yn], writes=[a1])
                    for kc in range(8):
                        P.op("pe", lambda e, a2=a2, kc=kc, o4=o4: e.matmul(a2[:, 0:Wc], lhsT=wv[:, 1, kc, o4 * 128:(o4 + 1) * 128], rhs=yml[:, kc, 0:Wc],
                                                                          start=(kc == 0), stop=(kc == 7)), reads=[wbt, yml], writes=[a2])
                    g1t, g2t, s1, s2, t1_, t2_ = mo[0 + (oc % 2) * 2], mo[1 + (oc % 2) * 2], nft(), nft(), nft(), nft()
                    P.dma("sp", g1t[:, 0:Wc], pT2.h.ap()[(8 + oc) * 128:(9 + oc) * 128, c0:c0 + Wc], writes=[g1t])
                    P.dma("sp", g2t[:, 0:Wc], pT2.h.ap()[(24 + oc) * 128:(25 + oc) * 128, c0:c0 + Wc], writes=[g2t])
                    P.op("act", lambda e, s1=s1, g1t=g1t: e.activation(out=s1[:, 0:Wc], in_=g1t[:, 0:Wc], func=AF.Sigmoid), reads=[g1t], writes=[s1])
                    P.op("act", lambda e, s2=s2, g2t=g2t: e.activation(out=s2[:, 0:Wc], in_=g2t[:, 0:Wc], func=AF.Sigmoid), reads=[g2t], writes=[s2])
                    P.op("dve", lambda e, a1=a1, s1=s1, t1_=t1_: e.tensor_tensor(out=t1_[:, 0:Wc], in0=a1[:, 0:Wc], in1=s1[:, 0:Wc], op=ALU.mult), reads=[a1, s1], writes=[t1_])
                    P.op("dve", lambda e, a2=a2, s2=s2, t2_=t2_: e.tensor_tensor(out=t2_[:, 0:Wc], in0=a2[:, 0:Wc], in1=s2[:, 0:Wc], op=ALU.mult), reads=[a2, s2], writes=[t2_])
                    P.op("dve", lambda e, t1_=t1_, t2_=t2_, oc=oc: e.tensor_tensor(out=merged[:, oc, 0:Wc], in0=t1_[:, 0:Wc], in1=t2_[:, 0:Wc], op=ALU.add),
                         reads=[t1_, t2_], writes=[merged])
            sched.append(dict(loads=[(w_brn.h.ap()[:, sgi * 512:(sgi + 1) * 512].rearrange("(kc p) n -> p kc n", p=128), 0, 8, 512, None),
                                     (w_brm.h.ap()[:, sgi * 512:(sgi + 1) * 512].rearrange("(kc p) n -> p kc n", p=128), 4096, 8, 512, None)],
                              run=run, pre=(prologue if sgi == 0 else None)))
        for sgi in range(4):
            def pre_o():
                P.op("act", lambda e: e.activation(out=xr[:, :, 0:Wc], in_=xr[:, :, 0:Wc], func=AF.Identity, scale=ALPHA), reads=[xr], writes=[xr])

            def run(wbt, sgi=sgi):
                wv = wbt.h.ap().rearrange("p (k n) -> p k n", k=16)
                for o4 in range(4):
                    oc = sgi * 4 + o4
                    a = nacc()
                    for kc in range(KC):
                        P.op("pe", lambda e, a=a, kc=kc, o4=o4: e.matmul(a[:, 0:Wc], lhsT=wv[:, kc, o4 * 128:(o4 + 1) * 128], rhs=merged[:, kc, 0:Wc],
                                                                        start=(kc == 0), stop=(kc == KC - 1)), reads=[wbt, merged], writes=[a])
                    P.op("dve", lambda e, a=a, oc=oc: e.scalar_tensor_tensor(out=xr[:, oc, 0:Wc], in0=a[:, 0:Wc], scalar=modT[:, oc:oc + 1], in1=xr[:, oc, 0:Wc],
                                                                            op0=ALU.mult, op1=ALU.add), reads=[a, modT, xr], writes=[xr])
            sched.append(dict(loads=[(dview(w_o, sgi * 512, (sgi + 1) * 512), 0, 16, 512, None)], run=run, pre=(pre_o if sgi == 0 else None)))
        for fp in range(NFC // 2):
            def pre_up():
                layer_norm_inplace(Wc, VEC_LG0, VEC_LB0)
                layer_norm_inplace(Wc, 0, 0, out_bf=h2, scale_t=sc2p, bias_t=Sub(modF, (slice(None), slice(48, 64))))

            def run(wbt, fp=fp):
                wv = wbt.h.ap().rearrange("p (k n) -> p k n", k=16)
                for f2 in range(2):
                    fc = fp * 2 + f2
                    aa = nacc()
                    for kc in range(KC):
                        P.op("pe", lambda e, aa=aa, kc=kc, f2=f2: e.matmul(aa[:, 0:Wc], lhsT=wv[:, kc, f2 * 128:(f2 + 1) * 128], rhs=h2[:, kc, 0:Wc],
                                                                          start=(kc == 0), stop=(kc == KC - 1)),
                             reads=[wbt, h2], writes=[aa])
                    if halo:
                        P.op("dve", lambda e, aa=aa, fc=fc: e.tensor_scalar(out=carry[:, fc, :], in0=aa[:, 0:2], scalar1=vecs[:, VEC_FLAG:VEC_FLAG + 1], scalar2=None, op0=ALU.mult),
                             reads=[aa, vecs], writes=[carry])
                        continue
                    ag = nacc()
                    for kc in range(KC):
                        P.op("pe", lambda e, ag=ag, kc=kc, f2=f2: e.matmul(ag[:, 0:Wc], lhsT=wv[:, kc, 256 + f2 * 128:256 + (f2 + 1) * 128], rhs=h2[:, kc, 0:Wc],
                                                                          start=(kc == 0), stop=(kc == KC - 1)),
                             reads=[wbt, h2], writes=[ag])
                    ab = abuf[fc % 2]
                    cv, sa = nft(), nft()
                    P.op("act", lambda e, ab=ab, aa=aa: e.activation(out=ab[:, 2:2 + Wc], in_=aa[:, 0:Wc], func=AF.Identity), reads=[aa], writes=[ab])
                    P.op("act", lambda e, ab=ab, fc=fc: e.activation(out=ab[:, 0:2], in_=carry[:, fc, :], func=AF.Identity), reads=[carry], writes=[ab])
                    cwc = VEC_CW + fc * 3
                    P.op("dve", lambda e, ab=ab, cv=cv, cwc=cwc: e.tensor_scalar(out=cv[:, 0:Wc], in0=ab[:, 0:Wc], scalar1=vecs[:, cwc:cwc + 1], scalar2=None, op0=ALU.mult),
                         reads=[ab, vecs], writes=[cv])
                    for j in (1, 2):
                        P.op("dve", lambda e, ab=ab, cv=cv, cwc=cwc, j=j: e.scalar_tensor_tensor(out=cv[:, 0:Wc], in0=ab[:, j:j + Wc], scalar=vecs[:, cwc + j:cwc + j + 1],
                                                                                                in1=cv[:, 0:Wc], op0=ALU.mult, op1=ALU.add),
                             reads=[ab, vecs, cv], writes=[cv])
                    P.op("act", lambda e, ab=ab, fc=fc: e.activation(out=carry[:, fc, :], in_=ab[:, Wc:Wc + 2], func=AF.Identity), reads=[ab], writes=[carry])
                    P.op("act", lambda e, cv=cv, sa=sa, fc=fc: e.activation(out=sa[:, 0:Wc], in_=cv[:, 0:Wc], func=AF.Silu, bias=vecs[:, VEC_CB + fc:VEC_CB + fc + 1]),
                         reads=[cv, vecs], writes=[sa])
                    P.op("dve", lambda e, sa=sa, ag=ag, fc=fc: e.tensor_tensor(out=u[:, fc, 0:Wc], in0=ag[:, 0:Wc], in1=sa[:, 0:Wc], op=ALU.mult), reads=[ag, sa], writes=[u])
            loads = [(dview(w_up, fp * 256, (fp + 1) * 256), 0, 16, 512, (0, 256))]
            if not halo:
                loads.append((dview(w_up, D_FF + fp * 256, D_FF + (fp + 1) * 256), 0, 16, 512, (256, 512)))
            sched.append(dict(loads=loads, run=run, pre=(pre_up if fp == 0 else None)))
        if halo:
            return
        for og in range(4):
            for s4 in range(4):
                def pre_d():
                    P.op("act", lambda e: e.activation(out=xr[:, :, 0:Wc], in_=xr[:, :, 0:Wc], func=AF.Identity, scale=ALPHA), reads=[xr], writes=[xr])

                def run(wbt, og=og, s4=s4):
                    wv = wbt.h.ap()[:, 0:11 * 512].rearrange("p (k n) -> p k n", k=11)
                    for o4 in range(4):
                        oc = og * 4 + o4
                        a = accs[o4]
                        for k2 in range(11):
                            fc = s4 * 11 + k2
                            P.op("pe", lambda e, a=a, k2=k2, fc=fc, o4=o4: e.matmul(a[:, 0:Wc], lhsT=wv[:, k2, o4 * 128:(o4 + 1) * 128], rhs=u[:, fc, 0:Wc],
                                                                                    start=(fc == 0), stop=(fc == NFC - 1)),
                                 reads=[wbt, u], writes=[a])
                        if s4 == 3:
                            P.op("dve", lambda e, a=a, oc=oc: e.scalar_tensor_tensor(out=xr[:, oc, 0:Wc], in0=a[:, 0:Wc], scalar=modT[:, 48 + oc:49 + oc], in1=xr[:, oc, 0:Wc],
                                                                                    op0=ALU.mult, op1=ALU.add), reads=[a, modT, xr], writes=[xr])
                    if og == 3 and s4 == 3:
                        layer_norm_inplace(Wc, VEC_LG1, VEC_LB1)
                        P.dma("sp", xoT.h.ap()[:, oc0:oc0 + Wc].rearrange("(kc p) n -> p kc n", p=128), xr[:, :, 0:Wc], reads=[xr])
                r0 = s4 * 11 * 128
                sched.append(dict(loads=[(w_down.h.ap()[r0:r0 + 11 * 128, og * 512:(og + 1) * 512].rearrange("(k p) n -> p k n", p=128), 0, 11, 512, None)],
                                  run=run, pre=(pre_d if (og == 0 and s4 == 0) else None)))

    add_tile(0, 2, True)
    for tt in range(NT // W):
        add_tile(2 + tt * W, W, False)

    def issue_dma(i):
        st = sched[i]
        bt = wb[i % NST]
        f = bt.h.ap()
        for (ap, off, k, n, cols) in st["loads"]:
            dst = f[:, off:off + k * n].rearrange("p (k n) -> p k n", n=n)
            if cols is not None:
                dst = dst[:, :, cols[0]:cols[1]]
            P.dma("pool", dst, ap, writes=[bt])

    n = len(sched)
    for i in range(min(NST, n)):
        issue_dma(i)
    for i in range(n):
        if sched[i].get("pre"):
            sched[i]["pre"]()
        sched[i]["run"](wb[i % NST])
        if i + NST < n:
            issue_dma(i + NST)
    P.finish()
    P.emit()
    return nc


def build_M():
    nc = bass.Bass("TRN2", target_bir_lowering=False)
    P = Prog(nc)
    cT = P.dram("cT", [128, KC], F32, "ExternalInput")
    wsl = P.dram("wsl", [D, 3072], F32, "ExternalInput")
    bsl = P.dram("bsl", [1, 3072], F32, "ExternalInput")
    one_d = P.dram("one", [1, 1], F32, "ExternalInput")
    modp = P.dram("modp", [128, 24], F32, "ExternalOutput")
    stage = [P.sb(f"st{i}", [128, KC, 512], F32) for i in range(2)]
    one1 = P.sb("one1", [1, 1], F32)
    ps_row = P.ps("ps_row", [128, 512], F32)
    ps_t = P.ps("ps_t", [128, 512], F32)
    P.dma("sp", one1[:], one_d[:], writes=[one1])
    modT = mod_vectors(P, cT, wsl, bsl, 3072, stage, ps_row, ps_t, one1, cw=512)
    P.dma("sp", modp[:], modT[:], reads=[modT])
    P.finish()
    P.emit()
    return nc


def b1_inputs(projT, cmp_pe, cmp_w1, cmp_w2, consts, h):
    g, r = h // 4, h % 4
    heads = [g * 4 + r] + [g * 4 + o for o in range(4) if o != r]
    q4 = np.stack([projT[hd * 128:(hd + 1) * 128] for hd in heads], axis=0)
    def kvrow(br, kvi):
        b0 = 1024 + ((br * 2 + kvi) * 2 + g) * 128
        return projT[b0:b0 + 128]
    gT = projT[2560 + h * 3: 2560 + h * 3 + 3]
    gat = np.ascontiguousarray(gT.T.reshape(128, 128, 3).transpose(1, 0, 2))
    m = dict(q4=np.ascontiguousarray(q4), kcmpT=np.ascontiguousarray(kvrow(0, 0)), vcmpT=np.ascontiguousarray(kvrow(0, 1)),
             kslcT=np.ascontiguousarray(kvrow(1, 0)), vslc=np.ascontiguousarray(kvrow(1, 1).T),
             kwinT=np.ascontiguousarray(kvrow(2, 0)), vwin=np.ascontiguousarray(kvrow(2, 1).T), gat=gat,
             peT=np.ascontiguousarray(cmp_pe.transpose(2, 0, 1)),
             w1=np.ascontiguousarray(cmp_w1.reshape(2, 32, 128, 256).transpose(0, 2, 1, 3)),
             w2=np.ascontiguousarray(cmp_w2.reshape(2, 2, 128, 128).transpose(2, 0, 1, 3)))
    m.update(consts)
    return m

def b2_inputs(projT, ml_conv_w, ml_conv_b, ml_gate_b, consts, i):
    hh, half = i // 2, i % 2
    QK0 = 2584; V0 = QK0 + 1024; IF0 = V0 + 1024
    m = dict(qraw=np.ascontiguousarray(projT[QK0 + hh * 128: QK0 + (hh + 1) * 128]),
             kraw=np.ascontiguousarray(projT[QK0 + 512 + hh * 128: QK0 + 512 + (hh + 1) * 128]),
             vtok=np.ascontiguousarray(projT[V0 + hh * 256 + half * 128: V0 + hh * 256 + (half + 1) * 128].T),
             gif=np.ascontiguousarray(np.stack([projT[IF0 + hh].reshape(128, 128), projT[IF0 + 4 + hh].reshape(128, 128)], axis=1)))
    cq = ml_conv_w[:, hh * 128:(hh + 1) * 128]
    ck = ml_conv_w[:, 512 + hh * 128:512 + (hh + 1) * 128]
    m["convw"] = np.ascontiguousarray(np.stack([cq.T, ck.T], axis=1)).astype(np.float32)
    m["convb"] = np.ascontiguousarray(np.stack([ml_conv_b[hh * 128:(hh + 1) * 128], ml_conv_b[512 + hh * 128:512 + (hh + 1) * 128]], axis=1)).astype(np.float32)
    m["gateb"] = np.ascontiguousarray(np.broadcast_to(np.array([ml_gate_b[hh], ml_gate_b[4 + hh]], np.float32)[None, :], (128, 2)))
    m.update(consts)
    return m

def c_consts():
    return dict(cst=np.concatenate([np.full((128, 128), 1.0 / 2048, np.float32), np.full((128, 128), 1.0 / 256, np.float32),
                                    np.ones((128, 1), np.float32)], axis=1))

def pvec(v):
    return np.ascontiguousarray(np.asarray(v, np.float32).reshape(-1, 128).T)

def c_inputs(xT_full, ynsaT, hmlT, projT, prm, l, i, consts, modT):
    t0 = i * 2048
    def halo(a):
        if i == 0:
            return np.ascontiguousarray(np.concatenate([np.zeros((a.shape[0], 2), a.dtype), a[:, 0:2048]], axis=1))
        return np.ascontiguousarray(a[:, t0 - 2:t0 + 2048])
    vecs = np.concatenate([pvec(prm["ml_norm_g"][l]), pvec(prm["ln_g"][l, 0]), pvec(prm["ln_b"][l, 0]), pvec(prm["ln_g"][l, 1]), pvec(prm["ln_b"][l, 1]),
                           np.ascontiguousarray(np.asarray(prm["ffn_conv_w"][l], np.float32).reshape(3, 44, 128).transpose(2, 1, 0)).reshape(128, 132),
                           pvec(prm["ffn_conv_b"][l]), np.full((128, 1), 0.0 if i == 0 else 1.0, np.float32)], axis=1)
    m = dict(xT=halo(xT_full), ynsaT=halo(ynsaT), hmlT=halo(hmlT), pT2=halo(projT[4640:9760]),
             w_brn=prm["w_br_nsa"][l], w_brm=prm["w_br_ml"][l], w_o=prm["w_o"][l], w_up=prm["w_up"][l], w_down=prm["w_down"][l],
             modT=modT, vecs=np.ascontiguousarray(vecs.astype(np.float32)))
    m.update(consts)
    return m


_PROGS = {}


def _prog(name):
    if name not in _PROGS:
        _PROGS[name] = {"M": build_M, "A": build_A, "B1": build_B1, "B2": build_B2, "C": build_C}[name]()
    return _PROGS[name]


def _run(name, in_maps):
    res = run_bass_kernel_spmd(_prog(name), in_maps, core_ids=list(range(NCORE)))
    return res.results


def kernel(**inp):
    prm = {k: np.asarray(v) for k, v in inp.items()}
    x = prm["x"][0]
    cT = np.ascontiguousarray(prm["c"][0].reshape(16, 128).T.astype(np.float32))
    one = np.ones((1, 1), np.float32)
    in_maps = []
    for i in range(NCORE):
        sl = slice(i * 1536, (i + 1) * 1536)
        in_maps.append(dict(cT=cT, wsl=np.ascontiguousarray(np.concatenate([prm["w_ada"][0][:, sl], prm["w_ada"][1][:, sl]], axis=1)),
                            bsl=np.ascontiguousarray(np.concatenate([prm["b_ada"][0][sl], prm["b_ada"][1][sl]])[None, :]), one=one))
    r = _run("M", in_maps)
    modT = [np.ascontiguousarray(np.concatenate([np.asarray(r[i]["modp"])[:, l * 12:(l + 1) * 12] for i in range(NCORE)], axis=1)) for l in range(2)]

    cstA = np.concatenate([np.full((128, 128), 1.0 / 2048, np.float32), np.ones((128, 1), np.float32)], axis=1)
    c1, c2, cc = b1_consts(), b2_consts(), c_consts()
    xT = np.ascontiguousarray(x.T)
    for l in range(2):
        r = _run("A", [dict(xT=np.ascontiguousarray(xT[:, i * NT:(i + 1) * NT]), modT=modT[l], w_in=prm["w_in"][l], cst=cstA) for i in range(NCORE)])
        projT = np.concatenate([np.asarray(r[i]["projT"]) for i in range(NCORE)], axis=1)
        r = _run("B1", [b1_inputs(projT, prm["cmp_pe"][l], prm["cmp_w1"][l], prm["cmp_w2"][l], c1, h) for h in range(NCORE)])
        ynsaT = np.concatenate([np.asarray(r[h]["yT"]) for h in range(NCORE)], axis=0)
        r = _run("B2", [b2_inputs(projT, prm["ml_conv_w"][l], prm["ml_conv_b"][l], prm["ml_gate_b"][l], c2, i) for i in range(NCORE)])
        hmlT = np.concatenate([np.asarray(r[i]["hT"]) for i in range(NCORE)], axis=0)
        r = _run("C", [c_inputs(xT, ynsaT, hmlT, projT, prm, l, i, cc, modT[l]) for i in range(NCORE)])
        xT = np.concatenate([np.asarray(r[i]["xoT"]) for i in range(NCORE)], axis=1)
    return np.ascontiguousarray(xT.T)[None].astype(np.float32)
```

```python
import ml_dtypes
from concourse.bass_utils import run_bass_kernel_spmd
import numpy as np
from contextlib import ExitStack
import concourse.bass as bass
import concourse.mybir as mybir

F32 = mybir.dt.float32
BF16 = mybir.dt.bfloat16
I32 = mybir.dt.int32
ALU = mybir.AluOpType
AF = mybir.ActivationFunctionType
AX = mybir.AxisListType

ENGS = ("pe", "act", "dve", "pool", "sp")


class Tile:
    def __init__(self, name, handle, space):
        self.name = name
        self.h = handle
        self.space = space
        self.last_w = None
        self.readers = {}
        self.dsem = None
        self.ssem = None

    def __getitem__(self, idx):
        return self.h[idx]


class Sub:
    def __init__(self, parent, idx):
        self.parent = parent
        self.idx = idx

    def __getitem__(self, i):
        return self.parent.h[self.idx][i]


def _par(t):
    return t.parent if isinstance(t, Sub) else t


class Prog:
    def __init__(self, nc):
        self.nc = nc
        self.es = ExitStack()
        self.prog = {e: [] for e in ENGS}
        self.sem = {}
        self.cnt = {e: 0 for e in ENGS}
        self.waited = {e: {} for e in ENGS}
        self.nsem = 0
        for e in ENGS:
            self.sem[e] = self._newsem("e_" + e)
        self.n_inst = 0
        self.inherit = {}
        self.scope_tiles = None
        self.scope_stack = []
        self.sem_pool = []
        self.all_dma_sems = []
        self.final_eng = []

    def _newsem(self, name):
        self.nsem += 1
        return self.nc.alloc_semaphore(name=name + "_%d" % self.nsem)

    def _track(self, t):
        t.readers = dict(self.inherit)
        if self.scope_tiles is not None:
            self.scope_tiles.append(t)
        return t

    def sb(self, name, shape, dtype):
        h = self.es.enter_context(self.nc.sbuf_tensor("s_" + name, list(shape), dtype))
        return self._track(Tile(name, h, "sb"))

    def ps(self, name, shape, dtype):
        h = self.es.enter_context(self.nc.psum_tensor("p_" + name, list(shape), dtype))
        return self._track(Tile(name, h, "ps"))

    def dram(self, name, shape, dtype, kind):
        h = self.nc.dram_tensor(name, list(shape), dtype, kind=kind)
        return Tile(name, h, "dram")

    def scope_begin(self):
        self.scope_stack.append((self.es, self.scope_tiles))
        self.es = ExitStack()
        self.scope_tiles = []

    def scope_end(self):
        for t in self.scope_tiles:
            toks = list(t.readers.values()) + ([t.last_w] if t.last_w is not None else [])
            for tok in toks:
                cur = self.inherit.get(tok[0])
                if cur is None or cur[2] < tok[2]:
                    self.inherit[tok[0]] = tok
            for rec in (t.dsem, t.ssem):
                if rec is not None:
                    self.sem_pool.append(rec)
            t.dsem = None
            t.ssem = None
        self.es.close()
        self.es, self.scope_tiles = self.scope_stack.pop()

    def new_epoch(self):
        for e in ENGS:
            if self.cnt[e] > 0:
                self.final_eng.append((e, self.sem[e], self.cnt[e]))
            self.sem[e] = self._newsem("e_" + e)
            self.cnt[e] = 0

    def _dma_sem(self, name):
        if self.sem_pool:
            return self.sem_pool.pop()
        rec = [self._newsem(name), 0]
        self.all_dma_sems.append(rec)
        return rec

    def _deps(self, eng, reads, writes, is_dma=False, dma_tile=None):
        deps = []
        for t in reads:
            if t.last_w is not None:
                deps.append((t.last_w, "raw"))
        for t in writes:
            if t.last_w is not None:
                deps.append((t.last_w, "waw"))
            for tok in t.readers.values():
                deps.append((tok, "war"))
        waits = []
        for (semkey, semh, val, src), kind in deps:
            if src == eng and not is_dma:
                if eng == "pe":
                    continue
                if kind == "war":
                    continue
            if (is_dma and kind == "waw" and src == "dma" and dma_tile is not None and dma_tile.dsem is not None
                    and semkey == id(dma_tile.dsem[0])):
                continue
            if self.waited[eng].get(semkey, 0) >= val:
                continue
            self.waited[eng][semkey] = val
            waits.append((semh, val))
        return waits

    def op(self, eng, fn, reads=(), writes=()):
        reads = [_par(t) for t in reads]
        writes = [_par(t) for t in writes]
        waits = self._deps(eng, reads, writes)
        self.cnt[eng] += 1
        tok = (id(self.sem[eng]), self.sem[eng], self.cnt[eng], eng)
        self.prog[eng].append((waits, fn, self.sem[eng], 1))
        for t in reads:
            t.readers[tok[0]] = tok
        for t in writes:
            t.last_w = tok
            t.readers = {}
        self.n_inst += 1

    def dma(self, q, out_ap, in_ap, reads=(), writes=(), **kw):
        def fn(e, out_ap=out_ap, in_ap=in_ap, kw=kw):
            return e.dma_start(out=out_ap, in_=in_ap, **kw)
        self.dma_fn(q, fn, reads, writes)

    def dma_fn(self, q, fn, reads=(), writes=(), inc=16):
        reads = [_par(t) for t in reads]
        writes = [_par(t) for t in writes]
        if writes:
            t = writes[0]
            if t.dsem is None:
                t.dsem = self._dma_sem("d_" + t.name)
            rec = t.dsem
            waits = self._deps(q, reads, writes, is_dma=True, dma_tile=t)
        else:
            t = reads[0]
            if t.ssem is None:
                t.ssem = self._dma_sem("s_" + t.name)
            rec = t.ssem
            waits = self._deps(q, reads, writes, is_dma=True)
        rec[1] += inc
        tok = (id(rec[0]), rec[0], rec[1], "dma")
        self.prog[q].append((waits, fn, rec[0], inc))
        for r in reads:
            r.readers[tok[0]] = tok
        for w in writes:
            w.last_w = tok
            w.readers = {}
        self.n_inst += 1

    def coll(self, q, fn, reads=(), writes=()):
        self.dma_fn(q, fn, reads, writes, inc=16)

    def finish(self):
        waits = []
        for rec in self.all_dma_sems:
            if rec[1] > 0:
                waits.append((rec[0], rec[1]))
        for e in ENGS:
            if e != "sp" and self.cnt[e] > 0:
                waits.append((self.sem[e], self.cnt[e]))
        for (e, semh, c) in self.final_eng:
            if e != "sp":
                waits.append((semh, c))
        self.prog["sp"].append((waits, None, None, 0))

    def emit(self):
        nc = self.nc
        prog = self.prog

        def replay(lst, e):
            for waits, fn, sem, inc in lst:
                for semh, val in waits:
                    e.wait_ge(semh, val)
                if fn is not None:
                    ins = fn(e)
                    ins.then_inc(sem, inc)

        with nc.Block() as block:
            @block.tensor
            def _(e):
                replay(prog["pe"], e)

            @block.scalar
            def _(e):
                replay(prog["act"], e)

            @block.vector
            def _(e):
                replay(prog["dve"], e)

            @block.gpsimd
            def _(e):
                replay(prog["pool"], e)

            @block.sync
            def _(e):
                replay(prog["sp"], e)
        self.es.close()


D = 2048
S = 16384
NCORE = 8
NT = S // NCORE
KC = D // 128
IN_COLS = 9760
D_FF = 5632
EPS = 1e-5
ALPHA = 4.0 ** 0.25


def dview(t, c0, c1):
    return t.h.ap()[:, c0:c1].rearrange("(kc p) n -> p kc n", p=128)


def mod_vectors(P, cT, wada, bada, ncols, stage, ps_row, ps_t, one1, cw=512):
    nch = ncols // cw
    sub = cw // 128
    cact = P.sb("cact", [128, KC], F32)
    brow = [P.sb(f"brow{i}", [1, cw], F32) for i in range(2)]
    mrow = [P.sb(f"mrow{i}", [1, cw], F32) for i in range(2)]
    modT = P.sb("modT", [128, ncols // 128], F32)
    P.dma("sp", cact[:], cT[:], writes=[cact])
    P.op("act", lambda e: e.activation(out=cact[:], in_=cact[:], func=AF.Silu), reads=[cact], writes=[cact])
    for j in range(nch):
        w = stage[j % 2]
        br = brow[j % 2]
        mr = mrow[j % 2]
        P.dma("sp", w[:], dview(wada, j * cw, (j + 1) * cw), writes=[w])
        P.dma("sp", br[:], bada.h.ap()[:, j * cw:(j + 1) * cw], writes=[br])
        for kc in range(KC):
            P.op("pe", lambda e, w=w, kc=kc: e.matmul(ps_row[0:1, 0:cw], lhsT=cact[:, kc:kc + 1], rhs=w[:, kc, :],
                                                      start=(kc == 0), stop=(kc == KC - 1)),
                 reads=[cact, w], writes=[ps_row])
        P.op("dve", lambda e, mr=mr, br=br: e.tensor_tensor(out=mr[0:1, :], in0=ps_row[0:1, 0:cw], in1=br[0:1, :], op=ALU.add),
             reads=[ps_row, br], writes=[mr])
        for c in range(sub):
            P.op("pe", lambda e, c=c, j=j, mr=mr: e.matmul(ps_t[:, sub * j + c:sub * j + c + 1], lhsT=mr[0:1, c * 128:(c + 1) * 128],
                                                          rhs=one1[0:1, 0:1], start=True, stop=True),
                 reads=[mr, one1], writes=[ps_t])
    P.op("dve", lambda e: e.tensor_copy(out=modT[:], in_=ps_t[:, 0:ncols // 128]), reads=[ps_t], writes=[modT])
    return modT


def ln_stats(P, z, sq, nkc, W, onesN, ps_a, ps_b, mean, rstd, tmpm):
    P.op("act", lambda e: e.activation(out=sq[:, 0:nkc, 0:W], in_=z[:, 0:nkc, 0:W], func=AF.Square), reads=[z], writes=[sq])
    for kc in range(nkc):
        P.op("pe", lambda e, kc=kc: e.matmul(ps_a[:, 0:W], lhsT=onesN[:], rhs=z[:, kc, 0:W], start=(kc == 0), stop=(kc == nkc - 1)),
             reads=[onesN, z], writes=[ps_a])
    for kc in range(nkc):
        P.op("pe", lambda e, kc=kc: e.matmul(ps_b[:, 0:W], lhsT=onesN[:], rhs=sq[:, kc, 0:W], start=(kc == 0), stop=(kc == nkc - 1)),
             reads=[onesN, sq], writes=[ps_b])
    P.op("act", lambda e: e.activation(out=mean[:, 0:W], in_=ps_a[:, 0:W], func=AF.Identity), reads=[ps_a], writes=[mean])
    P.op("dve", lambda e: e.tensor_tensor(out=tmpm[:, 0:W], in0=mean[:, 0:W], in1=mean[:, 0:W], op=ALU.mult), reads=[mean], writes=[tmpm])
    P.op("dve", lambda e: e.tensor_tensor(out=tmpm[:, 0:W], in0=ps_b[:, 0:W], in1=tmpm[:, 0:W], op=ALU.subtract), reads=[ps_b, tmpm], writes=[tmpm])
    P.op("act", lambda e: e.activation(out=tmpm[:, 0:W], in_=tmpm[:, 0:W], func=AF.Sqrt, bias=EPS), reads=[tmpm], writes=[tmpm])
    P.op("dve", lambda e: e.reciprocal(out=rstd[:, 0:W], in_=tmpm[:, 0:W]), reads=[tmpm], writes=[rstd])


def build_A():
    nc = bass.Bass("TRN2", target_bir_lowering=False)
    P = Prog(nc)
    xT = P.dram("xT", [D, NT], F32, "ExternalInput")
    modT_d = P.dram("modT", [128, 96], F32, "ExternalInput")
    w_in = P.dram("w_in", [D, IN_COLS], F32, "ExternalInput")
    cst = P.dram("cst", [128, 129], F32, "ExternalInput")
    projT = P.dram("projT", [IN_COLS, NT], BF16, "ExternalOutput")

    W = 512
    wf = [P.sb(f"wf{i}", [128, KC, W], F32) for i in range(2)]
    NST = 3
    wb = [P.sb(f"wb{i}", [128, KC, W], BF16) for i in range(NST)]
    hT = P.sb("hT", [128, KC, NT], BF16)
    ot = [P.sb(f"ot{i}", [128, NT], BF16) for i in range(2)]
    cs = P.sb("cs", [128, 129], F32)
    mean = P.sb("mean", [128, W], F32)
    rstd = P.sb("rstd", [128, W], F32)
    tmpm = P.sb("tmpm", [128, W], F32)
    t1 = [P.sb(f"t1_{i}", [128, W], F32) for i in range(2)]
    sc1p = P.sb("sc1p", [128, KC], F32)
    ps_a = P.ps("ps_a", [128, W], F32)
    ps_b = P.ps("ps_b", [128, W], F32)
    ps_row = P.ps("ps_row", [128, W], F32)
    ps_t = P.ps("ps_t", [128, W], F32)
    accs = [P.ps(f"acc{i}", [128, W], F32) for i in range(4)]

    P.dma("sp", cs[:], cst[:], writes=[cs])
    onesN = cs

    modT = P.sb("modT", [128, 96], F32)
    P.dma("sp", modT[:], modT_d[:], writes=[modT])
    P.op("dve", lambda e: e.tensor_scalar_add(out=sc1p[:], in0=modT[:, 16:32], scalar1=1.0), reads=[modT], writes=[sc1p])

    ncg = (IN_COLS + W - 1) // W

    def load_w(cg):
        c0 = cg * W
        cw = min(W, IN_COLS - c0)
        P.dma("pool", wb[cg % NST][:, :, 0:cw], dview(w_in, c0, c0 + cw), writes=[wb[cg % NST]])

    for cg in range(min(NST, ncg)):
        load_w(cg)

    z, sq = wf[0], wf[1]
    for tt in range(NT // W):
        P.dma("sp", z[:], xT.h.ap()[:, tt * W:(tt + 1) * W].rearrange("(kc p) n -> p kc n", p=128), writes=[z])
        ln_stats(P, z, sq, KC, W, Sub(cs, (slice(None), slice(0, 128))), ps_a, ps_b, mean, rstd, tmpm)
        for kc in range(KC):
            t = t1[kc % 2]
            P.op("pool", lambda e, t=t, kc=kc: e.tensor_tensor(out=t[:], in0=z[:, kc, :], in1=mean[:], op=ALU.subtract),
                 reads=[z, mean], writes=[t])
            P.op("dve", lambda e, t=t: e.tensor_tensor(out=t[:], in0=t[:], in1=rstd[:], op=ALU.mult), reads=[t, rstd], writes=[t])
            P.op("act", lambda e, t=t, kc=kc, tt=tt: e.activation(out=hT[:, kc, tt * W:(tt + 1) * W], in_=t[:], func=AF.Identity,
                                                                  scale=sc1p[:, kc:kc + 1], bias=modT[:, kc:kc + 1]),
                 reads=[t, sc1p, modT], writes=[hT])

    gi = 0
    oi = 0
    for cg in range(ncg):
        c0 = cg * W
        cw = min(W, IN_COLS - c0)
        bt = wb[cg % NST]
        for sub in range((cw + 127) // 128):
            m = min(128, cw - sub * 128)
            o = ot[oi % 2]
            oi += 1
            for tt in range(NT // W):
                acc = accs[gi % 4]
                for kc in range(KC):
                    P.op("pe", lambda e, acc=acc, bt=bt, kc=kc, sub=sub, m=m, tt=tt:
                         e.matmul(acc[0:m, :], lhsT=bt[:, kc, sub * 128:sub * 128 + m], rhs=hT[:, kc, tt * W:(tt + 1) * W],
                                  start=(kc == 0), stop=(kc == KC - 1)),
                         reads=[bt, hT], writes=[acc])
                if gi % 2 == 0:
                    P.op("act", lambda e, acc=acc, o=o, m=m, tt=tt: e.activation(out=o[0:m, tt * W:(tt + 1) * W], in_=acc[0:m, :], func=AF.Identity),
                         reads=[acc], writes=[o])
                else:
                    P.op("dve", lambda e, acc=acc, o=o, m=m, tt=tt: e.tensor_copy(out=o[0:m, tt * W:(tt + 1) * W], in_=acc[0:m, :]),
                         reads=[acc], writes=[o])
                gi += 1
            r0 = c0 + sub * 128
            P.dma("sp", projT.h.ap()[r0:r0 + m, :], o[0:m, :], reads=[o])
        if cg + NST < ncg:
            load_w(cg + NST)
    P.finish()
    P.emit()
    return nc


NEG = -30000.0
QW = 512
NQT = S // QW
SCALE = 128.0 ** -0.5
CMP_OFFS = [31, 31 - 512, 31 - 1024, 31 - 1536, 31 - 2048]


def b1_consts():
    bf = ml_dtypes.bfloat16
    p = np.arange(128)[:, None]
    f = np.arange(512)[None, :]
    c = {}
    c["ident"] = np.eye(128, dtype=np.float32).astype(bf)
    c["cmpmask"] = np.stack([np.where(f >= 16 * p + off, 0.0, NEG) for off in CMP_OFFS], axis=1).astype(bf)
    c["causal"] = np.stack([np.where(128 * i + p <= f, 0.0, NEG) for i in range(4)], axis=1).astype(bf)
    wm = []
    for i in range(8):
        dl = 128 * (i - 4)
        wm.append(np.where((f >= p + dl) & (f < p + dl + 512), 0.0, NEG))
    c["winmask"] = np.stack(wm, axis=1).astype(bf)
    bs = np.zeros((128, 64, 128), np.float32)
    for v in range(64):
        bs[2 * v, v, 0:64] = 1.0
        bs[2 * v + 1, v, 64:128] = 1.0
    c["bsel"] = bs.astype(bf)
    cc = (np.arange(8)[None, :, None] * 128 + np.arange(128)[:, None, None])
    s = np.arange(256)[None, None, :]
    ov = ((16 * cc < 64 * s + 64) & (16 * cc + 32 > 64 * s)).astype(np.float32)
    c["ov1"] = np.concatenate([ov, np.ones((128, 8, 1), np.float32)], axis=2).astype(bf)
    rel = np.arange(512)[None, :] - 256
    cur = (np.arange(128)[:, None] >= 64).astype(np.int64)
    c["cmv"] = (rel < cur - 1).astype(np.float32)
    c["cma"] = np.where((rel == cur) | (rel == cur - 1), 1e6, np.where(rel > cur, -1.0, 0.0)).astype(np.float32)
    return c


def build_B1():
    nc = bass.Bass("TRN2", target_bir_lowering=False)
    P = Prog(nc)
    DI = lambda n, sh, dt: P.dram(n, sh, dt, "ExternalInput")
    q4 = DI("q4", [4, 128, S], BF16)
    kcmpT = DI("kcmpT", [128, S], BF16)
    vcmpT = DI("vcmpT", [128, S], BF16)
    kslcT = DI("kslcT", [128, S], BF16)
    vslc = DI("vslc", [S, 128], BF16)
    kwinT = DI("kwinT", [128, S], BF16)
    vwin = DI("vwin", [S, 128], BF16)
    gat = DI("gat", [128, 128, 3], BF16)
    peT = DI("peT", [128, 2, 32], F32)
    w1 = DI("w1", [2, 128, 32, 256], F32)
    w2 = DI("w2", [128, 2, 2, 128], F32)
    ident_d = DI("ident", [128, 128], BF16)
    cmpmask_d = DI("cmpmask", [128, 5, 512], BF16)
    causal_d = DI("causal", [128, 4, 512], BF16)
    winmask_d = DI("winmask", [128, 8, 512], BF16)
    bsel_d = DI("bsel", [128, 64, 128], BF16)
    ov1_d = DI("ov1", [128, 8, 257], BF16)
    cmv_d = DI("cmv", [128, 512], F32)
    cma_d = DI("cma", [128, 512], F32)
    yT = P.dram("yT", [128, S], BF16, "ExternalOutput")

    ident = P.sb("ident", [128, 128], BF16)
    cmpmask = P.sb("cmpmask", [128, 5, 512], BF16)
    causal = P.sb("causal", [128, 4, 512], BF16)
    winmask = P.sb("winmask", [128, 8, 512], BF16)
    bsel = P.sb("bsel", [128, 64, 128], BF16)
    cmv = P.sb("cmv", [128, 512], F32)
    cma = P.sb("cma", [128, 512], F32)
    gates = P.sb("gates", [128, 128, 3], F32)
    ksT = P.sb("ksT", [128, S], BF16)
    vsa = P.sb("vsa", [128, 128, 129], BF16)
    kcT = P.sb("kcT", [128, 1024], BF16)
    vca = P.sb("vca", [128, 8, 385], BF16)
    for t, d in ((ident, ident_d), (cmpmask, cmpmask_d), (causal, causal_d), (winmask, winmask_d), (bsel, bsel_d),
                 (cmv, cmv_d), (cma, cma_d)):
        P.dma("sp", t[:], d[:], writes=[t])
    gtmp = P.sb("gtmp", [128, 128, 3], BF16)
    P.dma("sp", gtmp[:], gat[:], writes=[gtmp])
    P.op("act", lambda e: e.activation(out=gates[:], in_=gtmp[:], func=AF.Sigmoid), reads=[gtmp], writes=[gates])
    P.dma("sp", ksT[:], kslcT[:], writes=[ksT])
    P.dma("sp", vsa[:, :, 0:128], vslc.h.ap().rearrange("(j p) d -> p j d", p=128), writes=[vsa])
    P.op("pool", lambda e: e.memset(vsa[:, :, 128:129], 1.0), reads=[], writes=[vsa])
    P.dma("sp", vca[:, :, 0:257], ov1_d[:], writes=[vca])

    S_ps = [P.ps(f"S{i}", [128, 512], F32) for i in range(2)]
    acc = [P.ps(f"acc{i}", [128, 512], F32) for i in range(4)]
    tps = P.ps("tps", [128, 4, 128], BF16)
    mps = P.ps("mps", [128, 512], F32)

    P.scope_begin()
    xc = P.sb("xc", [128, S], BF16)
    w1f = P.sb("w1f", [128, 32, 256], F32)
    w1b = P.sb("w1b", [128, 32, 256], BF16)
    w2f = P.sb("w2f", [128, 2, 2, 128], F32)
    w2b = P.sb("w2b", [128, 2, 2, 128], BF16)
    pef = P.sb("pef", [128, 2, 32], F32)
    peb = P.sb("peb", [128, 2, 32], BF16)
    gel = [P.sb(f"gel{i}", [128, 1024], BF16) for i in range(2)]
    hb = P.sb("hb", [128, 1], F32)
    xh = P.sb("xh", [128, 512], F32)
    xu = P.sb("xu", [128, 512], F32)
    P.dma("sp", w2f[:], w2[:], writes=[w2f])
    P.dma("sp", pef[:], peT[:], writes=[pef])
    P.op("dve", lambda e: e.tensor_copy(out=w2b[:], in_=w2f[:]), reads=[w2f], writes=[w2b])
    P.op("dve", lambda e: e.tensor_copy(out=peb[:], in_=pef[:]), reads=[pef], writes=[peb])
    for kv in range(2):
        P.dma("sp", xc[:], (kcmpT if kv == 0 else vcmpT)[:], writes=[xc])
        P.dma("sp", w1f[:], w1.h.ap()[kv], writes=[w1f])
        P.op("dve", lambda e: e.tensor_copy(out=w1b[:], in_=w1f[:]), reads=[w1f], writes=[w1b])
        xv = xc.h.ap().rearrange("p (b s) -> p b s", s=16)
        for half in range(2):
            g_ = gel[half]
            P.op("pool", lambda e, g_=g_: e.memset(g_[:], 0.0), reads=[], writes=[g_])
            for j in range(32):
                P.op("pe", lambda e, j=j, half=half, kv=kv: e.matmul(mps[:, 0:1], lhsT=w1b[:, j, half * 128:(half + 1) * 128],
                                                                     rhs=peb[:, kv, j:j + 1], start=(j == 0), stop=(j == 31)),
                     reads=[w1b, peb], writes=[mps])
            P.op("dve", lambda e: e.tensor_copy(out=hb[:], in_=mps[:, 0:1]), reads=[mps], writes=[hb])
            for nci, (n0, cnt) in enumerate(((0, 512), (512, 511))):
                sp_ = S_ps[nci]
                for j in range(32):
                    b0 = n0 + j // 16
                    P.op("pe", lambda e, j=j, half=half, b0=b0, cnt=cnt, sp_=sp_, xv=xv:
                         e.matmul(sp_[:, 0:cnt], lhsT=w1b[:, j, half * 128:(half + 1) * 128], rhs=xv[:, b0:b0 + cnt, j % 16],
                                  start=(j == 0), stop=(j == 31)),
                         reads=[w1b, xc], writes=[sp_])
                P.op("act", lambda e, sp_=sp_, cnt=cnt: e.activation(out=xh[:, 0:cnt], in_=sp_[:, 0:cnt], func=AF.Identity, bias=hb[:, 0:1]),
                     reads=[sp_, hb], writes=[xh])
                P.op("dve", lambda e, cnt=cnt: e.tensor_tensor(out=xu[:, 0:cnt], in0=xh[:, 0:cnt], in1=xh[:, 0:cnt], op=ALU.mult), reads=[xh], writes=[xu])
                P.op("dve", lambda e, cnt=cnt: e.tensor_scalar(out=xu[:, 0:cnt], in0=xu[:, 0:cnt], scalar1=0.044715, scalar2=1.0,
                                                               op0=ALU.mult, op1=ALU.add), reads=[xu], writes=[xu])
                P.op("dve", lambda e, cnt=cnt: e.tensor_tensor(out=xu[:, 0:cnt], in0=xu[:, 0:cnt], in1=xh[:, 0:cnt], op=ALU.mult), reads=[xu, xh], writes=[xu])
                P.op("act", lambda e, cnt=cnt: e.activation(out=xu[:, 0:cnt], in_=xu[:, 0:cnt], func=AF.Sigmoid, scale=1.5957691216),
                     reads=[xu], writes=[xu])
                P.op("dve", lambda e, cnt=cnt, n0=n0, g_=g_: e.tensor_tensor(out=g_[:, n0:n0 + cnt], in0=xu[:, 0:cnt], in1=xh[:, 0:cnt], op=ALU.mult),
                     reads=[xu, xh], writes=[g_])
        if kv == 0:
            for nci in range(2):
                for half in range(2):
                    P.op("pe", lambda e, nci=nci, half=half: e.matmul(mps[:, :], lhsT=w2b[:, 0, half, :], rhs=gel[half][:, nci * 512:(nci + 1) * 512],
                                                                      start=(half == 0), stop=(half == 1)),
                         reads=[w2b, gel[half]], writes=[mps])
                P.op("dve", lambda e, nci=nci: e.tensor_copy(out=kcT[:, nci * 512:(nci + 1) * 512], in_=mps[:, :]), reads=[mps], writes=[kcT])
        else:
            for m in range(8):
                for half in range(2):
                    P.op("pe", lambda e, m=m, half=half: e.matmul(mps[:, 0:128], lhsT=gel[half][:, m * 128:(m + 1) * 128], rhs=w2b[:, 1, half, :],
                                                                  start=(half == 0), stop=(half == 1)),
                         reads=[w2b, gel[half]], writes=[mps])
                P.op("dve", lambda e, m=m: e.tensor_copy(out=vca[:, m, 257:385], in_=mps[:, 0:128]), reads=[mps], writes=[vca])

    P.scope_end()
    qt = [P.sb(f"qt{i}", [128, 4, QW], BF16) for i in range(2)]
    kwT = [P.sb(f"kwT{i}", [128, 1024], BF16) for i in range(2)]
    vwa = [P.sb(f"vwa{i}", [128, 8, 129], BF16) for i in range(2)]
    ET = [P.sb(f"ET{i}", [128, QW], BF16) for i in range(3)]
    imp = P.sb("imp", [128, 4, 256], F32)
    ocomb = P.sb("ocomb", [128, 4, 128], F32)
    ocb = P.sb("ocb", [128, 4, 128], BF16)
    rden = P.sb("rden", [128, 1], F32)
    gsc = P.sb("gsc", [128, 1], F32)
    score = P.sb("score", [128, 256], F32)
    sc2 = P.sb("sc2", [128, 256], F32)
    m8 = P.sb("m8", [128, 8], F32)
    negsel = P.sb("negsel", [128, 256], BF16)
    nsT = P.sb("nsT", [128, 2, QW], BF16)
    yo = [P.sb(f"yo{i}", [128, QW], BF16) for i in range(2)]
    for i in range(2):
        P.op("pool", lambda e, i=i: e.memset(vwa[i][:, :, 128:129], 1.0), reads=[], writes=[vwa[i]])
    cnt_s = [0]
    cnt_e = [0]

    def attend(qap, qtile, chunks, NV):
        n = len(chunks)
        sps = [None] * n

        def emit_qk(ci):
            kt_ap, kt_tile, v_ap, v_tile, masks = chunks[ci]
            sp_ = S_ps[cnt_s[0] % 2]
            cnt_s[0] += 1
            sps[ci] = sp_
            nm = len(masks)
            P.op("pe", lambda e, sp_=sp_, kt_ap=kt_ap, nm=nm: e.matmul(sp_[:, :], lhsT=kt_ap, rhs=qap, start=True, stop=(nm == 0)),
                 reads=[kt_tile, qtile], writes=[sp_])
            for mi, (ml, mr, mt) in enumerate(masks):
                P.op("pe", lambda e, sp_=sp_, ml=ml, mr=mr, mi=mi, nm=nm: e.matmul(sp_[:, :], lhsT=ml, rhs=mr, start=False, stop=(mi == nm - 1)),
                     reads=list(mt), writes=[sp_])

        emit_qk(0)
        for ci in range(n):
            if ci + 1 < n:
                emit_qk(ci + 1)
            kt_ap, kt_tile, v_ap, v_tile, masks = chunks[ci]
            sp_ = sps[ci]
            et = ET[cnt_e[0] % 3]
            cnt_e[0] += 1
            P.op("act", lambda e, sp_=sp_, et=et: e.activation(out=et[:, :], in_=sp_[:, :], func=AF.Exp, scale=SCALE), reads=[sp_], writes=[et])
            for qb in range(4):
                P.op("pe", lambda e, qb=qb, et=et, v_ap=v_ap, ci=ci: e.matmul(acc[qb][:, 0:NV], lhsT=et[:, qb * 128:(qb + 1) * 128], rhs=v_ap,
                                                                            start=(ci == 0), stop=(ci == n - 1)),
                     reads=[et, v_tile], writes=[acc[qb]])

    def fold_out(qb, col_den, col_o, gate_ap, first):
        a = acc[qb]
        P.op("dve", lambda e, a=a: e.tensor_scalar_max(out=rden[:], in0=a[:, col_den:col_den + 1], scalar1=1e-30), reads=[a], writes=[rden])
        P.op("dve", lambda e: e.reciprocal(out=rden[:], in_=rden[:]), reads=[rden], writes=[rden])
        P.op("dve", lambda e: e.tensor_tensor(out=gsc[:], in0=rden[:], in1=gate_ap, op=ALU.mult), reads=[rden, gates], writes=[gsc])
        if first:
            P.op("dve", lambda e, a=a: e.tensor_scalar(out=ocomb[:, qb, :], in0=a[:, col_o:col_o + 128], scalar1=gsc[:, 0:1], scalar2=None, op0=ALU.mult),
                 reads=[a, gsc], writes=[ocomb])
        else:
            P.op("dve", lambda e, a=a: e.scalar_tensor_tensor(out=ocomb[:, qb, :], in0=a[:, col_o:col_o + 128], scalar=gsc[:, 0:1], in1=ocomb[:, qb, :],
                                                              op0=ALU.mult, op1=ALU.add),
                 reads=[a, gsc, ocomb], writes=[ocomb])

    def load_tile(k):
        t0 = k * QW
        q_ = qt[k % 2]
        P.dma("sp", q_[:], q4.h.ap()[:, :, t0:t0 + QW].rearrange("h p t -> p h t"), writes=[q_])
        lo = max(0, t0 - 512)
        off = lo - (t0 - 512)
        P.dma("sp", kwT[k % 2][:, off:1024], kwinT.h.ap()[:, lo:t0 + 512], writes=[kwT[k % 2]])
        P.dma("sp", vwa[k % 2][:, off // 128:8, 0:128], vwin.h.ap()[lo:t0 + 512, :].rearrange("(j p) d -> p j d", p=128), writes=[vwa[k % 2]])

    load_tile(0)
    for k in range(NQT):
        t0 = k * QW
        if k + 1 < NQT:
            load_tile(k + 1)
        q_ = qt[k % 2]
        mmax = (t0 + 480) // 2048
        for hh in range(4):
            chunks = []
            NV = 385 if hh == 0 else 257
            for m in range(mmax + 1):
                off = 2048 * m + 31 - t0
                masks = []
                if off + 16 * 127 > 0:
                    mi = CMP_OFFS.index(off)
                    masks.append((ident[:], cmpmask[:, mi, :], (ident, cmpmask)))
                chunks.append((kcT[:, m * 128:(m + 1) * 128], kcT, vca[:, m, 0:NV], vca, masks))
            attend(q_[:, hh, :], q_, chunks, NV)
            for qb in range(4):
                a = acc[qb]
                b = k * 4 + qb
                if hh == 0:
                    fold_out(qb, 256, 257, gates[:, b, 0:1], True)
                    P.op("dve", lambda e, a=a, qb=qb: e.tensor_scalar(out=imp[:, qb, :], in0=a[:, 0:256], scalar1=rden[:, 0:1], scalar2=None, op0=ALU.mult),
                         reads=[a, rden], writes=[imp])
                else:
                    P.op("dve", lambda e, a=a: e.tensor_scalar_max(out=rden[:], in0=a[:, 256:257], scalar1=1e-30), reads=[a], writes=[rden])
                    P.op("dve", lambda e: e.reciprocal(out=rden[:], in_=rden[:]), reads=[rden], writes=[rden])
                    P.op("dve", lambda e, a=a, qb=qb: e.scalar_tensor_tensor(out=imp[:, qb, :], in0=a[:, 0:256], scalar=rden[:, 0:1], in1=imp[:, qb, :],
                                                                            op0=ALU.mult, op1=ALU.add),
                         reads=[a, rden, imp], writes=[imp])
        chunks = []
        kw_, vw_ = kwT[k % 2], vwa[k % 2]
        for i in range(8):
            if 4 * k - 4 + i < 0:
                continue
            chunks.append((kw_[:, i * 128:(i + 1) * 128], kw_, vw_[:, i, :], vw_, [(ident[:], winmask[:, i, :], (ident, winmask))]))
        attend(q_[:, 0, :], q_, chunks, 129)
        for qb in range(4):
            b = k * 4 + qb
            w0 = 256 - 2 * b
            P.op("dve", lambda e, qb=qb, w0=w0: e.tensor_tensor(out=score[:], in0=imp[:, qb, :], in1=cmv[:, w0:w0 + 256], op=ALU.mult), reads=[imp, cmv], writes=[score])
            P.op("dve", lambda e, w0=w0: e.tensor_tensor(out=score[:], in0=score[:], in1=cma[:, w0:w0 + 256], op=ALU.add), reads=[score, cma], writes=[score])
            P.op("dve", lambda e: e.memset(score[:, 0:1], 1e6), reads=[], writes=[score])
            P.op("dve", lambda e: e.max(out=m8[:], in_=score[:]), reads=[score], writes=[m8])
            P.op("dve", lambda e: e.match_replace(out=sc2[:], in_to_replace=m8[:], in_values=score[:], imm_value=-1e9), reads=[m8, score], writes=[sc2])
            P.op("dve", lambda e: e.max(out=m8[:], in_=sc2[:]), reads=[sc2], writes=[m8])
            P.op("dve", lambda e: e.tensor_scalar(out=negsel[:], in0=score[:], scalar1=m8[:, 7:8], scalar2=NEG, op0=ALU.is_lt, op1=ALU.mult),
                 reads=[score, m8], writes=[negsel])
            for hf in range(2):
                P.op("pe", lambda e, hf=hf: e.transpose(out=tps[:, hf, :], in_=negsel[:, hf * 128:(hf + 1) * 128], identity=ident[:]),
                     reads=[negsel, ident], writes=[tps])
            P.op("act", lambda e, qb=qb: e.activation(out=nsT[:, :, qb * 128:(qb + 1) * 128], in_=tps[:, 0:2, :], func=AF.Identity), reads=[tps], writes=[nsT])
        for qb in range(4):
            fold_out(qb, 128, 0, gates[:, k * 4 + qb, 2:3], False)
        chunks = []
        for j in range(4 * k + 4):
            masks = [(bsel[:, j % 64, :], nsT[:, j // 64, :], (bsel, nsT))]
            if j >= 4 * k:
                masks.append((ident[:], causal[:, j - 4 * k, :], (ident, causal)))
            chunks.append((ksT[:, j * 128:(j + 1) * 128], ksT, vsa[:, j, :], vsa, masks))
        attend(q_[:, 0, :], q_, chunks, 129)
        for qb in range(4):
            fold_out(qb, 128, 0, gates[:, k * 4 + qb, 1:2], False)
        P.op("act", lambda e: e.activation(out=ocb[:], in_=ocomb[:], func=AF.Identity), reads=[ocomb], writes=[ocb])
        for qb in range(4):
            P.op("pe", lambda e, qb=qb: e.transpose(out=tps[:, qb, :], in_=ocb[:, qb, :], identity=ident[:]), reads=[ocb, ident], writes=[tps])
        yo_ = yo[k % 2]
        P.op("act", lambda e, yo_=yo_: e.activation(out=yo_[:].rearrange("p (a b) -> p a b", b=128), in_=tps[:, :, :], func=AF.Identity), reads=[tps], writes=[yo_])
        P.dma("sp", yT.h.ap()[:, t0:t0 + QW], yo_[:], reads=[yo_])
    P.finish()
    P.emit()
    return nc


NPAIR = S // 128


def b2_consts():
    bf = ml_dtypes.bfloat16
    c = {}
    c["ident"] = np.eye(128, dtype=np.float32).astype(bf)
    c["identf"] = np.eye(128, dtype=np.float32)
    s = np.arange(128)[:, None]
    t = np.arange(128)[None, :]
    c["mask01"] = (((s // 64) == (t // 64)) & (s <= t)).astype(np.float32)
    c["onesf"] = np.ones((128, 128), np.float32)
    return c


def build_B2():
    nc = bass.Bass("TRN2", target_bir_lowering=False)
    P = Prog(nc)
    DI = lambda n, sh, dt: P.dram(n, sh, dt, "ExternalInput")
    qraw = DI("qraw", [128, S], BF16)
    kraw = DI("kraw", [128, S], BF16)
    vtok = DI("vtok", [S, 128], BF16)
    gif = DI("gif", [128, 2, 128], BF16)
    convw = DI("convw", [128, 2, 4], F32)
    convb = DI("convb", [128, 2], F32)
    gateb = DI("gateb", [128, 2], F32)
    ident_d = DI("ident", [128, 128], BF16)
    identf_d = DI("identf", [128, 128], F32)
    mask_d = DI("mask01", [128, 128], F32)
    ones_d = DI("onesf", [128, 128], F32)
    scr = P.dram("scr", [4, 256], F32, "Internal")
    hT = P.dram("hT", [128, S], BF16, "ExternalOutput")

    ident = P.sb("ident", [128, 128], BF16)
    identf = P.sb("identf", [128, 128], F32)
    mask01 = P.sb("mask01", [128, 128], F32)
    onesf = P.sb("onesf", [128, 128], F32)
    cw = P.sb("cw", [128, 2, 4], F32)
    cb = P.sb("cb", [128, 2], F32)
    gb = P.sb("gb", [128, 2], F32)
    for t, d in ((ident, ident_d), (identf, identf_d), (mask01, mask_d), (onesf, ones_d), (cw, convw), (cb, convb), (gb, gateb)):
        P.dma("sp", t[:], d[:], writes=[t])
    QT = P.sb("QT", [128, S], BF16)
    KT = P.sb("KT", [128, S], BF16)
    Ktok = P.sb("Ktok", [128, NPAIR, 128], BF16)
    Va = P.sb("Va", [128, NPAIR, 129], BF16)
    P.dma("sp", Va[:, :, 0:128], vtok.h.ap().rearrange("(j p) d -> p j d", p=128), writes=[Va])
    P.op("pool", lambda e: e.memset(Va[:, :, 128:129], 1.0), reads=[], writes=[Va])
    ewT = P.sb("ewT", [128, 128], F32)
    euT = P.sb("euT", [128, 128], F32)
    wiT = P.sb("wiT", [128, 128], F32)
    gdT = P.sb("gdT", [128, 128], F32)
    decb = P.sb("decb", [128, 256], F32)
    sc2b = P.sb("sc2b", [128, 256], F32)

    pKQ = [P.ps(f"pKQ{i}", [128, 512], F32) for i in range(2)]
    pBs = [P.ps(f"pB{i}", [128, 512], F32) for i in range(2)]
    pA = P.ps("pA", [128, 512], F32)
    pU = P.ps("pU", [128, 512], F32)
    pT = P.ps("pT", [128, 4, 128], BF16)
    pM = P.ps("pM", [128, 512], F32)

    P.scope_begin()
    xp = P.sb("xp", [128, S + 3], BF16)
    yseg = P.sb("yseg", [128, 4096], F32)
    P.op("pool", lambda e: e.memset(xp[:, 0:3], 0.0), reads=[], writes=[xp])
    for qk, (src, dst) in enumerate(((qraw, QT), (kraw, KT))):
        P.dma("sp", xp[:, 3:S + 3], src[:], writes=[xp])
        for sg in range(4):
            c0 = sg * 4096
            P.op("dve", lambda e, c0=c0, qk=qk: e.tensor_scalar(out=yseg[:], in0=xp[:, c0:c0 + 4096], scalar1=cw[:, qk, 0:1], scalar2=None, op0=ALU.mult),
                 reads=[xp, cw], writes=[yseg])
            for j in range(1, 4):
                P.op("dve", lambda e, c0=c0, qk=qk, j=j: e.scalar_tensor_tensor(out=yseg[:], in0=xp[:, c0 + j:c0 + j + 4096], scalar=cw[:, qk, j:j + 1],
                                                                              in1=yseg[:], op0=ALU.mult, op1=ALU.add),
                     reads=[xp, cw, yseg], writes=[yseg])
            if qk == 0:
                P.op("act", lambda e, c0=c0, dst=dst: e.activation(out=dst[:, c0:c0 + 4096], in_=yseg[:], func=AF.Silu, bias=cb[:, 0:1]),
                     reads=[yseg, cb], writes=[dst])
            else:
                P.op("act", lambda e: e.activation(out=yseg[:], in_=yseg[:], func=AF.Silu, bias=cb[:, 1:2]), reads=[yseg, cb], writes=[yseg])
                P.op("pool", lambda e, c0=c0, dst=dst: e.tensor_scalar(out=dst[:, c0:c0 + 4096], in0=yseg[:], scalar1=SCALE, scalar2=None, op0=ALU.mult),
                     reads=[yseg], writes=[dst])
    P.scope_end()
    for j in range(NPAIR):
        P.op("pe", lambda e, j=j: e.transpose(out=pT[:, j % 4, :], in_=KT[:, j * 128:(j + 1) * 128], identity=ident[:]), reads=[KT, ident], writes=[pT])
        if j % 4 == 3:
            P.op("act", lambda e, j=j: e.activation(out=Ktok[:, j - 3:j + 1, :], in_=pT[:, :, :], func=AF.Identity), reads=[pT], writes=[Ktok])

    P.scope_begin()
    G = lambda n: P.sb(n, [128, 128], F32)
    gtmp = P.sb("gtmp", [128, 2, 128], BF16)
    ig, lf, bcum, w_, cmw, tA, tB, mt = G("ig"), G("lf"), G("bcum"), G("w_"), G("cmw"), G("tA"), G("tB"), G("mt")
    ones64 = P.sb("ones64", [128, 64], F32)
    small = P.sb("small", [128, 8], F32)
    rows = P.sb("rows", [1, 4, 256], F32)
    P.dma("sp", gtmp[:], gif[:], writes=[gtmp])
    P.op("pool", lambda e: e.memset(ones64[:], 1.0), reads=[], writes=[ones64])
    P.op("dve", lambda e: e.tensor_scalar(out=ig[:], in0=gtmp[:, 0, :], scalar1=gb[:, 0:1], scalar2=None, op0=ALU.add), reads=[gtmp, gb], writes=[ig])
    P.op("dve", lambda e: e.tensor_scalar(out=lf[:], in0=gtmp[:, 1, :], scalar1=gb[:, 1:2], scalar2=None, op0=ALU.add), reads=[gtmp, gb], writes=[lf])
    P.op("act", lambda e: e.activation(out=lf[:], in_=lf[:], func=AF.Exp, scale=-1.0), reads=[lf], writes=[lf])
    P.op("act", lambda e: e.activation(out=lf[:], in_=lf[:], func=AF.Ln, bias=1.0), reads=[lf], writes=[lf])
    P.op("dve", lambda e: e.tensor_scalar(out=lf[:], in0=lf[:], scalar1=-1.0, scalar2=None, op0=ALU.mult), reads=[lf], writes=[lf])
    for a in range(2):
        sl = slice(a * 64, (a + 1) * 64)
        P.op("dve", lambda e, sl=sl: e.tensor_tensor_scan(out=bcum[:, sl], data0=ones64[:], data1=lf[:, sl], initial=0.0, op0=ALU.mult, op1=ALU.add),
             reads=[ones64, lf], writes=[bcum])
    P.op("dve", lambda e: e.tensor_tensor(out=w_[:], in0=ig[:], in1=bcum[:], op=ALU.subtract), reads=[ig, bcum], writes=[w_])
    for a in range(2):
        sl = slice(a * 64, (a + 1) * 64)
        P.op("dve", lambda e, sl=sl: e.tensor_tensor_scan(out=cmw[:, sl], data0=ones64[:], data1=w_[:, sl], initial=-1e30, op0=ALU.mult, op1=ALU.max),
             reads=[ones64, w_], writes=[cmw])
    for a in range(2):
        c = a * 64 + 63
        P.op("dve", lambda e, a=a, c=c: e.tensor_copy(out=small[:, a:a + 1], in_=bcum[:, c:c + 1]), reads=[bcum], writes=[small])
        P.op("dve", lambda e, a=a, c=c: e.tensor_tensor(out=small[:, 2 + a:3 + a], in0=cmw[:, c:c + 1], in1=bcum[:, c:c + 1], op=ALU.add),
             reads=[cmw, bcum], writes=[small])
    P.dma("sp", scr.h.ap()[0].rearrange("(p a) -> p a", a=2), small[:, 0:2], reads=[small], writes=[scr])
    P.dma("sp", scr.h.ap()[1].rearrange("(p a) -> p a", a=2), small[:, 2:4], reads=[small], writes=[scr])
    P.dma("sp", rows[0:1, 0:2, :], scr.h.ap()[0:2, :].rearrange("(o r) n -> o r n", o=1), reads=[scr], writes=[rows])
    P.op("dve", lambda e: e.tensor_tensor_scan(out=rows[0:1, 2, :], data0=rows[0:1, 0, :], data1=rows[0:1, 1, :], initial=0.0, op0=ALU.add, op1=ALU.max),
         reads=[rows], writes=[rows])
    P.op("dve", lambda e: e.memset(rows[0:1, 3, 0:1], 0.0), reads=[], writes=[rows])
    P.op("dve", lambda e: e.tensor_copy(out=rows[0:1, 3, 1:256], in_=rows[0:1, 2, 0:255]), reads=[rows], writes=[rows])
    P.dma("sp", scr.h.ap()[2:4, :].rearrange("(o r) n -> o r n", o=1), rows[0:1, 2:4, :], reads=[rows], writes=[scr])
    P.dma("sp", small[:, 4:6], scr.h.ap()[2].rearrange("(p a) -> p a", a=2), reads=[scr], writes=[small])
    P.dma("sp", small[:, 6:8], scr.h.ap()[3].rearrange("(p a) -> p a", a=2), reads=[scr], writes=[small])
    P.op("dve", lambda e: e.tensor_tensor(out=tA[:], in0=bcum[:], in1=cmw[:], op=ALU.add), reads=[bcum, cmw], writes=[tA])
    for a in range(2):
        sl = slice(a * 64, (a + 1) * 64)
        P.op("dve", lambda e, sl=sl, a=a: e.tensor_scalar(out=tB[:, sl], in0=bcum[:, sl], scalar1=small[:, 6 + a:7 + a], scalar2=None, op0=ALU.add),
             reads=[bcum, small], writes=[tB])
    P.op("dve", lambda e: e.tensor_tensor(out=mt[:], in0=tA[:], in1=tB[:], op=ALU.max), reads=[tA, tB], writes=[mt])
    P.op("dve", lambda e: e.tensor_tensor(out=tB[:], in0=tB[:], in1=mt[:], op=ALU.subtract), reads=[tB, mt], writes=[tB])
    P.op("dve", lambda e: e.tensor_tensor(out=tA[:], in0=bcum[:], in1=mt[:], op=ALU.subtract), reads=[bcum, mt], writes=[tA])
    P.op("act", lambda e: e.activation(out=tB[:], in_=tB[:], func=AF.Exp), reads=[tB], writes=[tB])
    P.op("act", lambda e: e.activation(out=tA[:], in_=tA[:], func=AF.Exp), reads=[tA], writes=[tA])
    P.op("act", lambda e: e.activation(out=mt[:], in_=mt[:], func=AF.Exp, scale=-1.0), reads=[mt], writes=[mt])
    P.op("act", lambda e: e.activation(out=w_[:], in_=w_[:], func=AF.Exp), reads=[w_], writes=[w_])
    for src, dst in ((w_, ewT), (tA, euT), (tB, wiT), (mt, gdT)):
        P.op("pe", lambda e, src=src: e.transpose(out=pM[:, 0:128], in_=src[:], identity=identf[:]), reads=[src, identf], writes=[pM])
        P.op("dve", lambda e, dst=dst: e.tensor_copy(out=dst[:], in_=pM[:, 0:128]), reads=[pM], writes=[dst])
    P.op("dve", lambda e: e.tensor_tensor(out=small[:, 2:4], in0=small[:, 0:2], in1=small[:, 4:6], op=ALU.subtract), reads=[small], writes=[small])
    P.op("dve", lambda e: e.tensor_tensor(out=small[:, 0:2], in0=small[:, 2:4], in1=small[:, 6:8], op=ALU.add), reads=[small], writes=[small])
    P.op("act", lambda e: e.activation(out=small[:, 0:4], in_=small[:, 0:4], func=AF.Exp), reads=[small], writes=[small])
    dg = P.sb("dg", [128, 128, 2], F32)
    for which, dst in ((0, decb), (2, sc2b)):
        for a in range(2):
            P.op("dve", lambda e, a=a, which=which: e.tensor_scalar(out=dg[:, :, a], in0=identf[:], scalar1=small[:, which + a:which + a + 1], scalar2=None, op0=ALU.mult),
                 reads=[identf, small], writes=[dg])
        P.op("pe", lambda e: e.matmul(pM[:, 0:256], lhsT=onesf[:], rhs=dg[:].rearrange("p a b -> p (a b)"), start=True, stop=True),
             reads=[onesf, dg], writes=[pM])
        P.op("dve", lambda e, dst=dst: e.tensor_copy(out=dst[:], in_=pM[:, 0:256]), reads=[pM], writes=[dst])
    P.scope_end()

    Cst = P.sb("Cst", [128, 129], F32)
    Cb = [P.sb(f"Cb{i}", [128, 129], BF16) for i in range(2)]
    Sm = [P.sb(f"Sm{i}", [128, 128], BF16) for i in range(2)]
    Vw = [P.sb(f"Vw{i}", [128, 129], BF16) for i in range(2)]
    tU = P.sb("tU", [128, 129], F32)
    tN = P.sb("tN", [128, 129], F32)
    num = P.sb("num", [128, 129], F32)
    dn = P.sb("dn", [128, 1], F32)
    hb = [P.sb(f"hb{i}", [128, 128], BF16) for i in range(2)]
    ho = [P.sb(f"ho{i}", [128, 512], BF16) for i in range(2)]
    P.op("dve", lambda e: e.memset(Cst[:], 0.0), reads=[], writes=[Cst])
    P.op("pool", lambda e: e.memset(Cb[0][:], 0.0), reads=[], writes=[Cb[0]])
    cist = [0]

    def front(j):
        cols = slice(j * 128, (j + 1) * 128)
        kq = pKQ[j % 2]
        sm, vw = Sm[j % 2], Vw[j % 2]
        pB = pBs[j % 2]
        P.op("pe", lambda e, kq=kq, cols=cols: e.matmul(kq[:, 0:128], lhsT=KT[:, cols], rhs=QT[:, cols], start=True, stop=True), reads=[KT, QT], writes=[kq])
        P.op("dve", lambda e, kq=kq, sm=sm: e.tensor_tensor(out=sm[:], in0=kq[:, 0:128], in1=mask01[:], op=ALU.mult), reads=[kq, mask01], writes=[sm])
        P.op("pool", lambda e, j=j, vw=vw: e.tensor_scalar(out=vw[:], in0=Va[:, j, :], scalar1=ewT[:, j:j + 1], scalar2=None, op0=ALU.mult),
             reads=[Va, ewT], writes=[vw])
        P.op("pe", lambda e, sm=sm, vw=vw, pB=pB: e.matmul(pB[:, 0:129], lhsT=sm[:], rhs=vw[:], start=True, stop=True), reads=[sm, vw], writes=[pB])

    def back(j):
        vw = Vw[j % 2]
        pB = pBs[j % 2]
        for a in range(2):
            c = 2 * j + a
            rs_ = slice(a * 64, (a + 1) * 64)
            cbc = Cb[cist[0] % 2]
            cbn = Cb[(cist[0] + 1) % 2]
            cist[0] += 1
            P.op("pe", lambda e, rs_=rs_, cbc=cbc, j=j: e.matmul(pA[rs_, 0:129], lhsT=QT[:, j * 128 + rs_.start:j * 128 + rs_.stop], rhs=cbc[:], start=True, stop=True),
                 reads=[QT, cbc], writes=[pA])
            P.op("pe", lambda e, rs_=rs_, j=j, vw=vw: e.matmul(pU[:, 0:129], lhsT=Ktok[rs_, j, :], rhs=vw[rs_, :], start=True, stop=True),
                 reads=[Ktok, vw], writes=[pU])
            P.op("dve", lambda e, c=c: e.tensor_scalar(out=tU[:], in0=pU[:, 0:129], scalar1=sc2b[:, c:c + 1], scalar2=None, op0=ALU.mult),
                 reads=[pU, sc2b], writes=[tU])
            P.op("dve", lambda e, c=c: e.scalar_tensor_tensor(out=Cst[:], in0=Cst[:], scalar=decb[:, c:c + 1], in1=tU[:], op0=ALU.mult, op1=ALU.add),
                 reads=[Cst, decb, tU], writes=[Cst])
            P.op("act", lambda e, cbn=cbn: e.activation(out=cbn[:], in_=Cst[:], func=AF.Identity), reads=[Cst], writes=[cbn])
        P.op("dve", lambda e, j=j: e.tensor_scalar(out=tN[:], in0=pA[:, 0:129], scalar1=wiT[:, j:j + 1], scalar2=None, op0=ALU.mult), reads=[pA, wiT], writes=[tN])
        P.op("dve", lambda e, j=j, pB=pB: e.scalar_tensor_tensor(out=num[:], in0=pB[:, 0:129], scalar=euT[:, j:j + 1], in1=tN[:], op0=ALU.mult, op1=ALU.add),
             reads=[pB, euT, tN], writes=[num])
        P.op("dve", lambda e: e.scalar_tensor_tensor(out=dn[:], in0=num[:, 128:129], scalar=-1.0, in1=num[:, 128:129], op0=ALU.mult, op1=ALU.max),
             reads=[num], writes=[dn])
        P.op("dve", lambda e, j=j: e.tensor_tensor(out=dn[:], in0=dn[:], in1=gdT[:, j:j + 1], op=ALU.max), reads=[dn, gdT], writes=[dn])
        P.op("dve", lambda e: e.reciprocal(out=dn[:], in_=dn[:]), reads=[dn], writes=[dn])
        h_ = hb[j % 2]
        P.op("dve", lambda e, h_=h_: e.tensor_scalar(out=h_[:], in0=num[:, 0:128], scalar1=dn[:, 0:1], scalar2=None, op0=ALU.mult), reads=[num, dn], writes=[h_])
        P.op("pe", lambda e, h_=h_, j=j: e.transpose(out=pT[:, j % 4, :], in_=h_[:], identity=ident[:]), reads=[h_, ident], writes=[pT])
        if j % 4 == 3:
            o_ = ho[(j // 4) % 2]
            P.op("act", lambda e, o_=o_: e.activation(out=o_[:].rearrange("p (a b) -> p a b", b=128), in_=pT[:, :, :], func=AF.Identity), reads=[pT], writes=[o_])
            P.dma("sp", hT.h.ap()[:, (j - 3) * 128:(j + 1) * 128], o_[:], reads=[o_])

    front(0)
    for j in range(NPAIR):
        if j + 1 < NPAIR:
            front(j + 1)
        back(j)
    P.finish()
    P.emit()
    return nc


NTH = NT + 2
NFC = D_FF // 128
VEC_NG, VEC_LG0, VEC_LB0, VEC_LG1, VEC_LB1, VEC_CW, VEC_CB, VEC_FLAG, VEC_N = 0, 8, 24, 40, 56, 72, 204, 248, 249


def build_C():
    nc = bass.Bass("TRN2", target_bir_lowering=False)
    P = Prog(nc)
    DI = lambda n, sh, dt: P.dram(n, sh, dt, "ExternalInput")
    xT = DI("xT", [D, NTH], F32)
    ynsaT = DI("ynsaT", [1024, NTH], BF16)
    hmlT = DI("hmlT", [1024, NTH], BF16)
    pT2 = DI("pT2", [5120, NTH], BF16)
    w_brn = DI("w_brn", [1024, D], F32)
    w_brm = DI("w_brm", [1024, D], F32)
    w_o = DI("w_o", [D, D], F32)
    w_up = DI("w_up", [D, 2 * D_FF], F32)
    w_down = DI("w_down", [D_FF, D], F32)
    modT_d = DI("modT", [128, 96], F32)
    vecs_d = DI("vecs", [128, VEC_N], F32)
    cst = DI("cst", [128, 257], F32)
    xoT = P.dram("xoT", [D, NT], F32, "ExternalOutput")

    W = 512
    NST = 3
    wb = [P.sb(f"wb{i}", [128, 8192], BF16) for i in range(NST)]
    xr = P.sb("xr", [128, KC, W], F32)
    yn = P.sb("yn", [128, 8, W], BF16)
    hm = P.sb("hm", [128, 8, W], BF16)
    yml = hm
    merged = P.sb("merged", [128, KC, W], BF16)
    h2 = merged
    u = P.sb("u", [128, NFC, W], BF16)
    abuf = [P.sb(f"abuf{i}", [128, W + 2], F32) for i in range(2)]
    carry = P.sb("carry", [128, NFC, 2], F32)
    ft = [P.sb(f"ft{i}", [128, W], F32) for i in range(6)]
    mean = P.sb("mean", [128, W], F32)
    rstd = P.sb("rstd", [128, W], F32)
    tmpm = P.sb("tmpm", [128, W], F32)
    sqt = [P.sb(f"sqt{i}", [128, W], F32) for i in range(2)]
    hsq = P.sb("hsq", [128, 2, W], BF16)
    mo = [P.sb(f"mo{i}", [128, W], BF16) for i in range(4)]
    vecs = P.sb("vecs", [128, VEC_N], F32)
    cs = P.sb("cs", [128, 257], F32)
    csb = P.sb("csb", [128, 128], BF16)
    sc2p = P.sb("sc2p", [128, KC], F32)
    ps_a = P.ps("ps_a", [128, W], F32)
    ps_b = P.ps("ps_b", [128, W], F32)
    accs = [P.ps(f"acc{i}", [128, W], F32) for i in range(4)]

    P.dma("sp", cs[:], cst[:], writes=[cs])
    P.dma("sp", vecs[:], vecs_d[:], writes=[vecs])
    P.op("dve", lambda e: e.tensor_copy(out=csb[:], in_=cs[:, 128:256]), reads=[cs], writes=[csb])
    onesN = Sub(cs, (slice(None), slice(0, 128)))
    one1 = Sub(cs, (slice(None), slice(256, 257)))
    modF = P.sb("modF", [128, 96], F32)
    P.dma("sp", modF[:], modT_d[:], writes=[modF])
    modT = Sub(modF, (slice(None), slice(32, 96)))
    P.op("dve", lambda e: e.tensor_scalar_add(out=sc2p[:], in0=modT[:, 32:48], scalar1=1.0), reads=[modT], writes=[sc2p])

    state = {"gi": 0, "fi": 0}

    def nacc():
        a = accs[state["gi"] % 4]
        state["gi"] += 1
        return a

    def nft():
        t = ft[state["fi"] % 6]
        state["fi"] += 1
        return t

    def layer_norm_inplace(Wc, gcol, bcol, out_bf=None, scale_t=None, bias_t=None):
        for kc in range(KC):
            sq = sqt[kc % 2]
            P.op("act", lambda e, sq=sq, kc=kc: e.activation(out=sq[:, 0:Wc], in_=xr[:, kc, 0:Wc], func=AF.Square), reads=[xr], writes=[sq])
            P.op("pe", lambda e, kc=kc: e.matmul(ps_a[:, 0:Wc], lhsT=onesN[:], rhs=xr[:, kc, 0:Wc], start=(kc == 0), stop=(kc == KC - 1)),
                 reads=[cs, xr], writes=[ps_a])
            P.op("pe", lambda e, kc=kc, sq=sq: e.matmul(ps_b[:, 0:Wc], lhsT=onesN[:], rhs=sq[:, 0:Wc], start=(kc == 0), stop=(kc == KC - 1)),
                 reads=[cs, sq], writes=[ps_b])
        P.op("act", lambda e: e.activation(out=mean[:, 0:Wc], in_=ps_a[:, 0:Wc], func=AF.Identity), reads=[ps_a], writes=[mean])
        P.op("dve", lambda e: e.tensor_tensor(out=tmpm[:, 0:Wc], in0=mean[:, 0:Wc], in1=mean[:, 0:Wc], op=ALU.mult), reads=[mean], writes=[tmpm])
        P.op("dve", lambda e: e.tensor_tensor(out=tmpm[:, 0:Wc], in0=ps_b[:, 0:Wc], in1=tmpm[:, 0:Wc], op=ALU.subtract), reads=[ps_b, tmpm], writes=[tmpm])
        P.op("act", lambda e: e.activation(out=tmpm[:, 0:Wc], in_=tmpm[:, 0:Wc], func=AF.Sqrt, bias=EPS), reads=[tmpm], writes=[tmpm])
        P.op("dve", lambda e: e.reciprocal(out=rstd[:, 0:Wc], in_=tmpm[:, 0:Wc]), reads=[tmpm], writes=[rstd])
        for kc in range(KC):
            t = nft()
            P.op("dve", lambda e, t=t, kc=kc: e.tensor_tensor(out=t[:, 0:Wc], in0=xr[:, kc, 0:Wc], in1=mean[:, 0:Wc], op=ALU.subtract), reads=[xr, mean], writes=[t])
            P.op("dve", lambda e, t=t: e.tensor_tensor(out=t[:, 0:Wc], in0=t[:, 0:Wc], in1=rstd[:, 0:Wc], op=ALU.mult), reads=[t, rstd], writes=[t])
            if out_bf is None:
                P.op("act", lambda e, t=t, kc=kc: e.activation(out=xr[:, kc, 0:Wc], in_=t[:, 0:Wc], func=AF.Identity,
                                                               scale=vecs[:, gcol + kc:gcol + kc + 1], bias=vecs[:, bcol + kc:bcol + kc + 1]),
                     reads=[t, vecs], writes=[xr])
            else:
                P.op("act", lambda e, t=t, kc=kc: e.activation(out=out_bf[:, kc, 0:Wc], in_=t[:, 0:Wc], func=AF.Identity,
                                                               scale=scale_t[:, kc:kc + 1], bias=bias_t[:, kc:kc + 1]),
                     reads=[t, scale_t, bias_t], writes=[out_bf])

    sched = []

    def prologue_mix(c0, Wc):
        P.dma("sp", yn[:, :, 0:Wc], ynsaT.h.ap()[:, c0:c0 + Wc].rearrange("(kc p) n -> p kc n", p=128), writes=[yn])
        P.dma("sp", hm[:, :, 0:Wc], hmlT.h.ap()[:, c0:c0 + Wc].rearrange("(kc p) n -> p kc n", p=128), writes=[hm])
        for hh in range(4):
            P.op("dve", lambda e, hh=hh: e.tensor_tensor(out=hsq[:, :, 0:Wc], in0=hm[:, 2 * hh:2 * hh + 2, 0:Wc], in1=hm[:, 2 * hh:2 * hh + 2, 0:Wc], op=ALU.mult),
                 reads=[hm], writes=[hsq])
            for c in range(2):
                P.op("pe", lambda e, hh=hh, c=c: e.matmul(ps_a[:, 0:Wc], lhsT=csb[:], rhs=hm[:, 2 * hh + c, 0:Wc], start=(c == 0), stop=(c == 1)),
                     reads=[csb, hm], writes=[ps_a])
            for c in range(2):
                P.op("pe", lambda e, c=c: e.matmul(ps_b[:, 0:Wc], lhsT=csb[:], rhs=hsq[:, c, 0:Wc], start=(c == 0), stop=(c == 1)),
                     reads=[csb, hsq], writes=[ps_b])
            P.op("act", lambda e: e.activation(out=mean[:, 0:Wc], in_=ps_a[:, 0:Wc], func=AF.Identity), reads=[ps_a], writes=[mean])
            P.op("dve", lambda e: e.tensor_tensor(out=tmpm[:, 0:Wc], in0=mean[:, 0:Wc], in1=mean[:, 0:Wc], op=ALU.mult), reads=[mean], writes=[tmpm])
            P.op("dve", lambda e: e.tensor_tensor(out=tmpm[:, 0:Wc], in0=ps_b[:, 0:Wc], in1=tmpm[:, 0:Wc], op=ALU.subtract), reads=[ps_b, tmpm], writes=[tmpm])
            P.op("dve", lambda e: e.tensor_scalar_max(out=tmpm[:, 0:Wc], in0=tmpm[:, 0:Wc], scalar1=0.0), reads=[tmpm], writes=[tmpm])
            P.op("act", lambda e: e.activation(out=tmpm[:, 0:Wc], in_=tmpm[:, 0:Wc], func=AF.Sqrt, bias=EPS), reads=[tmpm], writes=[tmpm])
            P.op("dve", lambda e: e.reciprocal(out=rstd[:, 0:Wc], in_=tmpm[:, 0:Wc]), reads=[tmpm], writes=[rstd])
            for c in range(2):
                ch = 2 * hh + c
                m_ = mo[ch % 4]
                t = nft()
                sg = nft()
                P.dma("sp", m_[:, 0:Wc], pT2.h.ap()[ch * 128:(ch + 1) * 128, c0:c0 + Wc], writes=[m_])
                P.op("act", lambda e, sg=sg, m_=m_: e.activation(out=sg[:, 0:Wc], in_=m_[:, 0:Wc], func=AF.Sigmoid), reads=[m_], writes=[sg])
                P.op("dve", lambda e, t=t, ch=ch: e.tensor_tensor(out=t[:, 0:Wc], in0=hm[:, ch, 0:Wc], in1=mean[:, 0:Wc], op=ALU.subtract), reads=[hm, mean], writes=[t])
                P.op("dve", lambda e, t=t: e.tensor_tensor(out=t[:, 0:Wc], in0=t[:, 0:Wc], in1=rstd[:, 0:Wc], op=ALU.mult), reads=[t, rstd], writes=[t])
                P.op("dve", lambda e, t=t, sg=sg, ch=ch: e.scalar_tensor_tensor(out=yml[:, ch, 0:Wc], in0=t[:, 0:Wc], scalar=vecs[:, VEC_NG + ch:VEC_NG + ch + 1],
                                                                              in1=sg[:, 0:Wc], op0=ALU.mult, op1=ALU.mult),
                     reads=[t, sg, vecs], writes=[yml])


    mix_done = set()

    def add_tile(c0, Wc, halo, nxt=None):
        oc0 = c0 - 2

        def prologue():
            P.dma("sp", xr[:, :, 0:Wc], xT.h.ap()[:, c0:c0 + Wc].rearrange("(kc p) n -> p kc n", p=128), writes=[xr])
            if c0 not in mix_done:
                mix_done.add(c0)
                prologue_mix(c0, Wc)

        for sgi in range(4):
            def run(wbt, sgi=sgi):
                wv = wbt.h.ap().rearrange("p (w k n) -> p w k n", w=2, k=8)
                for o4 in range(4):
                    oc = sgi * 4 + o4
                    a1, a2 = nacc(), nacc()
                    for kc in range(8):
                        P.op("pe", lambda e, a1=a1, kc=kc, o4=o4: e.matmul(a1[:, 0:Wc], lhsT=wv[:, 0, kc, o4 * 128:(o4 + 1) * 128], rhs=yn[:, kc, 0:Wc],
                                                                          start=(kc == 0), stop=(kc == 7)), reads=[wbt, yn], writes=[a1])
                    for kc in range(8):
                        P.op("pe", lambda e, a2=a2, kc=kc, o4=o4: e.matmul(a2[:, 0:Wc], lhsT=wv[:, 1, kc, o4 * 128:(o4 + 1) * 128], rhs=yml[:, kc, 0:Wc],
                                                                          start=(kc == 0), stop=(kc == 7)), reads=[wbt, yml], writes=[a2])
                    g1t, g2t, s1, s2, t1_, t2_ = mo[0 + (oc % 2) * 2], mo[1 + (oc % 2) * 2], nft(), nft(), nft(), nft()
                    P.dma("sp", g1t[:, 0:Wc], pT2.h.ap()[(8 + oc) * 128:(9 + oc) * 128, c0:c0 + Wc], writes=[g1t])
                    P.dma("sp", g2t[:, 0:Wc], pT2.h.ap()[(24 + oc) * 128:(25 + oc) * 128, c0:c0 + Wc], writes=[g2t])
                    P.op("act", lambda e, s1=s1, g1t=g1t: e.activation(out=s1[:, 0:Wc], in_=g1t[:, 0:Wc], func=AF.Sigmoid), reads=[g1t], writes=[s1])
                    P.op("act", lambda e, s2=s2, g2t=g2t: e.activation(out=s2[:, 0:Wc], in_=g2t[:, 0:Wc], func=AF.Sigmoid), reads=[g2t], writes=[s2])
                    P.op("dve", lambda e, a1=a1, s1=s1, t1_=t1_: e.tensor_tensor(out=t1_[:, 0:Wc], in0=a1[:, 0:Wc], in1=s1[:, 0:Wc], op=ALU.mult), reads=[a1, s1], writes=[t1_])
                    P.op("dve", lambda e, a2=a2, s2=s2, t2_=t2_: e.tensor_tensor(out=t2_[:, 0:Wc], in0=a2[:, 0:Wc], in1=s2[:, 0:Wc], op=ALU.mult), reads=[a2, s2], writes=[t2_])
                    P.op("dve", lambda e, t1_=t1_, t2_=t2_, oc=oc: e.tensor_tensor(out=merged[:, oc, 0:Wc], in0=t1_[:, 0:Wc], in1=t2_[:, 0:Wc], op=ALU.add),
                         reads=[t1_, t2_], writes=[merged])
            sched.append(dict(loads=[(w_brn.h.ap()[:, sgi * 512:(sgi + 1) * 512].rearrange("(kc p) n -> p kc n", p=128), 0, 8, 512, None),
                                     (w_brm.h.ap()[:, sgi * 512:(sgi + 1) * 512].rearrange("(kc p) n -> p kc n", p=128), 4096, 8, 512, None)],
                              run=run, pre=(prologue if sgi == 0 else None)))
        for sgi in range(4):
            def pre_o():
                P.op("act", lambda e: e.activation(out=xr[:, :, 0:Wc], in_=xr[:, :, 0:Wc], func=AF.Identity, scale=ALPHA), reads=[xr], writes=[xr])

            def run(wbt, sgi=sgi):
                wv = wbt.h.ap().rearrange("p (k n) -> p k n", k=16)
                for o4 in range(4):
                    oc = sgi * 4 + o4
                    a = nacc()
                    for kc in range(KC):
                        P.op("pe", lambda e, a=a, kc=kc, o4=o4: e.matmul(a[:, 0:Wc], lhsT=wv[:, kc, o4 * 128:(o4 + 1) * 128], rhs=merged[:, kc, 0:Wc],
                                                                        start=(kc == 0), stop=(kc == KC - 1)), reads=[wbt, merged], writes=[a])
                    P.op("dve", lambda e, a=a, oc=oc: e.scalar_tensor_tensor(out=xr[:, oc, 0:Wc], in0=a[:, 0:Wc], scalar=modT[:, oc:oc + 1], in1=xr[:, oc, 0:Wc],
                                                                            op0=ALU.mult, op1=ALU.add), reads=[a, modT, xr], writes=[xr])
            sched.append(dict(loads=[(dview(w_o, sgi * 512, (sgi + 1) * 512), 0, 16, 512, None)], run=run, pre=(pre_o if sgi == 0 else None)))
        for fp in range(NFC // 2):
            def pre_up():
                layer_norm_inplace(Wc, VEC_LG0, VEC_LB0)
                layer_norm_inplace(Wc, 0, 0, out_bf=h2, scale_t=sc2p, bias_t=Sub(modF, (slice(None), slice(48, 64))))

            def run(wbt, fp=fp):
                wv = wbt.h.ap().rearrange("p (k n) -> p k n", k=16)
                for f2 in range(2):
                    fc = fp * 2 + f2
                    aa = nacc()
                    for kc in range(KC):
                        P.op("pe", lambda e, aa=aa, kc=kc, f2=f2: e.matmul(aa[:, 0:Wc], lhsT=wv[:, kc, f2 * 128:(f2 + 1) * 128], rhs=h2[:, kc, 0:Wc],
                                                                          start=(kc == 0), stop=(kc == KC - 1)),
                             reads=[wbt, h2], writes=[aa])
                    if halo:
                        P.op("dve", lambda e, aa=aa, fc=fc: e.tensor_scalar(out=carry[:, fc, :], in0=aa[:, 0:2], scalar1=vecs[:, VEC_FLAG:VEC_FLAG + 1], scalar2=None, op0=ALU.mult),
                             reads=[aa, vecs], writes=[carry])
                        continue
                    ag = nacc()
                    for kc in range(KC):
                        P.op("pe", lambda e, ag=ag, kc=kc, f2=f2: e.matmul(ag[:, 0:Wc], lhsT=wv[:, kc, 256 + f2 * 128:256 + (f2 + 1) * 128], rhs=h2[:, kc, 0:Wc],
                                                                          start=(kc == 0), stop=(kc == KC - 1)),
                             reads=[wbt, h2], writes=[ag])
                    ab = abuf[fc % 2]
                    cv, sa = nft(), nft()
                    P.op("act", lambda e, ab=ab, aa=aa: e.activation(out=ab[:, 2:2 + Wc], in_=aa[:, 0:Wc], func=AF.Identity), reads=[aa], writes=[ab])
                    P.op("act", lambda e, ab=ab, fc=fc: e.activation(out=ab[:, 0:2], in_=carry[:, fc, :], func=AF.Identity), reads=[carry], writes=[ab])
                    cwc = VEC_CW + fc * 3
                    P.op("dve", lambda e, ab=ab, cv=cv, cwc=cwc: e.tensor_scalar(out=cv[:, 0:Wc], in0=ab[:, 0:Wc], scalar1=vecs[:, cwc:cwc + 1], scalar2=None, op0=ALU.mult),
                         reads=[ab, vecs], writes=[cv])
                    for j in (1, 2):
                        P.op("dve", lambda e, ab=ab, cv=cv, cwc=cwc, j=j: e.scalar_tensor_tensor(out=cv[:, 0:Wc], in0=ab[:, j:j + Wc], scalar=vecs[:, cwc + j:cwc + j + 1],
                                                                                                in1=cv[:, 0:Wc], op0=ALU.mult, op1=ALU.add),
                             reads=[ab, vecs, cv], writes=[cv])
                    P.op("act", lambda e, ab=ab, fc=fc: e.activation(out=carry[:, fc, :], in_=ab[:, Wc:Wc + 2], func=AF.Identity), reads=[ab], writes=[carry])
                    P.op("act", lambda e, cv=cv, sa=sa, fc=fc: e.activation(out=sa[:, 0:Wc], in_=cv[:, 0:Wc], func=AF.Silu, bias=vecs[:, VEC_CB + fc:VEC_CB + fc + 1]),
                         reads=[cv, vecs], writes=[sa])
                    P.op("dve", lambda e, sa=sa, ag=ag, fc=fc: e.tensor_tensor(out=u[:, fc, 0:Wc], in0=ag[:, 0:Wc], in1=sa[:, 0:Wc], op=ALU.mult), reads=[ag, sa], writes=[u])
            loads = [(dview(w_up, fp * 256, (fp + 1) * 256), 0, 16, 512, (0, 256))]
            if not halo:
                loads.append((dview(w_up, D_FF + fp * 256, D_FF + (fp + 1) * 256), 0, 16, 512, (256, 512)))
            sched.append(dict(loads=loads, run=run, pre=(pre_up if fp == 0 else None)))
        if halo:
            return
        for og in range(4):
            for s4 in range(4):
                def pre_d():
                    P.op("act", lambda e: e.activation(out=xr[:, :, 0:Wc], in_=xr[:, :, 0:Wc], func=AF.Identity, scale=ALPHA), reads=[xr], writes=[xr])
                    if nxt is not None and nxt[0] not in mix_done:
                        mix_done.add(nxt[0])
                        prologue_mix(nxt[0], nxt[1])

                def run(wbt, og=og, s4=s4):
                    wv = wbt.h.ap()[:, 0:11 * 512].rearrange("p (k n) -> p k n", k=11)
                    for o4 in range(4):
                        oc = og * 4 + o4
                        a = accs[o4]
                        for k2 in range(11):
                            fc = s4 * 11 + k2
                            P.op("pe", lambda e, a=a, k2=k2, fc=fc, o4=o4: e.matmul(a[:, 0:Wc], lhsT=wv[:, k2, o4 * 128:(o4 + 1) * 128], rhs=u[:, fc, 0:Wc],
                                                                                    start=(fc == 0), stop=(fc == NFC - 1)),
                                 reads=[wbt, u], writes=[a])
                        if s4 == 3:
                            P.op("dve", lambda e, a=a, oc=oc: e.scalar_tensor_tensor(out=xr[:, oc, 0:Wc], in0=a[:, 0:Wc], scalar=modT[:, 48 + oc:49 + oc], in1=xr[:, oc, 0:Wc],
                                                                                    op0=ALU.mult, op1=ALU.add), reads=[a, modT, xr], writes=[xr])
                    if og == 3 and s4 == 3:
                        layer_norm_inplace(Wc, VEC_LG1, VEC_LB1)
                        P.dma("sp", xoT.h.ap()[:, oc0:oc0 + Wc].rearrange("(kc p) n -> p kc n", p=128), xr[:, :, 0:Wc], reads=[xr])
                r0 = s4 * 11 * 128
                sched.append(dict(loads=[(w_down.h.ap()[r0:r0 + 11 * 128, og * 512:(og + 1) * 512].rearrange("(k p) n -> p k n", p=128), 0, 11, 512, None)],
                                  run=run, pre=(pre_d if (og == 0 and s4 == 0) else None)))

    add_tile(0, 2, True)
    for tt in range(NT // W):
        nxt = (2 + (tt + 1) * W, W) if tt + 1 < NT // W else None
        add_tile(2 + tt * W, W, False, nxt)

    def issue_dma(i):
        st = sched[i]
        bt = wb[i % NST]
        f = bt.h.ap()
        for (ap, off, k, n, cols) in st["loads"]:
            dst = f[:, off:off + k * n].rearrange("p (k n) -> p k n", n=n)
            if cols is not None:
                dst = dst[:, :, cols[0]:cols[1]]
            P.dma("pool", dst, ap, writes=[bt])

    n = len(sched)
    for i in range(min(NST, n)):
        issue_dma(i)
    for i in range(n):
        if sched[i].get("pre"):
            sched[i]["pre"]()
        sched[i]["run"](wb[i % NST])
        if i + NST < n:
            issue_dma(i + NST)
    P.finish()
    P.emit()
    return nc


def build_M():
    nc = bass.Bass("TRN2", target_bir_lowering=False)
    P = Prog(nc)
    cT = P.dram("cT", [128, KC], F32, "ExternalInput")
    wsl = P.dram("wsl", [D, 3072], F32, "ExternalInput")
    bsl = P.dram("bsl", [1, 3072], F32, "ExternalInput")
    one_d = P.dram("one", [1, 1], F32, "ExternalInput")
    modp = P.dram("modp", [128, 24], F32, "ExternalOutput")
    stage = [P.sb(f"st{i}", [128, KC, 512], F32) for i in range(2)]
    one1 = P.sb("one1", [1, 1], F32)
    ps_row = P.ps("ps_row", [128, 512], F32)
    ps_t = P.ps("ps_t", [128, 512], F32)
    P.dma("sp", one1[:], one_d[:], writes=[one1])
    modT = mod_vectors(P, cT, wsl, bsl, 3072, stage, ps_row, ps_t, one1, cw=512)
    P.dma("sp", modp[:], modT[:], reads=[modT])
    P.finish()
    P.emit()
    return nc


def b1_inputs(projT, cmp_pe, cmp_w1, cmp_w2, consts, h):
    g, r = h // 4, h % 4
    heads = [g * 4 + r] + [g * 4 + o for o in range(4) if o != r]
    q4 = np.stack([projT[hd * 128:(hd + 1) * 128] for hd in heads], axis=0)
    def kvrow(br, kvi):
        b0 = 1024 + ((br * 2 + kvi) * 2 + g) * 128
        return projT[b0:b0 + 128]
    gT = projT[2560 + h * 3: 2560 + h * 3 + 3]
    gat = np.ascontiguousarray(gT.T.reshape(128, 128, 3).transpose(1, 0, 2))
    m = dict(q4=np.ascontiguousarray(q4), kcmpT=np.ascontiguousarray(kvrow(0, 0)), vcmpT=np.ascontiguousarray(kvrow(0, 1)),
             kslcT=np.ascontiguousarray(kvrow(1, 0)), vslc=np.ascontiguousarray(kvrow(1, 1).T),
             kwinT=np.ascontiguousarray(kvrow(2, 0)), vwin=np.ascontiguousarray(kvrow(2, 1).T), gat=gat,
             peT=np.ascontiguousarray(cmp_pe.transpose(2, 0, 1)),
             w1=np.ascontiguousarray(cmp_w1.reshape(2, 32, 128, 256).transpose(0, 2, 1, 3)),
             w2=np.ascontiguousarray(cmp_w2.reshape(2, 2, 128, 128).transpose(2, 0, 1, 3)))
    m.update(consts)
    return m

def b2_inputs(projT, ml_conv_w, ml_conv_b, ml_gate_b, consts, i):
    hh, half = i // 2, i % 2
    QK0 = 2584; V0 = QK0 + 1024; IF0 = V0 + 1024
    m = dict(qraw=np.ascontiguousarray(projT[QK0 + hh * 128: QK0 + (hh + 1) * 128]),
             kraw=np.ascontiguousarray(projT[QK0 + 512 + hh * 128: QK0 + 512 + (hh + 1) * 128]),
             vtok=np.ascontiguousarray(projT[V0 + hh * 256 + half * 128: V0 + hh * 256 + (half + 1) * 128].T),
             gif=np.ascontiguousarray(np.stack([projT[IF0 + hh].reshape(128, 128), projT[IF0 + 4 + hh].reshape(128, 128)], axis=1)))
    cq = ml_conv_w[:, hh * 128:(hh + 1) * 128]
    ck = ml_conv_w[:, 512 + hh * 128:512 + (hh + 1) * 128]
    m["convw"] = np.ascontiguousarray(np.stack([cq.T, ck.T], axis=1)).astype(np.float32)
    m["convb"] = np.ascontiguousarray(np.stack([ml_conv_b[hh * 128:(hh + 1) * 128], ml_conv_b[512 + hh * 128:512 + (hh + 1) * 128]], axis=1)).astype(np.float32)
    m["gateb"] = np.ascontiguousarray(np.broadcast_to(np.array([ml_gate_b[hh], ml_gate_b[4 + hh]], np.float32)[None, :], (128, 2)))
    m.update(consts)
    return m

def c_consts():
    return dict(cst=np.concatenate([np.full((128, 128), 1.0 / 2048, np.float32), np.full((128, 128), 1.0 / 256, np.float32),
                                    np.ones((128, 1), np.float32)], axis=1))

def pvec(v):
    return np.ascontiguousarray(np.asarray(v, np.float32).reshape(-1, 128).T)

def c_inputs(xT_full, ynsaT, hmlT, projT, prm, l, i, consts, modT):
    t0 = i * 2048
    def halo(a):
        if i == 0:
            return np.ascontiguousarray(np.concatenate([np.zeros((a.shape[0], 2), a.dtype), a[:, 0:2048]], axis=1))
        return np.ascontiguousarray(a[:, t0 - 2:t0 + 2048])
    vecs = np.concatenate([pvec(prm["ml_norm_g"][l]), pvec(prm["ln_g"][l, 0]), pvec(prm["ln_b"][l, 0]), pvec(prm["ln_g"][l, 1]), pvec(prm["ln_b"][l, 1]),
                           np.ascontiguousarray(np.asarray(prm["ffn_conv_w"][l], np.float32).reshape(3, 44, 128).transpose(2, 1, 0)).reshape(128, 132),
                           pvec(prm["ffn_conv_b"][l]), np.full((128, 1), 0.0 if i == 0 else 1.0, np.float32)], axis=1)
    m = dict(xT=halo(xT_full), ynsaT=halo(ynsaT), hmlT=halo(hmlT), pT2=halo(projT[4640:9760]),
             w_brn=prm["w_br_nsa"][l], w_brm=prm["w_br_ml"][l], w_o=prm["w_o"][l], w_up=prm["w_up"][l], w_down=prm["w_down"][l],
             modT=modT, vecs=np.ascontiguousarray(vecs.astype(np.float32)))
    m.update(consts)
    return m


_PROGS = {}


def _prog(name):
    if name not in _PROGS:
        _PROGS[name] = {"M": build_M, "A": build_A, "B1": build_B1, "B2": build_B2, "C": build_C}[name]()
    return _PROGS[name]


def _run(name, in_maps):
    res = run_bass_kernel_spmd(_prog(name), in_maps, core_ids=list(range(NCORE)))
    return res.results


def kernel(**inp):
    prm = {k: np.asarray(v) for k, v in inp.items()}
    x = prm["x"][0]
    cT = np.ascontiguousarray(prm["c"][0].reshape(16, 128).T.astype(np.float32))
    one = np.ones((1, 1), np.float32)
    in_maps = []
    for i in range(NCORE):
        sl = slice(i * 1536, (i + 1) * 1536)
        in_maps.append(dict(cT=cT, wsl=np.ascontiguousarray(np.concatenate([prm["w_ada"][0][:, sl], prm["w_ada"][1][:, sl]], axis=1)),
                            bsl=np.ascontiguousarray(np.concatenate([prm["b_ada"][0][sl], prm["b_ada"][1][sl]])[None, :]), one=one))
    r = _run("M", in_maps)
    modT = [np.ascontiguousarray(np.concatenate([np.asarray(r[i]["modp"])[:, l * 12:(l + 1) * 12] for i in range(NCORE)], axis=1)) for l in range(2)]

    cstA = np.concatenate([np.full((128, 128), 1.0 / 2048, np.float32), np.ones((128, 1), np.float32)], axis=1)
    c1, c2, cc = b1_consts(), b2_consts(), c_consts()
    xT = np.ascontiguousarray(x.T)
    for l in range(2):
        r = _run("A", [dict(xT=np.ascontiguousarray(xT[:, i * NT:(i + 1) * NT]), modT=modT[l], w_in=prm["w_in"][l], cst=cstA) for i in range(NCORE)])
        projT = np.concatenate([np.asarray(r[i]["projT"]) for i in range(NCORE)], axis=1)
        r = _run("B1", [b1_inputs(projT, prm["cmp_pe"][l], prm["cmp_w1"][l], prm["cmp_w2"][l], c1, h) for h in range(NCORE)])
        ynsaT = np.concatenate([np.asarray(r[h]["yT"]) for h in range(NCORE)], axis=0)
        r = _run("B2", [b2_inputs(projT, prm["ml_conv_w"][l], prm["ml_conv_b"][l], prm["ml_gate_b"][l], c2, i) for i in range(NCORE)])
        hmlT = np.concatenate([np.asarray(r[i]["hT"]) for i in range(NCORE)], axis=0)
        r = _run("C", [c_inputs(xT, ynsaT, hmlT, projT, prm, l, i, cc, modT[l]) for i in range(NCORE)])
        xT = np.concatenate([np.asarray(r[i]["xoT"]) for i in range(NCORE)], axis=1)
    return np.ascontiguousarray(xT.T)[None].astype(np.float32)
```

```python
import ml_dtypes
from concourse.bass_utils import run_bass_kernel_spmd
import numpy as np
from contextlib import ExitStack
import concourse.bass as bass
import concourse.mybir as mybir

F32 = mybir.dt.float32
BF16 = mybir.dt.bfloat16
I32 = mybir.dt.int32
ALU = mybir.AluOpType
AF = mybir.ActivationFunctionType
AX = mybir.AxisListType

ENGS = ("pe", "act", "dve", "pool", "sp")


class Tile:
    def __init__(self, name, handle, space):
        self.name = name
        self.h = handle
        self.space = space
        self.last_w = None
        self.readers = {}
        self.dsem = None
        self.ssem = None

    def __getitem__(self, idx):
        return self.h[idx]


class Sub:
    def __init__(self, parent, idx):
        self.parent = parent
        self.idx = idx

    def __getitem__(self, i):
        return self.parent.h[self.idx][i]


def _par(t):
    return t.parent if isinstance(t, Sub) else t


class Prog:
    def __init__(self, nc):
        self.nc = nc
        self.es = ExitStack()
        self.prog = {e: [] for e in ENGS}
        self.sem = {}
        self.cnt = {e: 0 for e in ENGS}
        self.waited = {e: {} for e in ENGS}
        self.nsem = 0
        for e in ENGS:
            self.sem[e] = self._newsem("e_" + e)
        self.n_inst = 0
        self.inherit = {}
        self.scope_tiles = None
        self.scope_stack = []
        self.sem_pool = []
        self.all_dma_sems = []
        self.final_eng = []
        self.prefix = ""

    def _newsem(self, name):
        self.nsem += 1
        return self.nc.alloc_semaphore(name=name + "_%d" % self.nsem)

    def _track(self, t):
        t.readers = dict(self.inherit)
        if self.scope_tiles is not None:
            self.scope_tiles.append(t)
        return t

    def sb(self, name, shape, dtype):
        h = self.es.enter_context(self.nc.sbuf_tensor("s_" + self.prefix + name, list(shape), dtype))
        return self._track(Tile(name, h, "sb"))

    def ps(self, name, shape, dtype):
        h = self.es.enter_context(self.nc.psum_tensor("p_" + self.prefix + name, list(shape), dtype))
        return self._track(Tile(name, h, "ps"))

    def dram(self, name, shape, dtype, kind):
        h = self.nc.dram_tensor(self.prefix + name, list(shape), dtype, kind=kind)
        return Tile(name, h, "dram")

    def scope_begin(self):
        self.scope_stack.append((self.es, self.scope_tiles))
        self.es = ExitStack()
        self.scope_tiles = []

    def scope_end(self):
        for t in self.scope_tiles:
            toks = list(t.readers.values()) + ([t.last_w] if t.last_w is not None else [])
            for tok in toks:
                cur = self.inherit.get(tok[0])
                if cur is None or cur[2] < tok[2]:
                    self.inherit[tok[0]] = tok
            for rec in (t.dsem, t.ssem):
                if rec is not None:
                    self.sem_pool.append(rec)
            t.dsem = None
            t.ssem = None
        self.es.close()
        self.es, self.scope_tiles = self.scope_stack.pop()

    def new_epoch(self):
        for e in ENGS:
            if self.cnt[e] > 0:
                self.final_eng.append((e, self.sem[e], self.cnt[e]))
            self.sem[e] = self._newsem("e_" + e)
            self.cnt[e] = 0

    def _dma_sem(self, name):
        if self.sem_pool:
            return self.sem_pool.pop()
        rec = [self._newsem(name), 0]
        self.all_dma_sems.append(rec)
        return rec

    def _deps(self, eng, reads, writes, is_dma=False, dma_tile=None):
        deps = []
        for t in reads:
            if t.last_w is not None:
                deps.append((t.last_w, "raw"))
        for t in writes:
            if t.last_w is not None:
                deps.append((t.last_w, "waw"))
            for tok in t.readers.values():
                deps.append((tok, "war"))
        waits = []
        for (semkey, semh, val, src), kind in deps:
            if src == eng and not is_dma:
                if eng == "pe":
                    continue
                if kind == "war":
                    continue
            if (is_dma and kind == "waw" and src == "dma" and dma_tile is not None and dma_tile.dsem is not None
                    and semkey == id(dma_tile.dsem[0])):
                continue
            if self.waited[eng].get(semkey, 0) >= val:
                continue
            self.waited[eng][semkey] = val
            waits.append((semh, val))
        return waits

    def op(self, eng, fn, reads=(), writes=()):
        reads = [_par(t) for t in reads]
        writes = [_par(t) for t in writes]
        waits = self._deps(eng, reads, writes)
        self.cnt[eng] += 1
        tok = (id(self.sem[eng]), self.sem[eng], self.cnt[eng], eng)
        self.prog[eng].append((waits, fn, self.sem[eng], 1))
        for t in reads:
            t.readers[tok[0]] = tok
        for t in writes:
            t.last_w = tok
            t.readers = {}
        self.n_inst += 1

    def dma(self, q, out_ap, in_ap, reads=(), writes=(), **kw):
        def fn(e, out_ap=out_ap, in_ap=in_ap, kw=kw):
            return e.dma_start(out=out_ap, in_=in_ap, **kw)
        self.dma_fn(q, fn, reads, writes)

    def dma_fn(self, q, fn, reads=(), writes=(), inc=16):
        reads = [_par(t) for t in reads]
        writes = [_par(t) for t in writes]
        if writes:
            t = writes[0]
            if t.dsem is None:
                t.dsem = self._dma_sem("d_" + t.name)
            rec = t.dsem
            waits = self._deps(q, reads, writes, is_dma=True, dma_tile=t)
        else:
            t = reads[0]
            if t.ssem is None:
                t.ssem = self._dma_sem("s_" + t.name)
            rec = t.ssem
            waits = self._deps(q, reads, writes, is_dma=True)
        rec[1] += inc
        tok = (id(rec[0]), rec[0], rec[1], "dma")
        self.prog[q].append((waits, fn, rec[0], inc))
        for r in reads:
            r.readers[tok[0]] = tok
        for w in writes:
            w.last_w = tok
            w.readers = {}
        self.n_inst += 1

    def coll(self, q, fn, reads=(), writes=()):
        self.dma_fn(q, fn, reads, writes, inc=16)

    def finish(self):
        waits = []
        for rec in self.all_dma_sems:
            if rec[1] > 0:
                waits.append((rec[0], rec[1]))
        for e in ENGS:
            if e != "sp" and self.cnt[e] > 0:
                waits.append((self.sem[e], self.cnt[e]))
        for (e, semh, c) in self.final_eng:
            if e != "sp":
                waits.append((semh, c))
        self.prog["sp"].append((waits, None, None, 0))

    def emit(self):
        nc = self.nc
        prog = self.prog

        def replay(lst, e):
            for waits, fn, sem, inc in lst:
                for semh, val in waits:
                    e.wait_ge(semh, val)
                if fn is not None:
                    ins = fn(e)
                    ins.then_inc(sem, inc)

        with nc.Block() as block:
            @block.tensor
            def _(e):
                replay(prog["pe"], e)

            @block.scalar
            def _(e):
                replay(prog["act"], e)

            @block.vector
            def _(e):
                replay(prog["dve"], e)

            @block.gpsimd
            def _(e):
                replay(prog["pool"], e)

            @block.sync
            def _(e):
                replay(prog["sp"], e)
        self.es.close()


D = 2048
S = 16384
NCORE = 8
NT = S // NCORE
KC = D // 128
IN_COLS = 9760
D_FF = 5632
EPS = 1e-5
ALPHA = 4.0 ** 0.25


def dview(t, c0, c1):
    return t.h.ap()[:, c0:c1].rearrange("(kc p) n -> p kc n", p=128)


def mod_vectors(P, cT, wada, bada, ncols, stage, ps_row, ps_t, one1, cw=512):
    nch = ncols // cw
    sub = cw // 128
    cact = P.sb("cact", [128, KC], F32)
    brow = [P.sb(f"brow{i}", [1, cw], F32) for i in range(2)]
    mrow = [P.sb(f"mrow{i}", [1, cw], F32) for i in range(2)]
    modT = P.sb("modT", [128, ncols // 128], F32)
    P.dma("sp", cact[:], cT[:], writes=[cact])
    P.op("act", lambda e: e.activation(out=cact[:], in_=cact[:], func=AF.Silu), reads=[cact], writes=[cact])
    for j in range(nch):
        w = stage[j % 2]
        br = brow[j % 2]
        mr = mrow[j % 2]
        P.dma("sp", w[:], dview(wada, j * cw, (j + 1) * cw), writes=[w])
        P.dma("sp", br[:], bada.h.ap()[:, j * cw:(j + 1) * cw], writes=[br])
        for kc in range(KC):
            P.op("pe", lambda e, w=w, kc=kc: e.matmul(ps_row[0:1, 0:cw], lhsT=cact[:, kc:kc + 1], rhs=w[:, kc, :],
                                                      start=(kc == 0), stop=(kc == KC - 1)),
                 reads=[cact, w], writes=[ps_row])
        P.op("dve", lambda e, mr=mr, br=br: e.tensor_tensor(out=mr[0:1, :], in0=ps_row[0:1, 0:cw], in1=br[0:1, :], op=ALU.add),
             reads=[ps_row, br], writes=[mr])
        for c in range(sub):
            P.op("pe", lambda e, c=c, j=j, mr=mr: e.matmul(ps_t[:, sub * j + c:sub * j + c + 1], lhsT=mr[0:1, c * 128:(c + 1) * 128],
                                                          rhs=one1[0:1, 0:1], start=True, stop=True),
                 reads=[mr, one1], writes=[ps_t])
    P.op("dve", lambda e: e.tensor_copy(out=modT[:], in_=ps_t[:, 0:ncols // 128]), reads=[ps_t], writes=[modT])
    return modT


def ln_stats(P, z, sq, nkc, W, onesN, ps_a, ps_b, mean, rstd, tmpm):
    P.op("act", lambda e: e.activation(out=sq[:, 0:nkc, 0:W], in_=z[:, 0:nkc, 0:W], func=AF.Square), reads=[z], writes=[sq])
    for kc in range(nkc):
        P.op("pe", lambda e, kc=kc: e.matmul(ps_a[:, 0:W], lhsT=onesN[:], rhs=z[:, kc, 0:W], start=(kc == 0), stop=(kc == nkc - 1)),
             reads=[onesN, z], writes=[ps_a])
    for kc in range(nkc):
        P.op("pe", lambda e, kc=kc: e.matmul(ps_b[:, 0:W], lhsT=onesN[:], rhs=sq[:, kc, 0:W], start=(kc == 0), stop=(kc == nkc - 1)),
             reads=[onesN, sq], writes=[ps_b])
    P.op("act", lambda e: e.activation(out=mean[:, 0:W], in_=ps_a[:, 0:W], func=AF.Identity), reads=[ps_a], writes=[mean])
    P.op("dve", lambda e: e.tensor_tensor(out=tmpm[:, 0:W], in0=mean[:, 0:W], in1=mean[:, 0:W], op=ALU.mult), reads=[mean], writes=[tmpm])
    P.op("dve", lambda e: e.tensor_tensor(out=tmpm[:, 0:W], in0=ps_b[:, 0:W], in1=tmpm[:, 0:W], op=ALU.subtract), reads=[ps_b, tmpm], writes=[tmpm])
    P.op("act", lambda e: e.activation(out=tmpm[:, 0:W], in_=tmpm[:, 0:W], func=AF.Sqrt, bias=EPS), reads=[tmpm], writes=[tmpm])
    P.op("dve", lambda e: e.reciprocal(out=rstd[:, 0:W], in_=tmpm[:, 0:W]), reads=[tmpm], writes=[rstd])


def body_A(P, x_src=None):
    xT = P.dram("xT", [D, NT], F32, "ExternalInput") if x_src is None else x_src
    modT_d = P.dram("modT", [128, 96], F32, "ExternalInput")
    w_in = P.dram("w_in", [D, IN_COLS], F32, "ExternalInput")
    cst = P.dram("cst", [128, 129], F32, "ExternalInput")
    projT = P.dram("projT", [IN_COLS, NT], BF16, "ExternalOutput")

    W = 512
    wf = [P.sb(f"wf{i}", [128, KC, W], F32) for i in range(2)]
    NST = 3
    wb = [P.sb(f"wb{i}", [128, KC, W], BF16) for i in range(NST)]
    hT = P.sb("hT", [128, KC, NT], BF16)
    ot = [P.sb(f"ot{i}", [128, NT], BF16) for i in range(2)]
    cs = P.sb("cs", [128, 129], F32)
    mean = P.sb("mean", [128, W], F32)
    rstd = P.sb("rstd", [128, W], F32)
    tmpm = P.sb("tmpm", [128, W], F32)
    t1 = [P.sb(f"t1_{i}", [128, W], F32) for i in range(2)]
    sc1p = P.sb("sc1p", [128, KC], F32)
    ps_a = P.ps("ps_a", [128, W], F32)
    ps_b = P.ps("ps_b", [128, W], F32)
    ps_row = P.ps("ps_row", [128, W], F32)
    ps_t = P.ps("ps_t", [128, W], F32)
    accs = [P.ps(f"acc{i}", [128, W], F32) for i in range(4)]

    P.dma("sp", cs[:], cst[:], writes=[cs])
    onesN = cs

    modT = P.sb("modT", [128, 96], F32)
    P.dma("sp", modT[:], modT_d[:], writes=[modT])
    P.op("dve", lambda e: e.tensor_scalar_add(out=sc1p[:], in0=modT[:, 16:32], scalar1=1.0), reads=[modT], writes=[sc1p])

    ncg = (IN_COLS + W - 1) // W

    def load_w(cg):
        c0 = cg * W
        cw = min(W, IN_COLS - c0)
        P.dma("pool", wb[cg % NST][:, :, 0:cw], dview(w_in, c0, c0 + cw), writes=[wb[cg % NST]])

    for cg in range(min(NST, ncg)):
        load_w(cg)

    z, sq = wf[0], wf[1]
    for tt in range(NT // W):
        P.dma("sp", z[:], xT.h.ap()[:, tt * W:(tt + 1) * W].rearrange("(kc p) n -> p kc n", p=128), reads=([xT] if x_src is not None else []), writes=[z])
        ln_stats(P, z, sq, KC, W, Sub(cs, (slice(None), slice(0, 128))), ps_a, ps_b, mean, rstd, tmpm)
        for kc in range(KC):
            t = t1[kc % 2]
            P.op("pool", lambda e, t=t, kc=kc: e.tensor_tensor(out=t[:], in0=z[:, kc, :], in1=mean[:], op=ALU.subtract),
                 reads=[z, mean], writes=[t])
            P.op("dve", lambda e, t=t: e.tensor_tensor(out=t[:], in0=t[:], in1=rstd[:], op=ALU.mult), reads=[t, rstd], writes=[t])
            P.op("act", lambda e, t=t, kc=kc, tt=tt: e.activation(out=hT[:, kc, tt * W:(tt + 1) * W], in_=t[:], func=AF.Identity,
                                                                  scale=sc1p[:, kc:kc + 1], bias=modT[:, kc:kc + 1]),
                 reads=[t, sc1p, modT], writes=[hT])

    gi = 0
    oi = 0
    for cg in range(ncg):
        c0 = cg * W
        cw = min(W, IN_COLS - c0)
        bt = wb[cg % NST]
        for sub in range((cw + 127) // 128):
            m = min(128, cw - sub * 128)
            o = ot[oi % 2]
            oi += 1
            for tt in range(NT // W):
                acc = accs[gi % 4]
                for kc in range(KC):
                    P.op("pe", lambda e, acc=acc, bt=bt, kc=kc, sub=sub, m=m, tt=tt:
                         e.matmul(acc[0:m, :], lhsT=bt[:, kc, sub * 128:sub * 128 + m], rhs=hT[:, kc, tt * W:(tt + 1) * W],
                                  start=(kc == 0), stop=(kc == KC - 1)),
                         reads=[bt, hT], writes=[acc])
                if gi % 2 == 0:
                    P.op("act", lambda e, acc=acc, o=o, m=m, tt=tt: e.activation(out=o[0:m, tt * W:(tt + 1) * W], in_=acc[0:m, :], func=AF.Identity),
                         reads=[acc], writes=[o])
                else:
                    P.op("dve", lambda e, acc=acc, o=o, m=m, tt=tt: e.tensor_copy(out=o[0:m, tt * W:(tt + 1) * W], in_=acc[0:m, :]),
                         reads=[acc], writes=[o])
                gi += 1
            r0 = c0 + sub * 128
            P.dma("sp", projT.h.ap()[r0:r0 + m, :], o[0:m, :], reads=[o])
        if cg + NST < ncg:
            load_w(cg + NST)


NEG = -30000.0
QW = 512
NQT = S // QW
SCALE = 128.0 ** -0.5
CMP_OFFS = [31, 31 - 512, 31 - 1024, 31 - 1536, 31 - 2048]


def b1_consts():
    bf = ml_dtypes.bfloat16
    p = np.arange(128)[:, None]
    f = np.arange(512)[None, :]
    c = {}
    c["ident"] = np.eye(128, dtype=np.float32).astype(bf)
    c["cmpmask"] = np.stack([np.where(f >= 16 * p + off, 0.0, NEG) for off in CMP_OFFS], axis=1).astype(bf)
    c["causal"] = np.stack([np.where(128 * i + p <= f, 0.0, NEG) for i in range(4)], axis=1).astype(bf)
    wm = []
    for i in range(8):
        dl = 128 * (i - 4)
        wm.append(np.where((f >= p + dl) & (f < p + dl + 512), 0.0, NEG))
    c["winmask"] = np.stack(wm, axis=1).astype(bf)
    bs = np.zeros((128, 64, 128), np.float32)
    for v in range(64):
        bs[2 * v, v, 0:64] = 1.0
        bs[2 * v + 1, v, 64:128] = 1.0
    c["bsel"] = bs.astype(bf)
    cc = (np.arange(8)[None, :, None] * 128 + np.arange(128)[:, None, None])
    s = np.arange(256)[None, None, :]
    ov = ((16 * cc < 64 * s + 64) & (16 * cc + 32 > 64 * s)).astype(np.float32)
    c["ov1"] = np.concatenate([ov, np.ones((128, 8, 1), np.float32)], axis=2).astype(bf)
    rel = np.arange(512)[None, :] - 256
    cur = (np.arange(128)[:, None] >= 64).astype(np.int64)
    c["cmv"] = (rel < cur - 1).astype(np.float32)
    c["cma"] = np.where((rel == cur) | (rel == cur - 1), 1e6, np.where(rel > cur, -1.0, 0.0)).astype(np.float32)
    return c


def body_B1(P):
    DI = lambda n, sh, dt: P.dram(n, sh, dt, "ExternalInput")
    q4 = DI("q4", [4, 128, S], BF16)
    kcmpT = DI("kcmpT", [128, S], BF16)
    vcmpT = DI("vcmpT", [128, S], BF16)
    kslcT = DI("kslcT", [128, S], BF16)
    vslc = DI("vslc", [S, 128], BF16)
    kwinT = DI("kwinT", [128, S], BF16)
    vwin = DI("vwin", [S, 128], BF16)
    gat = DI("gat", [128, 128, 3], BF16)
    peT = DI("peT", [128, 2, 32], F32)
    w1 = DI("w1", [2, 128, 32, 256], F32)
    w2 = DI("w2", [128, 2, 2, 128], F32)
    ident_d = DI("ident", [128, 128], BF16)
    cmpmask_d = DI("cmpmask", [128, 5, 512], BF16)
    causal_d = DI("causal", [128, 4, 512], BF16)
    winmask_d = DI("winmask", [128, 8, 512], BF16)
    bsel_d = DI("bsel", [128, 64, 128], BF16)
    ov1_d = DI("ov1", [128, 8, 257], BF16)
    cmv_d = DI("cmv", [128, 512], F32)
    cma_d = DI("cma", [128, 512], F32)
    yT = P.dram("yT", [128, S], BF16, "ExternalOutput")

    ident = P.sb("ident", [128, 128], BF16)
    cmpmask = P.sb("cmpmask", [128, 5, 512], BF16)
    causal = P.sb("causal", [128, 4, 512], BF16)
    winmask = P.sb("winmask", [128, 8, 512], BF16)
    bsel = P.sb("bsel", [128, 64, 128], BF16)
    cmv = P.sb("cmv", [128, 512], F32)
    cma = P.sb("cma", [128, 512], F32)
    gates = P.sb("gates", [128, 128, 3], F32)
    ksT = P.sb("ksT", [128, S], BF16)
    vsa = P.sb("vsa", [128, 128, 129], BF16)
    kcT = P.sb("kcT", [128, 1024], BF16)
    vca = P.sb("vca", [128, 8, 385], BF16)
    for t, d in ((ident, ident_d), (cmpmask, cmpmask_d), (causal, causal_d), (winmask, winmask_d), (bsel, bsel_d),
                 (cmv, cmv_d), (cma, cma_d)):
        P.dma("sp", t[:], d[:], writes=[t])
    gtmp = P.sb("gtmp", [128, 128, 3], BF16)
    P.dma("sp", gtmp[:], gat[:], writes=[gtmp])
    P.op("act", lambda e: e.activation(out=gates[:], in_=gtmp[:], func=AF.Sigmoid), reads=[gtmp], writes=[gates])
    P.dma("sp", ksT[:], kslcT[:], writes=[ksT])
    P.dma("sp", vsa[:, :, 0:128], vslc.h.ap().rearrange("(j p) d -> p j d", p=128), writes=[vsa])
    P.op("pool", lambda e: e.memset(vsa[:, :, 128:129], 1.0), reads=[], writes=[vsa])
    P.dma("sp", vca[:, :, 0:257], ov1_d[:], writes=[vca])

    S_ps = [P.ps(f"S{i}", [128, 512], F32) for i in range(2)]
    acc = [P.ps(f"acc{i}", [128, 512], F32) for i in range(4)]
    tps = P.ps("tps", [128, 4, 128], BF16)
    mps = P.ps("mps", [128, 512], F32)

    P.scope_begin()
    xc = P.sb("xc", [128, S], BF16)
    w1f = P.sb("w1f", [128, 32, 256], F32)
    w1b = P.sb("w1b", [128, 32, 256], BF16)
    w2f = P.sb("w2f", [128, 2, 2, 128], F32)
    w2b = P.sb("w2b", [128, 2, 2, 128], BF16)
    pef = P.sb("pef", [128, 2, 32], F32)
    peb = P.sb("peb", [128, 2, 32], BF16)
    gel = [P.sb(f"gel{i}", [128, 1024], BF16) for i in range(2)]
    hb = P.sb("hb", [128, 1], F32)
    xh = P.sb("xh", [128, 512], F32)
    xu = P.sb("xu", [128, 512], F32)
    P.dma("sp", w2f[:], w2[:], writes=[w2f])
    P.dma("sp", pef[:], peT[:], writes=[pef])
    P.op("dve", lambda e: e.tensor_copy(out=w2b[:], in_=w2f[:]), reads=[w2f], writes=[w2b])
    P.op("dve", lambda e: e.tensor_copy(out=peb[:], in_=pef[:]), reads=[pef], writes=[peb])
    for kv in range(2):
        P.dma("sp", xc[:], (kcmpT if kv == 0 else vcmpT)[:], writes=[xc])
        P.dma("sp", w1f[:], w1.h.ap()[kv], writes=[w1f])
        P.op("dve", lambda e: e.tensor_copy(out=w1b[:], in_=w1f[:]), reads=[w1f], writes=[w1b])
        xv = xc.h.ap().rearrange("p (b s) -> p b s", s=16)
        for half in range(2):
            g_ = gel[half]
            P.op("pool", lambda e, g_=g_: e.memset(g_[:], 0.0), reads=[], writes=[g_])
            for j in range(32):
                P.op("pe", lambda e, j=j, half=half, kv=kv: e.matmul(mps[:, 0:1], lhsT=w1b[:, j, half * 128:(half + 1) * 128],
                                                                     rhs=peb[:, kv, j:j + 1], start=(j == 0), stop=(j == 31)),
                     reads=[w1b, peb], writes=[mps])
            P.op("dve", lambda e: e.tensor_copy(out=hb[:], in_=mps[:, 0:1]), reads=[mps], writes=[hb])
            for nci, (n0, cnt) in enumerate(((0, 512), (512, 511))):
                sp_ = S_ps[nci]
                for j in range(32):
                    b0 = n0 + j // 16
                    P.op("pe", lambda e, j=j, half=half, b0=b0, cnt=cnt, sp_=sp_, xv=xv:
                         e.matmul(sp_[:, 0:cnt], lhsT=w1b[:, j, half * 128:(half + 1) * 128], rhs=xv[:, b0:b0 + cnt, j % 16],
                                  start=(j == 0), stop=(j == 31)),
                         reads=[w1b, xc], writes=[sp_])
                P.op("act", lambda e, sp_=sp_, cnt=cnt: e.activation(out=xh[:, 0:cnt], in_=sp_[:, 0:cnt], func=AF.Identity, bias=hb[:, 0:1]),
                     reads=[sp_, hb], writes=[xh])
                P.op("dve", lambda e, cnt=cnt: e.tensor_tensor(out=xu[:, 0:cnt], in0=xh[:, 0:cnt], in1=xh[:, 0:cnt], op=ALU.mult), reads=[xh], writes=[xu])
                P.op("dve", lambda e, cnt=cnt: e.tensor_scalar(out=xu[:, 0:cnt], in0=xu[:, 0:cnt], scalar1=0.044715, scalar2=1.0,
                                                               op0=ALU.mult, op1=ALU.add), reads=[xu], writes=[xu])
                P.op("dve", lambda e, cnt=cnt: e.tensor_tensor(out=xu[:, 0:cnt], in0=xu[:, 0:cnt], in1=xh[:, 0:cnt], op=ALU.mult), reads=[xu, xh], writes=[xu])
                P.op("act", lambda e, cnt=cnt: e.activation(out=xu[:, 0:cnt], in_=xu[:, 0:cnt], func=AF.Sigmoid, scale=1.5957691216),
                     reads=[xu], writes=[xu])
                P.op("dve", lambda e, cnt=cnt, n0=n0, g_=g_: e.tensor_tensor(out=g_[:, n0:n0 + cnt], in0=xu[:, 0:cnt], in1=xh[:, 0:cnt], op=ALU.mult),
                     reads=[xu, xh], writes=[g_])
        if kv == 0:
            for nci in range(2):
                for half in range(2):
                    P.op("pe", lambda e, nci=nci, half=half: e.matmul(mps[:, :], lhsT=w2b[:, 0, half, :], rhs=gel[half][:, nci * 512:(nci + 1) * 512],
                                                                      start=(half == 0), stop=(half == 1)),
                         reads=[w2b, gel[half]], writes=[mps])
                P.op("dve", lambda e, nci=nci: e.tensor_copy(out=kcT[:, nci * 512:(nci + 1) * 512], in_=mps[:, :]), reads=[mps], writes=[kcT])
        else:
            for m in range(8):
                for half in range(2):
                    P.op("pe", lambda e, m=m, half=half: e.matmul(mps[:, 0:128], lhsT=gel[half][:, m * 128:(m + 1) * 128], rhs=w2b[:, 1, half, :],
                                                                  start=(half == 0), stop=(half == 1)),
                         reads=[w2b, gel[half]], writes=[mps])
                P.op("dve", lambda e, m=m: e.tensor_copy(out=vca[:, m, 257:385], in_=mps[:, 0:128]), reads=[mps], writes=[vca])

    P.scope_end()
    qt = [P.sb(f"qt{i}", [128, 4, QW], BF16) for i in range(2)]
    kwT = [P.sb(f"kwT{i}", [128, 1024], BF16) for i in range(2)]
    vwa = [P.sb(f"vwa{i}", [128, 8, 129], BF16) for i in range(2)]
    ET = [P.sb(f"ET{i}", [128, QW], BF16) for i in range(3)]
    imp = P.sb("imp", [128, 4, 256], F32)
    ocomb = P.sb("ocomb", [128, 4, 128], F32)
    ocb = P.sb("ocb", [128, 4, 128], BF16)
    rden = P.sb("rden", [128, 1], F32)
    gsc = P.sb("gsc", [128, 1], F32)
    score = P.sb("score", [128, 256], F32)
    sc2 = P.sb("sc2", [128, 256], F32)
    m8 = P.sb("m8", [128, 8], F32)
    negsel = P.sb("negsel", [128, 256], BF16)
    nsT = P.sb("nsT", [128, 2, QW], BF16)
    yo = [P.sb(f"yo{i}", [128, QW], BF16) for i in range(2)]
    for i in range(2):
        P.op("pool", lambda e, i=i: e.memset(vwa[i][:, :, 128:129], 1.0), reads=[], writes=[vwa[i]])
    cnt_s = [0]
    cnt_e = [0]

    def attend(qap, qtile, chunks, NV):
        n = len(chunks)
        sps = [None] * n

        def emit_qk(ci):
            kt_ap, kt_tile, v_ap, v_tile, masks = chunks[ci]
            sp_ = S_ps[cnt_s[0] % 2]
            cnt_s[0] += 1
            sps[ci] = sp_
            nm = len(masks)
            P.op("pe", lambda e, sp_=sp_, kt_ap=kt_ap, nm=nm: e.matmul(sp_[:, :], lhsT=kt_ap, rhs=qap, start=True, stop=(nm == 0)),
                 reads=[kt_tile, qtile], writes=[sp_])
            for mi, (ml, mr, mt) in enumerate(masks):
                P.op("pe", lambda e, sp_=sp_, ml=ml, mr=mr, mi=mi, nm=nm: e.matmul(sp_[:, :], lhsT=ml, rhs=mr, start=False, stop=(mi == nm - 1)),
                     reads=list(mt), writes=[sp_])

        emit_qk(0)
        for ci in range(n):
            if ci + 1 < n:
                emit_qk(ci + 1)
            kt_ap, kt_tile, v_ap, v_tile, masks = chunks[ci]
            sp_ = sps[ci]
            et = ET[cnt_e[0] % 3]
            cnt_e[0] += 1
            P.op("act", lambda e, sp_=sp_, et=et: e.activation(out=et[:, :], in_=sp_[:, :], func=AF.Exp, scale=SCALE), reads=[sp_], writes=[et])
            for qb in range(4):
                P.op("pe", lambda e, qb=qb, et=et, v_ap=v_ap, ci=ci: e.matmul(acc[qb][:, 0:NV], lhsT=et[:, qb * 128:(qb + 1) * 128], rhs=v_ap,
                                                                            start=(ci == 0), stop=(ci == n - 1)),
                     reads=[et, v_tile], writes=[acc[qb]])

    def fold_out(qb, col_den, col_o, gate_ap, first):
        a = acc[qb]
        P.op("dve", lambda e, a=a: e.tensor_scalar_max(out=rden[:], in0=a[:, col_den:col_den + 1], scalar1=1e-30), reads=[a], writes=[rden])
        P.op("dve", lambda e: e.reciprocal(out=rden[:], in_=rden[:]), reads=[rden], writes=[rden])
        P.op("dve", lambda e: e.tensor_tensor(out=gsc[:], in0=rden[:], in1=gate_ap, op=ALU.mult), reads=[rden, gates], writes=[gsc])
        if first:
            P.op("dve", lambda e, a=a: e.tensor_scalar(out=ocomb[:, qb, :], in0=a[:, col_o:col_o + 128], scalar1=gsc[:, 0:1], scalar2=None, op0=ALU.mult),
                 reads=[a, gsc], writes=[ocomb])
        else:
            P.op("dve", lambda e, a=a: e.scalar_tensor_tensor(out=ocomb[:, qb, :], in0=a[:, col_o:col_o + 128], scalar=gsc[:, 0:1], in1=ocomb[:, qb, :],
                                                              op0=ALU.mult, op1=ALU.add),
                 reads=[a, gsc, ocomb], writes=[ocomb])

    def load_tile(k):
        t0 = k * QW
        q_ = qt[k % 2]
        P.dma("sp", q_[:], q4.h.ap()[:, :, t0:t0 + QW].rearrange("h p t -> p h t"), writes=[q_])
        lo = max(0, t0 - 512)
        off = lo - (t0 - 512)
        P.dma("sp", kwT[k % 2][:, off:1024], kwinT.h.ap()[:, lo:t0 + 512], writes=[kwT[k % 2]])
        P.dma("sp", vwa[k % 2][:, off // 128:8, 0:128], vwin.h.ap()[lo:t0 + 512, :].rearrange("(j p) d -> p j d", p=128), writes=[vwa[k % 2]])

    load_tile(0)
    for k in range(NQT):
        t0 = k * QW
        if k + 1 < NQT:
            load_tile(k + 1)
        q_ = qt[k % 2]
        mmax = (t0 + 480) // 2048
        for hh in range(4):
            chunks = []
            NV = 385 if hh == 0 else 257
            for m in range(mmax + 1):
                off = 2048 * m + 31 - t0
                masks = []
                if off + 16 * 127 > 0:
                    mi = CMP_OFFS.index(off)
                    masks.append((ident[:], cmpmask[:, mi, :], (ident, cmpmask)))
                chunks.append((kcT[:, m * 128:(m + 1) * 128], kcT, vca[:, m, 0:NV], vca, masks))
            attend(q_[:, hh, :], q_, chunks, NV)
            for qb in range(4):
                a = acc[qb]
                b = k * 4 + qb
                if hh == 0:
                    fold_out(qb, 256, 257, gates[:, b, 0:1], True)
                    P.op("dve", lambda e, a=a, qb=qb: e.tensor_scalar(out=imp[:, qb, :], in0=a[:, 0:256], scalar1=rden[:, 0:1], scalar2=None, op0=ALU.mult),
                         reads=[a, rden], writes=[imp])
                else:
                    P.op("dve", lambda e, a=a: e.tensor_scalar_max(out=rden[:], in0=a[:, 256:257], scalar1=1e-30), reads=[a], writes=[rden])
                    P.op("dve", lambda e: e.reciprocal(out=rden[:], in_=rden[:]), reads=[rden], writes=[rden])
                    P.op("dve", lambda e, a=a, qb=qb: e.scalar_tensor_tensor(out=imp[:, qb, :], in0=a[:, 0:256], scalar=rden[:, 0:1], in1=imp[:, qb, :],
                                                                            op0=ALU.mult, op1=ALU.add),
                         reads=[a, rden, imp], writes=[imp])
        chunks = []
        kw_, vw_ = kwT[k % 2], vwa[k % 2]
        for i in range(8):
            if 4 * k - 4 + i < 0:
                continue
            chunks.append((kw_[:, i * 128:(i + 1) * 128], kw_, vw_[:, i, :], vw_, [(ident[:], winmask[:, i, :], (ident, winmask))]))
        attend(q_[:, 0, :], q_, chunks, 129)
        for qb in range(4):
            b = k * 4 + qb
            w0 = 256 - 2 * b
            P.op("dve", lambda e, qb=qb, w0=w0: e.tensor_tensor(out=score[:], in0=imp[:, qb, :], in1=cmv[:, w0:w0 + 256], op=ALU.mult), reads=[imp, cmv], writes=[score])
            P.op("dve", lambda e, w0=w0: e.tensor_tensor(out=score[:], in0=score[:], in1=cma[:, w0:w0 + 256], op=ALU.add), reads=[score, cma], writes=[score])
            P.op("dve", lambda e: e.memset(score[:, 0:1], 1e6), reads=[], writes=[score])
            P.op("dve", lambda e: e.max(out=m8[:], in_=score[:]), reads=[score], writes=[m8])
            P.op("dve", lambda e: e.match_replace(out=sc2[:], in_to_replace=m8[:], in_values=score[:], imm_value=-1e9), reads=[m8, score], writes=[sc2])
            P.op("dve", lambda e: e.max(out=m8[:], in_=sc2[:]), reads=[sc2], writes=[m8])
            P.op("dve", lambda e: e.tensor_scalar(out=negsel[:], in0=score[:], scalar1=m8[:, 7:8], scalar2=NEG, op0=ALU.is_lt, op1=ALU.mult),
                 reads=[score, m8], writes=[negsel])
            for hf in range(2):
                P.op("pe", lambda e, hf=hf: e.transpose(out=tps[:, hf, :], in_=negsel[:, hf * 128:(hf + 1) * 128], identity=ident[:]),
                     reads=[negsel, ident], writes=[tps])
            P.op("act", lambda e, qb=qb: e.activation(out=nsT[:, :, qb * 128:(qb + 1) * 128], in_=tps[:, 0:2, :], func=AF.Identity), reads=[tps], writes=[nsT])
        for qb in range(4):
            fold_out(qb, 128, 0, gates[:, k * 4 + qb, 2:3], False)
        chunks = []
        for j in range(4 * k + 4):
            masks = [(bsel[:, j % 64, :], nsT[:, j // 64, :], (bsel, nsT))]
            if j >= 4 * k:
                masks.append((ident[:], causal[:, j - 4 * k, :], (ident, causal)))
            chunks.append((ksT[:, j * 128:(j + 1) * 128], ksT, vsa[:, j, :], vsa, masks))
        attend(q_[:, 0, :], q_, chunks, 129)
        for qb in range(4):
            fold_out(qb, 128, 0, gates[:, k * 4 + qb, 1:2], False)
        P.op("act", lambda e: e.activation(out=ocb[:], in_=ocomb[:], func=AF.Identity), reads=[ocomb], writes=[ocb])
        for qb in range(4):
            P.op("pe", lambda e, qb=qb: e.transpose(out=tps[:, qb, :], in_=ocb[:, qb, :], identity=ident[:]), reads=[ocb, ident], writes=[tps])
        yo_ = yo[k % 2]
        P.op("act", lambda e, yo_=yo_: e.activation(out=yo_[:].rearrange("p (a b) -> p a b", b=128), in_=tps[:, :, :], func=AF.Identity), reads=[tps], writes=[yo_])
        P.dma("sp", yT.h.ap()[:, t0:t0 + QW], yo_[:], reads=[yo_])


NPAIR = S // 128


def b2_consts():
    bf = ml_dtypes.bfloat16
    c = {}
    c["ident"] = np.eye(128, dtype=np.float32).astype(bf)
    c["identf"] = np.eye(128, dtype=np.float32)
    s = np.arange(128)[:, None]
    t = np.arange(128)[None, :]
    c["mask01"] = (((s // 64) == (t // 64)) & (s <= t)).astype(np.float32)
    c["onesf"] = np.ones((128, 128), np.float32)
    return c


def body_B2(P):
    DI = lambda n, sh, dt: P.dram(n, sh, dt, "ExternalInput")
    qraw = DI("qraw", [128, S], BF16)
    kraw = DI("kraw", [128, S], BF16)
    vtok = DI("vtok", [S, 128], BF16)
    gif = DI("gif", [128, 2, 128], BF16)
    convw = DI("convw", [128, 2, 4], F32)
    convb = DI("convb", [128, 2], F32)
    gateb = DI("gateb", [128, 2], F32)
    ident_d = DI("ident", [128, 128], BF16)
    identf_d = DI("identf", [128, 128], F32)
    mask_d = DI("mask01", [128, 128], F32)
    ones_d = DI("onesf", [128, 128], F32)
    scr = P.dram("scr", [4, 256], F32, "Internal")
    hT = P.dram("hT", [128, S], BF16, "ExternalOutput")

    ident = P.sb("ident", [128, 128], BF16)
    identf = P.sb("identf", [128, 128], F32)
    mask01 = P.sb("mask01", [128, 128], F32)
    onesf = P.sb("onesf", [128, 128], F32)
    cw = P.sb("cw", [128, 2, 4], F32)
    cb = P.sb("cb", [128, 2], F32)
    gb = P.sb("gb", [128, 2], F32)
    for t, d in ((ident, ident_d), (identf, identf_d), (mask01, mask_d), (onesf, ones_d), (cw, convw), (cb, convb), (gb, gateb)):
        P.dma("sp", t[:], d[:], writes=[t])
    QT = P.sb("QT", [128, S], BF16)
    KT = P.sb("KT", [128, S], BF16)
    Ktok = P.sb("Ktok", [128, NPAIR, 128], BF16)
    Va = P.sb("Va", [128, NPAIR, 129], BF16)
    P.dma("sp", Va[:, :, 0:128], vtok.h.ap().rearrange("(j p) d -> p j d", p=128), writes=[Va])
    P.op("pool", lambda e: e.memset(Va[:, :, 128:129], 1.0), reads=[], writes=[Va])
    ewT = P.sb("ewT", [128, 128], F32)
    euT = P.sb("euT", [128, 128], F32)
    wiT = P.sb("wiT", [128, 128], F32)
    gdT = P.sb("gdT", [128, 128], F32)
    decb = P.sb("decb", [128, 256], F32)
    sc2b = P.sb("sc2b", [128, 256], F32)

    pKQ = [P.ps(f"pKQ{i}", [128, 512], F32) for i in range(2)]
    pAB = [P.ps(f"pAB{i}", [128, 512], F32) for i in range(3)]
    pUs = [P.ps(f"pU{i}", [128, 512], F32) for i in range(2)]
    pT = P.ps("pT", [128, 4, 128], BF16)
    pM = pUs[0]

    P.scope_begin()
    xp = P.sb("xp", [128, S + 3], BF16)
    yseg = P.sb("yseg", [128, 4096], F32)
    P.op("pool", lambda e: e.memset(xp[:, 0:3], 0.0), reads=[], writes=[xp])
    for qk, (src, dst) in enumerate(((qraw, QT), (kraw, KT))):
        P.dma("sp", xp[:, 3:S + 3], src[:], writes=[xp])
        for sg in range(4):
            c0 = sg * 4096
            P.op("dve", lambda e, c0=c0, qk=qk: e.tensor_scalar(out=yseg[:], in0=xp[:, c0:c0 + 4096], scalar1=cw[:, qk, 0:1], scalar2=None, op0=ALU.mult),
                 reads=[xp, cw], writes=[yseg])
            for j in range(1, 4):
                P.op("dve", lambda e, c0=c0, qk=qk, j=j: e.scalar_tensor_tensor(out=yseg[:], in0=xp[:, c0 + j:c0 + j + 4096], scalar=cw[:, qk, j:j + 1],
                                                                              in1=yseg[:], op0=ALU.mult, op1=ALU.add),
                     reads=[xp, cw, yseg], writes=[yseg])
            if qk == 0:
                P.op("act", lambda e, c0=c0, dst=dst: e.activation(out=dst[:, c0:c0 + 4096], in_=yseg[:], func=AF.Silu, bias=cb[:, 0:1]),
                     reads=[yseg, cb], writes=[dst])
            else:
                P.op("act", lambda e: e.activation(out=yseg[:], in_=yseg[:], func=AF.Silu, bias=cb[:, 1:2]), reads=[yseg, cb], writes=[yseg])
                P.op("pool", lambda e, c0=c0, dst=dst: e.tensor_scalar(out=dst[:, c0:c0 + 4096], in0=yseg[:], scalar1=SCALE, scalar2=None, op0=ALU.mult),
                     reads=[yseg], writes=[dst])
    P.scope_end()
    for j in range(NPAIR):
        P.op("pe", lambda e, j=j: e.transpose(out=pT[:, j % 4, :], in_=KT[:, j * 128:(j + 1) * 128], identity=ident[:]), reads=[KT, ident], writes=[pT])
        if j % 4 == 3:
            P.op("act", lambda e, j=j: e.activation(out=Ktok[:, j - 3:j + 1, :], in_=pT[:, :, :], func=AF.Identity), reads=[pT], writes=[Ktok])

    P.scope_begin()
    G = lambda n: P.sb(n, [128, 128], F32)
    gtmp = P.sb("gtmp", [128, 2, 128], BF16)
    ig, lf, bcum, w_, cmw, tA, tB, mt = G("ig"), G("lf"), G("bcum"), G("w_"), G("cmw"), G("tA"), G("tB"), G("mt")
    ones64 = P.sb("ones64", [128, 64], F32)
    small = P.sb("small", [128, 8], F32)
    rows = P.sb("rows", [1, 4, 256], F32)
    P.dma("sp", gtmp[:], gif[:], writes=[gtmp])
    P.op("pool", lambda e: e.memset(ones64[:], 1.0), reads=[], writes=[ones64])
    P.op("dve", lambda e: e.tensor_scalar(out=ig[:], in0=gtmp[:, 0, :], scalar1=gb[:, 0:1], scalar2=None, op0=ALU.add), reads=[gtmp, gb], writes=[ig])
    P.op("dve", lambda e: e.tensor_scalar(out=lf[:], in0=gtmp[:, 1, :], scalar1=gb[:, 1:2], scalar2=None, op0=ALU.add), reads=[gtmp, gb], writes=[lf])
    P.op("act", lambda e: e.activation(out=lf[:], in_=lf[:], func=AF.Exp, scale=-1.0), reads=[lf], writes=[lf])
    P.op("act", lambda e: e.activation(out=lf[:], in_=lf[:], func=AF.Ln, bias=1.0), reads=[lf], writes=[lf])
    P.op("dve", lambda e: e.tensor_scalar(out=lf[:], in0=lf[:], scalar1=-1.0, scalar2=None, op0=ALU.mult), reads=[lf], writes=[lf])
    for a in range(2):
        sl = slice(a * 64, (a + 1) * 64)
        P.op("dve", lambda e, sl=sl: e.tensor_tensor_scan(out=bcum[:, sl], data0=ones64[:], data1=lf[:, sl], initial=0.0, op0=ALU.mult, op1=ALU.add),
             reads=[ones64, lf], writes=[bcum])
    P.op("dve", lambda e: e.tensor_tensor(out=w_[:], in0=ig[:], in1=bcum[:], op=ALU.subtract), reads=[ig, bcum], writes=[w_])
    for a in range(2):
        sl = slice(a * 64, (a + 1) * 64)
        P.op("dve", lambda e, sl=sl: e.tensor_tensor_scan(out=cmw[:, sl], data0=ones64[:], data1=w_[:, sl], initial=-1e30, op0=ALU.mult, op1=ALU.max),
             reads=[ones64, w_], writes=[cmw])
    for a in range(2):
        c = a * 64 + 63
        P.op("dve", lambda e, a=a, c=c: e.tensor_copy(out=small[:, a:a + 1], in_=bcum[:, c:c + 1]), reads=[bcum], writes=[small])
        P.op("dve", lambda e, a=a, c=c: e.tensor_tensor(out=small[:, 2 + a:3 + a], in0=cmw[:, c:c + 1], in1=bcum[:, c:c + 1], op=ALU.add),
             reads=[cmw, bcum], writes=[small])
    P.dma("sp", scr.h.ap()[0].rearrange("(p a) -> p a", a=2), small[:, 0:2], reads=[small], writes=[scr])
    P.dma("sp", scr.h.ap()[1].rearrange("(p a) -> p a", a=2), small[:, 2:4], reads=[small], writes=[scr])
    P.dma("sp", rows[0:1, 0:2, :], scr.h.ap()[0:2, :].rearrange("(o r) n -> o r n", o=1), reads=[scr], writes=[rows])
    P.op("dve", lambda e: e.tensor_tensor_scan(out=rows[0:1, 2, :], data0=rows[0:1, 0, :], data1=rows[0:1, 1, :], initial=0.0, op0=ALU.add, op1=ALU.max),
         reads=[rows], writes=[rows])
    P.op("dve", lambda e: e.memset(rows[0:1, 3, 0:1], 0.0), reads=[], writes=[rows])
    P.op("dve", lambda e: e.tensor_copy(out=rows[0:1, 3, 1:256], in_=rows[0:1, 2, 0:255]), reads=[rows], writes=[rows])
    P.dma("sp", scr.h.ap()[2:4, :].rearrange("(o r) n -> o r n", o=1), rows[0:1, 2:4, :], reads=[rows], writes=[scr])
    P.dma("sp", small[:, 4:6], scr.h.ap()[2].rearrange("(p a) -> p a", a=2), reads=[scr], writes=[small])
    P.dma("sp", small[:, 6:8], scr.h.ap()[3].rearrange("(p a) -> p a", a=2), reads=[scr], writes=[small])
    P.op("dve", lambda e: e.tensor_tensor(out=tA[:], in0=bcum[:], in1=cmw[:], op=ALU.add), reads=[bcum, cmw], writes=[tA])
    for a in range(2):
        sl = slice(a * 64, (a + 1) * 64)
        P.op("dve", lambda e, sl=sl, a=a: e.tensor_scalar(out=tB[:, sl], in0=bcum[:, sl], scalar1=small[:, 6 + a:7 + a], scalar2=None, op0=ALU.add),
             reads=[bcum, small], writes=[tB])
    P.op("dve", lambda e: e.tensor_tensor(out=mt[:], in0=tA[:], in1=tB[:], op=ALU.max), reads=[tA, tB], writes=[mt])
    P.op("dve", lambda e: e.tensor_tensor(out=tB[:], in0=tB[:], in1=mt[:], op=ALU.subtract), reads=[tB, mt], writes=[tB])
    P.op("dve", lambda e: e.tensor_tensor(out=tA[:], in0=bcum[:], in1=mt[:], op=ALU.subtract), reads=[bcum, mt], writes=[tA])
    P.op("act", lambda e: e.activation(out=tB[:], in_=tB[:], func=AF.Exp), reads=[tB], writes=[tB])
    P.op("act", lambda e: e.activation(out=tA[:], in_=tA[:], func=AF.Exp), reads=[tA], writes=[tA])
    P.op("act", lambda e: e.activation(out=mt[:], in_=mt[:], func=AF.Exp, scale=-1.0), reads=[mt], writes=[mt])
    P.op("act", lambda e: e.activation(out=w_[:], in_=w_[:], func=AF.Exp), reads=[w_], writes=[w_])
    for src, dst in ((w_, ewT), (tA, euT), (tB, wiT), (mt, gdT)):
        P.op("pe", lambda e, src=src: e.transpose(out=pM[:, 0:128], in_=src[:], identity=identf[:]), reads=[src, identf], writes=[pM])
        P.op("dve", lambda e, dst=dst: e.tensor_copy(out=dst[:], in_=pM[:, 0:128]), reads=[pM], writes=[dst])
    P.op("dve", lambda e: e.tensor_tensor(out=small[:, 2:4], in0=small[:, 0:2], in1=small[:, 4:6], op=ALU.subtract), reads=[small], writes=[small])
    P.op("dve", lambda e: e.tensor_tensor(out=small[:, 0:2], in0=small[:, 2:4], in1=small[:, 6:8], op=ALU.add), reads=[small], writes=[small])
    P.op("act", lambda e: e.activation(out=small[:, 0:4], in_=small[:, 0:4], func=AF.Exp), reads=[small], writes=[small])
    dg = P.sb("dg", [128, 128, 2], F32)
    for which, dst in ((0, decb), (2, sc2b)):
        for a in range(2):
            P.op("dve", lambda e, a=a, which=which: e.tensor_scalar(out=dg[:, :, a], in0=identf[:], scalar1=small[:, which + a:which + a + 1], scalar2=None, op0=ALU.mult),
                 reads=[identf, small], writes=[dg])
        P.op("pe", lambda e: e.matmul(pM[:, 0:256], lhsT=onesf[:], rhs=dg[:].rearrange("p a b -> p (a b)"), start=True, stop=True),
             reads=[onesf, dg], writes=[pM])
        P.op("dve", lambda e, dst=dst: e.tensor_copy(out=dst[:], in_=pM[:, 0:256]), reads=[pM], writes=[dst])
    P.scope_end()

    Cst = P.sb("Cst", [128, 129], F32)
    Cb = [P.sb(f"Cb{i}", [128, 129], BF16) for i in range(2)]
    Sm = [P.sb(f"Sm{i}", [128, 128], BF16) for i in range(2)]
    Vw = [P.sb(f"Vw{i}", [128, 129], BF16) for i in range(2)]
    tU = P.sb("tU", [128, 129], F32)
    tN = P.sb("tN", [128, 129], F32)
    num = P.sb("num", [128, 129], F32)
    dn = P.sb("dn", [128, 1], F32)
    hb = [P.sb(f"hb{i}", [128, 128], BF16) for i in range(2)]
    ho = [P.sb(f"ho{i}", [128, 512], BF16) for i in range(2)]
    P.op("dve", lambda e: e.memset(Cst[:], 0.0), reads=[], writes=[Cst])
    P.op("pool", lambda e: e.memset(Cb[0][:], 0.0), reads=[], writes=[Cb[0]])
    cist = [0]
    tUs = [tU, P.sb("tU1", [128, 129], F32)]

    def front(j):
        cols = slice(j * 128, (j + 1) * 128)
        kq = pKQ[j % 2]
        sm, vw = Sm[j % 2], Vw[j % 2]
        pab = pAB[j % 3]
        P.op("pe", lambda e, kq=kq, cols=cols: e.matmul(kq[:, 0:128], lhsT=KT[:, cols], rhs=QT[:, cols], start=True, stop=True), reads=[KT, QT], writes=[kq])
        P.op("dve", lambda e, kq=kq, sm=sm: e.tensor_tensor(out=sm[:], in0=kq[:, 0:128], in1=mask01[:], op=ALU.mult), reads=[kq, mask01], writes=[sm])
        P.op("pool", lambda e, j=j, vw=vw: e.tensor_scalar(out=vw[:], in0=Va[:, j, :], scalar1=ewT[:, j:j + 1], scalar2=None, op0=ALU.mult),
             reads=[Va, ewT], writes=[vw])
        P.op("pe", lambda e, sm=sm, vw=vw, pab=pab: e.matmul(pab[:, 256:385], lhsT=sm[:], rhs=vw[:], start=True, stop=True), reads=[sm, vw], writes=[pab])

    def chain(j):
        vw = Vw[j % 2]
        pab = pAB[j % 3]
        for a in range(2):
            rs_ = slice(a * 64, (a + 1) * 64)
            P.op("pe", lambda e, rs_=rs_, j=j, vw=vw, a=a: e.matmul(pUs[a][:, 0:129], lhsT=Ktok[rs_, j, :], rhs=vw[rs_, :], start=True, stop=True),
                 reads=[Ktok, vw], writes=[pUs[a]])
        for a in range(2):
            c = 2 * j + a
            rs_ = slice(a * 64, (a + 1) * 64)
            cbc = Cb[cist[0] % 2]
            cbn = Cb[(cist[0] + 1) % 2]
            cist[0] += 1
            P.op("pe", lambda e, rs_=rs_, cbc=cbc, j=j, pab=pab: e.matmul(pab[rs_, 0:129], lhsT=QT[:, j * 128 + rs_.start:j * 128 + rs_.stop], rhs=cbc[:], start=True, stop=True),
                 reads=[QT, cbc], writes=[pab])
            tu = tUs[a]
            P.op("dve", lambda e, c=c, a=a, tu=tu: e.tensor_scalar(out=tu[:], in0=pUs[a][:, 0:129], scalar1=sc2b[:, c:c + 1], scalar2=None, op0=ALU.mult),
                 reads=[pUs[a], sc2b], writes=[tu])
            P.op("dve", lambda e, c=c, tu=tu: e.scalar_tensor_tensor(out=Cst[:], in0=Cst[:], scalar=decb[:, c:c + 1], in1=tu[:], op0=ALU.mult, op1=ALU.add),
                 reads=[Cst, decb, tu], writes=[Cst])
            P.op("act", lambda e, cbn=cbn: e.activation(out=cbn[:], in_=Cst[:], func=AF.Identity), reads=[Cst], writes=[cbn])

    def tail(j):
        pab = pAB[j % 3]
        P.op("dve", lambda e, j=j, pab=pab: e.tensor_scalar(out=tN[:], in0=pab[:, 0:129], scalar1=wiT[:, j:j + 1], scalar2=None, op0=ALU.mult), reads=[pab, wiT], writes=[tN])
        P.op("dve", lambda e, j=j, pab=pab: e.scalar_tensor_tensor(out=num[:], in0=pab[:, 256:385], scalar=euT[:, j:j + 1], in1=tN[:], op0=ALU.mult, op1=ALU.add),
             reads=[pab, euT, tN], writes=[num])
        P.op("dve", lambda e: e.scalar_tensor_tensor(out=dn[:], in0=num[:, 128:129], scalar=-1.0, in1=num[:, 128:129], op0=ALU.mult, op1=ALU.max),
             reads=[num], writes=[dn])
        P.op("dve", lambda e, j=j: e.tensor_tensor(out=dn[:], in0=dn[:], in1=gdT[:, j:j + 1], op=ALU.max), reads=[dn, gdT], writes=[dn])
        P.op("dve", lambda e: e.reciprocal(out=dn[:], in_=dn[:]), reads=[dn], writes=[dn])
        h_ = hb[j % 2]
        P.op("dve", lambda e, h_=h_: e.tensor_scalar(out=h_[:], in0=num[:, 0:128], scalar1=dn[:, 0:1], scalar2=None, op0=ALU.mult), reads=[num, dn], writes=[h_])
        P.op("pe", lambda e, h_=h_, j=j: e.transpose(out=pT[:, j % 4, :], in_=h_[:], identity=ident[:]), reads=[h_, ident], writes=[pT])
        if j % 4 == 3:
            o_ = ho[(j // 4) % 2]
            P.op("act", lambda e, o_=o_: e.activation(out=o_[:].rearrange("p (a b) -> p a b", b=128), in_=pT[:, :, :], func=AF.Identity), reads=[pT], writes=[o_])
            P.dma("sp", hT.h.ap()[:, (j - 3) * 128:(j + 1) * 128], o_[:], reads=[o_])

    front(0)
    for j in range(NPAIR):
        if j + 1 < NPAIR:
            front(j + 1)
        chain(j)
        if j >= 1:
            tail(j - 1)
    tail(NPAIR - 1)


NTH = NT + 2
NFC = D_FF // 128
VEC_NG, VEC_LG0, VEC_LB0, VEC_LG1, VEC_LB1, VEC_CW, VEC_CB, VEC_FLAG, VEC_N = 0, 8, 24, 40, 56, 72, 204, 248, 249


def body_C(P):
    DI = lambda n, sh, dt: P.dram(n, sh, dt, "ExternalInput")
    xT = DI("xT", [D, NTH], F32)
    ynsaT = DI("ynsaT", [1024, NTH], BF16)
    hmlT = DI("hmlT", [1024, NTH], BF16)
    pT2 = DI("pT2", [5120, NTH], BF16)
    w_brn = DI("w_brn", [1024, D], F32)
    w_brm = DI("w_brm", [1024, D], F32)
    w_o = DI("w_o", [D, D], F32)
    w_up = DI("w_up", [D, 2 * D_FF], F32)
    w_down = DI("w_down", [D_FF, D], F32)
    modT_d = DI("modT", [128, 96], F32)
    vecs_d = DI("vecs", [128, VEC_N], F32)
    cst = DI("cst", [128, 257], F32)
    xoT = P.dram("xoT", [D, NT], F32, "ExternalOutput")

    W = 512
    NST = 3
    wb = [P.sb(f"wb{i}", [128, 8192], BF16) for i in range(NST)]
    xr = P.sb("xr", [128, KC, W], F32)
    yn = P.sb("yn", [128, 8, W], BF16)
    hm = P.sb("hm", [128, 8, W], BF16)
    yml = hm
    merged = P.sb("merged", [128, KC, W], BF16)
    h2 = merged
    u = P.sb("u", [128, NFC, W], BF16)
    abuf = [P.sb(f"abuf{i}", [128, W + 2], F32) for i in range(2)]
    carry = P.sb("carry", [128, NFC, 2], F32)
    ft = [P.sb(f"ft{i}", [128, W], F32) for i in range(6)]
    mean = P.sb("mean", [128, W], F32)
    rstd = P.sb("rstd", [128, W], F32)
    tmpm = P.sb("tmpm", [128, W], F32)
    sqt = [P.sb(f"sqt{i}", [128, W], F32) for i in range(2)]
    hsq = P.sb("hsq", [128, 2, W], BF16)
    mo = [P.sb(f"mo{i}", [128, W], BF16) for i in range(4)]
    vecs = P.sb("vecs", [128, VEC_N], F32)
    cs = P.sb("cs", [128, 257], F32)
    csb = P.sb("csb", [128, 128], BF16)
    sc2p = P.sb("sc2p", [128, KC], F32)
    ps_a = P.ps("ps_a", [128, W], F32)
    ps_b = P.ps("ps_b", [128, W], F32)
    accs = [P.ps(f"acc{i}", [128, W], F32) for i in range(4)]

    P.dma("sp", cs[:], cst[:], writes=[cs])
    P.dma("sp", vecs[:], vecs_d[:], writes=[vecs])
    P.op("dve", lambda e: e.tensor_copy(out=csb[:], in_=cs[:, 128:256]), reads=[cs], writes=[csb])
    onesN = Sub(cs, (slice(None), slice(0, 128)))
    one1 = Sub(cs, (slice(None), slice(256, 257)))
    modF = P.sb("modF", [128, 96], F32)
    P.dma("sp", modF[:], modT_d[:], writes=[modF])
    modT = Sub(modF, (slice(None), slice(32, 96)))
    P.op("dve", lambda e: e.tensor_scalar_add(out=sc2p[:], in0=modT[:, 32:48], scalar1=1.0), reads=[modT], writes=[sc2p])

    state = {"gi": 0, "fi": 0}

    def nacc():
        a = accs[state["gi"] % 4]
        state["gi"] += 1
        return a

    def nft():
        t = ft[state["fi"] % 6]
        state["fi"] += 1
        return t

    def layer_norm_inplace(Wc, gcol, bcol, out_bf=None, scale_t=None, bias_t=None):
        for kc in range(KC):
            sq = sqt[kc % 2]
            P.op("act", lambda e, sq=sq, kc=kc: e.activation(out=sq[:, 0:Wc], in_=xr[:, kc, 0:Wc], func=AF.Square), reads=[xr], writes=[sq])
            P.op("pe", lambda e, kc=kc: e.matmul(ps_a[:, 0:Wc], lhsT=onesN[:], rhs=xr[:, kc, 0:Wc], start=(kc == 0), stop=(kc == KC - 1)),
                 reads=[cs, xr], writes=[ps_a])
            P.op("pe", lambda e, kc=kc, sq=sq: e.matmul(ps_b[:, 0:Wc], lhsT=onesN[:], rhs=sq[:, 0:Wc], start=(kc == 0), stop=(kc == KC - 1)),
                 reads=[cs, sq], writes=[ps_b])
        P.op("act", lambda e: e.activation(out=mean[:, 0:Wc], in_=ps_a[:, 0:Wc], func=AF.Identity), reads=[ps_a], writes=[mean])
        P.op("dve", lambda e: e.tensor_tensor(out=tmpm[:, 0:Wc], in0=mean[:, 0:Wc], in1=mean[:, 0:Wc], op=ALU.mult), reads=[mean], writes=[tmpm])
        P.op("dve", lambda e: e.tensor_tensor(out=tmpm[:, 0:Wc], in0=ps_b[:, 0:Wc], in1=tmpm[:, 0:Wc], op=ALU.subtract), reads=[ps_b, tmpm], writes=[tmpm])
        P.op("act", lambda e: e.activation(out=tmpm[:, 0:Wc], in_=tmpm[:, 0:Wc], func=AF.Sqrt, bias=EPS), reads=[tmpm], writes=[tmpm])
        P.op("dve", lambda e: e.reciprocal(out=rstd[:, 0:Wc], in_=tmpm[:, 0:Wc]), reads=[tmpm], writes=[rstd])
        for kc in range(KC):
            t = nft()
            P.op("dve", lambda e, t=t, kc=kc: e.tensor_tensor(out=t[:, 0:Wc], in0=xr[:, kc, 0:Wc], in1=mean[:, 0:Wc], op=ALU.subtract), reads=[xr, mean], writes=[t])
            P.op("dve", lambda e, t=t: e.tensor_tensor(out=t[:, 0:Wc], in0=t[:, 0:Wc], in1=rstd[:, 0:Wc], op=ALU.mult), reads=[t, rstd], writes=[t])
            if out_bf is None:
                P.op("act", lambda e, t=t, kc=kc: e.activation(out=xr[:, kc, 0:Wc], in_=t[:, 0:Wc], func=AF.Identity,
                                                               scale=vecs[:, gcol + kc:gcol + kc + 1], bias=vecs[:, bcol + kc:bcol + kc + 1]),
                     reads=[t, vecs], writes=[xr])
            else:
                P.op("act", lambda e, t=t, kc=kc: e.activation(out=out_bf[:, kc, 0:Wc], in_=t[:, 0:Wc], func=AF.Identity,
                                                               scale=scale_t[:, kc:kc + 1], bias=bias_t[:, kc:kc + 1]),
                     reads=[t, scale_t, bias_t], writes=[out_bf])

    sched = []

    def prologue_mix(c0, Wc):
        P.dma("sp", yn[:, :, 0:Wc], ynsaT.h.ap()[:, c0:c0 + Wc].rearrange("(kc p) n -> p kc n", p=128), writes=[yn])
        P.dma("sp", hm[:, :, 0:Wc], hmlT.h.ap()[:, c0:c0 + Wc].rearrange("(kc p) n -> p kc n", p=128), writes=[hm])
        for hh in range(4):
            P.op("dve", lambda e, hh=hh: e.tensor_tensor(out=hsq[:, :, 0:Wc], in0=hm[:, 2 * hh:2 * hh + 2, 0:Wc], in1=hm[:, 2 * hh:2 * hh + 2, 0:Wc], op=ALU.mult),
                 reads=[hm], writes=[hsq])
            for c in range(2):
                P.op("pe", lambda e, hh=hh, c=c: e.matmul(ps_a[:, 0:Wc], lhsT=csb[:], rhs=hm[:, 2 * hh + c, 0:Wc], start=(c == 0), stop=(c == 1)),
                     reads=[csb, hm], writes=[ps_a])
            for c in range(2):
                P.op("pe", lambda e, c=c: e.matmul(ps_b[:, 0:Wc], lhsT=csb[:], rhs=hsq[:, c, 0:Wc], start=(c == 0), stop=(c == 1)),
                     reads=[csb, hsq], writes=[ps_b])
            P.op("act", lambda e: e.activation(out=mean[:, 0:Wc], in_=ps_a[:, 0:Wc], func=AF.Identity), reads=[ps_a], writes=[mean])
            P.op("dve", lambda e: e.tensor_tensor(out=tmpm[:, 0:Wc], in0=mean[:, 0:Wc], in1=mean[:, 0:Wc], op=ALU.mult), reads=[mean], writes=[tmpm])
            P.op("dve", lambda e: e.tensor_tensor(out=tmpm[:, 0:Wc], in0=ps_b[:, 0:Wc], in1=tmpm[:, 0:Wc], op=ALU.subtract), reads=[ps_b, tmpm], writes=[tmpm])
            P.op("dve", lambda e: e.tensor_scalar_max(out=tmpm[:, 0:Wc], in0=tmpm[:, 0:Wc], scalar1=0.0), reads=[tmpm], writes=[tmpm])
            P.op("act", lambda e: e.activation(out=tmpm[:, 0:Wc], in_=tmpm[:, 0:Wc], func=AF.Sqrt, bias=EPS), reads=[tmpm], writes=[tmpm])
            P.op("dve", lambda e: e.reciprocal(out=rstd[:, 0:Wc], in_=tmpm[:, 0:Wc]), reads=[tmpm], writes=[rstd])
            for c in range(2):
                ch = 2 * hh + c
                m_ = mo[ch % 4]
                t = nft()
                sg = nft()
                P.dma("sp", m_[:, 0:Wc], pT2.h.ap()[ch * 128:(ch + 1) * 128, c0:c0 + Wc], writes=[m_])
                P.op("act", lambda e, sg=sg, m_=m_: e.activation(out=sg[:, 0:Wc], in_=m_[:, 0:Wc], func=AF.Sigmoid), reads=[m_], writes=[sg])
                P.op("dve", lambda e, t=t, ch=ch: e.tensor_tensor(out=t[:, 0:Wc], in0=hm[:, ch, 0:Wc], in1=mean[:, 0:Wc], op=ALU.subtract), reads=[hm, mean], writes=[t])
                P.op("dve", lambda e, t=t: e.tensor_tensor(out=t[:, 0:Wc], in0=t[:, 0:Wc], in1=rstd[:, 0:Wc], op=ALU.mult), reads=[t, rstd], writes=[t])
                P.op("dve", lambda e, t=t, sg=sg, ch=ch: e.scalar_tensor_tensor(out=yml[:, ch, 0:Wc], in0=t[:, 0:Wc], scalar=vecs[:, VEC_NG + ch:VEC_NG + ch + 1],
                                                                              in1=sg[:, 0:Wc], op0=ALU.mult, op1=ALU.mult),
                     reads=[t, sg, vecs], writes=[yml])


    mix_done = set()

    def add_tile(c0, Wc, halo, nxt=None):
        oc0 = c0 - 2

        def prologue():
            P.dma("sp", xr[:, :, 0:Wc], xT.h.ap()[:, c0:c0 + Wc].rearrange("(kc p) n -> p kc n", p=128), writes=[xr])
            if c0 not in mix_done:
                mix_done.add(c0)
                prologue_mix(c0, Wc)

        for sgi in range(4):
            def run(wbt, sgi=sgi):
                wv = wbt.h.ap().rearrange("p (w k n) -> p w k n", w=2, k=8)
                for o4 in range(4):
                    oc = sgi * 4 + o4
                    a1, a2 = nacc(), nacc()
                    for kc in range(8):
                        P.op("pe", lambda e, a1=a1, kc=kc, o4=o4: e.matmul(a1[:, 0:Wc], lhsT=wv[:, 0, kc, o4 * 128:(o4 + 1) * 128], rhs=yn[:, kc, 0:Wc],
                                                                          start=(kc == 0), stop=(kc == 7)), reads=[wbt, yn], writes=[a1])
                    for kc in range(8):
                        P.op("pe", lambda e, a2=a2, kc=kc, o4=o4: e.matmul(a2[:, 0:Wc], lhsT=wv[:, 1, kc, o4 * 128:(o4 + 1) * 128], rhs=yml[:, kc, 0:Wc],
                                                                          start=(kc == 0), stop=(kc == 7)), reads=[wbt, yml], writes=[a2])
                    g1t, g2t, s1, s2, t1_, t2_ = mo[0 + (oc % 2) * 2], mo[1 + (oc % 2) * 2], nft(), nft(), nft(), nft()
                    P.dma("sp", g1t[:, 0:Wc], pT2.h.ap()[(8 + oc) * 128:(9 + oc) * 128, c0:c0 + Wc], writes=[g1t])
                    P.dma("sp", g2t[:, 0:Wc], pT2.h.ap()[(24 + oc) * 128:(25 + oc) * 128, c0:c0 + Wc], writes=[g2t])
                    P.op("act", lambda e, s1=s1, g1t=g1t: e.activation(out=s1[:, 0:Wc], in_=g1t[:, 0:Wc], func=AF.Sigmoid), reads=[g1t], writes=[s1])
                    P.op("act", lambda e, s2=s2, g2t=g2t: e.activation(out=s2[:, 0:Wc], in_=g2t[:, 0:Wc], func=AF.Sigmoid), reads=[g2t], writes=[s2])
                    P.op("dve", lambda e, a1=a1, s1=s1, t1_=t1_: e.tensor_tensor(out=t1_[:, 0:Wc], in0=a1[:, 0:Wc], in1=s1[:, 0:Wc], op=ALU.mult), reads=[a1, s1], writes=[t1_])
                    P.op("dve", lambda e, a2=a2, s2=s2, t2_=t2_: e.tensor_tensor(out=t2_[:, 0:Wc], in0=a2[:, 0:Wc], in1=s2[:, 0:Wc], op=ALU.mult), reads=[a2, s2], writes=[t2_])
                    P.op("dve", lambda e, t1_=t1_, t2_=t2_, oc=oc: e.tensor_tensor(out=merged[:, oc, 0:Wc], in0=t1_[:, 0:Wc], in1=t2_[:, 0:Wc], op=ALU.add),
                         reads=[t1_, t2_], writes=[merged])
            sched.append(dict(loads=[(w_brn.h.ap()[:, sgi * 512:(sgi + 1) * 512].rearrange("(kc p) n -> p kc n", p=128), 0, 8, 512, None),
                                     (w_brm.h.ap()[:, sgi * 512:(sgi + 1) * 512].rearrange("(kc p) n -> p kc n", p=128), 4096, 8, 512, None)],
                              run=run, pre=(prologue if sgi == 0 else None)))
        for sgi in range(4):
            def pre_o():
                P.op("act", lambda e: e.activation(out=xr[:, :, 0:Wc], in_=xr[:, :, 0:Wc], func=AF.Identity, scale=ALPHA), reads=[xr], writes=[xr])

            def run(wbt, sgi=sgi):
                wv = wbt.h.ap().rearrange("p (k n) -> p k n", k=16)
                for o4 in range(4):
                    oc = sgi * 4 + o4
                    a = nacc()
                    for kc in range(KC):
                        P.op("pe", lambda e, a=a, kc=kc, o4=o4: e.matmul(a[:, 0:Wc], lhsT=wv[:, kc, o4 * 128:(o4 + 1) * 128], rhs=merged[:, kc, 0:Wc],
                                                                        start=(kc == 0), stop=(kc == KC - 1)), reads=[wbt, merged], writes=[a])
                    P.op("dve", lambda e, a=a, oc=oc: e.scalar_tensor_tensor(out=xr[:, oc, 0:Wc], in0=a[:, 0:Wc], scalar=modT[:, oc:oc + 1], in1=xr[:, oc, 0:Wc],
                                                                            op0=ALU.mult, op1=ALU.add), reads=[a, modT, xr], writes=[xr])
            sched.append(dict(loads=[(dview(w_o, sgi * 512, (sgi + 1) * 512), 0, 16, 512, None)], run=run, pre=(pre_o if sgi == 0 else None)))
        for fp in range(NFC // 2):
            def pre_up():
                layer_norm_inplace(Wc, VEC_LG0, VEC_LB0)
                layer_norm_inplace(Wc, 0, 0, out_bf=h2, scale_t=sc2p, bias_t=Sub(modF, (slice(None), slice(48, 64))))

            def run(wbt, fp=fp):
                wv = wbt.h.ap().rearrange("p (k n) -> p k n", k=16)
                for f2 in range(2):
                    fc = fp * 2 + f2
                    aa = nacc()
                    for kc in range(KC):
                        P.op("pe", lambda e, aa=aa, kc=kc, f2=f2: e.matmul(aa[:, 0:Wc], lhsT=wv[:, kc, f2 * 128:(f2 + 1) * 128], rhs=h2[:, kc, 0:Wc],
                                                                          start=(kc == 0), stop=(kc == KC - 1)),
                             reads=[wbt, h2], writes=[aa])
                    if halo:
                        P.op("dve", lambda e, aa=aa, fc=fc: e.tensor_scalar(out=carry[:, fc, :], in0=aa[:, 0:2], scalar1=vecs[:, VEC_FLAG:VEC_FLAG + 1], scalar2=None, op0=ALU.mult),
                             reads=[aa, vecs], writes=[carry])
                        continue
                    ag = nacc()
                    for kc in range(KC):
                        P.op("pe", lambda e, ag=ag, kc=kc, f2=f2: e.matmul(ag[:, 0:Wc], lhsT=wv[:, kc, 256 + f2 * 128:256 + (f2 + 1) * 128], rhs=h2[:, kc, 0:Wc],
                                                                          start=(kc == 0), stop=(kc == KC - 1)),
                             reads=[wbt, h2], writes=[ag])
                    ab = abuf[fc % 2]
                    cv, sa = nft(), nft()
                    P.op("act", lambda e, ab=ab, aa=aa: e.activation(out=ab[:, 2:2 + Wc], in_=aa[:, 0:Wc], func=AF.Identity), reads=[aa], writes=[ab])
                    P.op("act", lambda e, ab=ab, fc=fc: e.activation(out=ab[:, 0:2], in_=carry[:, fc, :], func=AF.Identity), reads=[carry], writes=[ab])
                    cwc = VEC_CW + fc * 3
                    P.op("dve", lambda e, ab=ab, cv=cv, cwc=cwc: e.tensor_scalar(out=cv[:, 0:Wc], in0=ab[:, 0:Wc], scalar1=vecs[:, cwc:cwc + 1], scalar2=None, op0=ALU.mult),
                         reads=[ab, vecs], writes=[cv])
                    for j in (1, 2):
                        P.op("dve", lambda e, ab=ab, cv=cv, cwc=cwc, j=j: e.scalar_tensor_tensor(out=cv[:, 0:Wc], in0=ab[:, j:j + Wc], scalar=vecs[:, cwc + j:cwc + j + 1],
                                                                                                in1=cv[:, 0:Wc], op0=ALU.mult, op1=ALU.add),
                             reads=[ab, vecs, cv], writes=[cv])
                    P.op("act", lambda e, ab=ab, fc=fc: e.activation(out=carry[:, fc, :], in_=ab[:, Wc:Wc + 2], func=AF.Identity), reads=[ab], writes=[carry])
                    P.op("act", lambda e, cv=cv, sa=sa, fc=fc: e.activation(out=sa[:, 0:Wc], in_=cv[:, 0:Wc], func=AF.Silu, bias=vecs[:, VEC_CB + fc:VEC_CB + fc + 1]),
                         reads=[cv, vecs], writes=[sa])
                    P.op("dve", lambda e, sa=sa, ag=ag, fc=fc: e.tensor_tensor(out=u[:, fc, 0:Wc], in0=ag[:, 0:Wc], in1=sa[:, 0:Wc], op=ALU.mult), reads=[ag, sa], writes=[u])
            loads = [(dview(w_up, fp * 256, (fp + 1) * 256), 0, 16, 512, (0, 256))]
            if not halo:
                loads.append((dview(w_up, D_FF + fp * 256, D_FF + (fp + 1) * 256), 0, 16, 512, (256, 512)))
            sched.append(dict(loads=loads, run=run, pre=(pre_up if fp == 0 else None)))
        if halo:
            return
        for og in range(4):
            for s4 in range(4):
                def pre_d():
                    P.op("act", lambda e: e.activation(out=xr[:, :, 0:Wc], in_=xr[:, :, 0:Wc], func=AF.Identity, scale=ALPHA), reads=[xr], writes=[xr])
                    if nxt is not None and nxt[0] not in mix_done:
                        mix_done.add(nxt[0])
                        prologue_mix(nxt[0], nxt[1])

                def run(wbt, og=og, s4=s4):
                    wv = wbt.h.ap()[:, 0:11 * 512].rearrange("p (k n) -> p k n", k=11)
                    for o4 in range(4):
                        oc = og * 4 + o4
                        a = accs[o4]
                        for k2 in range(11):
                            fc = s4 * 11 + k2
                            P.op("pe", lambda e, a=a, k2=k2, fc=fc, o4=o4: e.matmul(a[:, 0:Wc], lhsT=wv[:, k2, o4 * 128:(o4 + 1) * 128], rhs=u[:, fc, 0:Wc],
                                                                                    start=(fc == 0), stop=(fc == NFC - 1)),
                                 reads=[wbt, u], writes=[a])
                        if s4 == 3:
                            P.op("dve", lambda e, a=a, oc=oc: e.scalar_tensor_tensor(out=xr[:, oc, 0:Wc], in0=a[:, 0:Wc], scalar=modT[:, 48 + oc:49 + oc], in1=xr[:, oc, 0:Wc],
                                                                                    op0=ALU.mult, op1=ALU.add), reads=[a, modT, xr], writes=[xr])
                    if og == 3 and s4 == 3:
                        layer_norm_inplace(Wc, VEC_LG1, VEC_LB1)
                        P.dma("sp", xoT.h.ap()[:, oc0:oc0 + Wc].rearrange("(kc p) n -> p kc n", p=128), xr[:, :, 0:Wc], reads=[xr], writes=[xoT])
                r0 = s4 * 11 * 128
                sched.append(dict(loads=[(w_down.h.ap()[r0:r0 + 11 * 128, og * 512:(og + 1) * 512].rearrange("(k p) n -> p k n", p=128), 0, 11, 512, None)],
                                  run=run, pre=(pre_d if (og == 0 and s4 == 0) else None)))

    add_tile(0, 2, True)
    for tt in range(NT // W):
        nxt = (2 + (tt + 1) * W, W) if tt + 1 < NT // W else None
        add_tile(2 + tt * W, W, False, nxt)

    def issue_dma(i):
        st = sched[i]
        bt = wb[i % NST]
        f = bt.h.ap()
        for (ap, off, k, n, cols) in st["loads"]:
            dst = f[:, off:off + k * n].rearrange("p (k n) -> p k n", n=n)
            if cols is not None:
                dst = dst[:, :, cols[0]:cols[1]]
            P.dma("pool", dst, ap, writes=[bt])

    n = len(sched)
    for i in range(min(NST, n)):
        issue_dma(i)
    for i in range(n):
        if sched[i].get("pre"):
            sched[i]["pre"]()
        sched[i]["run"](wb[i % NST])
        if i + NST < n:
            issue_dma(i + NST)
    return xoT


def build_M():
    nc = bass.Bass("TRN2", target_bir_lowering=False)
    P = Prog(nc)
    cT = P.dram("cT", [128, KC], F32, "ExternalInput")
    wsl = P.dram("wsl", [D, 3072], F32, "ExternalInput")
    bsl = P.dram("bsl", [1, 3072], F32, "ExternalInput")
    one_d = P.dram("one", [1, 1], F32, "ExternalInput")
    modp = P.dram("modp", [128, 24], F32, "ExternalOutput")
    stage = [P.sb(f"st{i}", [128, KC, 512], F32) for i in range(2)]
    one1 = P.sb("one1", [1, 1], F32)
    ps_row = P.ps("ps_row", [128, 512], F32)
    ps_t = P.ps("ps_t", [128, 512], F32)
    P.dma("sp", one1[:], one_d[:], writes=[one1])
    modT = mod_vectors(P, cT, wsl, bsl, 3072, stage, ps_row, ps_t, one1, cw=512)
    P.dma("sp", modp[:], modT[:], reads=[modT])
    P.finish()
    P.emit()
    return nc


def b1_inputs(projT, cmp_pe, cmp_w1, cmp_w2, consts, h):
    g, r = h // 4, h % 4
    heads = [g * 4 + r] + [g * 4 + o for o in range(4) if o != r]
    q4 = np.stack([projT[hd * 128:(hd + 1) * 128] for hd in heads], axis=0)
    def kvrow(br, kvi):
        b0 = 1024 + ((br * 2 + kvi) * 2 + g) * 128
        return projT[b0:b0 + 128]
    gT = projT[2560 + h * 3: 2560 + h * 3 + 3]
    gat = np.ascontiguousarray(gT.T.reshape(128, 128, 3).transpose(1, 0, 2))
    m = dict(q4=np.ascontiguousarray(q4), kcmpT=np.ascontiguousarray(kvrow(0, 0)), vcmpT=np.ascontiguousarray(kvrow(0, 1)),
             kslcT=np.ascontiguousarray(kvrow(1, 0)), vslc=np.ascontiguousarray(kvrow(1, 1).T),
             kwinT=np.ascontiguousarray(kvrow(2, 0)), vwin=np.ascontiguousarray(kvrow(2, 1).T), gat=gat,
             peT=np.ascontiguousarray(cmp_pe.transpose(2, 0, 1)),
             w1=np.ascontiguousarray(cmp_w1.reshape(2, 32, 128, 256).transpose(0, 2, 1, 3)),
             w2=np.ascontiguousarray(cmp_w2.reshape(2, 2, 128, 128).transpose(2, 0, 1, 3)))
    m.update(consts)
    return m

def b2_inputs(projT, ml_conv_w, ml_conv_b, ml_gate_b, consts, i):
    hh, half = i // 2, i % 2
    QK0 = 2584; V0 = QK0 + 1024; IF0 = V0 + 1024
    m = dict(qraw=np.ascontiguousarray(projT[QK0 + hh * 128: QK0 + (hh + 1) * 128]),
             kraw=np.ascontiguousarray(projT[QK0 + 512 + hh * 128: QK0 + 512 + (hh + 1) * 128]),
             vtok=np.ascontiguousarray(projT[V0 + hh * 256 + half * 128: V0 + hh * 256 + (half + 1) * 128].T),
             gif=np.ascontiguousarray(np.stack([projT[IF0 + hh].reshape(128, 128), projT[IF0 + 4 + hh].reshape(128, 128)], axis=1)))
    cq = ml_conv_w[:, hh * 128:(hh + 1) * 128]
    ck = ml_conv_w[:, 512 + hh * 128:512 + (hh + 1) * 128]
    m["convw"] = np.ascontiguousarray(np.stack([cq.T, ck.T], axis=1)).astype(np.float32)
    m["convb"] = np.ascontiguousarray(np.stack([ml_conv_b[hh * 128:(hh + 1) * 128], ml_conv_b[512 + hh * 128:512 + (hh + 1) * 128]], axis=1)).astype(np.float32)
    m["gateb"] = np.ascontiguousarray(np.broadcast_to(np.array([ml_gate_b[hh], ml_gate_b[4 + hh]], np.float32)[None, :], (128, 2)))
    m.update(consts)
    return m

def c_consts():
    return dict(cst=np.concatenate([np.full((128, 128), 1.0 / 2048, np.float32), np.full((128, 128), 1.0 / 256, np.float32),
                                    np.ones((128, 1), np.float32)], axis=1))

def pvec(v):
    return np.ascontiguousarray(np.asarray(v, np.float32).reshape(-1, 128).T)

def c_inputs(xT_full, ynsaT, hmlT, projT, prm, l, i, consts, modT):
    t0 = i * 2048
    def halo(a):
        if i == 0:
            return np.ascontiguousarray(np.concatenate([np.zeros((a.shape[0], 2), a.dtype), a[:, 0:2048]], axis=1))
        return np.ascontiguousarray(a[:, t0 - 2:t0 + 2048])
    vecs = np.concatenate([pvec(prm["ml_norm_g"][l]), pvec(prm["ln_g"][l, 0]), pvec(prm["ln_b"][l, 0]), pvec(prm["ln_g"][l, 1]), pvec(prm["ln_b"][l, 1]),
                           np.ascontiguousarray(np.asarray(prm["ffn_conv_w"][l], np.float32).reshape(3, 44, 128).transpose(2, 1, 0)).reshape(128, 132),
                           pvec(prm["ffn_conv_b"][l]), np.full((128, 1), 0.0 if i == 0 else 1.0, np.float32)], axis=1)
    m = dict(xT=halo(xT_full), ynsaT=halo(ynsaT), hmlT=halo(hmlT), pT2=halo(projT[4640:9760]),
             w_brn=prm["w_br_nsa"][l], w_brm=prm["w_br_ml"][l], w_o=prm["w_o"][l], w_up=prm["w_up"][l], w_down=prm["w_down"][l],
             modT=modT, vecs=np.ascontiguousarray(vecs.astype(np.float32)))
    m.update(consts)
    return m


_PROGS = {}


def _prog(name):
    if name not in _PROGS:
        _PROGS[name] = {"M": build_M, "A": build_A, "B1": build_B1, "B2": build_B2, "B": build_B, "C": build_C, "CA": build_CA}[name]()
    return _PROGS[name]


def _run(name, in_maps):
    res = run_bass_kernel_spmd(_prog(name), in_maps, core_ids=list(range(NCORE)))
    return res.results


def kernel(**inp):
    prm = {k: np.asarray(v) for k, v in inp.items()}
    x = prm["x"][0]
    cT = np.ascontiguousarray(prm["c"][0].reshape(16, 128).T.astype(np.float32))
    one = np.ones((1, 1), np.float32)
    in_maps = []
    for i in range(NCORE):
        sl = slice(i * 1536, (i + 1) * 1536)
        in_maps.append(dict(cT=cT, wsl=np.ascontiguousarray(np.concatenate([prm["w_ada"][0][:, sl], prm["w_ada"][1][:, sl]], axis=1)),
                            bsl=np.ascontiguousarray(np.concatenate([prm["b_ada"][0][sl], prm["b_ada"][1][sl]])[None, :]), one=one))
    r = _run("M", in_maps)
    modT = [np.ascontiguousarray(np.concatenate([np.asarray(r[i]["modp"])[:, l * 12:(l + 1) * 12] for i in range(NCORE)], axis=1)) for l in range(2)]

    cstA = np.concatenate([np.full((128, 128), 1.0 / 2048, np.float32), np.ones((128, 1), np.float32)], axis=1)
    c1, c2, cc = b1_consts(), b2_consts(), c_consts()
    xT = np.ascontiguousarray(x.T)
    r = _run("A", [dict(xT=np.ascontiguousarray(xT[:, i * NT:(i + 1) * NT]), modT=modT[0], w_in=prm["w_in"][0], cst=cstA) for i in range(NCORE)])
    projT = np.concatenate([np.asarray(r[i]["projT"]) for i in range(NCORE)], axis=1)
    for l in range(2):
        ims = []
        for i in range(NCORE):
            m = {"n_" + k: v for k, v in b1_inputs(projT, prm["cmp_pe"][l], prm["cmp_w1"][l], prm["cmp_w2"][l], c1, i).items()}
            m.update({"m_" + k: v for k, v in b2_inputs(projT, prm["ml_conv_w"][l], prm["ml_conv_b"][l], prm["ml_gate_b"][l], c2, i).items()})
            ims.append(m)
        r = _run("B", ims)
        ynsaT = np.concatenate([np.asarray(r[h]["n_yT"]) for h in range(NCORE)], axis=0)
        hmlT = np.concatenate([np.asarray(r[i]["m_hT"]) for i in range(NCORE)], axis=0)
        cin = [c_inputs(xT, ynsaT, hmlT, projT, prm, l, i, cc, modT[l]) for i in range(NCORE)]
        if l == 0:
            ims = []
            for i in range(NCORE):
                m = {"c_" + k: v for k, v in cin[i].items()}
                m.update({"a_modT": modT[1], "a_w_in": prm["w_in"][1], "a_cst": cstA})
                ims.append(m)
            r = _run("CA", ims)
            xT = np.concatenate([np.asarray(r[i]["c_xoT"]) for i in range(NCORE)], axis=1)
            projT = np.concatenate([np.asarray(r[i]["a_projT"]) for i in range(NCORE)], axis=1)
        else:
            r = _run("C", cin)
            xT = np.concatenate([np.asarray(r[i]["xoT"]) for i in range(NCORE)], axis=1)
    return np.ascontiguousarray(xT.T)[None].astype(np.float32)


def build_B1():
    nc = bass.Bass("TRN2", target_bir_lowering=False)
    P = Prog(nc)
    body_B1(P)
    P.finish()
    P.emit()
    return nc


def build_B2():
    nc = bass.Bass("TRN2", target_bir_lowering=False)
    P = Prog(nc)
    body_B2(P)
    P.finish()
    P.emit()
    return nc


def build_B():
    nc = bass.Bass("TRN2", target_bir_lowering=False)
    P = Prog(nc)
    P.prefix = "n_"
    P.scope_begin()
    body_B1(P)
    P.scope_end()
    P.new_epoch()
    P.prefix = "m_"
    P.scope_begin()
    body_B2(P)
    P.scope_end()
    P.finish()
    P.emit()
    return nc


def build_A():
    nc = bass.Bass("TRN2", target_bir_lowering=False)
    P = Prog(nc)
    body_A(P)
    P.finish()
    P.emit()
    return nc


def build_C():
    nc = bass.Bass("TRN2", target_bir_lowering=False)
    P = Prog(nc)
    body_C(P)
    P.finish()
    P.emit()
    return nc


def build_CA():
    nc = bass.Bass("TRN2", target_bir_lowering=False)
    P = Prog(nc)
    P.prefix = "c_"
    P.scope_begin()
    xo = body_C(P)
    P.scope_end()
    P.new_epoch()
    P.prefix = "a_"
    P.scope_begin()
    body_A(P, x_src=xo)
    P.scope_end()
    P.finish()
    P.emit()
    return nc
```

```python
import ml_dtypes
from concourse.bass_utils import run_bass_kernel_spmd
import numpy as np
from contextlib import ExitStack
import concourse.bass as bass
import concourse.mybir as mybir

F32 = mybir.dt.float32
BF16 = mybir.dt.bfloat16
I32 = mybir.dt.int32
ALU = mybir.AluOpType
AF = mybir.ActivationFunctionType
AX = mybir.AxisListType

ENGS = ("pe", "act", "dve", "pool", "sp")


class Tile:
    def __init__(self, name, handle, space):
        self.name = name
        self.h = handle
        self.space = space
        self.last_w = None
        self.readers = {}
        self.dsem = None
        self.ssem = None

    def __getitem__(self, idx):
        return self.h[idx]


class Sub:
    def __init__(self, parent, idx):
        self.parent = parent
        self.idx = idx

    def __getitem__(self, i):
        return self.parent.h[self.idx][i]


def _par(t):
    return t.parent if isinstance(t, Sub) else t


class Prog:
    def __init__(self, nc):
        self.nc = nc
        self.es = ExitStack()
        self.prog = {e: [] for e in ENGS}
        self.sem = {}
        self.cnt = {e: 0 for e in ENGS}
        self.waited = {e: {} for e in ENGS}
        self.nsem = 0
        for e in ENGS:
            self.sem[e] = self._newsem("e_" + e)
        self.n_inst = 0
        self.inherit = {}
        self.scope_tiles = None
        self.scope_stack = []
        self.sem_pool = []
        self.all_dma_sems = []
        self.final_eng = []
        self.prefix = ""
        self.nosync_same = set()

    def _newsem(self, name):
        self.nsem += 1
        return self.nc.alloc_semaphore(name=name + "_%d" % self.nsem)

    def _track(self, t):
        t.readers = dict(self.inherit)
        if self.scope_tiles is not None:
            self.scope_tiles.append(t)
        return t

    def sb(self, name, shape, dtype):
        h = self.es.enter_context(self.nc.sbuf_tensor("s_" + self.prefix + name, list(shape), dtype))
        return self._track(Tile(name, h, "sb"))

    def ps(self, name, shape, dtype):
        h = self.es.enter_context(self.nc.psum_tensor("p_" + self.prefix + name, list(shape), dtype))
        return self._track(Tile(name, h, "ps"))

    def dram(self, name, shape, dtype, kind):
        h = self.nc.dram_tensor(self.prefix + name, list(shape), dtype, kind=kind)
        return Tile(name, h, "dram")

    def scope_begin(self):
        self.scope_stack.append((self.es, self.scope_tiles))
        self.es = ExitStack()
        self.scope_tiles = []

    def scope_end(self):
        for t in self.scope_tiles:
            toks = list(t.readers.values()) + ([t.last_w] if t.last_w is not None else [])
            for tok in toks:
                cur = self.inherit.get(tok[0])
                if cur is None or cur[2] < tok[2]:
                    self.inherit[tok[0]] = tok
            for rec in (t.dsem, t.ssem):
                if rec is not None:
                    self.sem_pool.append(rec)
            t.dsem = None
            t.ssem = None
        self.es.close()
        self.es, self.scope_tiles = self.scope_stack.pop()

    def new_epoch(self):
        for e in ENGS:
            if self.cnt[e] > 0:
                self.final_eng.append((e, self.sem[e], self.cnt[e]))
            self.sem[e] = self._newsem("e_" + e)
            self.cnt[e] = 0

    def _dma_sem(self, name):
        if self.sem_pool:
            return self.sem_pool.pop()
        rec = [self._newsem(name), 0]
        self.all_dma_sems.append(rec)
        return rec

    def _deps(self, eng, reads, writes, is_dma=False, dma_tile=None):
        deps = []
        for t in reads:
            if t.last_w is not None:
                deps.append((t.last_w, "raw"))
        for t in writes:
            if t.last_w is not None:
                deps.append((t.last_w, "waw"))
            for tok in t.readers.values():
                deps.append((tok, "war"))
        waits = []
        for (semkey, semh, val, src), kind in deps:
            if src == eng and not is_dma:
                if eng == "pe" or eng in self.nosync_same:
                    continue
                if kind == "war":
                    continue
            if (is_dma and kind == "waw" and src == "dma" and dma_tile is not None and dma_tile.dsem is not None
                    and semkey == id(dma_tile.dsem[0])):
                continue
            if self.waited[eng].get(semkey, 0) >= val:
                continue
            self.waited[eng][semkey] = val
            waits.append((semh, val))
        return waits

    def op(self, eng, fn, reads=(), writes=()):
        reads = [_par(t) for t in reads]
        writes = [_par(t) for t in writes]
        waits = self._deps(eng, reads, writes)
        self.cnt[eng] += 1
        tok = (id(self.sem[eng]), self.sem[eng], self.cnt[eng], eng)
        self.prog[eng].append((waits, fn, self.sem[eng], 1))
        for t in reads:
            t.readers[tok[0]] = tok
        for t in writes:
            t.last_w = tok
            t.readers = {}
        self.n_inst += 1

    def dma(self, q, out_ap, in_ap, reads=(), writes=(), **kw):
        def fn(e, out_ap=out_ap, in_ap=in_ap, kw=kw):
            return e.dma_start(out=out_ap, in_=in_ap, **kw)
        self.dma_fn(q, fn, reads, writes)

    def dma_fn(self, q, fn, reads=(), writes=(), inc=16):
        reads = [_par(t) for t in reads]
        writes = [_par(t) for t in writes]
        if writes:
            t = writes[0]
            if t.dsem is None:
                t.dsem = self._dma_sem("d_" + t.name)
            rec = t.dsem
            waits = self._deps(q, reads, writes, is_dma=True, dma_tile=t)
        else:
            t = reads[0]
            if t.ssem is None:
                t.ssem = self._dma_sem("s_" + t.name)
            rec = t.ssem
            waits = self._deps(q, reads, writes, is_dma=True)
        rec[1] += inc
        tok = (id(rec[0]), rec[0], rec[1], "dma")
        self.prog[q].append((waits, fn, rec[0], inc))
        for r in reads:
            r.readers[tok[0]] = tok
        for w in writes:
            w.last_w = tok
            w.readers = {}
        self.n_inst += 1

    def coll(self, q, fn, reads=(), writes=()):
        self.dma_fn(q, fn, reads, writes, inc=16)

    def finish(self):
        waits = []
        for rec in self.all_dma_sems:
            if rec[1] > 0:
                waits.append((rec[0], rec[1]))
        for e in ENGS:
            if e != "sp" and self.cnt[e] > 0:
                waits.append((self.sem[e], self.cnt[e]))
        for (e, semh, c) in self.final_eng:
            if e != "sp":
                waits.append((semh, c))
        self.prog["sp"].append((waits, None, None, 0))

    def emit(self):
        nc = self.nc
        prog = self.prog

        def replay(lst, e):
            for waits, fn, sem, inc in lst:
                for semh, val in waits:
                    e.wait_ge(semh, val)
                if fn is not None:
                    ins = fn(e)
                    ins.then_inc(sem, inc)

        with nc.Block() as block:
            @block.tensor
            def _(e):
                replay(prog["pe"], e)

            @block.scalar
            def _(e):
                replay(prog["act"], e)

            @block.vector
            def _(e):
                replay(prog["dve"], e)

            @block.gpsimd
            def _(e):
                replay(prog["pool"], e)

            @block.sync
            def _(e):
                replay(prog["sp"], e)
        self.es.close()


D = 2048
S = 16384
NCORE = 8
NT = S // NCORE
KC = D // 128
IN_COLS = 9760
D_FF = 5632
EPS = 1e-5
ALPHA = 4.0 ** 0.25


def dview(t, c0, c1):
    return t.h.ap()[:, c0:c1].rearrange("(kc p) n -> p kc n", p=128)


def mod_vectors(P, cT, wada, bada, ncols, stage, ps_row, ps_t, one1, cw=512):
    nch = ncols // cw
    sub = cw // 128
    cact = P.sb("cact", [128, KC], F32)
    brow = [P.sb(f"brow{i}", [1, cw], F32) for i in range(2)]
    mrow = [P.sb(f"mrow{i}", [1, cw], F32) for i in range(2)]
    modT = P.sb("modT", [128, ncols // 128], F32)
    P.dma("sp", cact[:], cT[:], writes=[cact])
    P.op("act", lambda e: e.activation(out=cact[:], in_=cact[:], func=AF.Silu), reads=[cact], writes=[cact])
    for j in range(nch):
        w = stage[j % 2]
        br = brow[j % 2]
        mr = mrow[j % 2]
        P.dma("sp", w[:], dview(wada, j * cw, (j + 1) * cw), writes=[w])
        P.dma("sp", br[:], bada.h.ap()[:, j * cw:(j + 1) * cw], writes=[br])
        for kc in range(KC):
            P.op("pe", lambda e, w=w, kc=kc: e.matmul(ps_row[0:1, 0:cw], lhsT=cact[:, kc:kc + 1], rhs=w[:, kc, :],
                                                      start=(kc == 0), stop=(kc == KC - 1)),
                 reads=[cact, w], writes=[ps_row])
        P.op("dve", lambda e, mr=mr, br=br: e.tensor_tensor(out=mr[0:1, :], in0=ps_row[0:1, 0:cw], in1=br[0:1, :], op=ALU.add),
             reads=[ps_row, br], writes=[mr])
        for c in range(sub):
            P.op("pe", lambda e, c=c, j=j, mr=mr: e.matmul(ps_t[:, sub * j + c:sub * j + c + 1], lhsT=mr[0:1, c * 128:(c + 1) * 128],
                                                          rhs=one1[0:1, 0:1], start=True, stop=True),
                 reads=[mr, one1], writes=[ps_t])
    P.op("dve", lambda e: e.tensor_copy(out=modT[:], in_=ps_t[:, 0:ncols // 128]), reads=[ps_t], writes=[modT])
    return modT


def ln_stats(P, z, sq, nkc, W, onesN, ps_a, ps_b, mean, rstd, tmpm):
    P.op("act", lambda e: e.activation(out=sq[:, 0:nkc, 0:W], in_=z[:, 0:nkc, 0:W], func=AF.Square), reads=[z], writes=[sq])
    for kc in range(nkc):
        P.op("pe", lambda e, kc=kc: e.matmul(ps_a[:, 0:W], lhsT=onesN[:], rhs=z[:, kc, 0:W], start=(kc == 0), stop=(kc == nkc - 1)),
             reads=[onesN, z], writes=[ps_a])
    for kc in range(nkc):
        P.op("pe", lambda e, kc=kc: e.matmul(ps_b[:, 0:W], lhsT=onesN[:], rhs=sq[:, kc, 0:W], start=(kc == 0), stop=(kc == nkc - 1)),
             reads=[onesN, sq], writes=[ps_b])
    P.op("act", lambda e: e.activation(out=mean[:, 0:W], in_=ps_a[:, 0:W], func=AF.Identity), reads=[ps_a], writes=[mean])
    P.op("dve", lambda e: e.tensor_tensor(out=tmpm[:, 0:W], in0=mean[:, 0:W], in1=mean[:, 0:W], op=ALU.mult), reads=[mean], writes=[tmpm])
    P.op("dve", lambda e: e.tensor_tensor(out=tmpm[:, 0:W], in0=ps_b[:, 0:W], in1=tmpm[:, 0:W], op=ALU.subtract), reads=[ps_b, tmpm], writes=[tmpm])
    P.op("act", lambda e: e.activation(out=tmpm[:, 0:W], in_=tmpm[:, 0:W], func=AF.Sqrt, bias=EPS), reads=[tmpm], writes=[tmpm])
    P.op("dve", lambda e: e.reciprocal(out=rstd[:, 0:W], in_=tmpm[:, 0:W]), reads=[tmpm], writes=[rstd])


def body_A(P, x_src=None):
    xT = P.dram("xT", [D, NT], F32, "ExternalInput") if x_src is None else x_src
    modT_d = P.dram("modT", [128, 96], F32, "ExternalInput")
    w_in = P.dram("w_in", [D, IN_COLS], F32, "ExternalInput")
    cst = P.dram("cst", [128, 129], F32, "ExternalInput")
    projT = P.dram("projT", [IN_COLS, NT], BF16, "ExternalOutput")

    W = 512
    wf = [P.sb(f"wf{i}", [128, KC, W], F32) for i in range(2)]
    NST = 3
    wb = [P.sb(f"wb{i}", [128, KC, W], BF16) for i in range(NST)]
    hT = P.sb("hT", [128, KC, NT], BF16)
    ot = [P.sb(f"ot{i}", [128, NT], BF16) for i in range(2)]
    cs = P.sb("cs", [128, 129], F32)
    mean = P.sb("mean", [128, W], F32)
    rstd = P.sb("rstd", [128, W], F32)
    tmpm = P.sb("tmpm", [128, W], F32)
    t1 = [P.sb(f"t1_{i}", [128, W], F32) for i in range(2)]
    sc1p = P.sb("sc1p", [128, KC], F32)
    ps_a = P.ps("ps_a", [128, W], F32)
    ps_b = P.ps("ps_b", [128, W], F32)
    ps_row = P.ps("ps_row", [128, W], F32)
    ps_t = P.ps("ps_t", [128, W], F32)
    accs = [P.ps(f"acc{i}", [128, W], F32) for i in range(4)]

    P.dma("sp", cs[:], cst[:], writes=[cs])
    onesN = cs

    modT = P.sb("modT", [128, 96], F32)
    P.dma("sp", modT[:], modT_d[:], writes=[modT])
    P.op("dve", lambda e: e.tensor_scalar_add(out=sc1p[:], in0=modT[:, 16:32], scalar1=1.0), reads=[modT], writes=[sc1p])

    ncg = (IN_COLS + W - 1) // W

    def load_w(cg):
        c0 = cg * W
        cw = min(W, IN_COLS - c0)
        P.dma("pool", wb[cg % NST][:, :, 0:cw], dview(w_in, c0, c0 + cw), writes=[wb[cg % NST]])

    for cg in range(min(NST, ncg)):
        load_w(cg)

    z, sq = wf[0], wf[1]
    for tt in range(NT // W):
        P.dma("sp", z[:], xT.h.ap()[:, tt * W:(tt + 1) * W].rearrange("(kc p) n -> p kc n", p=128), reads=([xT] if x_src is not None else []), writes=[z])
        ln_stats(P, z, sq, KC, W, Sub(cs, (slice(None), slice(0, 128))), ps_a, ps_b, mean, rstd, tmpm)
        for kc in range(KC):
            t = t1[kc % 2]
            P.op("pool", lambda e, t=t, kc=kc: e.tensor_tensor(out=t[:], in0=z[:, kc, :], in1=mean[:], op=ALU.subtract),
                 reads=[z, mean], writes=[t])
            P.op("dve", lambda e, t=t: e.tensor_tensor(out=t[:], in0=t[:], in1=rstd[:], op=ALU.mult), reads=[t, rstd], writes=[t])
            P.op("act", lambda e, t=t, kc=kc, tt=tt: e.activation(out=hT[:, kc, tt * W:(tt + 1) * W], in_=t[:], func=AF.Identity,
                                                                  scale=sc1p[:, kc:kc + 1], bias=modT[:, kc:kc + 1]),
                 reads=[t, sc1p, modT], writes=[hT])

    gi = 0
    oi = 0
    for cg in range(ncg):
        c0 = cg * W
        cw = min(W, IN_COLS - c0)
        bt = wb[cg % NST]
        for sub in range((cw + 127) // 128):
            m = min(128, cw - sub * 128)
            o = ot[oi % 2]
            oi += 1
            for tt in range(NT // W):
                acc = accs[gi % 4]
                for kc in range(KC):
                    P.op("pe", lambda e, acc=acc, bt=bt, kc=kc, sub=sub, m=m, tt=tt:
                         e.matmul(acc[0:m, :], lhsT=bt[:, kc, sub * 128:sub * 128 + m], rhs=hT[:, kc, tt * W:(tt + 1) * W],
                                  start=(kc == 0), stop=(kc == KC - 1)),
                         reads=[bt, hT], writes=[acc])
                if gi % 2 == 0:
                    P.op("act", lambda e, acc=acc, o=o, m=m, tt=tt: e.activation(out=o[0:m, tt * W:(tt + 1) * W], in_=acc[0:m, :], func=AF.Identity),
                         reads=[acc], writes=[o])
                else:
                    P.op("dve", lambda e, acc=acc, o=o, m=m, tt=tt: e.tensor_copy(out=o[0:m, tt * W:(tt + 1) * W], in_=acc[0:m, :]),
                         reads=[acc], writes=[o])
                gi += 1
            r0 = c0 + sub * 128
            P.dma("sp", projT.h.ap()[r0:r0 + m, :], o[0:m, :], reads=[o])
        if cg + NST < ncg:
            load_w(cg + NST)


NEG = -30000.0
QW = 512
NQT = S // QW
SCALE = 128.0 ** -0.5
CMP_OFFS = [31, 31 - 512, 31 - 1024, 31 - 1536, 31 - 2048]


def b1_consts():
    bf = ml_dtypes.bfloat16
    p = np.arange(128)[:, None]
    f = np.arange(512)[None, :]
    c = {}
    c["ident"] = np.eye(128, dtype=np.float32).astype(bf)
    c["cmpmask"] = np.stack([np.where(f >= 16 * p + off, 0.0, NEG) for off in CMP_OFFS], axis=1).astype(bf)
    c["causal"] = np.stack([np.where(128 * i + p <= f, 0.0, NEG) for i in range(4)], axis=1).astype(bf)
    wm = []
    for i in range(8):
        dl = 128 * (i - 4)
        wm.append(np.where((f >= p + dl) & (f < p + dl + 512), 0.0, NEG))
    c["winmask"] = np.stack(wm, axis=1).astype(bf)
    bs = np.zeros((128, 64, 128), np.float32)
    for v in range(64):
        bs[2 * v, v, 0:64] = 1.0
        bs[2 * v + 1, v, 64:128] = 1.0
    c["bsel"] = bs.astype(bf)
    cc = (np.arange(8)[None, :, None] * 128 + np.arange(128)[:, None, None])
    s = np.arange(256)[None, None, :]
    ov = ((16 * cc < 64 * s + 64) & (16 * cc + 32 > 64 * s)).astype(np.float32)
    c["ov1"] = np.concatenate([ov, np.ones((128, 8, 1), np.float32)], axis=2).astype(bf)
    rel = np.arange(512)[None, :] - 256
    cur = (np.arange(128)[:, None] >= 64).astype(np.int64)
    c["cmv"] = (rel < cur - 1).astype(np.float32)
    c["cma"] = np.where((rel == cur) | (rel == cur - 1), 1e6, np.where(rel > cur, -1.0, 0.0)).astype(np.float32)
    return c


def body_B1(P):
    DI = lambda n, sh, dt: P.dram(n, sh, dt, "ExternalInput")
    q4 = DI("q4", [4, 128, S], BF16)
    kcmpT = DI("kcmpT", [128, S], BF16)
    vcmpT = DI("vcmpT", [128, S], BF16)
    kslcT = DI("kslcT", [128, S], BF16)
    vslc = DI("vslc", [S, 128], BF16)
    kwinT = DI("kwinT", [128, S], BF16)
    vwin = DI("vwin", [S, 128], BF16)
    gat = DI("gat", [128, 128, 3], BF16)
    peT = DI("peT", [128, 2, 32], F32)
    w1 = DI("w1", [2, 128, 32, 256], F32)
    w2 = DI("w2", [128, 2, 2, 128], F32)
    ident_d = DI("ident", [128, 128], BF16)
    cmpmask_d = DI("cmpmask", [128, 5, 512], BF16)
    causal_d = DI("causal", [128, 4, 512], BF16)
    winmask_d = DI("winmask", [128, 8, 512], BF16)
    bsel_d = DI("bsel", [128, 64, 128], BF16)
    ov1_d = DI("ov1", [128, 8, 257], BF16)
    cmv_d = DI("cmv", [128, 512], F32)
    cma_d = DI("cma", [128, 512], F32)
    yT = P.dram("yT", [128, S], BF16, "ExternalOutput")

    ident = P.sb("ident", [128, 128], BF16)
    cmpmask = P.sb("cmpmask", [128, 5, 512], BF16)
    causal = P.sb("causal", [128, 4, 512], BF16)
    winmask = P.sb("winmask", [128, 8, 512], BF16)
    bsel = P.sb("bsel", [128, 64, 128], BF16)
    cmv = P.sb("cmv", [128, 512], F32)
    cma = P.sb("cma", [128, 512], F32)
    gates = P.sb("gates", [128, 128, 3], F32)
    ksT = P.sb("ksT", [128, S], BF16)
    vsa = P.sb("vsa", [128, 128, 129], BF16)
    kcT = P.sb("kcT", [128, 1024], BF16)
    vca = P.sb("vca", [128, 8, 385], BF16)
    for t, d in ((ident, ident_d), (cmpmask, cmpmask_d), (causal, causal_d), (winmask, winmask_d), (bsel, bsel_d),
                 (cmv, cmv_d), (cma, cma_d)):
        P.dma("sp", t[:], d[:], writes=[t])
    gtmp = P.sb("gtmp", [128, 128, 3], BF16)
    P.dma("sp", gtmp[:], gat[:], writes=[gtmp])
    P.op("act", lambda e: e.activation(out=gates[:], in_=gtmp[:], func=AF.Sigmoid), reads=[gtmp], writes=[gates])
    P.dma("sp", ksT[:], kslcT[:], writes=[ksT])
    P.dma("sp", vsa[:, :, 0:128], vslc.h.ap().rearrange("(j p) d -> p j d", p=128), writes=[vsa])
    P.op("pool", lambda e: e.memset(vsa[:, :, 128:129], 1.0), reads=[], writes=[vsa])
    P.dma("sp", vca[:, :, 0:257], ov1_d[:], writes=[vca])

    S_ps = [P.ps(f"S{i}", [128, 512], F32) for i in range(2)]
    acc = [P.ps(f"acc{i}", [128, 512], F32) for i in range(4)]
    tps = P.ps("tps", [128, 4, 128], BF16)
    mps = P.ps("mps", [128, 512], F32)

    P.scope_begin()
    xc = P.sb("xc", [128, S], BF16)
    w1f = P.sb("w1f", [128, 32, 256], F32)
    w1b = P.sb("w1b", [128, 32, 256], BF16)
    w2f = P.sb("w2f", [128, 2, 2, 128], F32)
    w2b = P.sb("w2b", [128, 2, 2, 128], BF16)
    pef = P.sb("pef", [128, 2, 32], F32)
    peb = P.sb("peb", [128, 2, 32], BF16)
    gel = [P.sb(f"gel{i}", [128, 1024], BF16) for i in range(2)]
    hb = P.sb("hb", [128, 1], F32)
    xh = P.sb("xh", [128, 512], F32)
    xu = P.sb("xu", [128, 512], F32)
    P.dma("sp", w2f[:], w2[:], writes=[w2f])
    P.dma("sp", pef[:], peT[:], writes=[pef])
    P.op("dve", lambda e: e.tensor_copy(out=w2b[:], in_=w2f[:]), reads=[w2f], writes=[w2b])
    P.op("dve", lambda e: e.tensor_copy(out=peb[:], in_=pef[:]), reads=[pef], writes=[peb])
    for kv in range(2):
        P.dma("sp", xc[:], (kcmpT if kv == 0 else vcmpT)[:], writes=[xc])
        P.dma("sp", w1f[:], w1.h.ap()[kv], writes=[w1f])
        P.op("dve", lambda e: e.tensor_copy(out=w1b[:], in_=w1f[:]), reads=[w1f], writes=[w1b])
        xv = xc.h.ap().rearrange("p (b s) -> p b s", s=16)
        for half in range(2):
            g_ = gel[half]
            P.op("pool", lambda e, g_=g_: e.memset(g_[:], 0.0), reads=[], writes=[g_])
            for j in range(32):
                P.op("pe", lambda e, j=j, half=half, kv=kv: e.matmul(mps[:, 0:1], lhsT=w1b[:, j, half * 128:(half + 1) * 128],
                                                                     rhs=peb[:, kv, j:j + 1], start=(j == 0), stop=(j == 31)),
                     reads=[w1b, peb], writes=[mps])
            P.op("dve", lambda e: e.tensor_copy(out=hb[:], in_=mps[:, 0:1]), reads=[mps], writes=[hb])
            for nci, (n0, cnt) in enumerate(((0, 512), (512, 511))):
                sp_ = S_ps[nci]
                for j in range(32):
                    b0 = n0 + j // 16
                    P.op("pe", lambda e, j=j, half=half, b0=b0, cnt=cnt, sp_=sp_, xv=xv:
                         e.matmul(sp_[:, 0:cnt], lhsT=w1b[:, j, half * 128:(half + 1) * 128], rhs=xv[:, b0:b0 + cnt, j % 16],
                                  start=(j == 0), stop=(j == 31)),
                         reads=[w1b, xc], writes=[sp_])
                P.op("act", lambda e, sp_=sp_, cnt=cnt: e.activation(out=xh[:, 0:cnt], in_=sp_[:, 0:cnt], func=AF.Identity, bias=hb[:, 0:1]),
                     reads=[sp_, hb], writes=[xh])
                P.op("dve", lambda e, cnt=cnt: e.tensor_tensor(out=xu[:, 0:cnt], in0=xh[:, 0:cnt], in1=xh[:, 0:cnt], op=ALU.mult), reads=[xh], writes=[xu])
                P.op("dve", lambda e, cnt=cnt: e.tensor_scalar(out=xu[:, 0:cnt], in0=xu[:, 0:cnt], scalar1=0.044715, scalar2=1.0,
                                                               op0=ALU.mult, op1=ALU.add), reads=[xu], writes=[xu])
                P.op("dve", lambda e, cnt=cnt: e.tensor_tensor(out=xu[:, 0:cnt], in0=xu[:, 0:cnt], in1=xh[:, 0:cnt], op=ALU.mult), reads=[xu, xh], writes=[xu])
                P.op("act", lambda e, cnt=cnt: e.activation(out=xu[:, 0:cnt], in_=xu[:, 0:cnt], func=AF.Sigmoid, scale=1.5957691216),
                     reads=[xu], writes=[xu])
                P.op("dve", lambda e, cnt=cnt, n0=n0, g_=g_: e.tensor_tensor(out=g_[:, n0:n0 + cnt], in0=xu[:, 0:cnt], in1=xh[:, 0:cnt], op=ALU.mult),
                     reads=[xu, xh], writes=[g_])
        if kv == 0:
            for nci in range(2):
                for half in range(2):
                    P.op("pe", lambda e, nci=nci, half=half: e.matmul(mps[:, :], lhsT=w2b[:, 0, half, :], rhs=gel[half][:, nci * 512:(nci + 1) * 512],
                                                                      start=(half == 0), stop=(half == 1)),
                         reads=[w2b, gel[half]], writes=[mps])
                P.op("dve", lambda e, nci=nci: e.tensor_copy(out=kcT[:, nci * 512:(nci + 1) * 512], in_=mps[:, :]), reads=[mps], writes=[kcT])
        else:
            for m in range(8):
                for half in range(2):
                    P.op("pe", lambda e, m=m, half=half: e.matmul(mps[:, 0:128], lhsT=gel[half][:, m * 128:(m + 1) * 128], rhs=w2b[:, 1, half, :],
                                                                  start=(half == 0), stop=(half == 1)),
                         reads=[w2b, gel[half]], writes=[mps])
                P.op("dve", lambda e, m=m: e.tensor_copy(out=vca[:, m, 257:385], in_=mps[:, 0:128]), reads=[mps], writes=[vca])

    P.scope_end()
    qt = [P.sb(f"qt{i}", [128, 4, QW], BF16) for i in range(2)]
    kwT = [P.sb(f"kwT{i}", [128, 1024], BF16) for i in range(2)]
    vwa = [P.sb(f"vwa{i}", [128, 8, 129], BF16) for i in range(2)]
    ET = [P.sb(f"ET{i}", [128, QW], BF16) for i in range(3)]
    imp = P.sb("imp", [128, 4, 256], F32)
    ocomb = P.sb("ocomb", [128, 4, 128], F32)
    ocb = P.sb("ocb", [128, 4, 128], BF16)
    rden = P.sb("rden", [128, 1], F32)
    gsc = P.sb("gsc", [128, 1], F32)
    score = P.sb("score", [128, 256], F32)
    sc2 = P.sb("sc2", [128, 256], F32)
    m8 = P.sb("m8", [128, 8], F32)
    negsel = P.sb("negsel", [128, 256], BF16)
    nsT = P.sb("nsT", [128, 2, QW], BF16)
    yo = [P.sb(f"yo{i}", [128, QW], BF16) for i in range(2)]
    for i in range(2):
        P.op("pool", lambda e, i=i: e.memset(vwa[i][:, :, 128:129], 1.0), reads=[], writes=[vwa[i]])
    cnt_s = [0]
    cnt_e = [0]

    def attend(qap, qtile, chunks, NV):
        n = len(chunks)
        sps = [None] * n

        def emit_qk(ci):
            kt_ap, kt_tile, v_ap, v_tile, masks = chunks[ci]
            sp_ = S_ps[cnt_s[0] % 2]
            cnt_s[0] += 1
            sps[ci] = sp_
            nm = len(masks)
            P.op("pe", lambda e, sp_=sp_, kt_ap=kt_ap, nm=nm: e.matmul(sp_[:, :], lhsT=kt_ap, rhs=qap, start=True, stop=(nm == 0)),
                 reads=[kt_tile, qtile], writes=[sp_])
            for mi, (ml, mr, mt) in enumerate(masks):
                P.op("pe", lambda e, sp_=sp_, ml=ml, mr=mr, mi=mi, nm=nm: e.matmul(sp_[:, :], lhsT=ml, rhs=mr, start=False, stop=(mi == nm - 1)),
                     reads=list(mt), writes=[sp_])

        emit_qk(0)
        for ci in range(n):
            if ci + 1 < n:
                emit_qk(ci + 1)
            kt_ap, kt_tile, v_ap, v_tile, masks = chunks[ci]
            sp_ = sps[ci]
            et = ET[cnt_e[0] % 3]
            cnt_e[0] += 1
            P.op("act", lambda e, sp_=sp_, et=et: e.activation(out=et[:, :], in_=sp_[:, :], func=AF.Exp, scale=SCALE), reads=[sp_], writes=[et])
            for qb in range(4):
                P.op("pe", lambda e, qb=qb, et=et, v_ap=v_ap, ci=ci: e.matmul(acc[qb][:, 0:NV], lhsT=et[:, qb * 128:(qb + 1) * 128], rhs=v_ap,
                                                                            start=(ci == 0), stop=(ci == n - 1)),
                     reads=[et, v_tile], writes=[acc[qb]])

    def fold_out(qb, col_den, col_o, gate_ap, first):
        a = acc[qb]
        P.op("dve", lambda e, a=a: e.tensor_scalar_max(out=rden[:], in0=a[:, col_den:col_den + 1], scalar1=1e-30), reads=[a], writes=[rden])
        P.op("dve", lambda e: e.reciprocal(out=rden[:], in_=rden[:]), reads=[rden], writes=[rden])
        P.op("dve", lambda e: e.tensor_tensor(out=gsc[:], in0=rden[:], in1=gate_ap, op=ALU.mult), reads=[rden, gates], writes=[gsc])
        if first:
            P.op("dve", lambda e, a=a: e.tensor_scalar(out=ocomb[:, qb, :], in0=a[:, col_o:col_o + 128], scalar1=gsc[:, 0:1], scalar2=None, op0=ALU.mult),
                 reads=[a, gsc], writes=[ocomb])
        else:
            P.op("dve", lambda e, a=a: e.scalar_tensor_tensor(out=ocomb[:, qb, :], in0=a[:, col_o:col_o + 128], scalar=gsc[:, 0:1], in1=ocomb[:, qb, :],
                                                              op0=ALU.mult, op1=ALU.add),
                 reads=[a, gsc, ocomb], writes=[ocomb])

    def load_tile(k):
        t0 = k * QW
        q_ = qt[k % 2]
        P.dma("sp", q_[:], q4.h.ap()[:, :, t0:t0 + QW].rearrange("h p t -> p h t"), writes=[q_])
        lo = max(0, t0 - 512)
        off = lo - (t0 - 512)
        P.dma("sp", kwT[k % 2][:, off:1024], kwinT.h.ap()[:, lo:t0 + 512], writes=[kwT[k % 2]])
        P.dma("sp", vwa[k % 2][:, off // 128:8, 0:128], vwin.h.ap()[lo:t0 + 512, :].rearrange("(j p) d -> p j d", p=128), writes=[vwa[k % 2]])

    load_tile(0)
    for k in range(NQT):
        t0 = k * QW
        if k + 1 < NQT:
            load_tile(k + 1)
        q_ = qt[k % 2]
        mmax = (t0 + 480) // 2048
        for hh in range(4):
            chunks = []
            NV = 385 if hh == 0 else 257
            for m in range(mmax + 1):
                off = 2048 * m + 31 - t0
                masks = []
                if off + 16 * 127 > 0:
                    mi = CMP_OFFS.index(off)
                    masks.append((ident[:], cmpmask[:, mi, :], (ident, cmpmask)))
                chunks.append((kcT[:, m * 128:(m + 1) * 128], kcT, vca[:, m, 0:NV], vca, masks))
            attend(q_[:, hh, :], q_, chunks, NV)
            for qb in range(4):
                a = acc[qb]
                b = k * 4 + qb
                if hh == 0:
                    fold_out(qb, 256, 257, gates[:, b, 0:1], True)
                    P.op("dve", lambda e, a=a, qb=qb: e.tensor_scalar(out=imp[:, qb, :], in0=a[:, 0:256], scalar1=rden[:, 0:1], scalar2=None, op0=ALU.mult),
                         reads=[a, rden], writes=[imp])
                else:
                    P.op("dve", lambda e, a=a: e.tensor_scalar_max(out=rden[:], in0=a[:, 256:257], scalar1=1e-30), reads=[a], writes=[rden])
                    P.op("dve", lambda e: e.reciprocal(out=rden[:], in_=rden[:]), reads=[rden], writes=[rden])
                    P.op("dve", lambda e, a=a, qb=qb: e.scalar_tensor_tensor(out=imp[:, qb, :], in0=a[:, 0:256], scalar=rden[:, 0:1], in1=imp[:, qb, :],
                                                                            op0=ALU.mult, op1=ALU.add),
                         reads=[a, rden, imp], writes=[imp])
        chunks = []
        kw_, vw_ = kwT[k % 2], vwa[k % 2]
        for i in range(8):
            if 4 * k - 4 + i < 0:
                continue
            chunks.append((kw_[:, i * 128:(i + 1) * 128], kw_, vw_[:, i, :], vw_, [(ident[:], winmask[:, i, :], (ident, winmask))]))
        attend(q_[:, 0, :], q_, chunks, 129)
        for qb in range(4):
            b = k * 4 + qb
            w0 = 256 - 2 * b
            P.op("dve", lambda e, qb=qb, w0=w0: e.tensor_tensor(out=score[:], in0=imp[:, qb, :], in1=cmv[:, w0:w0 + 256], op=ALU.mult), reads=[imp, cmv], writes=[score])
            P.op("dve", lambda e, w0=w0: e.tensor_tensor(out=score[:], in0=score[:], in1=cma[:, w0:w0 + 256], op=ALU.add), reads=[score, cma], writes=[score])
            P.op("dve", lambda e: e.memset(score[:, 0:1], 1e6), reads=[], writes=[score])
            P.op("dve", lambda e: e.max(out=m8[:], in_=score[:]), reads=[score], writes=[m8])
            P.op("dve", lambda e: e.match_replace(out=sc2[:], in_to_replace=m8[:], in_values=score[:], imm_value=-1e9), reads=[m8, score], writes=[sc2])
            P.op("dve", lambda e: e.max(out=m8[:], in_=sc2[:]), reads=[sc2], writes=[m8])
            P.op("dve", lambda e: e.tensor_scalar(out=negsel[:], in0=score[:], scalar1=m8[:, 7:8], scalar2=NEG, op0=ALU.is_lt, op1=ALU.mult),
                 reads=[score, m8], writes=[negsel])
            for hf in range(2):
                P.op("pe", lambda e, hf=hf: e.transpose(out=tps[:, hf, :], in_=negsel[:, hf * 128:(hf + 1) * 128], identity=ident[:]),
                     reads=[negsel, ident], writes=[tps])
            P.op("act", lambda e, qb=qb: e.activation(out=nsT[:, :, qb * 128:(qb + 1) * 128], in_=tps[:, 0:2, :], func=AF.Identity), reads=[tps], writes=[nsT])
        for qb in range(4):
            fold_out(qb, 128, 0, gates[:, k * 4 + qb, 2:3], False)
        chunks = []
        for j in range(4 * k + 4):
            masks = [(bsel[:, j % 64, :], nsT[:, j // 64, :], (bsel, nsT))]
            if j >= 4 * k:
                masks.append((ident[:], causal[:, j - 4 * k, :], (ident, causal)))
            chunks.append((ksT[:, j * 128:(j + 1) * 128], ksT, vsa[:, j, :], vsa, masks))
        attend(q_[:, 0, :], q_, chunks, 129)
        for qb in range(4):
            fold_out(qb, 128, 0, gates[:, k * 4 + qb, 1:2], False)
        P.op("act", lambda e: e.activation(out=ocb[:], in_=ocomb[:], func=AF.Identity), reads=[ocomb], writes=[ocb])
        for qb in range(4):
            P.op("pe", lambda e, qb=qb: e.transpose(out=tps[:, qb, :], in_=ocb[:, qb, :], identity=ident[:]), reads=[ocb, ident], writes=[tps])
        yo_ = yo[k % 2]
        P.op("act", lambda e, yo_=yo_: e.activation(out=yo_[:].rearrange("p (a b) -> p a b", b=128), in_=tps[:, :, :], func=AF.Identity), reads=[tps], writes=[yo_])
        P.dma("sp", yT.h.ap()[:, t0:t0 + QW], yo_[:], reads=[yo_])


NPAIR = S // 128


def b2_consts():
    bf = ml_dtypes.bfloat16
    c = {}
    c["ident"] = np.eye(128, dtype=np.float32).astype(bf)
    c["identf"] = np.eye(128, dtype=np.float32)
    s = np.arange(128)[:, None]
    t = np.arange(128)[None, :]
    c["mask01"] = (((s // 64) == (t // 64)) & (s <= t)).astype(np.float32)
    c["onesf"] = np.ones((128, 128), np.float32)
    return c


def body_B2(P):
    DI = lambda n, sh, dt: P.dram(n, sh, dt, "ExternalInput")
    qraw = DI("qraw", [128, S], BF16)
    kraw = DI("kraw", [128, S], BF16)
    vtok = DI("vtok", [S, 128], BF16)
    gif = DI("gif", [128, 2, 128], BF16)
    convw = DI("convw", [128, 2, 4], F32)
    convb = DI("convb", [128, 2], F32)
    gateb = DI("gateb", [128, 2], F32)
    ident_d = DI("ident", [128, 128], BF16)
    identf_d = DI("identf", [128, 128], F32)
    mask_d = DI("mask01", [128, 128], F32)
    ones_d = DI("onesf", [128, 128], F32)
    scr = P.dram("scr", [4, 256], F32, "Internal")
    hT = P.dram("hT", [128, S], BF16, "ExternalOutput")

    ident = P.sb("ident", [128, 128], BF16)
    identf = P.sb("identf", [128, 128], F32)
    mask01 = P.sb("mask01", [128, 128], F32)
    onesf = P.sb("onesf", [128, 128], F32)
    cw = P.sb("cw", [128, 2, 4], F32)
    cb = P.sb("cb", [128, 2], F32)
    gb = P.sb("gb", [128, 2], F32)
    for t, d in ((ident, ident_d), (identf, identf_d), (mask01, mask_d), (onesf, ones_d), (cw, convw), (cb, convb), (gb, gateb)):
        P.dma("sp", t[:], d[:], writes=[t])
    QT = P.sb("QT", [128, S], BF16)
    KT = P.sb("KT", [128, S], BF16)
    Ktok = P.sb("Ktok", [128, NPAIR, 128], BF16)
    Va = P.sb("Va", [128, NPAIR, 129], BF16)
    P.dma("sp", Va[:, :, 0:128], vtok.h.ap().rearrange("(j p) d -> p j d", p=128), writes=[Va])
    P.op("pool", lambda e: e.memset(Va[:, :, 128:129], 1.0), reads=[], writes=[Va])
    ewT = P.sb("ewT", [128, 128], F32)
    euT = P.sb("euT", [128, 128], F32)
    wiT = P.sb("wiT", [128, 128], F32)
    gdT = P.sb("gdT", [128, 128], F32)
    decb = P.sb("decb", [128, 256], F32)
    sc2b = P.sb("sc2b", [128, 256], F32)

    pKQ = [P.ps(f"pKQ{i}", [128, 512], F32) for i in range(2)]
    pAB = [P.ps(f"pAB{i}", [128, 512], F32) for i in range(3)]
    pUs = [P.ps(f"pU{i}", [128, 512], F32) for i in range(2)]
    pT = P.ps("pT", [128, 4, 128], BF16)
    pM = pUs[0]

    P.scope_begin()
    xp = P.sb("xp", [128, S + 3], BF16)
    yseg = P.sb("yseg", [128, 4096], F32)
    P.op("pool", lambda e: e.memset(xp[:, 0:3], 0.0), reads=[], writes=[xp])
    for qk, (src, dst) in enumerate(((qraw, QT), (kraw, KT))):
        P.dma("sp", xp[:, 3:S + 3], src[:], writes=[xp])
        for sg in range(4):
            c0 = sg * 4096
            P.op("dve", lambda e, c0=c0, qk=qk: e.tensor_scalar(out=yseg[:], in0=xp[:, c0:c0 + 4096], scalar1=cw[:, qk, 0:1], scalar2=None, op0=ALU.mult),
                 reads=[xp, cw], writes=[yseg])
            for j in range(1, 4):
                P.op("dve", lambda e, c0=c0, qk=qk, j=j: e.scalar_tensor_tensor(out=yseg[:], in0=xp[:, c0 + j:c0 + j + 4096], scalar=cw[:, qk, j:j + 1],
                                                                              in1=yseg[:], op0=ALU.mult, op1=ALU.add),
                     reads=[xp, cw, yseg], writes=[yseg])
            if qk == 0:
                P.op("act", lambda e, c0=c0, dst=dst: e.activation(out=dst[:, c0:c0 + 4096], in_=yseg[:], func=AF.Silu, bias=cb[:, 0:1]),
                     reads=[yseg, cb], writes=[dst])
            else:
                P.op("act", lambda e: e.activation(out=yseg[:], in_=yseg[:], func=AF.Silu, bias=cb[:, 1:2]), reads=[yseg, cb], writes=[yseg])
                P.op("pool", lambda e, c0=c0, dst=dst: e.tensor_scalar(out=dst[:, c0:c0 + 4096], in0=yseg[:], scalar1=SCALE, scalar2=None, op0=ALU.mult),
                     reads=[yseg], writes=[dst])
    P.scope_end()
    for j in range(NPAIR):
        P.op("pe", lambda e, j=j: e.transpose(out=pT[:, j % 4, :], in_=KT[:, j * 128:(j + 1) * 128], identity=ident[:]), reads=[KT, ident], writes=[pT])
        if j % 4 == 3:
            P.op("act", lambda e, j=j: e.activation(out=Ktok[:, j - 3:j + 1, :], in_=pT[:, :, :], func=AF.Identity), reads=[pT], writes=[Ktok])

    P.scope_begin()
    G = lambda n: P.sb(n, [128, 128], F32)
    gtmp = P.sb("gtmp", [128, 2, 128], BF16)
    ig, lf, bcum, w_, cmw, tA, tB, mt = G("ig"), G("lf"), G("bcum"), G("w_"), G("cmw"), G("tA"), G("tB"), G("mt")
    ones64 = P.sb("ones64", [128, 64], F32)
    small = P.sb("small", [128, 8], F32)
    rows = P.sb("rows", [1, 4, 256], F32)
    P.dma("sp", gtmp[:], gif[:], writes=[gtmp])
    P.op("pool", lambda e: e.memset(ones64[:], 1.0), reads=[], writes=[ones64])
    P.op("dve", lambda e: e.tensor_scalar(out=ig[:], in0=gtmp[:, 0, :], scalar1=gb[:, 0:1], scalar2=None, op0=ALU.add), reads=[gtmp, gb], writes=[ig])
    P.op("dve", lambda e: e.tensor_scalar(out=lf[:], in0=gtmp[:, 1, :], scalar1=gb[:, 1:2], scalar2=None, op0=ALU.add), reads=[gtmp, gb], writes=[lf])
    P.op("act", lambda e: e.activation(out=lf[:], in_=lf[:], func=AF.Exp, scale=-1.0), reads=[lf], writes=[lf])
    P.op("act", lambda e: e.activation(out=lf[:], in_=lf[:], func=AF.Ln, bias=1.0), reads=[lf], writes=[lf])
    P.op("dve", lambda e: e.tensor_scalar(out=lf[:], in0=lf[:], scalar1=-1.0, scalar2=None, op0=ALU.mult), reads=[lf], writes=[lf])
    for a in range(2):
        sl = slice(a * 64, (a + 1) * 64)
        P.op("dve", lambda e, sl=sl: e.tensor_tensor_scan(out=bcum[:, sl], data0=ones64[:], data1=lf[:, sl], initial=0.0, op0=ALU.mult, op1=ALU.add),
             reads=[ones64, lf], writes=[bcum])
    P.op("dve", lambda e: e.tensor_tensor(out=w_[:], in0=ig[:], in1=bcum[:], op=ALU.subtract), reads=[ig, bcum], writes=[w_])
    for a in range(2):
        sl = slice(a * 64, (a + 1) * 64)
        P.op("dve", lambda e, sl=sl: e.tensor_tensor_scan(out=cmw[:, sl], data0=ones64[:], data1=w_[:, sl], initial=-1e30, op0=ALU.mult, op1=ALU.max),
             reads=[ones64, w_], writes=[cmw])
    for a in range(2):
        c = a * 64 + 63
        P.op("dve", lambda e, a=a, c=c: e.tensor_copy(out=small[:, a:a + 1], in_=bcum[:, c:c + 1]), reads=[bcum], writes=[small])
        P.op("dve", lambda e, a=a, c=c: e.tensor_tensor(out=small[:, 2 + a:3 + a], in0=cmw[:, c:c + 1], in1=bcum[:, c:c + 1], op=ALU.add),
             reads=[cmw, bcum], writes=[small])
    P.dma("sp", scr.h.ap()[0].rearrange("(p a) -> p a", a=2), small[:, 0:2], reads=[small], writes=[scr])
    P.dma("sp", scr.h.ap()[1].rearrange("(p a) -> p a", a=2), small[:, 2:4], reads=[small], writes=[scr])
    P.dma("sp", rows[0:1, 0:2, :], scr.h.ap()[0:2, :].rearrange("(o r) n -> o r n", o=1), reads=[scr], writes=[rows])
    P.op("dve", lambda e: e.tensor_tensor_scan(out=rows[0:1, 2, :], data0=rows[0:1, 0, :], data1=rows[0:1, 1, :], initial=0.0, op0=ALU.add, op1=ALU.max),
         reads=[rows], writes=[rows])
    P.op("dve", lambda e: e.memset(rows[0:1, 3, 0:1], 0.0), reads=[], writes=[rows])
    P.op("dve", lambda e: e.tensor_copy(out=rows[0:1, 3, 1:256], in_=rows[0:1, 2, 0:255]), reads=[rows], writes=[rows])
    P.dma("sp", scr.h.ap()[2:4, :].rearrange("(o r) n -> o r n", o=1), rows[0:1, 2:4, :], reads=[rows], writes=[scr])
    P.dma("sp", small[:, 4:6], scr.h.ap()[2].rearrange("(p a) -> p a", a=2), reads=[scr], writes=[small])
    P.dma("sp", small[:, 6:8], scr.h.ap()[3].rearrange("(p a) -> p a", a=2), reads=[scr], writes=[small])
    P.op("dve", lambda e: e.tensor_tensor(out=tA[:], in0=bcum[:], in1=cmw[:], op=ALU.add), reads=[bcum, cmw], writes=[tA])
    for a in range(2):
        sl = slice(a * 64, (a + 1) * 64)
        P.op("dve", lambda e, sl=sl, a=a: e.tensor_scalar(out=tB[:, sl], in0=bcum[:, sl], scalar1=small[:, 6 + a:7 + a], scalar2=None, op0=ALU.add),
             reads=[bcum, small], writes=[tB])
    P.op("dve", lambda e: e.tensor_tensor(out=mt[:], in0=tA[:], in1=tB[:], op=ALU.max), reads=[tA, tB], writes=[mt])
    P.op("dve", lambda e: e.tensor_tensor(out=tB[:], in0=tB[:], in1=mt[:], op=ALU.subtract), reads=[tB, mt], writes=[tB])
    P.op("dve", lambda e: e.tensor_tensor(out=tA[:], in0=bcum[:], in1=mt[:], op=ALU.subtract), reads=[bcum, mt], writes=[tA])
    P.op("act", lambda e: e.activation(out=tB[:], in_=tB[:], func=AF.Exp), reads=[tB], writes=[tB])
    P.op("act", lambda e: e.activation(out=tA[:], in_=tA[:], func=AF.Exp), reads=[tA], writes=[tA])
    P.op("act", lambda e: e.activation(out=mt[:], in_=mt[:], func=AF.Exp, scale=-1.0), reads=[mt], writes=[mt])
    P.op("act", lambda e: e.activation(out=w_[:], in_=w_[:], func=AF.Exp), reads=[w_], writes=[w_])
    for src, dst in ((w_, ewT), (tA, euT), (tB, wiT), (mt, gdT)):
        P.op("pe", lambda e, src=src: e.transpose(out=pM[:, 0:128], in_=src[:], identity=identf[:]), reads=[src, identf], writes=[pM])
        P.op("dve", lambda e, dst=dst: e.tensor_copy(out=dst[:], in_=pM[:, 0:128]), reads=[pM], writes=[dst])
    P.op("dve", lambda e: e.tensor_tensor(out=small[:, 2:4], in0=small[:, 0:2], in1=small[:, 4:6], op=ALU.subtract), reads=[small], writes=[small])
    P.op("dve", lambda e: e.tensor_tensor(out=small[:, 0:2], in0=small[:, 2:4], in1=small[:, 6:8], op=ALU.add), reads=[small], writes=[small])
    P.op("act", lambda e: e.activation(out=small[:, 0:4], in_=small[:, 0:4], func=AF.Exp), reads=[small], writes=[small])
    dg = P.sb("dg", [128, 128, 2], F32)
    for which, dst in ((0, decb), (2, sc2b)):
        for a in range(2):
            P.op("dve", lambda e, a=a, which=which: e.tensor_scalar(out=dg[:, :, a], in0=identf[:], scalar1=small[:, which + a:which + a + 1], scalar2=None, op0=ALU.mult),
                 reads=[identf, small], writes=[dg])
        P.op("pe", lambda e: e.matmul(pM[:, 0:256], lhsT=onesf[:], rhs=dg[:].rearrange("p a b -> p (a b)"), start=True, stop=True),
             reads=[onesf, dg], writes=[pM])
        P.op("dve", lambda e, dst=dst: e.tensor_copy(out=dst[:], in_=pM[:, 0:256]), reads=[pM], writes=[dst])
    P.scope_end()

    Cst = P.sb("Cst", [128, 129], F32)
    Cb = [P.sb(f"Cb{i}", [128, 129], BF16) for i in range(2)]
    Sm = [P.sb(f"Sm{i}", [128, 128], BF16) for i in range(2)]
    Vw = [P.sb(f"Vw{i}", [128, 129], BF16) for i in range(2)]
    tU = P.sb("tU", [128, 129], F32)
    tN = P.sb("tN", [128, 129], F32)
    num = P.sb("num", [128, 129], F32)
    dn = P.sb("dn", [128, 1], F32)
    hb = [P.sb(f"hb{i}", [128, 128], BF16) for i in range(2)]
    ho = [P.sb(f"ho{i}", [128, 512], BF16) for i in range(2)]
    P.op("dve", lambda e: e.memset(Cst[:], 0.0), reads=[], writes=[Cst])
    P.op("pool", lambda e: e.memset(Cb[0][:], 0.0), reads=[], writes=[Cb[0]])
    cist = [0]
    tUs = [tU, P.sb("tU1", [128, 129], F32)]

    def front(j):
        cols = slice(j * 128, (j + 1) * 128)
        kq = pKQ[j % 2]
        sm, vw = Sm[j % 2], Vw[j % 2]
        pab = pAB[j % 3]
        P.op("pe", lambda e, kq=kq, cols=cols: e.matmul(kq[:, 0:128], lhsT=KT[:, cols], rhs=QT[:, cols], start=True, stop=True), reads=[KT, QT], writes=[kq])
        P.op("dve", lambda e, kq=kq, sm=sm: e.tensor_tensor(out=sm[:], in0=kq[:, 0:128], in1=mask01[:], op=ALU.mult), reads=[kq, mask01], writes=[sm])
        P.op("pool", lambda e, j=j, vw=vw: e.tensor_scalar(out=vw[:], in0=Va[:, j, :], scalar1=ewT[:, j:j + 1], scalar2=None, op0=ALU.mult),
             reads=[Va, ewT], writes=[vw])
        P.op("pe", lambda e, sm=sm, vw=vw, pab=pab: e.matmul(pab[:, 256:385], lhsT=sm[:], rhs=vw[:], start=True, stop=True), reads=[sm, vw], writes=[pab])

    def chain(j):
        vw = Vw[j % 2]
        pab = pAB[j % 3]
        for a in range(2):
            rs_ = slice(a * 64, (a + 1) * 64)
            P.op("pe", lambda e, rs_=rs_, j=j, vw=vw, a=a: e.matmul(pUs[a][:, 0:129], lhsT=Ktok[rs_, j, :], rhs=vw[rs_, :], start=True, stop=True),
                 reads=[Ktok, vw], writes=[pUs[a]])
        for a in range(2):
            c = 2 * j + a
            rs_ = slice(a * 64, (a + 1) * 64)
            cbc = Cb[cist[0] % 2]
            cbn = Cb[(cist[0] + 1) % 2]
            cist[0] += 1
            P.op("pe", lambda e, rs_=rs_, cbc=cbc, j=j, pab=pab: e.matmul(pab[rs_, 0:129], lhsT=QT[:, j * 128 + rs_.start:j * 128 + rs_.stop], rhs=cbc[:], start=True, stop=True),
                 reads=[QT, cbc], writes=[pab])
            tu = tUs[a]
            P.op("dve", lambda e, c=c, a=a, tu=tu: e.tensor_scalar(out=tu[:], in0=pUs[a][:, 0:129], scalar1=sc2b[:, c:c + 1], scalar2=None, op0=ALU.mult),
                 reads=[pUs[a], sc2b], writes=[tu])
            P.op("dve", lambda e, c=c, tu=tu: e.scalar_tensor_tensor(out=Cst[:], in0=Cst[:], scalar=decb[:, c:c + 1], in1=tu[:], op0=ALU.mult, op1=ALU.add),
                 reads=[Cst, decb, tu], writes=[Cst])
            P.op("act", lambda e, cbn=cbn: e.activation(out=cbn[:], in_=Cst[:], func=AF.Identity), reads=[Cst], writes=[cbn])

    def tail(j):
        pab = pAB[j % 3]
        P.op("dve", lambda e, j=j, pab=pab: e.tensor_scalar(out=tN[:], in0=pab[:, 0:129], scalar1=wiT[:, j:j + 1], scalar2=None, op0=ALU.mult), reads=[pab, wiT], writes=[tN])
        P.op("dve", lambda e, j=j, pab=pab: e.scalar_tensor_tensor(out=num[:], in0=pab[:, 256:385], scalar=euT[:, j:j + 1], in1=tN[:], op0=ALU.mult, op1=ALU.add),
             reads=[pab, euT, tN], writes=[num])
        P.op("dve", lambda e: e.scalar_tensor_tensor(out=dn[:], in0=num[:, 128:129], scalar=-1.0, in1=num[:, 128:129], op0=ALU.mult, op1=ALU.max),
             reads=[num], writes=[dn])
        P.op("dve", lambda e, j=j: e.tensor_tensor(out=dn[:], in0=dn[:], in1=gdT[:, j:j + 1], op=ALU.max), reads=[dn, gdT], writes=[dn])
        P.op("dve", lambda e: e.reciprocal(out=dn[:], in_=dn[:]), reads=[dn], writes=[dn])
        h_ = hb[j % 2]
        P.op("dve", lambda e, h_=h_: e.tensor_scalar(out=h_[:], in0=num[:, 0:128], scalar1=dn[:, 0:1], scalar2=None, op0=ALU.mult), reads=[num, dn], writes=[h_])
        P.op("pe", lambda e, h_=h_, j=j: e.transpose(out=pT[:, j % 4, :], in_=h_[:], identity=ident[:]), reads=[h_, ident], writes=[pT])
        if j % 4 == 3:
            o_ = ho[(j // 4) % 2]
            P.op("act", lambda e, o_=o_: e.activation(out=o_[:].rearrange("p (a b) -> p a b", b=128), in_=pT[:, :, :], func=AF.Identity), reads=[pT], writes=[o_])
            P.dma("sp", hT.h.ap()[:, (j - 3) * 128:(j + 1) * 128], o_[:], reads=[o_])

    front(0)
    for j in range(NPAIR):
        if j + 1 < NPAIR:
            front(j + 1)
        chain(j)
        if j >= 1:
            tail(j - 1)
    tail(NPAIR - 1)


NTH = NT + 2
NFC = D_FF // 128
VEC_NG, VEC_LG0, VEC_LB0, VEC_LG1, VEC_LB1, VEC_CW, VEC_CB, VEC_FLAG, VEC_N = 0, 8, 24, 40, 56, 72, 204, 248, 249


def body_C(P):
    DI = lambda n, sh, dt: P.dram(n, sh, dt, "ExternalInput")
    xT = DI("xT", [D, NTH], F32)
    ynsaT = DI("ynsaT", [1024, NTH], BF16)
    hmlT = DI("hmlT", [1024, NTH], BF16)
    pT2 = DI("pT2", [5120, NTH], BF16)
    w_brn = DI("w_brn", [1024, D], F32)
    w_brm = DI("w_brm", [1024, D], F32)
    w_o = DI("w_o", [D, D], F32)
    w_up = DI("w_up", [D, 2 * D_FF], F32)
    w_down = DI("w_down", [D_FF, D], F32)
    modT_d = DI("modT", [128, 96], F32)
    vecs_d = DI("vecs", [128, VEC_N], F32)
    cst = DI("cst", [128, 257], F32)
    xoT = P.dram("xoT", [D, NT], F32, "ExternalOutput")

    W = 512
    NST = 3
    wb = [P.sb(f"wb{i}", [128, 8192], BF16) for i in range(NST)]
    xr = P.sb("xr", [128, KC, W], F32)
    XR = [P._track(Tile(f"xr_c{k}", None, "sb")) for k in range(KC)]
    yn = P.sb("yn", [128, 8, W], BF16)
    hm = P.sb("hm", [128, 8, W], BF16)
    yml = hm
    merged = P.sb("merged", [128, KC, W], BF16)
    h2 = merged
    u = P.sb("u", [128, NFC, W], BF16)
    abuf = [P.sb(f"abuf{i}", [128, W + 2], F32) for i in range(2)]
    carry = P.sb("carry", [128, NFC, 2], F32)
    ft = [P.sb(f"ft{i}", [128, W], F32) for i in range(6)]
    mean = P.sb("mean", [128, W], F32)
    rstd = P.sb("rstd", [128, W], F32)
    tmpm = P.sb("tmpm", [128, W], F32)
    sqt = [P.sb(f"sqt{i}", [128, W], F32) for i in range(2)]
    hsq = P.sb("hsq", [128, 2, W], BF16)
    mo = [P.sb(f"mo{i}", [128, W], BF16) for i in range(4)]
    vecs = P.sb("vecs", [128, VEC_N], F32)
    cs = P.sb("cs", [128, 257], F32)
    csb = P.sb("csb", [128, 128], BF16)
    sc2p = P.sb("sc2p", [128, KC], F32)
    ps_a = P.ps("ps_a", [128, W], F32)
    ps_b = P.ps("ps_b", [128, W], F32)
    accs = [P.ps(f"acc{i}", [128, W], F32) for i in range(4)]

    P.dma("sp", cs[:], cst[:], writes=[cs])
    P.dma("sp", vecs[:], vecs_d[:], writes=[vecs])
    P.op("dve", lambda e: e.tensor_copy(out=csb[:], in_=cs[:, 128:256]), reads=[cs], writes=[csb])
    onesN = Sub(cs, (slice(None), slice(0, 128)))
    one1 = Sub(cs, (slice(None), slice(256, 257)))
    modF = P.sb("modF", [128, 96], F32)
    P.dma("sp", modF[:], modT_d[:], writes=[modF])
    modT = Sub(modF, (slice(None), slice(32, 96)))
    P.op("dve", lambda e: e.tensor_scalar_add(out=sc2p[:], in0=modT[:, 32:48], scalar1=1.0), reads=[modT], writes=[sc2p])

    state = {"gi": 0, "fi": 0}

    def nacc():
        a = accs[state["gi"] % 4]
        state["gi"] += 1
        return a

    def nft():
        t = ft[state["fi"] % 6]
        state["fi"] += 1
        return t

    def layer_norm_inplace(Wc, gcol, bcol, out_bf=None, scale_t=None, bias_t=None):
        for kc in range(KC):
            sq = sqt[kc % 2]
            P.op("act", lambda e, sq=sq, kc=kc: e.activation(out=sq[:, 0:Wc], in_=xr[:, kc, 0:Wc], func=AF.Square), reads=[XR[kc]], writes=[sq])
            P.op("pe", lambda e, kc=kc: e.matmul(ps_a[:, 0:Wc], lhsT=onesN[:], rhs=xr[:, kc, 0:Wc], start=(kc == 0), stop=(kc == KC - 1)),
                 reads=[cs, XR[kc]], writes=[ps_a])
            P.op("pe", lambda e, kc=kc, sq=sq: e.matmul(ps_b[:, 0:Wc], lhsT=onesN[:], rhs=sq[:, 0:Wc], start=(kc == 0), stop=(kc == KC - 1)),
                 reads=[cs, sq], writes=[ps_b])
        P.op("act", lambda e: e.activation(out=mean[:, 0:Wc], in_=ps_a[:, 0:Wc], func=AF.Identity), reads=[ps_a], writes=[mean])
        P.op("dve", lambda e: e.tensor_tensor(out=tmpm[:, 0:Wc], in0=mean[:, 0:Wc], in1=mean[:, 0:Wc], op=ALU.mult), reads=[mean], writes=[tmpm])
        P.op("dve", lambda e: e.tensor_tensor(out=tmpm[:, 0:Wc], in0=ps_b[:, 0:Wc], in1=tmpm[:, 0:Wc], op=ALU.subtract), reads=[ps_b, tmpm], writes=[tmpm])
        P.op("act", lambda e: e.activation(out=tmpm[:, 0:Wc], in_=tmpm[:, 0:Wc], func=AF.Sqrt, bias=EPS), reads=[tmpm], writes=[tmpm])
        P.op("dve", lambda e: e.reciprocal(out=rstd[:, 0:Wc], in_=tmpm[:, 0:Wc]), reads=[tmpm], writes=[rstd])
        for kc in range(KC):
            t = nft()
            P.op("dve", lambda e, t=t, kc=kc: e.tensor_tensor(out=t[:, 0:Wc], in0=xr[:, kc, 0:Wc], in1=mean[:, 0:Wc], op=ALU.subtract), reads=[XR[kc], mean], writes=[t])
            P.op("dve", lambda e, t=t: e.tensor_tensor(out=t[:, 0:Wc], in0=t[:, 0:Wc], in1=rstd[:, 0:Wc], op=ALU.mult), reads=[t, rstd], writes=[t])
            if out_bf is None:
                P.op("act", lambda e, t=t, kc=kc: e.activation(out=xr[:, kc, 0:Wc], in_=t[:, 0:Wc], func=AF.Identity,
                                                               scale=vecs[:, gcol + kc:gcol + kc + 1], bias=vecs[:, bcol + kc:bcol + kc + 1]),
                     reads=[t, vecs], writes=[XR[kc]])
            else:
                P.op("act", lambda e, t=t, kc=kc: e.activation(out=out_bf[:, kc, 0:Wc], in_=t[:, 0:Wc], func=AF.Identity,
                                                               scale=scale_t[:, kc:kc + 1], bias=bias_t[:, kc:kc + 1]),
                     reads=[t, scale_t, bias_t], writes=[out_bf])

    sched = []

    def prologue_mix(c0, Wc):
        P.dma("sp", yn[:, :, 0:Wc], ynsaT.h.ap()[:, c0:c0 + Wc].rearrange("(kc p) n -> p kc n", p=128), writes=[yn])
        P.dma("sp", hm[:, :, 0:Wc], hmlT.h.ap()[:, c0:c0 + Wc].rearrange("(kc p) n -> p kc n", p=128), writes=[hm])
        for hh in range(4):
            P.op("dve", lambda e, hh=hh: e.tensor_tensor(out=hsq[:, :, 0:Wc], in0=hm[:, 2 * hh:2 * hh + 2, 0:Wc], in1=hm[:, 2 * hh:2 * hh + 2, 0:Wc], op=ALU.mult),
                 reads=[hm], writes=[hsq])
            for c in range(2):
                P.op("pe", lambda e, hh=hh, c=c: e.matmul(ps_a[:, 0:Wc], lhsT=csb[:], rhs=hm[:, 2 * hh + c, 0:Wc], start=(c == 0), stop=(c == 1)),
                     reads=[csb, hm], writes=[ps_a])
            for c in range(2):
                P.op("pe", lambda e, c=c: e.matmul(ps_b[:, 0:Wc], lhsT=csb[:], rhs=hsq[:, c, 0:Wc], start=(c == 0), stop=(c == 1)),
                     reads=[csb, hsq], writes=[ps_b])
            P.op("act", lambda e: e.activation(out=mean[:, 0:Wc], in_=ps_a[:, 0:Wc], func=AF.Identity), reads=[ps_a], writes=[mean])
            P.op("dve", lambda e: e.tensor_tensor(out=tmpm[:, 0:Wc], in0=mean[:, 0:Wc], in1=mean[:, 0:Wc], op=ALU.mult), reads=[mean], writes=[tmpm])
            P.op("dve", lambda e: e.tensor_tensor(out=tmpm[:, 0:Wc], in0=ps_b[:, 0:Wc], in1=tmpm[:, 0:Wc], op=ALU.subtract), reads=[ps_b, tmpm], writes=[tmpm])
            P.op("dve", lambda e: e.tensor_scalar_max(out=tmpm[:, 0:Wc], in0=tmpm[:, 0:Wc], scalar1=0.0), reads=[tmpm], writes=[tmpm])
            P.op("act", lambda e: e.activation(out=tmpm[:, 0:Wc], in_=tmpm[:, 0:Wc], func=AF.Sqrt, bias=EPS), reads=[tmpm], writes=[tmpm])
            P.op("dve", lambda e: e.reciprocal(out=rstd[:, 0:Wc], in_=tmpm[:, 0:Wc]), reads=[tmpm], writes=[rstd])
            for c in range(2):
                ch = 2 * hh + c
                m_ = mo[ch % 4]
                t = nft()
                sg = nft()
                P.dma("sp", m_[:, 0:Wc], pT2.h.ap()[ch * 128:(ch + 1) * 128, c0:c0 + Wc], writes=[m_])
                P.op("act", lambda e, sg=sg, m_=m_: e.activation(out=sg[:, 0:Wc], in_=m_[:, 0:Wc], func=AF.Sigmoid), reads=[m_], writes=[sg])
                P.op("dve", lambda e, t=t, ch=ch: e.tensor_tensor(out=t[:, 0:Wc], in0=hm[:, ch, 0:Wc], in1=mean[:, 0:Wc], op=ALU.subtract), reads=[hm, mean], writes=[t])
                P.op("dve", lambda e, t=t: e.tensor_tensor(out=t[:, 0:Wc], in0=t[:, 0:Wc], in1=rstd[:, 0:Wc], op=ALU.mult), reads=[t, rstd], writes=[t])
                P.op("dve", lambda e, t=t, sg=sg, ch=ch: e.scalar_tensor_tensor(out=yml[:, ch, 0:Wc], in0=t[:, 0:Wc], scalar=vecs[:, VEC_NG + ch:VEC_NG + ch + 1],
                                                                              in1=sg[:, 0:Wc], op0=ALU.mult, op1=ALU.mult),
                     reads=[t, sg, vecs], writes=[yml])


    mix_done = set()

    def add_tile(c0, Wc, halo, nxt=None):
        oc0 = c0 - 2

        def prologue():
            P.dma("sp", xr[:, :, 0:Wc], xT.h.ap()[:, c0:c0 + Wc].rearrange("(kc p) n -> p kc n", p=128), writes=XR)
            if c0 not in mix_done:
                mix_done.add(c0)
                prologue_mix(c0, Wc)

        for sgi in range(4):
            def run(wbt, sgi=sgi):
                wv = wbt.h.ap().rearrange("p (w k n) -> p w k n", w=2, k=8)
                for o4 in range(4):
                    oc = sgi * 4 + o4
                    a1, a2 = nacc(), nacc()
                    for kc in range(8):
                        P.op("pe", lambda e, a1=a1, kc=kc, o4=o4: e.matmul(a1[:, 0:Wc], lhsT=wv[:, 0, kc, o4 * 128:(o4 + 1) * 128], rhs=yn[:, kc, 0:Wc],
                                                                          start=(kc == 0), stop=(kc == 7)), reads=[wbt, yn], writes=[a1])
                    for kc in range(8):
                        P.op("pe", lambda e, a2=a2, kc=kc, o4=o4: e.matmul(a2[:, 0:Wc], lhsT=wv[:, 1, kc, o4 * 128:(o4 + 1) * 128], rhs=yml[:, kc, 0:Wc],
                                                                          start=(kc == 0), stop=(kc == 7)), reads=[wbt, yml], writes=[a2])
                    g1t, g2t, s1, s2, t1_, t2_ = mo[0 + (oc % 2) * 2], mo[1 + (oc % 2) * 2], nft(), nft(), nft(), nft()
                    P.dma("pool", g1t[:, 0:Wc], pT2.h.ap()[(8 + oc) * 128:(9 + oc) * 128, c0:c0 + Wc], writes=[g1t])
                    P.dma("pool", g2t[:, 0:Wc], pT2.h.ap()[(24 + oc) * 128:(25 + oc) * 128, c0:c0 + Wc], writes=[g2t])
                    P.op("act", lambda e, s1=s1, g1t=g1t: e.activation(out=s1[:, 0:Wc], in_=g1t[:, 0:Wc], func=AF.Sigmoid), reads=[g1t], writes=[s1])
                    P.op("act", lambda e, s2=s2, g2t=g2t: e.activation(out=s2[:, 0:Wc], in_=g2t[:, 0:Wc], func=AF.Sigmoid), reads=[g2t], writes=[s2])
                    P.op("dve", lambda e, a1=a1, s1=s1, t1_=t1_: e.tensor_tensor(out=t1_[:, 0:Wc], in0=a1[:, 0:Wc], in1=s1[:, 0:Wc], op=ALU.mult), reads=[a1, s1], writes=[t1_])
                    P.op("dve", lambda e, a2=a2, s2=s2, t2_=t2_: e.tensor_tensor(out=t2_[:, 0:Wc], in0=a2[:, 0:Wc], in1=s2[:, 0:Wc], op=ALU.mult), reads=[a2, s2], writes=[t2_])
                    P.op("dve", lambda e, t1_=t1_, t2_=t2_, oc=oc: e.tensor_tensor(out=merged[:, oc, 0:Wc], in0=t1_[:, 0:Wc], in1=t2_[:, 0:Wc], op=ALU.add),
                         reads=[t1_, t2_], writes=[merged])
            sched.append(dict(loads=[(w_brn.h.ap()[:, sgi * 512:(sgi + 1) * 512].rearrange("(kc p) n -> p kc n", p=128), 0, 8, 512, None),
                                     (w_brm.h.ap()[:, sgi * 512:(sgi + 1) * 512].rearrange("(kc p) n -> p kc n", p=128), 4096, 8, 512, None)],
                              run=run, pre=(prologue if sgi == 0 else None)))
        for sgi in range(4):
            def pre_o():
                P.op("act", lambda e: e.activation(out=xr[:, :, 0:Wc], in_=xr[:, :, 0:Wc], func=AF.Identity, scale=ALPHA), reads=XR, writes=XR)

            def run(wbt, sgi=sgi):
                wv = wbt.h.ap().rearrange("p (k n) -> p k n", k=16)
                for o4 in range(4):
                    oc = sgi * 4 + o4
                    a = nacc()
                    for kc in range(KC):
                        P.op("pe", lambda e, a=a, kc=kc, o4=o4: e.matmul(a[:, 0:Wc], lhsT=wv[:, kc, o4 * 128:(o4 + 1) * 128], rhs=merged[:, kc, 0:Wc],
                                                                        start=(kc == 0), stop=(kc == KC - 1)), reads=[wbt, merged], writes=[a])
                    P.op("dve", lambda e, a=a, oc=oc: e.scalar_tensor_tensor(out=xr[:, oc, 0:Wc], in0=a[:, 0:Wc], scalar=modT[:, oc:oc + 1], in1=xr[:, oc, 0:Wc],
                                                                            op0=ALU.mult, op1=ALU.add), reads=[a, modT, XR[oc]], writes=[XR[oc]])
            sched.append(dict(loads=[(dview(w_o, sgi * 512, (sgi + 1) * 512), 0, 16, 512, None)], run=run, pre=(pre_o if sgi == 0 else None)))
        for fp in range(NFC // 2):
            def pre_up():
                layer_norm_inplace(Wc, VEC_LG0, VEC_LB0)
                layer_norm_inplace(Wc, 0, 0, out_bf=h2, scale_t=sc2p, bias_t=Sub(modF, (slice(None), slice(48, 64))))

            def run(wbt, fp=fp):
                wv = wbt.h.ap().rearrange("p (k n) -> p k n", k=16)
                for f2 in range(2):
                    fc = fp * 2 + f2
                    aa = nacc()
                    for kc in range(KC):
                        P.op("pe", lambda e, aa=aa, kc=kc, f2=f2: e.matmul(aa[:, 0:Wc], lhsT=wv[:, kc, f2 * 128:(f2 + 1) * 128], rhs=h2[:, kc, 0:Wc],
                                                                          start=(kc == 0), stop=(kc == KC - 1)),
                             reads=[wbt, h2], writes=[aa])
                    if halo:
                        P.op("dve", lambda e, aa=aa, fc=fc: e.tensor_scalar(out=carry[:, fc, :], in0=aa[:, 0:2], scalar1=vecs[:, VEC_FLAG:VEC_FLAG + 1], scalar2=None, op0=ALU.mult),
                             reads=[aa, vecs], writes=[carry])
                        continue
                    ag = nacc()
                    for kc in range(KC):
                        P.op("pe", lambda e, ag=ag, kc=kc, f2=f2: e.matmul(ag[:, 0:Wc], lhsT=wv[:, kc, 256 + f2 * 128:256 + (f2 + 1) * 128], rhs=h2[:, kc, 0:Wc],
                                                                          start=(kc == 0), stop=(kc == KC - 1)),
                             reads=[wbt, h2], writes=[ag])
                    ab = abuf[fc % 2]
                    cv, sa = nft(), nft()
                    P.op("act", lambda e, ab=ab, aa=aa: e.activation(out=ab[:, 2:2 + Wc], in_=aa[:, 0:Wc], func=AF.Identity), reads=[aa], writes=[ab])
                    P.op("act", lambda e, ab=ab, fc=fc: e.activation(out=ab[:, 0:2], in_=carry[:, fc, :], func=AF.Identity), reads=[carry], writes=[ab])
                    cwc = VEC_CW + fc * 3
                    P.op("dve", lambda e, ab=ab, cv=cv, cwc=cwc: e.tensor_scalar(out=cv[:, 0:Wc], in0=ab[:, 0:Wc], scalar1=vecs[:, cwc:cwc + 1], scalar2=None, op0=ALU.mult),
                         reads=[ab, vecs], writes=[cv])
                    for j in (1, 2):
                        P.op("dve", lambda e, ab=ab, cv=cv, cwc=cwc, j=j: e.scalar_tensor_tensor(out=cv[:, 0:Wc], in0=ab[:, j:j + Wc], scalar=vecs[:, cwc + j:cwc + j + 1],
                                                                                                in1=cv[:, 0:Wc], op0=ALU.mult, op1=ALU.add),
                             reads=[ab, vecs, cv], writes=[cv])
                    P.op("act", lambda e, ab=ab, fc=fc: e.activation(out=carry[:, fc, :], in_=ab[:, Wc:Wc + 2], func=AF.Identity), reads=[ab], writes=[carry])
                    P.op("act", lambda e, cv=cv, sa=sa, fc=fc: e.activation(out=sa[:, 0:Wc], in_=cv[:, 0:Wc], func=AF.Silu, bias=vecs[:, VEC_CB + fc:VEC_CB + fc + 1]),
                         reads=[cv, vecs], writes=[sa])
                    P.op("dve", lambda e, sa=sa, ag=ag, fc=fc: e.tensor_tensor(out=u[:, fc, 0:Wc], in0=ag[:, 0:Wc], in1=sa[:, 0:Wc], op=ALU.mult), reads=[ag, sa], writes=[u])
            loads = [(dview(w_up, fp * 256, (fp + 1) * 256), 0, 16, 512, (0, 256))]
            if not halo:
                loads.append((dview(w_up, D_FF + fp * 256, D_FF + (fp + 1) * 256), 0, 16, 512, (256, 512)))
            sched.append(dict(loads=loads, run=run, pre=(pre_up if fp == 0 else None)))
        if halo:
            return
        for og in range(4):
            for s4 in range(4):
                def pre_d():
                    P.op("act", lambda e: e.activation(out=xr[:, :, 0:Wc], in_=xr[:, :, 0:Wc], func=AF.Identity, scale=ALPHA), reads=XR, writes=XR)
                    if nxt is not None and nxt[0] not in mix_done:
                        mix_done.add(nxt[0])
                        prologue_mix(nxt[0], nxt[1])

                def run(wbt, og=og, s4=s4):
                    wv = wbt.h.ap()[:, 0:11 * 512].rearrange("p (k n) -> p k n", k=11)
                    for o4 in range(4):
                        oc = og * 4 + o4
                        a = accs[o4]
                        for k2 in range(11):
                            fc = s4 * 11 + k2
                            P.op("pe", lambda e, a=a, k2=k2, fc=fc, o4=o4: e.matmul(a[:, 0:Wc], lhsT=wv[:, k2, o4 * 128:(o4 + 1) * 128], rhs=u[:, fc, 0:Wc],
                                                                                    start=(fc == 0), stop=(fc == NFC - 1)),
                                 reads=[wbt, u], writes=[a])
                        if s4 == 3:
                            P.op("dve", lambda e, a=a, oc=oc: e.scalar_tensor_tensor(out=xr[:, oc, 0:Wc], in0=a[:, 0:Wc], scalar=modT[:, 48 + oc:49 + oc], in1=xr[:, oc, 0:Wc],
                                                                                    op0=ALU.mult, op1=ALU.add), reads=[a, modT, XR[oc]], writes=[XR[oc]])
                    if og == 3 and s4 == 3:
                        layer_norm_inplace(Wc, VEC_LG1, VEC_LB1)
                        P.dma("sp", xoT.h.ap()[:, oc0:oc0 + Wc].rearrange("(kc p) n -> p kc n", p=128), xr[:, :, 0:Wc], reads=XR, writes=[xoT])
                r0 = s4 * 11 * 128
                sched.append(dict(loads=[(w_down.h.ap()[r0:r0 + 11 * 128, og * 512:(og + 1) * 512].rearrange("(k p) n -> p k n", p=128), 0, 11, 512, None)],
                                  run=run, pre=(pre_d if (og == 0 and s4 == 0) else None)))

    add_tile(0, 2, True)
    for tt in range(NT // W):
        nxt = (2 + (tt + 1) * W, W) if tt + 1 < NT // W else None
        add_tile(2 + tt * W, W, False, nxt)

    def issue_dma(i):
        st = sched[i]
        bt = wb[i % NST]
        f = bt.h.ap()
        for (ap, off, k, n, cols) in st["loads"]:
            dst = f[:, off:off + k * n].rearrange("p (k n) -> p k n", n=n)
            if cols is not None:
                dst = dst[:, :, cols[0]:cols[1]]
            P.dma("pool", dst, ap, writes=[bt])

    n = len(sched)
    for i in range(min(NST, n)):
        issue_dma(i)
    for i in range(n):
        if sched[i].get("pre"):
            sched[i]["pre"]()
        sched[i]["run"](wb[i % NST])
        if i + NST < n:
            issue_dma(i + NST)
    return xoT


def build_M():
    nc = bass.Bass("TRN2", target_bir_lowering=False)
    P = Prog(nc)
    cT = P.dram("cT", [128, KC], F32, "ExternalInput")
    wsl = P.dram("wsl", [D, 3072], F32, "ExternalInput")
    bsl = P.dram("bsl", [1, 3072], F32, "ExternalInput")
    one_d = P.dram("one", [1, 1], F32, "ExternalInput")
    modp = P.dram("modp", [128, 24], F32, "ExternalOutput")
    stage = [P.sb(f"st{i}", [128, KC, 512], F32) for i in range(2)]
    one1 = P.sb("one1", [1, 1], F32)
    ps_row = P.ps("ps_row", [128, 512], F32)
    ps_t = P.ps("ps_t", [128, 512], F32)
    P.dma("sp", one1[:], one_d[:], writes=[one1])
    modT = mod_vectors(P, cT, wsl, bsl, 3072, stage, ps_row, ps_t, one1, cw=512)
    P.dma("sp", modp[:], modT[:], reads=[modT])
    P.finish()
    P.emit()
    return nc


def b1_inputs(projT, cmp_pe, cmp_w1, cmp_w2, consts, h):
    g, r = h // 4, h % 4
    heads = [g * 4 + r] + [g * 4 + o for o in range(4) if o != r]
    q4 = np.stack([projT[hd * 128:(hd + 1) * 128] for hd in heads], axis=0)
    def kvrow(br, kvi):
        b0 = 1024 + ((br * 2 + kvi) * 2 + g) * 128
        return projT[b0:b0 + 128]
    gT = projT[2560 + h * 3: 2560 + h * 3 + 3]
    gat = np.ascontiguousarray(gT.T.reshape(128, 128, 3).transpose(1, 0, 2))
    m = dict(q4=np.ascontiguousarray(q4), kcmpT=np.ascontiguousarray(kvrow(0, 0)), vcmpT=np.ascontiguousarray(kvrow(0, 1)),
             kslcT=np.ascontiguousarray(kvrow(1, 0)), vslc=np.ascontiguousarray(kvrow(1, 1).T),
             kwinT=np.ascontiguousarray(kvrow(2, 0)), vwin=np.ascontiguousarray(kvrow(2, 1).T), gat=gat,
             peT=np.ascontiguousarray(cmp_pe.transpose(2, 0, 1)),
             w1=np.ascontiguousarray(cmp_w1.reshape(2, 32, 128, 256).transpose(0, 2, 1, 3)),
             w2=np.ascontiguousarray(cmp_w2.reshape(2, 2, 128, 128).transpose(2, 0, 1, 3)))
    m.update(consts)
    return m

def b2_inputs(projT, ml_conv_w, ml_conv_b, ml_gate_b, consts, i):
    hh, half = i // 2, i % 2
    QK0 = 2584; V0 = QK0 + 1024; IF0 = V0 + 1024
    m = dict(qraw=np.ascontiguousarray(projT[QK0 + hh * 128: QK0 + (hh + 1) * 128]),
             kraw=np.ascontiguousarray(projT[QK0 + 512 + hh * 128: QK0 + 512 + (hh + 1) * 128]),
             vtok=np.ascontiguousarray(projT[V0 + hh * 256 + half * 128: V0 + hh * 256 + (half + 1) * 128].T),
             gif=np.ascontiguousarray(np.stack([projT[IF0 + hh].reshape(128, 128), projT[IF0 + 4 + hh].reshape(128, 128)], axis=1)))
    cq = ml_conv_w[:, hh * 128:(hh + 1) * 128]
    ck = ml_conv_w[:, 512 + hh * 128:512 + (hh + 1) * 128]
    m["convw"] = np.ascontiguousarray(np.stack([cq.T, ck.T], axis=1)).astype(np.float32)
    m["convb"] = np.ascontiguousarray(np.stack([ml_conv_b[hh * 128:(hh + 1) * 128], ml_conv_b[512 + hh * 128:512 + (hh + 1) * 128]], axis=1)).astype(np.float32)
    m["gateb"] = np.ascontiguousarray(np.broadcast_to(np.array([ml_gate_b[hh], ml_gate_b[4 + hh]], np.float32)[None, :], (128, 2)))
    m.update(consts)
    return m

def c_consts():
    return dict(cst=np.concatenate([np.full((128, 128), 1.0 / 2048, np.float32), np.full((128, 128), 1.0 / 256, np.float32),
                                    np.ones((128, 1), np.float32)], axis=1))

def pvec(v):
    return np.ascontiguousarray(np.asarray(v, np.float32).reshape(-1, 128).T)

def c_inputs(xT_full, ynsaT, hmlT, projT, prm, l, i, consts, modT):
    t0 = i * 2048
    def halo(a):
        if i == 0:
            return np.ascontiguousarray(np.concatenate([np.zeros((a.shape[0], 2), a.dtype), a[:, 0:2048]], axis=1))
        return np.ascontiguousarray(a[:, t0 - 2:t0 + 2048])
    vecs = np.concatenate([pvec(prm["ml_norm_g"][l]), pvec(prm["ln_g"][l, 0]), pvec(prm["ln_b"][l, 0]), pvec(prm["ln_g"][l, 1]), pvec(prm["ln_b"][l, 1]),
                           np.ascontiguousarray(np.asarray(prm["ffn_conv_w"][l], np.float32).reshape(3, 44, 128).transpose(2, 1, 0)).reshape(128, 132),
                           pvec(prm["ffn_conv_b"][l]), np.full((128, 1), 0.0 if i == 0 else 1.0, np.float32)], axis=1)
    m = dict(xT=halo(xT_full), ynsaT=halo(ynsaT), hmlT=halo(hmlT), pT2=halo(projT[4640:9760]),
             w_brn=prm["w_br_nsa"][l], w_brm=prm["w_br_ml"][l], w_o=prm["w_o"][l], w_up=prm["w_up"][l], w_down=prm["w_down"][l],
             modT=modT, vecs=np.ascontiguousarray(vecs.astype(np.float32)))
    m.update(consts)
    return m


_PROGS = {}


def _prog(name):
    if name not in _PROGS:
        _PROGS[name] = {"M": build_M, "A": build_A, "B1": build_B1, "B2": build_B2, "B": build_B, "C": build_C, "CA": build_CA}[name]()
    return _PROGS[name]


def _run(name, in_maps):
    res = run_bass_kernel_spmd(_prog(name), in_maps, core_ids=list(range(NCORE)))
    return res.results


def kernel(**inp):
    prm = {k: np.asarray(v) for k, v in inp.items()}
    x = prm["x"][0]
    cT = np.ascontiguousarray(prm["c"][0].reshape(16, 128).T.astype(np.float32))
    one = np.ones((1, 1), np.float32)
    in_maps = []
    for i in range(NCORE):
        sl = slice(i * 1536, (i + 1) * 1536)
        in_maps.append(dict(cT=cT, wsl=np.ascontiguousarray(np.concatenate([prm["w_ada"][0][:, sl], prm["w_ada"][1][:, sl]], axis=1)),
                            bsl=np.ascontiguousarray(np.concatenate([prm["b_ada"][0][sl], prm["b_ada"][1][sl]])[None, :]), one=one))
    r = _run("M", in_maps)
    modT = [np.ascontiguousarray(np.concatenate([np.asarray(r[i]["modp"])[:, l * 12:(l + 1) * 12] for i in range(NCORE)], axis=1)) for l in range(2)]

    cstA = np.concatenate([np.full((128, 128), 1.0 / 2048, np.float32), np.ones((128, 1), np.float32)], axis=1)
    c1, c2, cc = b1_consts(), b2_consts(), c_consts()
    xT = np.ascontiguousarray(x.T)
    r = _run("A", [dict(xT=np.ascontiguousarray(xT[:, i * NT:(i + 1) * NT]), modT=modT[0], w_in=prm["w_in"][0], cst=cstA) for i in range(NCORE)])
    projT = np.concatenate([np.asarray(r[i]["projT"]) for i in range(NCORE)], axis=1)
    for l in range(2):
        ims = []
        for i in range(NCORE):
            m = {"n_" + k: v for k, v in b1_inputs(projT, prm["cmp_pe"][l], prm["cmp_w1"][l], prm["cmp_w2"][l], c1, i).items()}
            m.update({"m_" + k: v for k, v in b2_inputs(projT, prm["ml_conv_w"][l], prm["ml_conv_b"][l], prm["ml_gate_b"][l], c2, i).items()})
            ims.append(m)
        r = _run("B", ims)
        ynsaT = np.concatenate([np.asarray(r[h]["n_yT"]) for h in range(NCORE)], axis=0)
        hmlT = np.concatenate([np.asarray(r[i]["m_hT"]) for i in range(NCORE)], axis=0)
        cin = [c_inputs(xT, ynsaT, hmlT, projT, prm, l, i, cc, modT[l]) for i in range(NCORE)]
        if l == 0:
            ims = []
            for i in range(NCORE):
                m = {"c_" + k: v for k, v in cin[i].items()}
                m.update({"a_modT": modT[1], "a_w_in": prm["w_in"][1], "a_cst": cstA})
                ims.append(m)
            r = _run("CA", ims)
            xT = np.concatenate([np.asarray(r[i]["c_xoT"]) for i in range(NCORE)], axis=1)
            projT = np.concatenate([np.asarray(r[i]["a_projT"]) for i in range(NCORE)], axis=1)
        else:
            r = _run("C", cin)
            xT = np.concatenate([np.asarray(r[i]["xoT"]) for i in range(NCORE)], axis=1)
    return np.ascontiguousarray(xT.T)[None].astype(np.float32)


def build_B1():
    nc = bass.Bass("TRN2", target_bir_lowering=False)
    P = Prog(nc)
    body_B1(P)
    P.finish()
    P.emit()
    return nc


def build_B2():
    nc = bass.Bass("TRN2", target_bir_lowering=False)
    P = Prog(nc)
    body_B2(P)
    P.finish()
    P.emit()
    return nc


def build_B():
    nc = bass.Bass("TRN2", target_bir_lowering=False)
    P = Prog(nc)
    P.prefix = "n_"
    P.scope_begin()
    body_B1(P)
    P.scope_end()
    P.new_epoch()
    P.prefix = "m_"
    P.scope_begin()
    body_B2(P)
    P.scope_end()
    P.finish()
    P.emit()
    return nc


def build_A():
    nc = bass.Bass("TRN2", target_bir_lowering=False)
    P = Prog(nc)
    body_A(P)
    P.finish()
    P.emit()
    return nc


def build_C():
    nc = bass.Bass("TRN2", target_bir_lowering=False)
    P = Prog(nc)
    body_C(P)
    P.finish()
    P.emit()
    return nc


def build_CA():
    nc = bass.Bass("TRN2", target_bir_lowering=False)
    P = Prog(nc)
    P.prefix = "c_"
    P.scope_begin()
    xo = body_C(P)
    P.scope_end()
    P.new_epoch()
    P.prefix = "a_"
    P.scope_begin()
    body_A(P, x_src=xo)
    P.scope_end()
    P.finish()
    P.emit()
    return nc
```
